# Optimizing a Trainium2 kernel written in Bass

```python
import math
import jax
import jax.numpy as jnp
from jax import lax
import numpy as np

D_MODEL = 1024
BATCH = 8
SEQ = 2048
DEPTH = 4
DEC_BATCH = 32
DEC_SEQ = 1
PAST_LEN = 8192
PAGE_SIZE = 128

N_MIXERS = 4
PLE_DIM = 256
D_FF = 2816
ALPHA = (2 * DEPTH) ** 0.25
BETA = (8 * DEPTH) ** -0.25
LN_EPS = 1e-5

CHUNK = 128
GM_WIDTH = 2 * D_MODEL
GM_GROUPS = 8
GM_GROUP_DIM = GM_WIDTH // GM_GROUPS

SSM_D_INNER = 2 * D_MODEL
SSM_HEAD_DIM = 64
SSM_HEADS = SSM_D_INNER // SSM_HEAD_DIM
SSM_GROUPS = 4
SSM_HPG = SSM_HEADS // SSM_GROUPS
SSM_STATE = 128
SSM_CONV = 4
SSM_CONV_DIM = SSM_D_INNER + 2 * SSM_GROUPS * SSM_STATE
SSM_IN_DIM = SSM_D_INNER + SSM_CONV_DIM + SSM_HEADS
SSM_CHUNK = 128

ATT_HEADS = 16
ATT_KV_HEADS = 4
ATT_HEAD_DIM = D_MODEL // ATT_HEADS
ROPE_DIM = ATT_HEAD_DIM // 4
ROPE_THETA = 500000.0
IDX_HEADS = 8
IDX_DIM = 64
IDX_ROPE_DIM = IDX_DIM // 4
TOPK_MAX = 256
Q_BLOCK = 128
ATT_Q_DIM = ATT_HEADS * ATT_HEAD_DIM
ATT_KV_DIM = ATT_KV_HEADS * ATT_HEAD_DIM
ATT_IN_SPLITS = (ATT_Q_DIM, ATT_Q_DIM + ATT_KV_DIM, ATT_Q_DIM + 2 * ATT_KV_DIM,
                 ATT_Q_DIM + 2 * ATT_KV_DIM + IDX_HEADS * IDX_DIM,
                 ATT_Q_DIM + 2 * ATT_KV_DIM + IDX_HEADS * IDX_DIM + IDX_DIM)
ATT_IN_DIM = ATT_IN_SPLITS[-1] + IDX_HEADS

RW_HEAD = 64
RW_HEADS = D_MODEL // RW_HEAD
RW_DECAY_LORA = 64
RW_AAA_LORA = 64
RW_GATE_LORA = 128
RW_GN_EPS = 64e-5

kernel_name = 'hybrid_interleaved_decoder_step'


def layer_norm(x, g, b):
    xf = x.astype(jnp.float32)
    mu = jnp.mean(xf, -1, keepdims=True)
    var = jnp.mean(jnp.square(xf - mu), -1, keepdims=True)
    return ((xf - mu) * lax.rsqrt(var + LN_EPS)).astype(x.dtype) * g + b


def post_norm(x, f, g, b):
    return layer_norm(ALPHA * x + f, g, b)


def swiglu(x, w_up, w_down):
    a, b = jnp.split(x @ w_up, 2, axis=-1)
    return (jax.nn.silu(a) * b) @ w_down


def ple_add(x, p, w_p, w_g, b_g):
    return x + jax.nn.sigmoid(x @ w_g + b_g) * (p @ w_p)


def rope_partial(x, pos, rot_dim):
    half = rot_dim // 2
    inv = ROPE_THETA ** (-jnp.arange(half, dtype=jnp.float32) / half)
    ang = pos.astype(jnp.float32)[:, None] * inv[None, :]
    cos = jnp.cos(ang)[:, None, :]
    sin = jnp.sin(ang)[:, None, :]
    xf = x[..., :rot_dim].astype(jnp.float32)
    x1, x2 = xf[..., :half], xf[..., half:]
    rot = jnp.concatenate([x1 * cos - x2 * sin, x2 * cos + x1 * sin], axis=-1).astype(x.dtype)
    return jnp.concatenate([rot, x[..., rot_dim:]], axis=-1)


def gather_rows(rows, idx):
    return jax.vmap(lambda r, i: r[i])(rows, idx)


def gmlp_mixer(x, w_in, ln_g, ln_b, ws, bs, w_out):
    B_, T, _ = x.shape
    u, v = jnp.split(jax.nn.gelu(x @ w_in), 2, axis=-1)
    v = layer_norm(v, ln_g, ln_b)
    l = min(T, CHUNK)
    c = T // l
    mask = jnp.tril(jnp.ones((l, l), dtype=bool))
    w = jnp.where(mask, ws[:, :l, :l], 0.0)
    vc = v.reshape(B_, c, l, GM_GROUPS, GM_GROUP_DIM)
    mixed = jnp.einsum('gts,bcsgd->bctgd', w, vc) + jnp.transpose(bs[:, :l])[:, :, None]
    return (u * mixed.reshape(B_, T, GM_WIDTH)) @ w_out, v


def ssd_chunked(xs, dt, a, bm, cm, h0):
    B_, T = xs.shape[:2]
    l = min(T, SSM_CHUNK)
    c = T // l
    blk = lambda t: t.reshape((B_, c, l) + t.shape[2:])
    xdt = blk(xs.astype(jnp.float32) * dt[..., None])
    bc, cc, acum = blk(bm), blk(cm), jnp.cumsum(blk(a), axis=2)
    at = jnp.moveaxis(acum, 2, -1)
    causal = jnp.tril(jnp.ones((l, l), dtype=bool))
    seg = jnp.exp(jnp.where(causal, at[..., :, None] - at[..., None, :], -jnp.inf))
    cb = jnp.einsum('bctgn,bcsgn->bcgts', cc, bc)
    y_diag = jnp.einsum('bcgts,bcgets,bcsgep->bctgep', cb, seg, xdt)
    states = jnp.einsum('bclgn,bclge,bclgep->bcgepn', bc, jnp.exp(acum[:, :, -1:] - acum), xdt)
    chunk_decay = jnp.exp(acum[:, :, -1])

    def step(h, inp):
        dec, st = inp
        return h * dec[..., None, None] + st, h

    h_last, h_in = lax.scan(step, h0, (jnp.moveaxis(chunk_decay, 1, 0), jnp.moveaxis(states, 1, 0)))
    y_off = jnp.einsum('bctgn,bcgepn,bctge->bctgep', cc, jnp.moveaxis(h_in, 0, 1), jnp.exp(acum))
    return (y_diag + y_off).reshape(B_, T, SSM_GROUPS, SSM_HPG, SSM_HEAD_DIM), h_last


def mamba2_mixer(x, conv_state, ssm_state, w_in, conv_w, conv_b, dt_bias, a_log, d_skip, norm_g, w_out):
    B_, T, _ = x.shape
    z, xbc, dt = jnp.split(x @ w_in, [SSM_D_INNER, SSM_D_INNER + SSM_CONV_DIM], axis=-1)
    xbc_ext = jnp.concatenate([conv_state, xbc], axis=1)
    conv = conv_b
    for j in range(SSM_CONV):
        conv = conv + xbc_ext[:, j:j + T] * conv_w[j]
    xbc = jax.nn.silu(conv)
    xs, bm, cm = jnp.split(xbc, [SSM_D_INNER, SSM_D_INNER + SSM_GROUPS * SSM_STATE], axis=-1)
    xs = xs.reshape(B_, T, SSM_GROUPS, SSM_HPG, SSM_HEAD_DIM)
    bm = bm.reshape(B_, T, SSM_GROUPS, SSM_STATE)
    cm = cm.reshape(B_, T, SSM_GROUPS, SSM_STATE)
    dt = jax.nn.softplus((dt + dt_bias).astype(jnp.float32)).reshape(B_, T, SSM_GROUPS, SSM_HPG)
    a_neg = -jnp.exp(a_log.astype(jnp.float32)).reshape(SSM_GROUPS, SSM_HPG)
    h0 = ssm_state.astype(jnp.float32).reshape(B_, SSM_GROUPS, SSM_HPG, SSM_HEAD_DIM, SSM_STATE)
    y, h_last = ssd_chunked(xs, dt, dt * a_neg, bm, cm, h0)
    y = y.astype(x.dtype) + xs * d_skip.reshape(SSM_GROUPS, SSM_HPG, 1)
    yg = (y.reshape(B_, T, SSM_D_INNER) * jax.nn.silu(z)).reshape(B_, T, SSM_GROUPS, -1).astype(jnp.float32)
    yg = (yg * lax.rsqrt(jnp.mean(jnp.square(yg), -1, keepdims=True) + LN_EPS)).astype(x.dtype)
    yg = yg.reshape(B_, T, SSM_D_INNER) * norm_g
    new_ssm = h_last.reshape(B_, SSM_HEADS, SSM_HEAD_DIM, SSM_STATE).astype(ssm_state.dtype)
    return yg @ w_out, xbc_ext[:, T:], new_ssm


def dsa_project(x, pos, w_in, kn_g, kn_b):
    B_, T, _ = x.shape
    q, k, v, iq, ik, iw = jnp.split(x @ w_in, list(ATT_IN_SPLITS), axis=-1)
    q = rope_partial(q.reshape(B_, T, ATT_HEADS, ATT_HEAD_DIM), pos, ROPE_DIM)
    k = rope_partial(k.reshape(B_, T, ATT_KV_HEADS, ATT_HEAD_DIM), pos, ROPE_DIM)
    v = v.reshape(B_, T, ATT_KV_HEADS, ATT_HEAD_DIM)
    iq = rope_partial(iq.reshape(B_, T, IDX_HEADS, IDX_DIM), pos, IDX_ROPE_DIM)
    ik = rope_partial(layer_norm(ik, kn_g, kn_b)[:, :, None, :], pos, IDX_ROPE_DIM)[:, :, 0, :]
    iw = iw * (IDX_HEADS ** -0.5 * IDX_DIM ** -0.5)
    return q, k, v, iq, ik, iw


def dsa_select(iq, iw, ik, qpos, topk):
    s = jnp.einsum('bqhd,bsd->bqhs', iq, ik)
    score = jnp.einsum('bqh,bqhs->bqs', iw, jax.nn.relu(s)).astype(jnp.float32)
    adm = jnp.arange(ik.shape[1])[None, :] <= qpos[:, None]
    score = jnp.where(adm[None], score, -jnp.inf)
    _, idx = lax.top_k(score, topk)
    return idx, idx <= qpos[None, :, None]


def sparse_attend(q, k_sel, v_sel, valid):
    B_, Q = q.shape[:2]
    qg = q.reshape(B_, Q, ATT_KV_HEADS, ATT_HEADS // ATT_KV_HEADS, ATT_HEAD_DIM)
    s = jnp.einsum('bqhgd,bqkhd->bqhgk', qg, k_sel).astype(jnp.float32) * (ATT_HEAD_DIM ** -0.5)
    s = jnp.where(valid[:, :, None, None, :], s, -jnp.inf)
    p = jax.nn.softmax(s, axis=-1).astype(v_sel.dtype)
    o = jnp.einsum('bqhgk,bqkhd->bqhgd', p, v_sel)
    return o.reshape(B_, Q, ATT_Q_DIM)


def dsa_prompt(x, w_in, kn_g, kn_b, w_out):
    B_, T, _ = x.shape
    q, k, v, iq, ik, iw = dsa_project(x, jnp.arange(T), w_in, kn_g, kn_b)
    topk = min(TOPK_MAX, T // 4)

    def block(bi):
        t0 = bi * Q_BLOCK
        sl = lambda t: lax.dynamic_slice_in_dim(t, t0, Q_BLOCK, axis=1)
        qpos = t0 + jnp.arange(Q_BLOCK)
        idx, valid = dsa_select(sl(iq), sl(iw), ik, qpos, topk)
        return sparse_attend(sl(q), gather_rows(k, idx), gather_rows(v, idx), valid)

    o = lax.map(block, jnp.arange(T // Q_BLOCK))
    o = jnp.moveaxis(o, 0, 1).reshape(B_, T, ATT_Q_DIM)
    return o @ w_out, k, v, ik


def dsa_sample(x, cache_k, cache_v, cache_idx_k, page_table, w_in, kn_g, kn_b, w_out):
    B_, T, _ = x.shape
    past = page_table.shape[1] * cache_k.shape[1]
    pos = past + jnp.arange(T)
    q, k, v, iq, ik, iw = dsa_project(x, pos, w_in, kn_g, kn_b)
    ik_all = jnp.concatenate([cache_idx_k[page_table].reshape(B_, past, IDX_DIM), ik], axis=1)
    idx, valid = dsa_select(iq, iw, ik_all, pos, min(TOPK_MAX, (past + T) // 4))
    past_idx = jnp.minimum(idx, past - 1)
    phys = jnp.take_along_axis(page_table, (past_idx // PAGE_SIZE).reshape(B_, -1), axis=1).reshape(idx.shape)
    off = past_idx % PAGE_SIZE
    new_idx = jnp.clip(idx - past, 0, T - 1)
    is_new = (idx >= past)[..., None, None]
    k_sel = jnp.where(is_new, gather_rows(k, new_idx), cache_k[phys, off])
    v_sel = jnp.where(is_new, gather_rows(v, new_idx), cache_v[phys, off])
    o = sparse_attend(q, k_sel, v_sel, valid)
    return o @ w_out, k, v, ik


def rwkv7_mixer(x, shift, wkv, mu, w_r, w_k, w_v, w_o, w0, w1, w2, a0, a1, a2, g1, g2,
                k_k, k_a, r_k, gn_g, gn_b):
    B_, T, _ = x.shape
    x_prev = jnp.concatenate([shift[:, None, :], x[:, :-1]], axis=1)
    xm = x[None] + (x_prev - x)[None] * mu[:, None, None, :]
    xr, xw, xk, xv, xa, xg = xm
    r = xr @ w_r
    w_log = -jax.nn.softplus(-(w0 + jnp.tanh(xw @ w1) @ w2)) - 0.5
    k = xk @ w_k
    v = xv @ w_v
    a = jax.nn.sigmoid(a0 + (xa @ a1) @ a2)
    g = jax.nn.sigmoid(xg @ g1) @ g2
    heads = lambda t: t.reshape(B_, T, RW_HEADS, RW_HEAD)
    kk = heads(k * k_k).astype(jnp.float32)
    kk = kk / jnp.maximum(jnp.sqrt(jnp.sum(kk * kk, -1, keepdims=True)), 1e-12)
    k = k * (1.0 + (a - 1.0) * k_a)
    decay = jnp.exp(-jnp.exp(w_log.astype(jnp.float32)))
    r, k, v, a, decay = heads(r), heads(k), heads(v), heads(a), heads(decay)
    seq = tuple(jnp.moveaxis(t.astype(jnp.float32), 1, 0) for t in (r, decay, k, v, kk, kk * a))

    def step(s, inp):
        r_t, d_t, k_t, v_t, kk_t, b_t = inp
        sa = jnp.einsum('bhij,bhj->bhi', s, kk_t)
        s = s * d_t[:, :, None, :] - sa[..., None] * b_t[:, :, None, :] + v_t[..., None] * k_t[:, :, None, :]
        return s, jnp.einsum('bhij,bhj->bhi', s, r_t)

    s_last, y = lax.scan(step, wkv.astype(jnp.float32), seq)
    y = jnp.moveaxis(y, 0, 1)
    mu_y = jnp.mean(y, -1, keepdims=True)
    var_y = jnp.mean(jnp.square(y - mu_y), -1, keepdims=True)
    yn = ((y - mu_y) * lax.rsqrt(var_y + RW_GN_EPS)).reshape(B_, T, D_MODEL).astype(x.dtype) * gn_g + gn_b
    bonus = (jnp.sum(r * k * r_k, -1, keepdims=True) * v).reshape(B_, T, D_MODEL)
    return ((yn + bonus) * g) @ w_o, x[:, -1], s_last.astype(wkv.dtype)


def setup_inputs(seed: int = 0) -> dict:
    key = jax.random.key(seed)
    keys = jax.random.split(key, 96)
    counter = [0]

    def nk():
        counter[0] += 1
        return keys[counter[0] - 1]

    f32 = jnp.float32

    def nrm(shape, scale=1.0):
        return jax.random.normal(nk(), shape, f32) * scale

    def gain(shape):
        return 1.0 + nrm(shape, 0.02)

    D = D_MODEL
    n_pages = PAST_LEN // PAGE_SIZE
    n_pool = (5 * DEC_BATCH * n_pages) // 4
    page_table = jax.random.permutation(nk(), n_pool)[: DEC_BATCH * n_pages].reshape(DEC_BATCH, n_pages).astype(jnp.int32)

    ssm_w_in = nrm((D, SSM_IN_DIM), D ** -0.5)
    ssm_w_in = ssm_w_in.at[:, -SSM_HEADS:].multiply(0.1)
    dt0 = jnp.exp(jax.random.uniform(nk(), (SSM_HEADS,), f32, math.log(1e-3), math.log(1e-1)))

    return {
        'x_prompt': nrm((BATCH, SEQ, D)),
        'x_sample': nrm((DEC_BATCH, DEC_SEQ, D)),
        'state_ssm_conv': nrm((DEC_BATCH, SSM_CONV - 1, SSM_CONV_DIM)),
        'state_ssm': nrm((DEC_BATCH, SSM_HEADS, SSM_HEAD_DIM, SSM_STATE), 0.1),
        'cache_k': nrm((n_pool, PAGE_SIZE, ATT_KV_HEADS, ATT_HEAD_DIM)),
        'cache_v': nrm((n_pool, PAGE_SIZE, ATT_KV_HEADS, ATT_HEAD_DIM)),
        'cache_idx_k': nrm((n_pool, PAGE_SIZE, IDX_DIM)),
        'state_rwkv_shift': nrm((DEC_BATCH, D)),
        'state_rwkv_wkv': nrm((DEC_BATCH, RW_HEADS, RW_HEAD, RW_HEAD), 0.1),
        'page_table': page_table,
        'p_prompt': nrm((DEPTH, BATCH, SEQ, PLE_DIM)),
        'p_sample': nrm((DEPTH, DEC_BATCH, DEC_SEQ, PLE_DIM)),
        'ln_g': gain((DEPTH, 3, D)),
        'ln_b': nrm((DEPTH, 3, D), 0.02),
        'ffn_w_up': nrm((DEPTH, 2, D, 2 * D_FF), D ** -0.5),
        'ffn_w_down': nrm((DEPTH, 2, D_FF, D), BETA * D_FF ** -0.5),
        'ple_w_p': nrm((DEPTH, PLE_DIM, D), PLE_DIM ** -0.5),
        'ple_w_g': nrm((DEPTH, D, D), D ** -0.5),
        'ple_b_g': nrm((DEPTH, D), 0.02),
        'gm_w_in': nrm((D, 2 * GM_WIDTH), D ** -0.5),
        'gm_ln_g': gain((GM_WIDTH,)),
        'gm_ln_b': nrm((GM_WIDTH,), 0.02),
        'gm_ws': nrm((GM_GROUPS, CHUNK, CHUNK), CHUNK ** -0.5),
        'gm_bs': 1.0 + nrm((GM_GROUPS, CHUNK), 0.1),
        'gm_w_out': nrm((GM_WIDTH, D), BETA * GM_WIDTH ** -0.5),
        'ssm_w_in': ssm_w_in,
        'ssm_conv_w': nrm((SSM_CONV, SSM_CONV_DIM), SSM_CONV ** -0.5),
        'ssm_conv_b': nrm((SSM_CONV_DIM,), 0.02),
        'ssm_dt_bias': dt0 + jnp.log(-jnp.expm1(-dt0)),
        'ssm_a_log': jnp.log(jax.random.uniform(nk(), (SSM_HEADS,), f32, 1.0, 16.0)),
        'ssm_d': 1.0 + nrm((SSM_HEADS,), 0.1),
        'ssm_norm_g': gain((SSM_D_INNER,)),
        'ssm_w_out': nrm((SSM_D_INNER, D), BETA * SSM_D_INNER ** -0.5),
        'att_w_in': nrm((D, ATT_IN_DIM), D ** -0.5),
        'att_kn_g': gain((IDX_DIM,)),
        'att_kn_b': nrm((IDX_DIM,), 0.02),
        'att_w_out': nrm((ATT_Q_DIM, D), BETA * ATT_Q_DIM ** -0.5),
        'rw_mu': jax.random.uniform(nk(), (6, D), f32),
        'rw_w_r': nrm((D, D), D ** -0.5),
        'rw_w_k': nrm((D, D), D ** -0.5),
        'rw_w_v': nrm((D, D), D ** -0.5),
        'rw_w_o': nrm((D, D), BETA * D ** -0.5),
        'rw_w0': jax.random.uniform(nk(), (D,), f32, -6.0, 1.0),
        'rw_w1': nrm((D, RW_DECAY_LORA), D ** -0.5),
        'rw_w2': nrm((RW_DECAY_LORA, D), 0.5 * RW_DECAY_LORA ** -0.5),
        'rw_a0': nrm((D,), 0.1),
        'rw_a1': nrm((D, RW_AAA_LORA), D ** -0.5),
        'rw_a2': nrm((RW_AAA_LORA, D), 0.5 * RW_AAA_LORA ** -0.5),
        'rw_g1': nrm((D, RW_GATE_LORA), D ** -0.5),
        'rw_g2': nrm((RW_GATE_LORA, D), RW_GATE_LORA ** -0.5),
        'rw_k_k': 0.85 + nrm((D,), 0.02),
        'rw_k_a': 1.0 + nrm((D,), 0.02),
        'rw_r_k': nrm((RW_HEADS, RW_HEAD), 0.1),
        'rw_gn_g': gain((D,)),
        'rw_gn_b': nrm((D,), 0.02),
    }


def reference(x_prompt, x_sample, state_ssm_conv, state_ssm, cache_k, cache_v, cache_idx_k,
              state_rwkv_shift, state_rwkv_wkv, page_table, p_prompt, p_sample,
              ln_g, ln_b, ffn_w_up, ffn_w_down, ple_w_p, ple_w_g, ple_b_g,
              gm_w_in, gm_ln_g, gm_ln_b, gm_ws, gm_bs, gm_w_out,
              ssm_w_in, ssm_conv_w, ssm_conv_b, ssm_dt_bias, ssm_a_log, ssm_d, ssm_norm_g, ssm_w_out,
              att_w_in, att_kn_g, att_kn_b, att_w_out,
              rw_mu, rw_w_r, rw_w_k, rw_w_v, rw_w_o, rw_w0, rw_w1, rw_w2, rw_a0, rw_a1, rw_a2,
              rw_g1, rw_g2, rw_k_k, rw_k_a, rw_r_k, rw_gn_g, rw_gn_b):
    B_ = x_prompt.shape[0]

    def ffn_sub(x, i, j):
        return post_norm(x, 0.5 * swiglu(x, ffn_w_up[i, j], ffn_w_down[i, j]), ln_g[i, 2 * j], ln_b[i, 2 * j])

    yp, ys = x_prompt, x_sample
    for i in range(DEPTH):
        yp, ys = ffn_sub(yp, i, 0), ffn_sub(ys, i, 0)
        m = i % N_MIXERS
        if m == 0:
            hp, _ = gmlp_mixer(yp, gm_w_in, gm_ln_g, gm_ln_b, gm_ws, gm_bs, gm_w_out)
            hs, gm_v_s = gmlp_mixer(ys, gm_w_in, gm_ln_g, gm_ln_b, gm_ws, gm_bs, gm_w_out)
        elif m == 1:
            ssm_args = (ssm_w_in, ssm_conv_w, ssm_conv_b, ssm_dt_bias, ssm_a_log, ssm_d, ssm_norm_g, ssm_w_out)
            hp, conv_p, ssm_p = mamba2_mixer(
                yp, jnp.zeros((B_, SSM_CONV - 1, SSM_CONV_DIM), yp.dtype),
                jnp.zeros((B_, SSM_HEADS, SSM_HEAD_DIM, SSM_STATE), yp.dtype), *ssm_args)
            hs, conv_s, ssm_s = mamba2_mixer(ys, state_ssm_conv, state_ssm, *ssm_args)
        elif m == 2:
            hp, k_p, v_p, ik_p = dsa_prompt(yp, att_w_in, att_kn_g, att_kn_b, att_w_out)
            hs, k_s, v_s, ik_s = dsa_sample(ys, cache_k, cache_v, cache_idx_k, page_table,
                                            att_w_in, att_kn_g, att_kn_b, att_w_out)
        else:
            rw_args = (rw_mu, rw_w_r, rw_w_k, rw_w_v, rw_w_o, rw_w0, rw_w1, rw_w2, rw_a0, rw_a1, rw_a2,
                       rw_g1, rw_g2, rw_k_k, rw_k_a, rw_r_k, rw_gn_g, rw_gn_b)
            hp, sh_p, wkv_p = rwkv7_mixer(
                yp, jnp.zeros((B_, D_MODEL), yp.dtype),
                jnp.zeros((B_, RW_HEADS, RW_HEAD, RW_HEAD), yp.dtype), *rw_args)
            hs, sh_s, wkv_s = rwkv7_mixer(ys, state_rwkv_shift, state_rwkv_wkv, *rw_args)
        yp = post_norm(yp, hp, ln_g[i, 1], ln_b[i, 1])
        ys = post_norm(ys, hs, ln_g[i, 1], ln_b[i, 1])
        yp, ys = ffn_sub(yp, i, 1), ffn_sub(ys, i, 1)
        yp = ple_add(yp, p_prompt[i], ple_w_p[i], ple_w_g[i], ple_b_g[i])
        ys = ple_add(ys, p_sample[i], ple_w_p[i], ple_w_g[i], ple_b_g[i])

    return (yp, ys, gm_v_s, conv_p, ssm_p, conv_s, ssm_s, k_p, v_p, ik_p, k_s, v_s, ik_s,
            sh_p, wkv_p, sh_s, wkv_s)
```

```python
import numpy as np
import concourse.bass as bass
import concourse.mybir as mybir
from concourse.bass_utils import run_bass_kernel_spmd

F32 = mybir.dt.float32
BF16 = mybir.dt.bfloat16
I32 = mybir.dt.int32
AF = mybir.ActivationFunctionType
ALU = mybir.AluOpType
AX = mybir.AxisListType

NCORES = 8
D = 1024
KC = 8
SEQ = 2048
NS = 4
NCOL = SEQ + NS
DEPTH = 4
DFF = 2816
NFF = 22
PLE = 256
ALPHA = (2 * DEPTH) ** 0.25
LN_EPS = 1e-5
TILES = [(0, 512), (512, 512), (1024, 512), (1536, 512), (2048, NS)]

SB_BASE = 16512
SB_END = 229376


class Buf:
    __slots__ = ("name", "w", "r")

    def __init__(self, name):
        self.name = name
        self.w = None
        self.r = {}


class Op:
    __slots__ = ("fn", "waits", "flag", "clock", "dma")

    def __init__(self, fn, waits, clock, dma):
        self.fn = fn
        self.waits = waits
        self.flag = False
        self.clock = clock
        self.dma = dma


ENGS = ("pe", "act", "dve", "pool", "sp")
SEM_LIMIT = 8000
N_DMA_SEMS = 10


class Prog:
    def __init__(self, nc):
        self.nc = nc
        self.ops = {e: [] for e in ENGS}
        self.clock = {e: {} for e in ENGS}
        self.dma_cnt = {}
        self.dma_rr = {q: 0 for q in ENGS}
        self.direct = {e: {} for e in ENGS}

    def _deps(self, eng, reads, writes):
        deps = set()
        for b in reads:
            if b.w is not None:
                deps.add(b.w)
        for b in writes:
            if b.w is not None and (b.w[0] != eng or eng != "pe"):
                deps.add(b.w)
            for k, s in b.r.items():
                if k != eng or eng != "pe":
                    deps.add((k, s))
        return deps

    def _add(self, eng, fn, deps, dma=None, force=False):
        clk = self.clock[eng]
        waits = []
        for (k, s) in sorted(deps, key=lambda t: (str(t[0]), t[1])):
            if clk.get(k, 0) >= s and not (force and isinstance(k, str) and self.direct[eng].get(k, 0) < s):
                continue
            if isinstance(k, str):
                self.direct[eng][k] = max(self.direct[eng].get(k, 0), s)
            waits.append((k, s))
            clk[k] = max(clk.get(k, 0), s)
            if isinstance(k, str):
                op = self.ops[k][s - 1]
                op.flag = True
                for kk, ss in op.clock.items():
                    if clk.get(kk, 0) < ss:
                        clk[kk] = ss
        seq = len(self.ops[eng]) + 1
        self.ops[eng].append(Op(fn, waits, dict(clk), dma))
        return seq

    def op(self, eng, fn, reads=(), writes=()):
        deps = self._deps(eng, reads, writes)
        seq = self._add(eng, fn, deps)
        for b in writes:
            b.w = (eng, seq)
            b.r = {}
        for b in reads:
            b.r[eng] = seq
        return seq

    def dma(self, q, fn, reads=(), writes=()):
        deps = self._deps(("dma", q, -1), reads, writes)
        j = self.dma_rr[q]
        self.dma_rr[q] = (j + 1) % N_DMA_SEMS
        key = ("dma", q, j)
        c = self.dma_cnt.get(key, 0) + 1
        self.dma_cnt[key] = c
        if c > 1:
            deps.add((key, c - 1))
        self._add(q, fn, deps, dma=(key, c))
        for b in writes:
            b.w = (key, c)
            b.r = {}
        for b in reads:
            b.r[key] = c

    def barrier(self):
        deps = set()
        for e in ENGS:
            for idx in range(len(self.ops[e]), 0, -1):
                o = self.ops[e][idx - 1]
                if o.fn is not None and o.dma is None:
                    deps.add((e, idx))
                    break
        for key, c in self.dma_cnt.items():
            deps.add((key, c))
        for e in ENGS:
            self._add(e, None, set(d for d in deps if d[0] != e), force=True)

    def emit(self):
        nc = self.nc
        import contextlib
        with contextlib.ExitStack() as st:
            sems = {}
            pref = {}
            for e in ENGS:
                cnt = 0
                p = []
                for o in self.ops[e]:
                    if o.flag:
                        cnt += 1
                    p.append(cnt)
                pref[e] = p
                nsem = max(1, (cnt + SEM_LIMIT - 1) // SEM_LIMIT)
                sems[e] = [st.enter_context(nc.semaphore(f"s_{e}_{i}")) for i in range(nsem)]
            dsem = {}
            for key in self.dma_cnt:
                dsem[key] = st.enter_context(nc.semaphore(f"d_{key[1]}_{key[2]}"))

            def resolve(k, s):
                if isinstance(k, str):
                    c = pref[k][s - 1]
                    return sems[k][(c - 1) // SEM_LIMIT], (c - 1) % SEM_LIMIT + 1
                return dsem[k], 16 * s

            block = st.enter_context(nc.Block())

            def run(e, name):
                p = pref[name]
                for i, o in enumerate(self.ops[name]):
                    for (k, s) in o.waits:
                        sem, val = resolve(k, s)
                        e.wait_ge(sem, val)
                    if o.fn is None:
                        continue
                    ins = o.fn(e)
                    if o.dma is not None:
                        ins.then_inc(dsem[o.dma[0]], 16)
                    elif o.flag:
                        c = p[i]
                        ins.then_inc(sems[name][(c - 1) // SEM_LIMIT], 1)

            if self.ops["pe"]:
                @block.tensor
                def _(e):
                    run(e, "pe")

            if self.ops["act"]:
                @block.scalar
                def _(e):
                    run(e, "act")

            if self.ops["dve"]:
                @block.vector
                def _(e):
                    run(e, "dve")

            if any(o.fn is not None for o in self.ops["pool"]):
                @block.gpsimd
                def _(e):
                    run(e, "pool")

            @block.sync
            def _(e):
                run(e, "sp")
                for key, c in self.dma_cnt.items():
                    e.wait_ge(dsem[key], 16 * c)


class SB:
    def __init__(self, nc):
        self.nc = nc
        self.off = SB_BASE
        self.n = 0

    def alloc(self, shape, dtype, parts=128):
        nbytes = int(np.prod(shape[1:])) * (2 if dtype == BF16 else 4)
        off = (self.off + 63) // 64 * 64
        assert off + nbytes <= SB_END, f"SBUF overflow: need {off + nbytes - SB_END} more bytes"
        self.n += 1
        t = self.nc.alloc_sbuf_tensor_at(f"t{self.n}", list(shape), dtype, offset=off)
        self.off = off + nbytes
        return t.ap()

    def mark(self):
        return self.off

    def release(self, m):
        self.off = m


def build(stop_after=99, only=None, dbg=99, mixers=("gm", "ssm", "att", "rw", "rwp"), only_mixer=None):
    nc = bass.Bass("TRN2", target_bir_lowering=False)
    P = Prog(nc)
    sb = SB(nc)

    in_names = []
    dbg_names = {}

    def din(name, shape, dt=F32):
        if only is not None and name not in only:
            return None
        in_names.append(name)
        return nc.dram_tensor(name, list(shape), dt, kind="ExternalInput").ap()

    out_names = []

    def dout(name, shape, dt=F32):
        out_names.append(name)
        return nc.dram_tensor(name, list(shape), dt, kind="ExternalOutput").ap()

    xp_d = din("xp", [SEQ, D])
    xs_d = din("xs", [NS, D])
    pp_d = din("pp", [DEPTH, SEQ, PLE])
    ps_d = din("psm", [DEPTH, NS, PLE])
    ln_g_d = din("ln_g", [DEPTH * 3 * KC, 128])
    ln_b_d = din("ln_b", [DEPTH * 3 * KC, 128])
    w_up_d = din("ffn_w_up", [DEPTH, 2, D, 2 * DFF])
    w_dn_d = din("ffn_w_down", [DEPTH, 2, DFF, D])
    ple_wp_d = din("ple_w_p", [DEPTH, PLE, D])
    ple_wg_d = din("ple_w_g", [DEPTH, D, D])
    ple_bg_d = din("ple_b_g", [DEPTH * KC, 128])
    gm_w_in_d = din("gm_w_in", [D, 4096])
    gm_w_out_d = din("gm_w_out", [2048, D])
    gm_lng_c_d = din("gm_lng_c", [16, 128])
    gm_lnb_c_d = din("gm_lnb_c", [16, 128])
    gm_wT_d = din("gm_wT", [128, 8, 128])
    gm_bs_bc_d = din("gm_bs_bc", [128, 8, 128])
    gm_w00_d = din("gm_w00", [128, 8])
    gm_bs0_d = din("gm_bs0", [128, 8])
    gm_lng_bc_d = din("gm_lng_bc", [32, 2048])
    gm_lnb_bc_d = din("gm_lnb_bc", [32, 2048])
    ssm_w_in_d = din("ssm_w_in", [D, 5152])
    ssm_w_out_d = din("ssm_w_out", [2048, D])
    ssm_cw_c_d = din("ssm_cw_c", [96, 128])
    ssm_cb_c_d = din("ssm_cb_c", [24, 128])
    ssm_ng_c_d = din("ssm_ng_c", [16, 128])
    ssm_hv_bc_d = din("ssm_hv_bc", [128, 3, 32])
    ssm_cw_bc_d = din("ssm_cw_bc", [32, 4, 3072])
    ssm_cb_bc_d = din("ssm_cb_bc", [32, 3072])
    ssm_d_c_d = din("ssm_d_c", [16, 128])
    ssm_sel_d = din("ssm_sel", [128, NS, 128])
    st_conv_d = din("st_conv", [NS, 3, 3072])
    st_ssm_d = din("st_ssm", [NS, 2048, 128])
    att_w_in_d = din("att_w_in", [D, 2120])
    att_w_out_d = din("att_w_out", [D, D])
    att_kn_bc_d = din("att_kn_bc", [128, 2, 64])
    cache_k_d = din("cache_k", [2560 * 128, 256])
    cache_v_d = din("cache_v", [2560 * 128, 256])
    cache_ik_d = din("cache_ik", [2560 * 128, 64])
    pt_bc_d = din("pt_bc", [128, NS * 64], I32)
    piota_d = din("piota", [128, 1])
    negp_d = din("negp", [128, 1])
    scr_d = nc.dram_tensor("scr", [NS, 65 * 128], F32, kind="Internal").ap()
    rw_mu_c_d = din("rw_mu_c", [48, 128])
    rw_gn_c_d = din("rw_gn_c", [16, 128])
    rw_vec_bc_d = din("rw_vec_bc", [32, 5, D])
    rw_blk_d = din("rw_blk", [128, 128])
    rw_w_d = {n: din("rw_w_" + n, [D, D]) for n in ("r", "k", "v", "o")}
    rw_w1_d = din("rw_w1", [D, 64])
    rw_w2_d = din("rw_w2", [64, D])
    rw_a1_d = din("rw_a1", [D, 64])
    rw_a2_d = din("rw_a2", [64, D])
    rw_g1_d = din("rw_g1", [D, 128])
    rw_g2_d = din("rw_g2", [128, D])
    rw_vec128_d = din("rw_vec128", [128, 7, D])
    rw_mbd_d = din("rw_mbd", [16, D])
    scr_h_d = nc.dram_tensor("scr_h", [3, SEQ, D], BF16, kind="Internal").ap()
    scr_y_d = nc.dram_tensor("scr_y", [SEQ, D], F32, kind="Internal").ap()
    st_shift_d = din("st_shift", [NS, D])
    st_wkv_d = din("st_wkv", [NS, 16, 64, 64])
    negm_d = din("negm", [128, 128])
    rope_d = din("rope", [128, 17, 16])
    m1_d = din("m1", [128, 128])
    onesf_d = din("onesf", [128, 128])
    ident_d = din("ident", [128, 128])
    cmask_d = din("cmask", [128, 128])

    yp_d = dout("yp", [SEQ, D])
    ys_d = dout("ys", [NS, D])
    gmv_d = dout("gmv", [NS, 2048])
    conv_p_d = dout("conv_p", [3, 3072])
    ssm_p_d = dout("ssm_p", [32, 64, 128])
    conv_s_d = dout("conv_s", [NS, 3, 3072])
    ssm_s_d = dout("ssm_s", [NS, 32, 64, 128])
    k_p_d = dout("k_p", [SEQ, 256])
    v_p_d = dout("v_p", [SEQ, 256])
    ik_p_d = dout("ik_p", [SEQ, 64])
    k_s_d = dout("k_s", [NS, 256])
    v_s_d = dout("v_s", [NS, 256])
    ik_s_d = dout("ik_s", [NS, 64])
    sh_p_d = dout("sh_p", [1, D])
    wkv_p_d = dout("wkv_p", [16, 64, 64])
    sh_s_d = dout("sh_s", [NS, D])
    wkv_s_d = dout("wkv_s", [NS, 16, 64, 64])

    x32 = sb.alloc([128, KC, NCOL], F32)
    xbf = sb.alloc([128, KC, NCOL], BF16)
    ident = sb.alloc([128, 128], F32)
    identb = sb.alloc([128, 128], BF16)
    ones_bf = sb.alloc([128, 128], BF16)
    lng = sb.alloc([128, DEPTH * 3 * KC], F32)
    lnb = sb.alloc([128, DEPTH * 3 * KC], F32)
    plebg = sb.alloc([128, DEPTH * KC], F32)
    epsc = sb.alloc([128, 1], F32)
    psum = nc.alloc_psum_tensor("psum", [128, 8 * 512], F32).ap()

    def bank(b, w=512, n=1):
        return psum[:, b * 512:b * 512 + w] if n == 1 else psum[:, b * 512:(b + n) * 512]

    B_x32 = [[Buf(f"x32_{m}_{t}") for t in range(5)] for m in range(KC)]
    B_xbf = [[Buf(f"xbf_{m}_{t}") for t in range(5)] for m in range(KC)]
    B_ps = [Buf(f"ps{b}") for b in range(8)]
    B_const = Buf("const")
    B_const2 = Buf("const2")
    B_cols = Buf("cols")

    P.dma("sp", lambda e: e.dma_start(out=ident[:], in_=ident_d[:, :]), writes=[B_const])
    P.op("act", lambda e: e.activation(out=identb[:], in_=ident[:], func=AF.Copy), reads=[B_const], writes=[B_const2])
    P.op("dve", lambda e: e.memset(ones_bf[:], 1.0 / D), reads=[B_const2], writes=[B_const2])
    P.op("dve", lambda e: e.memset(epsc[:], LN_EPS / (ALPHA * ALPHA)), reads=[B_const2], writes=[B_const2])

    def load_cols(dst, src_d, rows, stage, B_stage):
        for r0 in range(0, rows, 128):
            r = min(128, rows - r0)
            if r < 128:
                P.op("dve", lambda e: e.memset(stage[:, :], 0.0), writes=[B_stage])
            P.dma("sp", lambda e, r0=r0, r=r: e.dma_start(out=stage[:r, :], in_=src_d[r0:r0 + r, :]), writes=[B_stage])
            P.op("pe", lambda e: e.transpose(bank(7, 128), stage[:, :], ident[:, :]), reads=[B_stage, B_const], writes=[B_ps[7]])
            P.op("dve", lambda e, r0=r0, r=r: e.tensor_copy(out=dst[:, r0:r0 + r], in_=bank(7, r)), reads=[B_ps[7], B_cols], writes=[B_cols])

    m0 = sb.mark()
    stage = sb.alloc([128, 128], F32)
    B_stage = Buf("stage")
    if dbg >= 2:
        load_cols(lng, ln_g_d, DEPTH * 3 * KC, stage, B_stage)
        load_cols(lnb, ln_b_d, DEPTH * 3 * KC, stage, B_stage)
        load_cols(plebg, ple_bg_d, DEPTH * KC, stage, B_stage)

    def load_tm_to_fm(src_d, ntok, col0, feat_chunks, dst32, dstbf, Bd32, Bdbf, feat_w=128):
        F = feat_chunks * 128
        tin = sb.alloc([128, 2, F], F32)
        Bt = [Buf("tin0"), Buf("tin1")]
        k = 0
        for t0 in range(0, ntok, 128):
            n = min(128, ntok - t0)
            s = k % 2
            if n < 128:
                P.op("dve", lambda e, s=s: e.memset(tin[:, s, :], 0.0), writes=[Bt[s]])
            P.dma("sp", lambda e, t0=t0, n=n, s=s: e.dma_start(out=tin[:n, s, :], in_=src_d[t0:t0 + n, :]), writes=[Bt[s]])
            for c in range(feat_chunks):
                b = 4 + (k * feat_chunks + c) % 4
                P.op("pe", lambda e, s=s, c=c, b=b: e.transpose(bank(b, 128), tin[:, s, c * 128:(c + 1) * 128], ident[:, :]),
                     reads=[Bt[s], B_const], writes=[B_ps[b]])
                col = col0 + t0
                ti = min(col // 512, 4)
                wr = []
                if dst32 is not None:
                    wr.append(Bd32[c][ti])
                    P.op("dve", lambda e, c=c, col=col, n=n, b=b: e.tensor_copy(out=dst32[:, c, col:col + n], in_=bank(b, n)),
                         reads=[B_ps[b]], writes=[Bd32[c][ti]])
                if dst32 is not None:
                    P.op("act", lambda e, c=c, col=col, n=n: e.activation(out=dstbf[:, c, col:col + n], in_=dst32[:, c, col:col + n], func=AF.Copy),
                         reads=[Bd32[c][ti]], writes=[Bdbf[c][ti]])
                else:
                    P.op("act", lambda e, c=c, col=col, n=n, b=b: e.activation(out=dstbf[:, c, col:col + n], in_=bank(b, n), func=AF.Copy),
                         reads=[B_ps[b]], writes=[Bdbf[c][ti]])
            k += 1

    m1 = sb.mark()
    if dbg >= 3:
        load_tm_to_fm(xp_d, SEQ, 0, KC, x32, xbf, B_x32, B_xbf)
    if dbg >= 4:
        load_tm_to_fm(xs_d, NS, SEQ, KC, x32, xbf, B_x32, B_xbf)
    sb.release(m1)
    sb.release(m0)
    P.barrier()

    def layer_norm(gi, ln):
        zb, zsq, st = ln["zb"], ln["zsq"], ln["st"]
        for ti, (c0, w) in enumerate(TILES):
            for m in range(KC):
                P.op("act", lambda e, m=m, c0=c0, w=w: e.activation(out=zb[:, m, :w], in_=x32[:, m, c0:c0 + w], func=AF.Copy),
                     reads=[B_x32[m][ti]], writes=[ln["Bzb"][m]])
                P.op("act", lambda e, m=m, c0=c0, w=w: e.activation(out=zsq[:, m, :w], in_=x32[:, m, c0:c0 + w], func=AF.Square),
                     reads=[B_x32[m][ti]], writes=[ln["Bzsq"][m]])
            for m in range(KC):
                P.op("pe", lambda e, m=m, w=w: e.matmul(bank(6, w), ones_bf[:], zb[:, m, :w], start=(m == 0), stop=(m == KC - 1)),
                     reads=[ln["Bzb"][m], B_const2], writes=[B_ps[6]])
            for m in range(KC):
                P.op("pe", lambda e, m=m, w=w: e.matmul(bank(7, w), ones_bf[:], zsq[:, m, :w], start=(m == 0), stop=(m == KC - 1)),
                     reads=[ln["Bzsq"][m], B_const2], writes=[B_ps[7]])
            mean, var, rstd, mr = st[:, 0, :w], st[:, 1, :w], st[:, 2, :w], st[:, 3, :w]
            Bs = ln["Bst"]
            P.op("act", lambda e, w=w, mean=mean: e.activation(out=mean, in_=bank(6, w), func=AF.Copy), reads=[B_ps[6]], writes=[Bs[0]])
            P.op("dve", lambda e, var=var, mean=mean: e.tensor_tensor(out=var, in0=mean, in1=mean, op=ALU.mult), reads=[Bs[0]], writes=[Bs[1]])
            P.op("dve", lambda e, w=w, var=var: e.tensor_tensor(out=var, in0=bank(7, w), in1=var, op=ALU.subtract), reads=[B_ps[7], Bs[1]], writes=[Bs[1]])
            P.op("act", lambda e, var=var, rstd=rstd: e.activation(out=rstd, in_=var, func=AF.Sqrt, bias=epsc[:, 0:1], scale=1.0),
                 reads=[Bs[1], B_const2], writes=[Bs[2]])
            P.op("dve", lambda e, rstd=rstd: e.reciprocal(out=rstd, in_=rstd), reads=[Bs[2]], writes=[Bs[2]])
            P.op("dve", lambda e, mr=mr, mean=mean, rstd=rstd: e.tensor_tensor(out=mr, in0=mean, in1=rstd, op=ALU.mult), reads=[Bs[0], Bs[2]], writes=[Bs[3]])
            for m in range(KC):
                tmp = ln["tmp"][:, m % 2, :w]
                Bt = ln["Btmp"][m % 2]
                gcol = lng[:, gi * KC + m:gi * KC + m + 1]
                bcol = lnb[:, gi * KC + m:gi * KC + m + 1]
                P.op("dve", lambda e, tmp=tmp, m=m, c0=c0, w=w, rstd=rstd: e.tensor_tensor(out=tmp, in0=x32[:, m, c0:c0 + w], in1=rstd, op=ALU.mult),
                     reads=[B_x32[m][ti], Bs[2]], writes=[Bt])
                P.op("dve", lambda e, tmp=tmp, mr=mr: e.tensor_tensor(out=tmp, in0=tmp, in1=mr, op=ALU.subtract), reads=[Bt, Bs[3]], writes=[Bt])
                P.op("act", lambda e, tmp=tmp, m=m, c0=c0, w=w, gcol=gcol, bcol=bcol: e.activation(
                    out=x32[:, m, c0:c0 + w], in_=tmp, func=AF.Identity, bias=bcol, scale=gcol),
                    reads=[Bt, B_cols], writes=[B_x32[m][ti]])
                P.op("act", lambda e, tmp=tmp, m=m, c0=c0, w=w, gcol=gcol, bcol=bcol: e.activation(
                    out=xbf[:, m, c0:c0 + w], in_=tmp, func=AF.Identity, bias=bcol, scale=gcol),
                    reads=[Bt, B_cols], writes=[B_xbf[m][ti]])

    def alloc_ln():
        ln = {}
        ln["zb"] = sb.alloc([128, KC, 512], BF16)
        ln["zsq"] = sb.alloc([128, KC, 512], BF16)
        ln["st"] = sb.alloc([128, 4, 512], F32)
        ln["tmp"] = sb.alloc([128, 2, 512], F32)
        ln["Bzb"] = [Buf("zb") for _ in range(KC)]
        ln["Bzsq"] = [Buf("zsq") for _ in range(KC)]
        ln["Bst"] = [Buf("st") for _ in range(4)]
        ln["Btmp"] = [Buf("tmp") for _ in range(2)]
        return ln

    HALF = NFF // 2

    def ffn(i, j):
        mk = sb.mark()
        h = sb.alloc([128, HALF, NCOL], BF16)
        NWU = 3
        wu = sb.alloc([128, NWU, 2, KC, 256], BF16)
        wd = sb.alloc([128, 2, HALF, 256], BF16)
        sg = sb.alloc([128, 2, 512], F32)
        B_h = [[Buf("h") for _ in range(5)] for _ in range(HALF)]
        B_wu = [Buf("wu") for _ in range(NWU)]
        B_wd = [Buf("wd") for _ in range(2)]
        B_sg = [Buf("sg") for _ in range(2)]
        c_res = 0.5 / ALPHA
        nslab = 0
        ndslab = 0
        cnt = 0
        for g in range(2):
            chunks = list(range(g * HALF, (g + 1) * HALF))
            for s0 in range(0, HALF, 2):
                sl = chunks[s0:s0 + 2]
                ncol = len(sl) * 128
                s = nslab % NWU
                nslab += 1
                a0 = sl[0] * 128
                P.dma("pool", lambda e, s=s, a0=a0, ncol=ncol: e.dma_start(
                    out=wu[:, s, 0, :, :ncol], in_=w_up_d[i, j, :, a0:a0 + ncol].rearrange("(kc p) n -> p kc n", p=128)), writes=[B_wu[s]])
                P.dma("pool", lambda e, s=s, a0=a0, ncol=ncol: e.dma_start(
                    out=wu[:, s, 1, :, :ncol], in_=w_up_d[i, j, :, DFF + a0:DFF + a0 + ncol].rearrange("(kc p) n -> p kc n", p=128)), writes=[B_wu[s]])
                for li, jj in enumerate(sl):
                    hj = jj - g * HALF
                    for ti, (c0, w) in enumerate(TILES):
                        ba = (cnt % 2) * 2
                        bb = ba + 1
                        sgi = cnt % 2
                        cnt += 1
                        for kc in range(KC):
                            P.op("pe", lambda e, s=s, li=li, kc=kc, c0=c0, w=w, ba=ba: e.matmul(
                                bank(ba, w), wu[:, s, 0, kc, li * 128:(li + 1) * 128], xbf[:, kc, c0:c0 + w], start=(kc == 0), stop=(kc == KC - 1)),
                                reads=[B_wu[s], B_xbf[kc][ti]], writes=[B_ps[ba]])
                        for kc in range(KC):
                            P.op("pe", lambda e, s=s, li=li, kc=kc, c0=c0, w=w, bb=bb: e.matmul(
                                bank(bb, w), wu[:, s, 1, kc, li * 128:(li + 1) * 128], xbf[:, kc, c0:c0 + w], start=(kc == 0), stop=(kc == KC - 1)),
                                reads=[B_wu[s], B_xbf[kc][ti]], writes=[B_ps[bb]])
                        P.op("act", lambda e, sgi=sgi, w=w, ba=ba: e.activation(out=sg[:, sgi, :w], in_=bank(ba, w), func=AF.Silu),
                             reads=[B_ps[ba]], writes=[B_sg[sgi]])
                        P.op("dve", lambda e, sgi=sgi, w=w, bb=bb, hj=hj, c0=c0: e.tensor_tensor(
                            out=h[:, hj, c0:c0 + w], in0=sg[:, sgi, :w], in1=bank(bb, w), op=ALU.mult),
                            reads=[B_sg[sgi], B_ps[bb]], writes=[B_h[hj][ti]])
            for dq in range(4):
                s = ndslab % 2
                ndslab += 1
                r0 = g * HALF * 128
                P.dma("pool", lambda e, s=s, r0=r0, dq=dq: e.dma_start(
                    out=wd[:, s, :, :], in_=w_dn_d[i, j, r0:r0 + HALF * 128, dq * 256:(dq + 1) * 256].rearrange("(c p) n -> p c n", p=128)),
                    writes=[B_wd[s]])
                for mi in range(2):
                    m = dq * 2 + mi
                    for ti, (c0, w) in enumerate(TILES):
                        b = 4 + (cnt % 2)
                        cnt += 1
                        for hj in range(HALF):
                            P.op("pe", lambda e, s=s, hj=hj, mi=mi, c0=c0, w=w, b=b: e.matmul(
                                bank(b, w), wd[:, s, hj, mi * 128:(mi + 1) * 128], h[:, hj, c0:c0 + w], start=(hj == 0), stop=(hj == HALF - 1)),
                                reads=[B_wd[s], B_h[hj][ti]], writes=[B_ps[b]])
                        P.op("dve", lambda e, m=m, c0=c0, w=w, b=b: e.scalar_tensor_tensor(
                            out=x32[:, m, c0:c0 + w], in0=bank(b, w), scalar=c_res, in1=x32[:, m, c0:c0 + w], op0=ALU.mult, op1=ALU.add),
                            reads=[B_ps[b], B_x32[m][ti]], writes=[B_x32[m][ti]])
        sb.release(mk)
        P.barrier()
        mk = sb.mark()
        ln = alloc_ln()
        layer_norm(i * 3 + 2 * j, ln)
        sb.release(mk)
        P.barrier()

    def ple(i):
        mk = sb.mark()
        pT = sb.alloc([128, 2, NCOL], BF16)
        B_pT = [[Buf("pT") for _ in range(5)] for _ in range(2)]
        m1 = sb.mark()
        load_tm_to_fm(pp_d[i], SEQ, 0, 2, None, pT, None, B_pT)
        load_tm_to_fm(ps_d[i], NS, SEQ, 2, None, pT, None, B_pT)
        sb.release(m1)
        wg = sb.alloc([128, KC, D], BF16)
        wp = sb.alloc([128, 2, D], BF16)
        sgt = sb.alloc([128, 2, 512], F32)
        B_wg, B_wp = Buf("wg"), Buf("wp")
        B_sg = [Buf("sg") for _ in range(2)]
        P.barrier()
        P.dma("pool", lambda e: e.dma_start(out=wg[:], in_=ple_wg_d[i].rearrange("(kc p) n -> p kc n", p=128)), writes=[B_wg])
        P.dma("pool", lambda e: e.dma_start(out=wp[:], in_=ple_wp_d[i].rearrange("(kc p) n -> p kc n", p=128)), writes=[B_wp])
        cnt = 0
        for m in range(KC):
            for ti, (c0, w) in enumerate(TILES):
                ba = (cnt % 2) * 2
                bb = ba + 1
                si = cnt % 2
                cnt += 1
                for kc in range(KC):
                    P.op("pe", lambda e, kc=kc, m=m, c0=c0, w=w, ba=ba: e.matmul(
                        bank(ba, w), wg[:, kc, m * 128:(m + 1) * 128], xbf[:, kc, c0:c0 + w], start=(kc == 0), stop=(kc == KC - 1)),
                        reads=[B_wg, B_xbf[kc][ti]], writes=[B_ps[ba]])
                for kc in range(2):
                    P.op("pe", lambda e, kc=kc, m=m, c0=c0, w=w, bb=bb: e.matmul(
                        bank(bb, w), wp[:, kc, m * 128:(m + 1) * 128], pT[:, kc, c0:c0 + w], start=(kc == 0), stop=(kc == 1)),
                        reads=[B_wp, B_pT[kc][ti]], writes=[B_ps[bb]])
                bcol = plebg[:, i * KC + m:i * KC + m + 1]
                P.op("act", lambda e, si=si, w=w, ba=ba, bcol=bcol: e.activation(out=sgt[:, si, :w], in_=bank(ba, w), func=AF.Sigmoid, bias=bcol, scale=1.0),
                     reads=[B_ps[ba], B_cols], writes=[B_sg[si]])
                P.op("dve", lambda e, si=si, w=w, bb=bb: e.tensor_tensor(out=sgt[:, si, :w], in0=sgt[:, si, :w], in1=bank(bb, w), op=ALU.mult),
                     reads=[B_sg[si], B_ps[bb]], writes=[B_sg[si]])
                P.op("dve", lambda e, si=si, m=m, c0=c0, w=w: e.tensor_tensor(out=x32[:, m, c0:c0 + w], in0=x32[:, m, c0:c0 + w], in1=sgt[:, si, :w], op=ALU.add),
                     reads=[B_sg[si], B_x32[m][ti]], writes=[B_x32[m][ti]])
        for m in range(KC):
            for ti, (c0, w) in enumerate(TILES):
                P.op("act", lambda e, m=m, c0=c0, w=w: e.activation(out=xbf[:, m, c0:c0 + w], in_=x32[:, m, c0:c0 + w], func=AF.Copy),
                     reads=[B_x32[m][ti]], writes=[B_xbf[m][ti]])
        sb.release(mk)
        P.barrier()


    def gmlp():
        GELU = AF.Gelu_apprx_tanh
        mk = sb.mark()
        gmg = sb.alloc([128, 16], F32)
        gmb = sb.alloc([128, 16], F32)
        wT = sb.alloc([128, 8, 128], BF16)
        Cb = sb.alloc([128, 16, 128], F32)
        eps1 = sb.alloc([128, 1], F32)
        As = sb.alloc([128, 16], F32)
        Cs = sb.alloc([128, 16], F32)
        B_gc = Buf("gm_consts")
        mt = sb.mark()
        stage = sb.alloc([128, 128], F32)
        wT32 = sb.alloc([128, 8, 128], F32)
        cm = sb.alloc([128, 128], F32)
        bsbc = sb.alloc([128, 8, 128], F32)
        w00c = sb.alloc([128, 8], F32)
        bs0c = sb.alloc([128, 8], F32)
        ones1 = sb.alloc([128, 128], BF16)
        B_stage, B_t = Buf("stage"), Buf("gmtmp")
        for dst, src in ((gmg, gm_lng_c_d), (gmb, gm_lnb_c_d)):
            P.op("dve", lambda e: e.memset(stage[:, :], 0.0), writes=[B_stage])
            P.dma("sp", lambda e, src=src: e.dma_start(out=stage[:16, :], in_=src[:, :]), writes=[B_stage])
            P.op("pe", lambda e: e.transpose(bank(7, 128), stage[:, :], ident[:, :]), reads=[B_stage, B_const], writes=[B_ps[7]])
            P.op("dve", lambda e, dst=dst: e.tensor_copy(out=dst[:, :], in_=bank(7, 16)), reads=[B_ps[7], B_gc], writes=[B_gc])
        P.dma("sp", lambda e: e.dma_start(out=wT32[:], in_=gm_wT_d[:, :, :]), writes=[B_t])
        P.dma("sp", lambda e: e.dma_start(out=cm[:], in_=cmask_d[:, :]), writes=[B_t])
        P.dma("sp", lambda e: e.dma_start(out=bsbc[:], in_=gm_bs_bc_d[:, :, :]), writes=[B_t])
        P.dma("sp", lambda e: e.dma_start(out=w00c[:], in_=gm_w00_d[:, :]), writes=[B_t])
        P.dma("sp", lambda e: e.dma_start(out=bs0c[:], in_=gm_bs0_d[:, :]), writes=[B_t])
        P.op("dve", lambda e: e.memset(ones1[:], 1.0), reads=[B_t], writes=[B_t])
        P.op("dve", lambda e: e.memset(eps1[:], LN_EPS), reads=[B_gc], writes=[B_gc])
        P.op("dve", lambda e: e.tensor_tensor(out=wT[:], in0=wT32[:], in1=cm[:, :].unsqueeze(1).to_broadcast([128, 8, 128]), op=ALU.mult),
             reads=[B_t, B_gc], writes=[B_gc])
        for hb in range(2):
            P.op("pe", lambda e, hb=hb: e.matmul(bank(6, 512), ones1[:], wT[:, hb * 4:(hb + 1) * 4, :], start=True, stop=True),
                 reads=[B_t, B_gc], writes=[B_ps[6]])
            for gg in range(4):
                g = hb * 4 + gg
                for hh in range(2):
                    fc = 2 * g + hh
                    P.op("dve", lambda e, gg=gg, g=g, fc=fc: e.scalar_tensor_tensor(
                        out=Cb[:, fc, :], in0=bank(6, 512)[:, gg * 128:(gg + 1) * 128], scalar=gmb[:, fc:fc + 1], in1=bsbc[:, g, :], op0=ALU.mult, op1=ALU.add),
                        reads=[B_ps[6], B_t, B_gc], writes=[B_gc])
        w00x = w00c[:, :].unsqueeze(2).to_broadcast([128, 8, 2])
        bs0x = bs0c[:, :].unsqueeze(2).to_broadcast([128, 8, 2])
        v3 = lambda t: t[:, :].rearrange("p (g h) -> p g h", h=2)
        P.op("dve", lambda e: e.tensor_tensor(out=v3(As), in0=v3(gmg), in1=w00x, op=ALU.mult), reads=[B_t, B_gc], writes=[B_gc])
        P.op("dve", lambda e: e.tensor_tensor(out=v3(Cs), in0=v3(gmb), in1=w00x, op=ALU.mult), reads=[B_t, B_gc], writes=[B_gc])
        P.op("dve", lambda e: e.tensor_tensor(out=v3(Cs), in0=v3(Cs), in1=bs0x, op=ALU.add), reads=[B_t, B_gc], writes=[B_gc])
        P.barrier()
        sb.release(mt)

        wins = sb.alloc([128, 3, KC, 256], BF16)
        wos = sb.alloc([128, 2, 16, 128], BF16)
        u = sb.alloc([128, 16, 512], BF16)
        v32 = sb.alloc([128, 4, 2048], F32)
        vbf = sb.alloc([128, 4, 2048], BF16)
        st6 = sb.alloc([128, 4, 4, 6], F32)
        mv = sb.alloc([128, 4, 2], F32)
        rs = sb.alloc([128, 4], F32)
        tmp = sb.alloc([128, 2, 512], F32)
        B_wins = [Buf("wins") for _ in range(3)]
        B_wos = [Buf("wos") for _ in range(2)]
        B_u = [Buf("u") for _ in range(16)]
        B_v32 = [Buf("v32") for _ in range(4)]
        B_vbf = [Buf("vbf") for _ in range(4)]
        B_st = [Buf("st") for _ in range(4)]
        B_tmp = [Buf("tmp") for _ in range(2)]
        state = {"nsl": 0, "nwo": 0, "cnt": 0}

        def load_win(col0):
            sl = state["nsl"] % 3
            state["nsl"] += 1
            P.dma("pool", lambda e, sl=sl, col0=col0: e.dma_start(
                out=wins[:, sl, :, :], in_=gm_w_in_d[:, col0:col0 + 256].rearrange("(kc p) n -> p kc n", p=128)), writes=[B_wins[sl]])
            return sl

        def v_chunk(ch, tok):
            for q in range(4):
                P.op("dve", lambda e, ch=ch, q=q: e.bn_stats(out=st6[:, ch, q, :], in_=v32[:, ch, q * 512:(q + 1) * 512]),
                     reads=[B_v32[ch]], writes=[B_st[ch]])
            P.op("dve", lambda e, ch=ch: e.bn_aggr(out=mv[:, ch, :], in_=st6[:, ch, :, :].rearrange("p a b -> p (a b)")),
                 reads=[B_st[ch]], writes=[B_st[ch]])
            P.op("act", lambda e, ch=ch: e.activation(out=rs[:, ch:ch + 1], in_=mv[:, ch, 1:2], func=AF.Sqrt, bias=eps1[:, 0:1], scale=1.0),
                 reads=[B_st[ch], B_gc], writes=[B_st[ch]])
            P.op("dve", lambda e, ch=ch: e.reciprocal(out=rs[:, ch:ch + 1], in_=rs[:, ch:ch + 1]), reads=[B_st[ch]], writes=[B_st[ch]])

        def v_slabs(chunks):
            for slb in range(8):
                sl = load_win(2048 + slb * 256)
                for (ch, tok) in chunks:
                    b = state["cnt"] % 2
                    state["cnt"] += 1
                    for kc in range(KC):
                        P.op("pe", lambda e, kc=kc, tok=tok, sl=sl, b=b: e.matmul(
                            bank(b, 256), xbf[:, kc, tok:tok + 128], wins[:, sl, kc, :], start=(kc == 0), stop=(kc == KC - 1)),
                            reads=[B_wins[sl]] + [B_xbf[kc][t] for t in range(5)], writes=[B_ps[b]])
                    P.op("act", lambda e, ch=ch, slb=slb, b=b: e.activation(out=v32[:, ch, slb * 256:(slb + 1) * 256], in_=bank(b, 256), func=GELU),
                         reads=[B_ps[b]], writes=[B_v32[ch]])

        def u_slabs(c0, w, udst, B_ud):
            for slb in range(8):
                sl = load_win(slb * 256)
                for li in range(2):
                    fc = slb * 2 + li
                    b = 2 + state["cnt"] % 2
                    state["cnt"] += 1
                    for kc in range(KC):
                        P.op("pe", lambda e, kc=kc, sl=sl, li=li, b=b: e.matmul(
                            bank(b, w), wins[:, sl, kc, li * 128:(li + 1) * 128], xbf[:, kc, c0:c0 + w], start=(kc == 0), stop=(kc == KC - 1)),
                            reads=[B_wins[sl]] + [B_xbf[kc][t] for t in range(5)], writes=[B_ps[b]])
                    P.op("act", lambda e, fc=fc, b=b: e.activation(out=udst[:, fc, :w], in_=bank(b, w), func=GELU),
                         reads=[B_ps[b]], writes=[B_ud[fc]])

        def out_proj(c0, w, ti, gsrc, B_g):
            for m in range(KC):
                so = state["nwo"] % 2
                state["nwo"] += 1
                P.dma("pool", lambda e, so=so, m=m: e.dma_start(
                    out=wos[:, so, :, :], in_=gm_w_out_d[:, m * 128:(m + 1) * 128].rearrange("(fc p) n -> p fc n", p=128)), writes=[B_wos[so]])
                b = state["cnt"] % 2
                state["cnt"] += 1
                for fc in range(16):
                    P.op("pe", lambda e, so=so, fc=fc, b=b: e.matmul(
                        bank(b, w), wos[:, so, fc, :], gsrc[:, fc, :w], start=(fc == 0), stop=(fc == 15)),
                        reads=[B_wos[so], B_g[fc]], writes=[B_ps[b]])
                P.op("dve", lambda e, m=m, b=b: e.scalar_tensor_tensor(
                    out=x32[:, m, c0:c0 + w], in0=bank(b, w), scalar=1.0 / ALPHA, in1=x32[:, m, c0:c0 + w], op0=ALU.mult, op1=ALU.add),
                    reads=[B_ps[b], B_x32[m][ti]], writes=[B_x32[m][ti]])

        for ti in range(4):
            c0 = 512 * ti
            v_slabs([(ch, c0 + 128 * ch) for ch in range(4)])
            for ch in range(4):
                v_chunk(ch, c0 + 128 * ch)
                P.op("dve", lambda e, ch=ch: e.tensor_scalar(out=vbf[:, ch, :], in0=v32[:, ch, :], scalar1=mv[:, ch, 0:1], scalar2=rs[:, ch:ch + 1],
                                                              op0=ALU.subtract, op1=ALU.mult),
                     reads=[B_v32[ch], B_st[ch]], writes=[B_vbf[ch]])
            u_slabs(c0, 512, u, B_u)
            for g in range(8):
                pb = 4 + 2 * (g % 2)
                for hh in range(2):
                    fc = 2 * g + hh
                    for ch in range(4):
                        o0 = (pb + hh) * 512 + ch * 128
                        P.op("pe", lambda e, o0=o0, ch=ch, fc=fc, g=g: e.matmul(
                            psum[:, o0:o0 + 128], vbf[:, ch, fc * 128:(fc + 1) * 128], wT[:, g, :], start=True, stop=True),
                            reads=[B_vbf[ch], B_gc], writes=[B_ps[pb + hh]])
                    tt = tmp[:, hh, :]
                    P.op("dve", lambda e, tt=tt, pb=pb, hh=hh, fc=fc: e.scalar_tensor_tensor(
                        out=tt.rearrange("p (c t) -> p c t", c=4), in0=bank(pb + hh, 512).rearrange("p (c t) -> p c t", c=4),
                        scalar=gmg[:, fc:fc + 1], in1=Cb[:, fc:fc + 1, :].to_broadcast([128, 4, 128]), op0=ALU.mult, op1=ALU.add),
                        reads=[B_ps[pb + hh], B_gc], writes=[B_tmp[hh]])
                    P.op("dve", lambda e, tt=tt, fc=fc: e.tensor_tensor(out=u[:, fc, :], in0=tt, in1=u[:, fc, :], op=ALU.mult),
                         reads=[B_tmp[hh], B_u[fc]], writes=[B_u[fc]])
            out_proj(c0, 512, ti, u, B_u)

        TOK = NCOL - 128
        v_slabs([(0, TOK)])
        v_chunk(0, TOK)
        P.op("dve", lambda e: e.tensor_scalar(out=v32[:, 0, :], in0=v32[:, 0, :], scalar1=mv[:, 0, 0:1], scalar2=rs[:, 0:1],
                                              op0=ALU.subtract, op1=ALU.mult), reads=[B_v32[0], B_st[0]], writes=[B_v32[0]])
        P.dma("sp", lambda e: e.dma_start(out=v32[96:128, 2, :], in_=gm_lng_bc_d[:, :]), writes=[B_v32[2]])
        P.dma("sp", lambda e: e.dma_start(out=v32[96:128, 3, :], in_=gm_lnb_bc_d[:, :]), writes=[B_v32[3]])
        P.op("dve", lambda e: e.tensor_tensor(out=v32[96:128, 1, :], in0=v32[96:128, 0, :], in1=v32[96:128, 2, :], op=ALU.mult),
             reads=[B_v32[0], B_v32[2]], writes=[B_v32[1]])
        P.op("dve", lambda e: e.tensor_tensor(out=v32[96:128, 1, :], in0=v32[96:128, 1, :], in1=v32[96:128, 3, :], op=ALU.add),
             reads=[B_v32[1], B_v32[3]], writes=[B_v32[1]])
        P.dma("sp", lambda e: e.dma_start(out=gmv_d[:, :], in_=v32[124:128, 1, :]), reads=[B_v32[1]])
        us = vbf[:, 1, :].rearrange("p (f t) -> p f t", f=16)[:, :, 0:NS]
        B_us = [Buf("us") for _ in range(16)]
        u_slabs(SEQ, NS, us, B_us)
        ms = tmp[:, 0, :64].rearrange("p (f t) -> p f t", f=16)
        for fc in range(16):
            b = 4 + fc % 4
            P.op("pe", lambda e, fc=fc, b=b: e.transpose(bank(b, 128), v32[:, 0, fc * 128:(fc + 1) * 128], ident[:, :]),
                 reads=[B_v32[0], B_const], writes=[B_ps[b]])
            P.op("act", lambda e, fc=fc, b=b: e.activation(out=ms[:, fc, :], in_=bank(b, 128)[:, 124:128], func=AF.Identity,
                                                           bias=Cs[:, fc:fc + 1], scale=As[:, fc:fc + 1]),
                 reads=[B_ps[b], B_gc], writes=[B_tmp[0]])
            P.op("dve", lambda e, fc=fc: e.tensor_tensor(out=us[:, fc, :], in0=ms[:, fc, :], in1=us[:, fc, :], op=ALU.mult),
                 reads=[B_tmp[0], B_us[fc]], writes=[B_us[fc]])
        out_proj(SEQ, NS, 4, us, B_us)
        sb.release(mk)
        P.barrier()


    def mamba_prompt():
        NCH = 2
        TW = NCH * 128
        mk = sb.mark()
        cw = sb.alloc([128, 4, 24], F32)
        cbias = sb.alloc([128, 24], F32)
        ngc = sb.alloc([128, 16], F32)
        hv = sb.alloc([128, 3, 32], F32)
        M1 = sb.alloc([128, 128], F32)
        M2 = sb.alloc([128, 128], F32)
        onesf = sb.alloc([128, 128], F32)
        eps1 = sb.alloc([128, 1], F32)
        one1 = sb.alloc([128, 1], F32)
        wdt = sb.alloc([128, KC, 32], BF16)
        B_c = Buf("ssm_consts")
        B_carry = [Buf("carry") for _ in range(24)]
        B_HT32, B_HTbf = Buf("HT32"), Buf("HTbf")
        mt = sb.mark()
        stage = sb.alloc([128, 128], F32)
        B_stage = Buf("stage")
        cw_flat = cw[:, :, :].rearrange("p a b -> p (a b)")
        for dst, src, rows in ((cw_flat, ssm_cw_c_d, 96), (cbias, ssm_cb_c_d, 24), (ngc, ssm_ng_c_d, 16)):
            P.op("dve", lambda e: e.memset(stage[:, :], 0.0), writes=[B_stage])
            P.dma("sp", lambda e, src=src, rows=rows: e.dma_start(out=stage[:rows, :], in_=src[:, :]), writes=[B_stage])
            P.op("pe", lambda e: e.transpose(bank(7, 128), stage[:, :], ident[:, :]), reads=[B_stage, B_const], writes=[B_ps[7]])
            P.op("dve", lambda e, dst=dst, rows=rows: e.tensor_copy(out=dst[:, 0:rows], in_=bank(7, rows)), reads=[B_ps[7], B_c], writes=[B_c])
        P.dma("sp", lambda e: e.dma_start(out=hv[:], in_=ssm_hv_bc_d[:, :, :]), reads=[B_c], writes=[B_c])
        P.op("act", lambda e: e.activation(out=hv[:, 1, :], in_=hv[:, 1, :], func=AF.Exp), reads=[B_c], writes=[B_c])
        P.op("dve", lambda e: e.tensor_scalar(out=hv[:, 1, :], in0=hv[:, 1, :], scalar1=-1.0, scalar2=None, op0=ALU.mult), reads=[B_c], writes=[B_c])
        P.dma("sp", lambda e: e.dma_start(out=M1[:], in_=m1_d[:, :]), reads=[B_c], writes=[B_c])
        P.dma("sp", lambda e: e.dma_start(out=M2[:], in_=cmask_d[:, :]), reads=[B_c], writes=[B_c])
        P.dma("sp", lambda e: e.dma_start(out=onesf[:], in_=onesf_d[:, :]), reads=[B_c], writes=[B_c])
        P.dma("pool", lambda e: e.dma_start(out=wdt[:], in_=ssm_w_in_d[:, 5120:5152].rearrange("(kc p) n -> p kc n", p=128)), reads=[B_c], writes=[B_c])
        P.op("dve", lambda e: e.memset(eps1[:], LN_EPS), reads=[B_c], writes=[B_c])
        P.op("dve", lambda e: e.memset(one1[:], 1.0), reads=[B_c], writes=[B_c])
        P.barrier()
        sb.release(mt)

        mwork = sb.mark()
        carry = sb.alloc([128, 24, 3], F32)
        HT32 = sb.alloc([128, 2048], F32)
        HTbf = sb.alloc([128, 2048], BF16)
        P.op("dve", lambda e: e.memset(carry[:], 0.0), writes=B_carry)
        P.op("dve", lambda e: e.memset(HT32[:], 0.0), writes=[B_HT32])
        P.op("dve", lambda e: e.memset(HTbf[:], 0.0), writes=[B_HTbf])
        wins = sb.alloc([128, 3, KC, 256], BF16)
        wos = sb.alloc([128, 2, 16, 128], BF16)
        stg = sb.alloc([128, 2, TW + 3], F32)
        acc = sb.alloc([128, 2, TW], F32)
        xs_tm = sb.alloc([128, NCH, 2048], BF16)
        BT = sb.alloc([128, 4, TW], BF16)
        CT = sb.alloc([128, 4, TW], BF16)
        B_tm = sb.alloc([128, NCH, 512], BF16)
        sz = sb.alloc([128, NCH, 2048], BF16)
        dt = sb.alloc([128, NCH, 32], F32)
        a_tm = sb.alloc([128, NCH, 32], F32)
        small = sb.alloc([128, 4, 32], F32)
        rb = sb.alloc([128, 2, 512], F32)
        seg = sb.alloc([128, 2, 512], F32)
        eab = sb.alloc([128, 2, 512], F32)
        LT = sb.alloc([128, 2, 512], BF16)
        Csc = sb.alloc([128, 2, 512], BF16)
        cbm = sb.alloc([128, 4, 128], F32)
        dcy = sb.alloc([128, 32], F32)
        xdt = sb.alloc([128, 2048], BF16)
        xdtd = sb.alloc([128, 2048], BF16)
        y32 = sb.alloc([128, 2048], F32)
        junk = sb.alloc([128, 512], BF16)
        ssq = sb.alloc([128, 8], F32)
        ynT = sb.alloc([128, 16, TW], BF16)
        Bw = [Buf("wins") for _ in range(3)]
        Bwo = [Buf("wos") for _ in range(2)]
        Bstg = [Buf("stg") for _ in range(2)]
        Bacc = [Buf("acc") for _ in range(2)]
        Bxs = [Buf("xs_tm") for _ in range(NCH)]
        BBT, BCT = [Buf("BT") for _ in range(4)], [Buf("CT") for _ in range(4)]
        BBtm = [Buf("B_tm") for _ in range(NCH)]
        Bsz = [Buf("sz") for _ in range(NCH)]
        Bdt = [Buf("dt") for _ in range(NCH)]
        Bsm = Buf("small")
        Brb, Bseg, Beab, BLT, BCsc = ([Buf(n) for _ in range(2)] for n in ("rb", "seg", "eab", "LT", "Csc"))
        Bcbm, Bdcy, Bxdt, Bxdtd, By32, Bssq = Buf("cbm"), Buf("dcy"), Buf("xdt"), Buf("xdtd"), Buf("y32"), Buf("ssq")
        BynT = [Buf("ynT") for _ in range(16)]
        state = {"nsl": 0, "nwo": 0, "cnt": 0}
        allx = lambda kc: [B_xbf[kc][t] for t in range(5)]
        dbg_names.update(dt=dt.name, a_tm=a_tm.name, HT32=HT32.name, xs_tm=xs_tm.name, sz=sz.name, y32=y32.name, BT=BT.name, CT=CT.name,
                         small=small.name, dcy=dcy.name, ynT=ynT.name, xdt=xdt.name, cbm=cbm.name, LT=LT.name, seg=seg.name, eab=eab.name, B_tm=B_tm.name)

        def load_win(col0, ncol=256):
            sl = state["nsl"] % 3
            state["nsl"] += 1
            P.dma("pool", lambda e, sl=sl, col0=col0, ncol=ncol: e.dma_start(
                out=wins[:, sl, :, :ncol], in_=ssm_w_in_d[:, col0:col0 + ncol].rearrange("(kc p) n -> p kc n", p=128)), writes=[Bw[sl]])
            return sl

        def nb(lo=0, n=4):
            b = lo + state["cnt"] % n
            state["cnt"] += 1
            return b

        for tile in range(SEQ // TW):
            tok0 = tile * TW
            for slb in range(12):
                sl = load_win(2048 + slb * 256)
                for li in range(2):
                    fcx = slb * 2 + li
                    si = fcx % 2
                    b = nb(0, 4)
                    for kc in range(KC):
                        P.op("pe", lambda e, kc=kc, sl=sl, li=li, b=b, tok0=tok0: e.matmul(
                            bank(b, TW), wins[:, sl, kc, li * 128:(li + 1) * 128], xbf[:, kc, tok0:tok0 + TW], start=(kc == 0), stop=(kc == KC - 1)),
                            reads=[Bw[sl]] + allx(kc), writes=[B_ps[b]])
                    P.op("act", lambda e, si=si, b=b: e.activation(out=stg[:, si, 3:3 + TW], in_=bank(b, TW), func=AF.Copy), reads=[B_ps[b]], writes=[Bstg[si]])
                    P.op("dve", lambda e, si=si, fcx=fcx: e.tensor_copy(out=stg[:, si, 0:3], in_=carry[:, fcx, :]), reads=[B_carry[fcx], Bstg[si]], writes=[Bstg[si]])
                    P.op("dve", lambda e, si=si, fcx=fcx: e.tensor_copy(out=carry[:, fcx, :], in_=stg[:, si, TW:TW + 3]), reads=[Bstg[si]], writes=[B_carry[fcx]])
                    P.op("act", lambda e, si=si, fcx=fcx: e.activation(out=acc[:, si, :], in_=stg[:, si, 3:3 + TW], func=AF.Identity,
                                                                      bias=cbias[:, fcx:fcx + 1], scale=cw[:, 3, fcx:fcx + 1]),
                         reads=[Bstg[si], B_c], writes=[Bacc[si]])
                    for j in range(3):
                        P.op("dve", lambda e, si=si, fcx=fcx, j=j: e.scalar_tensor_tensor(
                            out=acc[:, si, :], in0=stg[:, si, j:j + TW], scalar=cw[:, j, fcx:fcx + 1], in1=acc[:, si, :], op0=ALU.mult, op1=ALU.add),
                            reads=[Bstg[si], Bacc[si], B_c], writes=[Bacc[si]])
                    if fcx < 16:
                        P.op("act", lambda e, si=si: e.activation(out=acc[:, si, :], in_=acc[:, si, :], func=AF.Silu), reads=[Bacc[si]], writes=[Bacc[si]])
                        for ch in range(NCH):
                            bt = nb(4, 4)
                            P.op("pe", lambda e, si=si, ch=ch, bt=bt: e.transpose(bank(bt, 128), acc[:, si, ch * 128:(ch + 1) * 128], ident[:, :]),
                                 reads=[Bacc[si], B_const], writes=[B_ps[bt]])
                            P.op("act", lambda e, ch=ch, fcx=fcx, bt=bt: e.activation(out=xs_tm[:, ch, fcx * 128:(fcx + 1) * 128], in_=bank(bt, 128), func=AF.Copy),
                                 reads=[B_ps[bt]], writes=[Bxs[ch]])
                    elif fcx < 20:
                        g = fcx - 16
                        P.op("act", lambda e, si=si: e.activation(out=acc[:, si, :], in_=acc[:, si, :], func=AF.Silu), reads=[Bacc[si]], writes=[Bacc[si]])
                        P.op("dve", lambda e, si=si, g=g: e.tensor_copy(out=BT[:, g, :], in_=acc[:, si, :]), reads=[Bacc[si]], writes=[BBT[g]])
                        for ch in range(NCH):
                            bt = nb(4, 4)
                            P.op("pe", lambda e, si=si, ch=ch, bt=bt: e.transpose(bank(bt, 128), acc[:, si, ch * 128:(ch + 1) * 128], ident[:, :]),
                                 reads=[Bacc[si], B_const], writes=[B_ps[bt]])
                            P.op("act", lambda e, ch=ch, g=g, bt=bt: e.activation(out=B_tm[:, ch, g * 128:(g + 1) * 128], in_=bank(bt, 128), func=AF.Copy),
                                 reads=[B_ps[bt]], writes=[BBtm[ch]])
                    else:
                        g = fcx - 20
                        P.op("act", lambda e, si=si, g=g: e.activation(out=CT[:, g, :], in_=acc[:, si, :], func=AF.Silu), reads=[Bacc[si]], writes=[BCT[g]])
            for slb in range(8):
                sl = load_win(slb * 256)
                for ch in range(NCH):
                    b = nb(0, 4)
                    tk = tok0 + ch * 128
                    for kc in range(KC):
                        P.op("pe", lambda e, kc=kc, sl=sl, tk=tk, b=b: e.matmul(
                            bank(b, 256), xbf[:, kc, tk:tk + 128], wins[:, sl, kc, :], start=(kc == 0), stop=(kc == KC - 1)),
                            reads=[Bw[sl]] + allx(kc), writes=[B_ps[b]])
                    P.op("act", lambda e, ch=ch, slb=slb, b=b: e.activation(out=sz[:, ch, slb * 256:(slb + 1) * 256], in_=bank(b, 256), func=AF.Silu),
                         reads=[B_ps[b]], writes=[Bsz[ch]])
            for ch in range(NCH):
                b = nb(0, 4)
                tk = tok0 + ch * 128
                for kc in range(KC):
                    P.op("pe", lambda e, kc=kc, tk=tk, b=b: e.matmul(bank(b, 32), xbf[:, kc, tk:tk + 128], wdt[:, kc, :], start=(kc == 0), stop=(kc == KC - 1)),
                         reads=[B_c] + allx(kc), writes=[B_ps[b]])
                P.op("dve", lambda e, ch=ch, b=b: e.tensor_tensor(out=dt[:, ch, :], in0=bank(b, 32), in1=hv[:, 0, :], op=ALU.add), reads=[B_ps[b], B_c], writes=[Bdt[ch]])
                P.op("act", lambda e, ch=ch: e.activation(out=dt[:, ch, :], in_=dt[:, ch, :], func=AF.Exp), reads=[Bdt[ch]], writes=[Bdt[ch]])
                P.op("act", lambda e, ch=ch: e.activation(out=dt[:, ch, :], in_=dt[:, ch, :], func=AF.Ln, bias=one1[:, 0:1], scale=1.0), reads=[Bdt[ch], B_c], writes=[Bdt[ch]])
                P.op("dve", lambda e, ch=ch: e.tensor_tensor(out=a_tm[:, ch, :], in0=dt[:, ch, :], in1=hv[:, 1, :], op=ALU.mult), reads=[Bdt[ch], B_c], writes=[Bdt[ch]])
            for ch in range(NCH):
                cs = slice(ch * 128, (ch + 1) * 128)
                for g in range(4):
                    P.op("pe", lambda e, g=g, cs=cs: e.matmul(bank(0, 512)[:, g * 128:(g + 1) * 128], BT[:, g, cs], CT[:, g, cs], start=True, stop=True),
                         reads=[BBT[g], BCT[g]], writes=[B_ps[0]])
                P.op("dve", lambda e: e.tensor_tensor(out=cbm[:], in0=bank(0, 512).rearrange("p (g t) -> p g t", g=4),
                                                      in1=M2[:, :].unsqueeze(1).to_broadcast([128, 4, 128]), op=ALU.mult),
                     reads=[B_ps[0], B_c], writes=[Bcbm])
                P.op("pe", lambda e, ch=ch: e.matmul(bank(3, 64)[:, 0:32], M2[:, :], a_tm[:, ch, :], start=True, stop=True), reads=[B_c, Bdt[ch]], writes=[B_ps[3]])
                P.op("pe", lambda e, ch=ch: e.matmul(bank(3, 64)[:, 32:64], onesf[:, :], a_tm[:, ch, :], start=True, stop=True), reads=[B_c, Bdt[ch]], writes=[B_ps[3]])
                P.op("act", lambda e: e.activation(out=small[:, 0, :], in_=bank(3, 64)[:, 32:64], func=AF.Copy), reads=[B_ps[3]], writes=[Bsm])
                P.op("dve", lambda e: e.tensor_tensor(out=small[:, 1, :], in0=small[:, 0, :], in1=bank(3, 64)[:, 0:32], op=ALU.subtract), reads=[B_ps[3], Bsm], writes=[Bsm])
                P.op("act", lambda e: e.activation(out=small[:, 1, :], in_=small[:, 1, :], func=AF.Exp), reads=[Bsm], writes=[Bsm])
                P.op("dve", lambda e, ch=ch: e.tensor_tensor(out=small[:, 2, :], in0=small[:, 1, :], in1=dt[:, ch, :], op=ALU.mult), reads=[Bsm, Bdt[ch]], writes=[Bsm])
                xv = xs_tm[:, ch, :].rearrange("p (h q) -> p h q", h=32)
                P.op("dve", lambda e, ch=ch, xv=xv: e.tensor_tensor(out=xdt[:, :].rearrange("p (h q) -> p h q", h=32), in0=xv,
                                                                    in1=dt[:, ch, :].unsqueeze(2).to_broadcast([128, 32, 64]), op=ALU.mult),
                     reads=[Bxs[ch], Bdt[ch]], writes=[Bxdt])
                P.op("dve", lambda e, xv=xv: e.tensor_tensor(out=xdtd[:, :].rearrange("p (h q) -> p h q", h=32), in0=xv,
                                                             in1=small[:, 2, :].unsqueeze(2).to_broadcast([128, 32, 64]), op=ALU.mult),
                     reads=[Bxs[ch], Bsm], writes=[Bxdtd])
                for bq in range(8):
                    pi = bq % 2
                    g = bq // 2
                    h0 = bq * 4
                    P.op("dve", lambda e, pi=pi, ch=ch, h0=h0: e.tensor_tensor(
                        out=rb[:, pi, :].rearrange("p (h t) -> p h t", h=4), in0=M2[:, :].unsqueeze(1).to_broadcast([128, 4, 128]),
                        in1=a_tm[:, ch, h0:h0 + 4].unsqueeze(2).to_broadcast([128, 4, 128]), op=ALU.mult),
                        reads=[B_c, Bdt[ch]], writes=[Brb[pi]])
                    P.op("pe", lambda e, pi=pi: e.matmul(bank(1, 512), M1[:, :], rb[:, pi, :], start=True, stop=True), reads=[B_c, Brb[pi]], writes=[B_ps[1]])
                    P.op("pe", lambda e, pi=pi: e.matmul(bank(2, 512), onesf[:, :], rb[:, pi, :], start=True, stop=True), reads=[B_c, Brb[pi]], writes=[B_ps[2]])
                    P.op("act", lambda e, pi=pi: e.activation(out=seg[:, pi, :], in_=bank(1, 512), func=AF.Exp), reads=[B_ps[1]], writes=[Bseg[pi]])
                    P.op("act", lambda e, pi=pi: e.activation(out=eab[:, pi, :], in_=bank(2, 512), func=AF.Exp), reads=[B_ps[2]], writes=[Beab[pi]])
                    P.op("dve", lambda e, pi=pi, g=g: e.tensor_tensor(
                        out=LT[:, pi, :].rearrange("p (h t) -> p h t", h=4), in0=seg[:, pi, :].rearrange("p (h t) -> p h t", h=4),
                        in1=cbm[:, g:g + 1, :].to_broadcast([128, 4, 128]), op=ALU.mult), reads=[Bseg[pi], Bcbm], writes=[BLT[pi]])
                    P.op("dve", lambda e, pi=pi, g=g, cs=cs: e.tensor_tensor(
                        out=Csc[:, pi, :].rearrange("p (h t) -> p h t", h=4), in0=eab[:, pi, :].rearrange("p (h t) -> p h t", h=4),
                        in1=CT[:, g:g + 1, cs].to_broadcast([128, 4, 128]), op=ALU.mult), reads=[Beab[pi], BCT[g]], writes=[BCsc[pi]])
                    P.op("dve", lambda e, pi=pi, h0=h0: e.tensor_copy(out=dcy[:, h0:h0 + 4], in_=eab[:, pi, :].rearrange("p (h t) -> p h t", h=4)[:, :, 127]),
                         reads=[Beab[pi]], writes=[Bdcy])
                    for hh in range(4):
                        h = h0 + hh
                        yb = 4 + h // 8
                        o0 = yb * 512 + (h % 8) * 64
                        P.op("pe", lambda e, pi=pi, hh=hh, h=h, o0=o0: e.matmul(
                            psum[:, o0:o0 + 64], LT[:, pi, hh * 128:(hh + 1) * 128], xdt[:, h * 64:(h + 1) * 64], start=True, stop=False),
                            reads=[BLT[pi], Bxdt], writes=[B_ps[yb]])
                        P.op("pe", lambda e, pi=pi, hh=hh, h=h, o0=o0: e.matmul(
                            psum[:, o0:o0 + 64], Csc[:, pi, hh * 128:(hh + 1) * 128], HTbf[:, h * 64:(h + 1) * 64], start=False, stop=True),
                            reads=[BCsc[pi], B_HTbf], writes=[B_ps[yb]])
                P.op("dve", lambda e, xv=xv: e.tensor_tensor(out=y32[:, :].rearrange("p (h q) -> p h q", h=32), in0=xv,
                                                             in1=hv[:, 2, :].unsqueeze(2).to_broadcast([128, 32, 64]), op=ALU.mult),
                     reads=[Bxs[ch], B_c], writes=[By32])
                for q in range(4):
                    P.op("dve", lambda e, q=q: e.tensor_tensor(out=y32[:, q * 512:(q + 1) * 512], in0=y32[:, q * 512:(q + 1) * 512], in1=bank(4 + q, 512), op=ALU.add),
                         reads=[By32, B_ps[4 + q]], writes=[By32])
                P.op("dve", lambda e, ch=ch: e.tensor_tensor(out=y32[:, :], in0=y32[:, :], in1=sz[:, ch, :], op=ALU.mult), reads=[By32, Bsz[ch]], writes=[By32])
                for q in range(4):
                    P.op("act", lambda e, q=q: e.activation(out=junk[:, :], in_=y32[:, q * 512:(q + 1) * 512], func=AF.Square, accum_out=ssq[:, q:q + 1]),
                         reads=[By32], writes=[Bssq])
                P.op("act", lambda e: e.activation(out=ssq[:, 4:8], in_=ssq[:, 0:4], func=AF.Sqrt, bias=eps1[:, 0:1], scale=1.0 / 512), reads=[Bssq, B_c], writes=[Bssq])
                P.op("dve", lambda e: e.reciprocal(out=ssq[:, 4:8], in_=ssq[:, 4:8]), reads=[Bssq], writes=[Bssq])
                P.op("dve", lambda e: e.tensor_tensor(out=y32[:, :].rearrange("p (g q) -> p g q", g=4), in0=y32[:, :].rearrange("p (g q) -> p g q", g=4),
                                                      in1=ssq[:, 4:8].unsqueeze(2).to_broadcast([128, 4, 512]), op=ALU.mult), reads=[By32, Bssq], writes=[By32])
                for g in range(4):
                    P.op("pe", lambda e, g=g, ch=ch: e.matmul(bank(4 + g, 512), B_tm[:, ch, g * 128:(g + 1) * 128], xdtd[:, g * 512:(g + 1) * 512], start=True, stop=True),
                         reads=[BBtm[ch], Bxdtd, By32], writes=[B_ps[4 + g]])
                P.op("dve", lambda e: e.tensor_tensor(out=HT32[:, :].rearrange("p (h q) -> p h q", h=32), in0=HT32[:, :].rearrange("p (h q) -> p h q", h=32),
                                                      in1=dcy[:, :].unsqueeze(2).to_broadcast([128, 32, 64]), op=ALU.mult), reads=[B_HT32, Bdcy, B_HTbf], writes=[B_HT32])
                for g in range(4):
                    P.op("dve", lambda e, g=g: e.tensor_tensor(out=HT32[:, g * 512:(g + 1) * 512], in0=HT32[:, g * 512:(g + 1) * 512], in1=bank(4 + g, 512), op=ALU.add),
                         reads=[B_HT32, B_ps[4 + g]], writes=[B_HT32])
                P.op("act", lambda e: e.activation(out=HTbf[:, :], in_=HT32[:, :], func=AF.Copy), reads=[B_HT32], writes=[B_HTbf])
                for fc in range(16):
                    bt = nb(0, 4)
                    P.op("pe", lambda e, fc=fc, bt=bt: e.transpose(bank(bt, 128), y32[:, fc * 128:(fc + 1) * 128], ident[:, :]), reads=[By32, B_const], writes=[B_ps[bt]])
                    P.op("act", lambda e, fc=fc, bt=bt, cs=cs: e.activation(out=ynT[:, fc, cs], in_=bank(bt, 128), func=AF.Identity, scale=ngc[:, fc:fc + 1], bias=0.0),
                         reads=[B_ps[bt], B_c], writes=[BynT[fc]])
            ti = tok0 // 512
            for m in range(KC):
                so = state["nwo"] % 2
                state["nwo"] += 1
                P.dma("pool", lambda e, so=so, m=m: e.dma_start(
                    out=wos[:, so, :, :], in_=ssm_w_out_d[:, m * 128:(m + 1) * 128].rearrange("(fc p) n -> p fc n", p=128)), writes=[Bwo[so]])
                b = nb(0, 4)
                for fc in range(16):
                    P.op("pe", lambda e, so=so, fc=fc, b=b: e.matmul(bank(b, TW), wos[:, so, fc, :], ynT[:, fc, :], start=(fc == 0), stop=(fc == 15)),
                         reads=[Bwo[so], BynT[fc]], writes=[B_ps[b]])
                P.op("dve", lambda e, m=m, b=b, tok0=tok0: e.scalar_tensor_tensor(
                    out=x32[:, m, tok0:tok0 + TW], in0=bank(b, TW), scalar=1.0 / ALPHA, in1=x32[:, m, tok0:tok0 + TW], op0=ALU.mult, op1=ALU.add),
                    reads=[B_ps[b], B_x32[m][ti]], writes=[B_x32[m][ti]])
        import os
        if os.environ.get("KDBG_SSM"):
            return "stop"
        for j in range(3):
            P.dma("sp", lambda e, j=j: e.dma_start(out=conv_p_d[j:j + 1, :].rearrange("o (f p) -> p (o f)", p=128), in_=carry[:, :, j],
                                                   allow_slow_non_contiguous=True), reads=B_carry)
        hout = sb.alloc([128, 2, 128], F32)
        Bho = [Buf("hout") for _ in range(2)]
        for r in range(16):
            bt = nb(0, 4)
            so = r % 2
            P.op("pe", lambda e, r=r, bt=bt: e.transpose(bank(bt, 128), HT32[:, r * 128:(r + 1) * 128], ident[:, :]), reads=[B_HT32, B_const], writes=[B_ps[bt]])
            P.op("act", lambda e, so=so, bt=bt: e.activation(out=hout[:, so, :], in_=bank(bt, 128), func=AF.Copy), reads=[B_ps[bt]], writes=[Bho[so]])
            P.dma("sp", lambda e, r=r, so=so: e.dma_start(out=ssm_p_d[2 * r:2 * r + 2, :, :].rearrange("h p n -> (h p) n"), in_=hout[:, so, :]), reads=[Bho[so]])

        P.barrier()
        sb.release(mwork)
        TOK = NCOL - 128
        R0, R1 = 96, 128
        wins2 = sb.alloc([128, 3, KC, 256], BF16)
        wos2 = sb.alloc([128, 2, 16, 128], BF16)
        proj = sb.alloc([128, 5152], F32)
        cst = sb.alloc([128, 3072], F32)
        cwb = sb.alloc([128, 3072], F32)
        act_s = sb.alloc([128, 3072], F32)
        tmx = sb.alloc([128, 2048], F32)
        sm = sb.alloc([128, 4, 32], F32)
        sel = sb.alloc([128, NS, 128], F32)
        dsc = sb.alloc([128, 16], F32)
        fm = sb.alloc([128, 5, 16, NS], F32)
        bcB = sb.alloc([128, 512], F32)
        bcC = sb.alloc([128, 512], F32)
        hst = sb.alloc([128, 16, 128], F32)
        jk = sb.alloc([128, 128], F32)
        ysq = sb.alloc([128, 16, NS], BF16)
        rst = sb.alloc([128, 4, NS], F32)
        ynb = sb.alloc([128, 16, NS], BF16)
        Bw2 = [Buf("wins2") for _ in range(3)]
        Bwo2 = [Buf("wos2") for _ in range(2)]
        Bproj, Bcst, Bcw, Bact, Btmx, Bsm2, Bsel, Bfm = (Buf(n) for n in ("proj", "cst", "cwb", "act_s", "tmx", "sm", "sel", "fm"))
        BbcB, BbcC, Bhst, Bjk, Bysq, Brst, Bynb = (Buf(n) for n in ("bcB", "bcC", "hst", "jk", "ysq", "rst", "ynb"))
        state["nsl"] = 0
        state["nwo"] = 0

        def load_win2(col0, ncol=256):
            sl = state["nsl"] % 3
            state["nsl"] += 1
            P.dma("pool", lambda e, sl=sl, col0=col0, ncol=ncol: e.dma_start(
                out=wins2[:, sl, :, :ncol], in_=ssm_w_in_d[:, col0:col0 + ncol].rearrange("(kc p) n -> p kc n", p=128)), writes=[Bw2[sl]])
            return sl
        stage2 = sb.alloc([128, 128], F32)
        Bst2 = Buf("stage2")
        P.op("dve", lambda e: e.memset(stage2[:, :], 0.0), writes=[Bst2])
        P.dma("sp", lambda e: e.dma_start(out=stage2[:16, :], in_=ssm_d_c_d[:, :]), writes=[Bst2])
        P.op("pe", lambda e: e.transpose(bank(7, 128), stage2[:, :], ident[:, :]), reads=[Bst2, B_const], writes=[B_ps[7]])
        P.op("dve", lambda e: e.tensor_copy(out=dsc[:, :], in_=bank(7, 16)), reads=[B_ps[7]], writes=[Bsel])
        P.dma("sp", lambda e: e.dma_start(out=sel[:], in_=ssm_sel_d[:, :, :]), reads=[Bsel], writes=[Bsel])
        P.op("dve", lambda e: e.memset(cst[:], 0.0), writes=[Bcst])
        P.op("dve", lambda e: e.memset(proj[:], 0.0), writes=[Bproj])
        P.op("dve", lambda e: e.memset(act_s[:], 0.0), writes=[Bact])
        col = 0
        while col < 5152:
            ncol = min(256, 5152 - col)
            sl = load_win2(col, ncol)
            b = nb(0, 4)
            for kc in range(KC):
                P.op("pe", lambda e, kc=kc, sl=sl, b=b, ncol=ncol: e.matmul(bank(b, ncol), xbf[:, kc, TOK:TOK + 128], wins2[:, sl, kc, :ncol], start=(kc == 0), stop=(kc == KC - 1)),
                     reads=[Bw2[sl]] + allx(kc), writes=[B_ps[b]])
            P.op("act", lambda e, b=b, col=col, ncol=ncol: e.activation(out=proj[R0:R1, col:col + ncol], in_=bank(b, ncol)[R0:R1, :], func=AF.Copy),
                 reads=[B_ps[b]], writes=[Bproj])
            col += ncol
        xbc = proj[R0:R1, 2048:5120]
        P.dma("sp", lambda e: e.dma_start(out=cwb[R0:R1, :], in_=ssm_cw_bc_d[:, 3, :]), writes=[Bcw])
        P.op("dve", lambda e: e.tensor_tensor(out=act_s[R0:R1, :], in0=xbc, in1=cwb[R0:R1, :], op=ALU.mult), reads=[Bproj, Bcw], writes=[Bact])
        P.dma("sp", lambda e: e.dma_start(out=cwb[R0:R1, :], in_=ssm_cb_bc_d[:, :]), reads=[Bcw], writes=[Bcw])
        P.op("dve", lambda e: e.tensor_tensor(out=act_s[R0:R1, :], in0=act_s[R0:R1, :], in1=cwb[R0:R1, :], op=ALU.add), reads=[Bact, Bcw], writes=[Bact])
        for j in range(3):
            P.dma("sp", lambda e, j=j: e.dma_start(out=cwb[R0:R1, :], in_=ssm_cw_bc_d[:, j, :]), reads=[Bcw], writes=[Bcw])
            P.dma("sp", lambda e, j=j: e.dma_start(out=cst[124:128, :], in_=st_conv_d[:, j, :]), reads=[Bcst], writes=[Bcst])
            P.op("dve", lambda e: e.tensor_tensor(out=cwb[R0:R1, :], in0=cwb[R0:R1, :], in1=cst[R0:R1, :], op=ALU.mult), reads=[Bcw, Bcst], writes=[Bcw])
            P.op("dve", lambda e: e.tensor_tensor(out=act_s[R0:R1, :], in0=act_s[R0:R1, :], in1=cwb[R0:R1, :], op=ALU.add), reads=[Bact, Bcw], writes=[Bact])
        P.op("act", lambda e: e.activation(out=act_s[R0:R1, :], in_=act_s[R0:R1, :], func=AF.Silu), reads=[Bact], writes=[Bact])
        P.dma("sp", lambda e: e.dma_start(out=conv_s_d[:, 0:2, :], in_=st_conv_d[:, 1:3, :]))
        P.dma("sp", lambda e: e.dma_start(out=conv_s_d[:, 2, :], in_=proj[124:128, 2048:5120]), reads=[Bproj])
        P.op("dve", lambda e: e.tensor_tensor(out=sm[R0:R1, 0, :], in0=proj[R0:R1, 5120:5152], in1=hv[R0:R1, 0, :], op=ALU.add), reads=[Bproj, B_c], writes=[Bsm2])
        P.op("act", lambda e: e.activation(out=sm[R0:R1, 0, :], in_=sm[R0:R1, 0, :], func=AF.Exp), reads=[Bsm2], writes=[Bsm2])
        P.op("act", lambda e: e.activation(out=sm[R0:R1, 0, :], in_=sm[R0:R1, 0, :], func=AF.Ln, bias=one1[R0:R1, 0:1], scale=1.0), reads=[Bsm2, B_c], writes=[Bsm2])
        P.op("dve", lambda e: e.tensor_tensor(out=sm[R0:R1, 1, :], in0=sm[R0:R1, 0, :], in1=hv[R0:R1, 1, :], op=ALU.mult), reads=[Bsm2, B_c], writes=[Bsm2])
        P.op("act", lambda e: e.activation(out=sm[R0:R1, 1, :], in_=sm[R0:R1, 1, :], func=AF.Exp), reads=[Bsm2], writes=[Bsm2])
        P.op("dve", lambda e: e.memset(tmx[:], 0.0), writes=[Btmx])
        v3 = lambda ap: ap.rearrange("p (h q) -> p h q", h=32)

        def to_fm(src, k, Bsrc):
            for fc in range(16):
                bt = nb(4, 4)
                P.op("pe", lambda e, fc=fc, bt=bt: e.transpose(bank(bt, 128), src[:, fc * 128:(fc + 1) * 128], ident[:, :]), reads=[Bsrc, B_const], writes=[B_ps[bt]])
                P.op("act", lambda e, fc=fc, bt=bt: e.activation(out=fm[:, k, fc, :], in_=bank(bt, 128)[:, 124:128], func=AF.Copy), reads=[B_ps[bt], Bfm], writes=[Bfm])
        P.op("dve", lambda e: e.tensor_tensor(out=v3(tmx[R0:R1, :]), in0=v3(act_s[R0:R1, 0:2048]), in1=sm[R0:R1, 0, :].unsqueeze(2).to_broadcast([32, 32, 64]), op=ALU.mult),
             reads=[Bact, Bsm2, Btmx], writes=[Btmx])
        to_fm(tmx[:, :], 0, Btmx)
        P.op("dve", lambda e: e.tensor_copy(out=v3(tmx[R0:R1, :]), in_=sm[R0:R1, 1, :].unsqueeze(2).to_broadcast([32, 32, 64])), reads=[Bsm2, Btmx], writes=[Btmx])
        to_fm(tmx[:, :], 1, Btmx)
        to_fm(act_s[:, 0:2048], 2, Bact)
        P.op("act", lambda e: e.activation(out=tmx[R0:R1, :], in_=proj[R0:R1, 0:2048], func=AF.Silu), reads=[Bproj, Btmx], writes=[Btmx])
        to_fm(tmx[:, :], 3, Btmx)
        for b_ in range(NS):
            P.dma("sp", lambda e, b_=b_: e.dma_start(out=hst[:], in_=st_ssm_d[b_].rearrange("(fc p) n -> p fc n", p=128)), writes=[Bhst])
            P.op("pe", lambda e, b_=b_: e.matmul(bank(0, 512), sel[:, b_, :], act_s[:, 2048:2560], start=True, stop=True), reads=[Bsel, Bact], writes=[B_ps[0]])
            P.op("pe", lambda e, b_=b_: e.matmul(bank(1, 512), sel[:, b_, :], act_s[:, 2560:3072], start=True, stop=True), reads=[Bsel, Bact], writes=[B_ps[1]])
            P.op("act", lambda e: e.activation(out=bcB[:], in_=bank(0, 512), func=AF.Copy), reads=[B_ps[0]], writes=[BbcB])
            P.op("act", lambda e: e.activation(out=bcC[:], in_=bank(1, 512), func=AF.Copy), reads=[B_ps[1]], writes=[BbcC])
            for fc in range(16):
                g = fc // 4
                P.op("dve", lambda e, fc=fc, b_=b_: e.tensor_scalar(out=hst[:, fc, :], in0=hst[:, fc, :], scalar1=fm[:, 1, fc, b_:b_ + 1], scalar2=None, op0=ALU.mult),
                     reads=[Bhst, Bfm], writes=[Bhst])
                P.op("dve", lambda e, fc=fc, b_=b_, g=g: e.scalar_tensor_tensor(out=hst[:, fc, :], in0=bcB[:, g * 128:(g + 1) * 128], scalar=fm[:, 0, fc, b_:b_ + 1],
                                                                              in1=hst[:, fc, :], op0=ALU.mult, op1=ALU.add),
                     reads=[Bhst, Bfm, BbcB], writes=[Bhst])
                P.op("dve", lambda e, fc=fc, b_=b_, g=g: e.scalar_tensor_tensor(out=jk[:, :], in0=hst[:, fc, :], scalar=1.0, in1=bcC[:, g * 128:(g + 1) * 128],
                                                                              op0=ALU.mult, op1=ALU.mult, accum_out=fm[:, 4, fc, b_:b_ + 1]),
                     reads=[Bhst, BbcC, Bfm, Bjk], writes=[Bjk, Bfm])
            P.dma("sp", lambda e, b_=b_: e.dma_start(out=ssm_s_d[b_].rearrange("h q n -> (h q) n").rearrange("(fc p) n -> p fc n", p=128), in_=hst[:]), reads=[Bhst])
        P.op("dve", lambda e: e.tensor_tensor(out=fm[:, 2], in0=fm[:, 2], in1=dsc[:, :].unsqueeze(2).to_broadcast([128, 16, NS]), op=ALU.mult), reads=[Bfm, Bsel], writes=[Bfm])
        P.op("dve", lambda e: e.tensor_tensor(out=fm[:, 4], in0=fm[:, 4], in1=fm[:, 2], op=ALU.add), reads=[Bfm], writes=[Bfm])
        P.op("dve", lambda e: e.tensor_tensor(out=fm[:, 4], in0=fm[:, 4], in1=fm[:, 3], op=ALU.mult), reads=[Bfm], writes=[Bfm])
        P.op("act", lambda e: e.activation(out=ysq[:], in_=fm[:, 4], func=AF.Square), reads=[Bfm], writes=[Bysq])
        ones_bf1 = sb.alloc([128, 128], BF16)
        P.op("dve", lambda e: e.memset(ones_bf1[:], 1.0), writes=[Bsel])
        for g in range(4):
            for q in range(4):
                fc = g * 4 + q
                P.op("pe", lambda e, g=g, q=q, fc=fc: e.matmul(bank(2, 16)[:, g * NS:(g + 1) * NS], ones_bf1[:], ysq[:, fc, :], start=(q == 0), stop=(q == 3)),
                     reads=[Bysq, Bsel], writes=[B_ps[2]])
        P.op("act", lambda e: e.activation(out=rst[:].rearrange("p g b -> p (g b)"), in_=bank(2, 16), func=AF.Sqrt, bias=eps1[:, 0:1], scale=1.0 / 512), reads=[B_ps[2], B_c], writes=[Brst])
        P.op("dve", lambda e: e.reciprocal(out=rst[:], in_=rst[:]), reads=[Brst], writes=[Brst])
        for g in range(4):
            P.op("dve", lambda e, g=g: e.tensor_tensor(out=fm[:, 4, g * 4:(g + 1) * 4, :], in0=fm[:, 4, g * 4:(g + 1) * 4, :],
                                                       in1=rst[:, g:g + 1, :].to_broadcast([128, 4, NS]), op=ALU.mult), reads=[Bfm, Brst], writes=[Bfm])
        P.op("dve", lambda e: e.tensor_tensor(out=ynb[:], in0=fm[:, 4], in1=ngc[:, :].unsqueeze(2).to_broadcast([128, 16, NS]), op=ALU.mult), reads=[Bfm, B_c], writes=[Bynb])
        for m in range(KC):
            so = state["nwo"] % 2
            state["nwo"] += 1
            P.dma("pool", lambda e, so=so, m=m: e.dma_start(
                out=wos2[:, so, :, :], in_=ssm_w_out_d[:, m * 128:(m + 1) * 128].rearrange("(fc p) n -> p fc n", p=128)), writes=[Bwo2[so]])
            b = nb(0, 2)
            for fc in range(16):
                P.op("pe", lambda e, so=so, fc=fc, b=b: e.matmul(bank(b, NS), wos2[:, so, fc, :], ynb[:, fc, :], start=(fc == 0), stop=(fc == 15)),
                     reads=[Bwo2[so], Bynb], writes=[B_ps[b]])
            P.op("dve", lambda e, m=m, b=b: e.scalar_tensor_tensor(
                out=x32[:, m, SEQ:SEQ + NS], in0=bank(b, NS), scalar=1.0 / ALPHA, in1=x32[:, m, SEQ:SEQ + NS], op0=ALU.mult, op1=ALU.add),
                reads=[B_ps[b], B_x32[m][4]], writes=[B_x32[m][4]])
        sb.release(mk)
        P.barrier()


    def dsa():
        mk = sb.mark()
        rope = sb.alloc([128, 17, 16], F32)
        eps1 = sb.alloc([128, 1], F32)
        rt = sb.alloc([128, 2, 5, 128], F32)
        kT = sb.alloc([64, 4, SEQ], BF16)
        ikT = sb.alloc([64, SEQ], BF16)
        vaug = sb.alloc([128, 16, 4, 65], BF16)
        snew = sb.alloc([128, NS, 576], BF16)
        mA = sb.mark()
        wkv = sb.alloc([128, KC, 512], BF16)
        wik = sb.alloc([128, KC, 64], BF16)
        knb = sb.alloc([128, 2, 64], F32)
        kv32 = sb.alloc([128, 2, 512], F32)
        ik32 = sb.alloc([128, 2, 64], F32)
        st6 = sb.alloc([128, 2, 8], F32)
        BkT = [Buf("kT") for _ in range(16)]
        Bva = [Buf("vaug") for _ in range(16)]
        P.op("dve", lambda e: e.memset(vaug[:], 1.0), writes=Bva)
        Bsnew = Buf("snew")
        P.op("dve", lambda e: e.memset(snew[:], 0.0), writes=[Bsnew])
        Bc = Buf("att_consts")
        Bkv = [Buf("kv32") for _ in range(2)]
        Bik = [Buf("ik32") for _ in range(2)]
        Brt = [Buf("rt") for _ in range(2)]
        P.dma("pool", lambda e: e.dma_start(out=wkv[:], in_=att_w_in_d[:, 1024:1536].rearrange("(kc p) n -> p kc n", p=128)), writes=[Bc])
        P.dma("pool", lambda e: e.dma_start(out=wik[:], in_=att_w_in_d[:, 2048:2112].rearrange("(kc p) n -> p kc n", p=128)), reads=[Bc], writes=[Bc])
        P.dma("sp", lambda e: e.dma_start(out=knb[:], in_=att_kn_bc_d[:, :, :]), reads=[Bc], writes=[Bc])
        P.dma("sp", lambda e: e.dma_start(out=rope[:], in_=rope_d[:, :, :]), reads=[Bc], writes=[Bc])
        P.op("dve", lambda e: e.memset(eps1[:], LN_EPS), reads=[Bc], writes=[Bc])
        allx = lambda kc: [B_xbf[kc][t] for t in range(5)]

        def rope_apply(xh, nh, c, si, Bx):
            cos = rope[:, c:c + 1, 0:8].to_broadcast([128, nh, 8])
            sin = rope[:, c:c + 1, 8:16].to_broadcast([128, nh, 8])
            x1, x2 = xh[:, :, 0:8], xh[:, :, 8:16]
            t = [rt[:, si, k, 0:nh * 8].rearrange("p (h d) -> p h d", h=nh) for k in range(5)]
            P.op("dve", lambda e: e.tensor_tensor(out=t[0], in0=x1, in1=cos, op=ALU.mult), reads=[Bx, Bc], writes=[Brt[si]])
            P.op("dve", lambda e: e.tensor_tensor(out=t[1], in0=x2, in1=sin, op=ALU.mult), reads=[Bx, Bc, Brt[si]], writes=[Brt[si]])
            P.op("dve", lambda e: e.tensor_tensor(out=t[2], in0=x2, in1=cos, op=ALU.mult), reads=[Bx, Bc, Brt[si]], writes=[Brt[si]])
            P.op("dve", lambda e: e.tensor_tensor(out=t[3], in0=x1, in1=sin, op=ALU.mult), reads=[Bx, Bc, Brt[si]], writes=[Brt[si]])
            P.op("dve", lambda e: e.tensor_tensor(out=x1, in0=t[0], in1=t[1], op=ALU.subtract), reads=[Brt[si], Bx], writes=[Bx])
            P.op("dve", lambda e: e.tensor_tensor(out=x2, in0=t[2], in1=t[3], op=ALU.add), reads=[Brt[si], Bx], writes=[Bx])

        for ch in range(17):
            si = ch % 2
            tk = ch * 128 if ch < 16 else NCOL - 128
            b0, b1 = 2 * si, 2 * si + 1
            for kc in range(KC):
                P.op("pe", lambda e, kc=kc, tk=tk, b0=b0: e.matmul(bank(b0, 512), xbf[:, kc, tk:tk + 128], wkv[:, kc, :], start=(kc == 0), stop=(kc == KC - 1)),
                     reads=[Bc] + allx(kc), writes=[B_ps[b0]])
            for kc in range(KC):
                P.op("pe", lambda e, kc=kc, tk=tk, b1=b1: e.matmul(bank(b1, 64), xbf[:, kc, tk:tk + 128], wik[:, kc, :], start=(kc == 0), stop=(kc == KC - 1)),
                     reads=[Bc] + allx(kc), writes=[B_ps[b1]])
            P.op("act", lambda e, si=si, b0=b0: e.activation(out=kv32[:, si, :], in_=bank(b0, 512), func=AF.Copy), reads=[B_ps[b0]], writes=[Bkv[si]])
            P.op("act", lambda e, si=si, b1=b1: e.activation(out=ik32[:, si, :], in_=bank(b1, 64), func=AF.Copy), reads=[B_ps[b1]], writes=[Bik[si]])
            P.op("dve", lambda e, si=si: e.bn_stats(out=st6[:, si, 0:6], in_=ik32[:, si, :]), reads=[Bik[si]], writes=[Brt[si]])
            P.op("dve", lambda e, si=si: e.bn_aggr(out=st6[:, si, 6:8], in_=st6[:, si, 0:6]), reads=[Brt[si]], writes=[Brt[si]])
            P.op("act", lambda e, si=si: e.activation(out=st6[:, si, 7:8], in_=st6[:, si, 7:8], func=AF.Sqrt, bias=eps1[:, 0:1], scale=1.0), reads=[Brt[si], Bc], writes=[Brt[si]])
            P.op("dve", lambda e, si=si: e.reciprocal(out=st6[:, si, 7:8], in_=st6[:, si, 7:8]), reads=[Brt[si]], writes=[Brt[si]])
            P.op("dve", lambda e, si=si: e.tensor_scalar(out=ik32[:, si, :], in0=ik32[:, si, :], scalar1=st6[:, si, 6:7], scalar2=st6[:, si, 7:8],
                                                         op0=ALU.subtract, op1=ALU.mult), reads=[Bik[si], Brt[si]], writes=[Bik[si]])
            P.op("dve", lambda e, si=si: e.tensor_tensor(out=ik32[:, si, :], in0=ik32[:, si, :], in1=knb[:, 0, :], op=ALU.mult), reads=[Bik[si], Bc], writes=[Bik[si]])
            P.op("dve", lambda e, si=si: e.tensor_tensor(out=ik32[:, si, :], in0=ik32[:, si, :], in1=knb[:, 1, :], op=ALU.add), reads=[Bik[si], Bc], writes=[Bik[si]])
            c = min(ch, 16)
            rope_apply(kv32[:, si, 0:256].rearrange("p (h d) -> p h d", h=4), 4, c, si, Bkv[si])
            rope_apply(ik32[:, si, :].rearrange("p (h d) -> p h d", h=1), 1, c, si, Bik[si])
            if ch < 16:
                for hk in range(5):
                    bt = 4 + hk % 4
                    src = kv32[:, si, hk * 64:(hk + 1) * 64] if hk < 4 else ik32[:, si, :]
                    P.op("pe", lambda e, src=src, bt=bt: e.transpose(bank(bt, 128)[0:64, :], src, ident[:, :]),
                         reads=[Bkv[si] if hk < 4 else Bik[si], B_const], writes=[B_ps[bt]])
                    dst = kT[:, hk, tk:tk + 128] if hk < 4 else ikT[:, tk:tk + 128]
                    P.op("act", lambda e, dst=dst, bt=bt: e.activation(out=dst, in_=bank(bt, 128)[0:64, :], func=AF.Copy), reads=[B_ps[bt], BkT[ch]], writes=[BkT[ch]])
                P.op("act", lambda e, si=si, ch=ch: e.activation(out=vaug[:, ch, :, 0:64], in_=kv32[:, si, 256:512].rearrange("p (h d) -> p h d", h=4), func=AF.Copy),
                     reads=[Bkv[si], Bva[ch]], writes=[Bva[ch]])
                P.dma("sp", lambda e, si=si, tk=tk: e.dma_start(out=k_p_d[tk:tk + 128, :], in_=kv32[:, si, 0:256]), reads=[Bkv[si]])
                P.dma("sp", lambda e, si=si, tk=tk: e.dma_start(out=v_p_d[tk:tk + 128, :], in_=kv32[:, si, 256:512]), reads=[Bkv[si]])
                P.dma("sp", lambda e, si=si, tk=tk: e.dma_start(out=ik_p_d[tk:tk + 128, :], in_=ik32[:, si, :]), reads=[Bik[si]])
            else:
                P.dma("sp", lambda e, si=si: e.dma_start(out=k_s_d[:, :], in_=kv32[124:128, si, 0:256]), reads=[Bkv[si]])
                P.dma("sp", lambda e, si=si: e.dma_start(out=v_s_d[:, :], in_=kv32[124:128, si, 256:512]), reads=[Bkv[si]])
                P.dma("sp", lambda e, si=si: e.dma_start(out=ik_s_d[:, :], in_=ik32[124:128, si, :]), reads=[Bik[si]])
                for b_ in range(NS):
                    P.dma("pool", lambda e, si=si, b_=b_: e.dma_start(out=snew[0:1, b_, 0:512], in_=kv32[124 + b_:125 + b_, si, :]), reads=[Bkv[si], Bsnew], writes=[Bsnew])
                    P.dma("pool", lambda e, si=si, b_=b_: e.dma_start(out=snew[0:1, b_, 512:576], in_=ik32[124 + b_:125 + b_, si, :]), reads=[Bik[si], Bsnew], writes=[Bsnew])

        P.barrier()
        sb.release(mA)
        mB = sb.mark()
        TOPK = 256
        wq = sb.alloc([128, KC, 1024], BF16)
        wiq = sb.alloc([128, KC, 520], BF16)
        wo_s = sb.alloc([128, 1, KC, 128], BF16)
        negm = sb.alloc([128, 128], F32)
        q32 = sb.alloc([128, 1024], F32)
        iq32 = sb.alloc([128, 520], F32)
        qT = sb.alloc([64, 16, 128], BF16)
        iqT = sb.alloc([64, 8, 128], BF16)
        score = sb.alloc([128, SEQ], F32)
        work = sb.alloc([128, SEQ], F32)
        m8 = sb.alloc([128, 8], F32)
        thr = sb.alloc([128, 1], F32)
        maskT = sb.alloc([128, 16, 128], BF16)
        Pt = sb.alloc([128, 2, 512], BF16)
        rden = sb.alloc([128, 16], F32)
        o32 = sb.alloc([128, 1024], F32)
        oT = sb.alloc([128, KC, 128], BF16)
        Bw2 = Buf("attw")
        Bwo = [Buf("wo_s") for _ in range(2)]
        Bq32, Biq32, BqT, BiqT, Bscore, Bwork, Bm8, Bthr, BmaskT, Brden, Bo32 = (Buf(n) for n in (
            "q32", "iq32", "qT", "iqT", "score", "work", "m8", "thr", "maskT", "rden", "o32"))
        BPt = [Buf("Pt") for _ in range(2)]
        BoT = [Buf("oT") for _ in range(KC)]
        P.dma("pool", lambda e: e.dma_start(out=wq[:], in_=att_w_in_d[:, 0:1024].rearrange("(kc p) n -> p kc n", p=128)), writes=[Bw2])
        P.dma("pool", lambda e: e.dma_start(out=wiq[:, :, 0:512], in_=att_w_in_d[:, 1536:2048].rearrange("(kc p) n -> p kc n", p=128)), reads=[Bw2], writes=[Bw2])
        P.dma("pool", lambda e: e.dma_start(out=wiq[:, :, 512:520], in_=att_w_in_d[:, 2112:2120].rearrange("(kc p) n -> p kc n", p=128)), reads=[Bw2], writes=[Bw2])
        P.dma("sp", lambda e: e.dma_start(out=negm[:], in_=negm_d[:, :]), reads=[Bw2], writes=[Bw2])
        st2 = {"cnt": 0, "nwo": 0}

        def nb2(lo, n):
            b = lo + st2["cnt"] % n
            st2["cnt"] += 1
            return b

        for qi in range(16):
            tk = qi * 128
            nk = qi + 1
            W = nk * 128
            ti = tk // 512
            for half in range(2):
                b = nb2(0, 4)
                for kc in range(KC):
                    P.op("pe", lambda e, kc=kc, tk=tk, half=half, b=b: e.matmul(bank(b, 512), xbf[:, kc, tk:tk + 128], wq[:, kc, half * 512:(half + 1) * 512],
                                                                            start=(kc == 0), stop=(kc == KC - 1)), reads=[Bw2] + allx(kc), writes=[B_ps[b]])
                P.op("act", lambda e, half=half, b=b: e.activation(out=q32[:, half * 512:(half + 1) * 512], in_=bank(b, 512), func=AF.Copy), reads=[B_ps[b]], writes=[Bq32])
            b = nb2(0, 4)
            for kc in range(KC):
                P.op("pe", lambda e, kc=kc, tk=tk, b=b: e.matmul(bank(b, 512), xbf[:, kc, tk:tk + 128], wiq[:, kc, 0:512], start=(kc == 0), stop=(kc == KC - 1)),
                     reads=[Bw2] + allx(kc), writes=[B_ps[b]])
            P.op("act", lambda e, b=b: e.activation(out=iq32[:, 0:512], in_=bank(b, 512), func=AF.Copy), reads=[B_ps[b]], writes=[Biq32])
            b = nb2(0, 4)
            for kc in range(KC):
                P.op("pe", lambda e, kc=kc, tk=tk, b=b: e.matmul(bank(b, 8), xbf[:, kc, tk:tk + 128], wiq[:, kc, 512:520], start=(kc == 0), stop=(kc == KC - 1)),
                     reads=[Bw2] + allx(kc), writes=[B_ps[b]])
            P.op("act", lambda e, b=b: e.activation(out=iq32[:, 512:520], in_=bank(b, 8), func=AF.Copy, scale=float(8 ** -0.5 * 64 ** -0.5)), reads=[B_ps[b], Biq32], writes=[Biq32])
            rope_apply(q32[:, :].rearrange("p (h d) -> p h d", h=16), 16, qi, 0, Bq32)
            rope_apply(iq32[:, 0:512].rearrange("p (h d) -> p h d", h=8), 8, qi, 1, Biq32)
            for h in range(24):
                bt = 4 + h % 4
                src = q32[:, h * 64:(h + 1) * 64] if h < 16 else iq32[:, (h - 16) * 64:(h - 15) * 64]
                dst = qT[:, h, :] if h < 16 else iqT[:, h - 16, :]
                P.op("pe", lambda e, src=src, bt=bt: e.transpose(bank(bt, 128)[0:64, :], src, ident[:, :]), reads=[Bq32 if h < 16 else Biq32, B_const], writes=[B_ps[bt]])
                P.op("act", lambda e, dst=dst, bt=bt: e.activation(out=dst, in_=bank(bt, 128)[0:64, :], func=AF.Copy),
                     reads=[B_ps[bt], BqT if h < 16 else BiqT], writes=[BqT if h < 16 else BiqT])
            for h in range(8):
                b0 = 4 * (h % 2)
                for c4 in range((W + 511) // 512):
                    w = min(512, W - c4 * 512)
                    P.op("pe", lambda e, h=h, c4=c4, w=w, b0=b0: e.matmul(bank(b0 + c4, w), iqT[:, h, :], ikT[:, c4 * 512:c4 * 512 + w], start=True, stop=True),
                         reads=[BiqT] + BkT[c4 * 4:c4 * 4 + 4], writes=[B_ps[b0 + c4]])
                P.op("act", lambda e, b0=b0, W=W: e.activation(out=work[:, 0:W], in_=psum[:, b0 * 512:b0 * 512 + W], func=AF.Relu),
                     reads=[B_ps[b0 + c] for c in range(4)] + [Bwork], writes=[Bwork])
                if h == 0:
                    P.op("dve", lambda e, W=W: e.tensor_scalar(out=score[:, 0:W], in0=work[:, 0:W], scalar1=iq32[:, 512:513], scalar2=None, op0=ALU.mult),
                         reads=[Bwork, Biq32, Bscore], writes=[Bscore])
                else:
                    P.op("dve", lambda e, W=W, h=h: e.scalar_tensor_tensor(out=score[:, 0:W], in0=work[:, 0:W], scalar=iq32[:, 512 + h:513 + h], in1=score[:, 0:W],
                                                                           op0=ALU.mult, op1=ALU.add), reads=[Bwork, Biq32, Bscore], writes=[Bscore])
            P.op("dve", lambda e, tk=tk: e.tensor_tensor(out=score[:, tk:tk + 128], in0=score[:, tk:tk + 128], in1=negm[:, :], op=ALU.add), reads=[Bscore, Bw2], writes=[Bscore])
            if W <= TOPK:
                P.op("dve", lambda e: e.memset(thr[:], -1e29), reads=[Bthr], writes=[Bthr])
            else:
                P.op("act", lambda e, W=W: e.activation(out=work[:, 0:W], in_=score[:, 0:W], func=AF.Copy), reads=[Bscore, Bwork], writes=[Bwork])
                for r in range(TOPK // 8):
                    P.op("dve", lambda e, W=W: e.max(out=m8[:, :], in_=work[:, 0:W]), reads=[Bwork, Bm8], writes=[Bm8])
                    if r < TOPK // 8 - 1:
                        P.op("dve", lambda e, W=W: e.match_replace(out=work[:, 0:W], in_to_replace=m8[:, :], in_values=work[:, 0:W], imm_value=-1e30),
                             reads=[Bwork, Bm8], writes=[Bwork])
                P.op("dve", lambda e: e.tensor_scalar(out=thr[:], in0=m8[:, 7:8], scalar1=-1e29, scalar2=None, op0=ALU.max), reads=[Bm8, Bthr], writes=[Bthr])
            P.op("dve", lambda e, W=W: e.tensor_scalar(out=work[:, 0:W], in0=score[:, 0:W], scalar1=thr[:, 0:1], scalar2=None, op0=ALU.is_ge),
                 reads=[Bscore, Bthr, Bwork], writes=[Bwork])
            for kc in range(nk):
                bt = 4 + kc % 4
                P.op("pe", lambda e, kc=kc, bt=bt: e.transpose(bank(bt, 128), work[:, kc * 128:(kc + 1) * 128], ident[:, :]), reads=[Bwork, B_const], writes=[B_ps[bt]])
                P.op("act", lambda e, kc=kc, bt=bt: e.activation(out=maskT[:, kc, :], in_=bank(bt, 128), func=AF.Copy), reads=[B_ps[bt], BmaskT], writes=[BmaskT])
            for kvh in range(4):
                for kc in range(nk):
                    sb_ = nb2(0, 2)
                    pi = kc % 2
                    P.op("pe", lambda e, kvh=kvh, kc=kc, sb_=sb_: e.matmul(bank(sb_, 512), kT[:, kvh, kc * 128:(kc + 1) * 128],
                                                                          qT[:, kvh * 4:(kvh + 1) * 4, :].rearrange("p h q -> p (h q)"), start=True, stop=True),
                         reads=[BkT[kc], BqT], writes=[B_ps[sb_]])
                    P.op("act", lambda e, pi=pi, sb_=sb_: e.activation(out=Pt[:, pi, :], in_=bank(sb_, 512), func=AF.Exp, scale=0.125), reads=[B_ps[sb_], BPt[pi]], writes=[BPt[pi]])
                    P.op("dve", lambda e, pi=pi, kc=kc: e.tensor_tensor(out=Pt[:, pi, :].rearrange("p (h q) -> p h q", h=4), in0=Pt[:, pi, :].rearrange("p (h q) -> p h q", h=4),
                                                                        in1=maskT[:, kc:kc + 1, :].to_broadcast([128, 4, 128]), op=ALU.mult), reads=[BPt[pi], BmaskT], writes=[BPt[pi]])
                    for hq in range(4):
                        P.op("pe", lambda e, pi=pi, hq=hq, kc=kc, kvh=kvh, nk=nk: e.matmul(bank(4 + hq, 65), Pt[:, pi, hq * 128:(hq + 1) * 128], vaug[:, kc, kvh, :],
                                                                                     start=(kc == 0), stop=(kc == nk - 1)), reads=[BPt[pi], Bva[kc]], writes=[B_ps[4 + hq]])
                for hq in range(4):
                    h = kvh * 4 + hq
                    P.op("dve", lambda e, h=h, hq=hq: e.reciprocal(out=rden[:, h:h + 1], in_=bank(4 + hq, 65)[:, 64:65]), reads=[B_ps[4 + hq], Brden], writes=[Brden])
                    P.op("dve", lambda e, h=h, hq=hq: e.tensor_scalar(out=o32[:, h * 64:(h + 1) * 64], in0=bank(4 + hq, 65)[:, 0:64], scalar1=rden[:, h:h + 1], scalar2=None, op0=ALU.mult),
                         reads=[B_ps[4 + hq], Brden, Bo32], writes=[Bo32])
            for c in range(KC):
                bt = nb2(0, 4)
                P.op("pe", lambda e, c=c, bt=bt: e.transpose(bank(bt, 128), o32[:, c * 128:(c + 1) * 128], ident[:, :]), reads=[Bo32, B_const], writes=[B_ps[bt]])
                P.op("act", lambda e, c=c, bt=bt: e.activation(out=oT[:, c, :], in_=bank(bt, 128), func=AF.Copy), reads=[B_ps[bt]], writes=[BoT[c]])
            for m in range(KC):
                so = 0
                P.dma("pool", lambda e, so=so, m=m: e.dma_start(out=wo_s[:, so, :, :], in_=att_w_out_d[:, m * 128:(m + 1) * 128].rearrange("(kc p) n -> p kc n", p=128)), writes=[Bwo[so]])
                b = nb2(0, 4)
                for c in range(KC):
                    P.op("pe", lambda e, so=so, c=c, b=b: e.matmul(bank(b, 128), wo_s[:, so, c, :], oT[:, c, :], start=(c == 0), stop=(c == KC - 1)),
                         reads=[Bwo[so], BoT[c]], writes=[B_ps[b]])
                P.op("dve", lambda e, m=m, b=b, tk=tk: e.scalar_tensor_tensor(out=x32[:, m, tk:tk + 128], in0=bank(b, 128), scalar=1.0 / ALPHA, in1=x32[:, m, tk:tk + 128],
                                                                             op0=ALU.mult, op1=ALU.add), reads=[B_ps[b], B_x32[m][ti]], writes=[B_x32[m][ti]])

        P.barrier()
        sb.release(mB)
        NPG = 65
        GP = 16
        selc = sb.alloc([128, NS, 128], F32)
        onesc = sb.alloc([128, 128], F32)
        negp = sb.alloc([128, 1], F32)
        pio = sb.alloc([128, 1], F32)
        ptb = sb.alloc([128, NS * 64], I32)
        idx = sb.alloc([128, NS * 64], I32)
        qs32 = sb.alloc([128, 1024], F32)
        iqs32 = sb.alloc([128, 520], F32)
        q_bc = sb.alloc([128, 1024], F32)
        iq_bc = sb.alloc([128, 520], F32)
        sc_km = sb.alloc([128, NS, NPG], F32)
        mk_km = sb.alloc([128, NS, NPG], F32)
        thr_bc = sb.alloc([128, NS], F32)
        oTs = sb.alloc([64, 16, NS], BF16)
        Bcw, Bcc, Bidx, Bqs, Biqs, Bqbc, Biqbc, Bikg, Bkg, Bvg, Bvag, Btmpc, Bshh, Bsc, Bmk, Bflat, Bm8c, Bthrc, BS, BPk, Bop, Brdc, BoTs = (
            Buf(n) for n in ("wq_c", "cconst", "idx", "qs32", "iqs32", "q_bc", "iq_bc", "ikg", "kg", "vg", "vag", "tmpc", "shh", "sc_km", "mk_km",
                             "flat", "m8c", "thr_bc", "S_km", "P_km", "o_pad", "rdc", "oTs"))
        mC = sb.mark()
        wq_c = sb.alloc([128, KC, 1024], BF16)
        wiq_c = sb.alloc([128, KC, 520], BF16)
        P.dma("pool", lambda e: e.dma_start(out=wq_c[:], in_=att_w_in_d[:, 0:1024].rearrange("(kc p) n -> p kc n", p=128)), writes=[Bcw])
        P.dma("pool", lambda e: e.dma_start(out=wiq_c[:, :, 0:512], in_=att_w_in_d[:, 1536:2048].rearrange("(kc p) n -> p kc n", p=128)), reads=[Bcw], writes=[Bcw])
        P.dma("pool", lambda e: e.dma_start(out=wiq_c[:, :, 512:520], in_=att_w_in_d[:, 2112:2120].rearrange("(kc p) n -> p kc n", p=128)), reads=[Bcw], writes=[Bcw])
        for dst, src in ((selc, ssm_sel_d), (onesc, onesf_d), (negp, negp_d), (pio, piota_d), (ptb, pt_bc_d)):
            P.dma("sp", lambda e, dst=dst, src=src: e.dma_start(out=dst[:], in_=src), reads=[Bcc], writes=[Bcc])
        P.op("dve", lambda e: e.tensor_scalar(out=idx[:], in0=ptb[:], scalar1=128.0, scalar2=pio[:, 0:1], op0=ALU.mult, op1=ALU.add), reads=[Bcc], writes=[Bidx])
        TOKS = NCOL - 128
        for half in range(2):
            b = nb2(0, 4)
            for kc in range(KC):
                P.op("pe", lambda e, kc=kc, half=half, b=b: e.matmul(bank(b, 512), xbf[:, kc, TOKS:TOKS + 128], wq_c[:, kc, half * 512:(half + 1) * 512],
                                                                  start=(kc == 0), stop=(kc == KC - 1)), reads=[Bcw] + allx(kc), writes=[B_ps[b]])
            P.op("act", lambda e, half=half, b=b: e.activation(out=qs32[:, half * 512:(half + 1) * 512], in_=bank(b, 512), func=AF.Copy), reads=[B_ps[b], Bqs], writes=[Bqs])
        b = nb2(0, 4)
        for kc in range(KC):
            P.op("pe", lambda e, kc=kc, b=b: e.matmul(bank(b, 512), xbf[:, kc, TOKS:TOKS + 128], wiq_c[:, kc, 0:512], start=(kc == 0), stop=(kc == KC - 1)),
                 reads=[Bcw] + allx(kc), writes=[B_ps[b]])
        P.op("act", lambda e, b=b: e.activation(out=iqs32[:, 0:512], in_=bank(b, 512), func=AF.Copy), reads=[B_ps[b], Biqs], writes=[Biqs])
        b = nb2(0, 4)
        for kc in range(KC):
            P.op("pe", lambda e, kc=kc, b=b: e.matmul(bank(b, 8), xbf[:, kc, TOKS:TOKS + 128], wiq_c[:, kc, 512:520], start=(kc == 0), stop=(kc == KC - 1)),
                 reads=[Bcw] + allx(kc), writes=[B_ps[b]])
        P.op("act", lambda e, b=b: e.activation(out=iqs32[:, 512:520], in_=bank(b, 8), func=AF.Copy, scale=float(8 ** -0.5 * 64 ** -0.5)), reads=[B_ps[b], Biqs], writes=[Biqs])
        rope_apply(qs32[:, :].rearrange("p (h d) -> p h d", h=16), 16, 16, 0, Bqs)
        rope_apply(iqs32[:, 0:512].rearrange("p (h d) -> p h d", h=8), 8, 16, 1, Biqs)

        def gather(dst, cache_d, b_, j0, nj, Bd, src_off, width):
            for jj in range(nj):
                j = j0 + jj
                if j < 64:
                    c = b_ * 64 + j
                    P.dma("pool", lambda e, jj=jj, c=c: e.indirect_dma_start(
                        out=dst[:, jj, :], out_offset=None, in_=cache_d[:, :], in_offset=bass.IndirectOffsetOnAxis(ap=idx[:, c:c + 1], axis=0)),
                        reads=[Bidx, Bd], writes=[Bd])
                else:
                    P.op("act", lambda e, jj=jj: e.activation(out=dst[:, jj, :], in_=snew[:, b_, src_off:src_off + width], func=AF.Copy), reads=[Bsnew, Bd], writes=[Bd])

        groups = [(0, 16), (16, 16), (32, 16), (48, 16), (64, 1)]
        P.barrier()
        sb.release(mC)
        ikg = sb.alloc([128, GP, 64], F32)
        tmpc = sb.alloc([128, GP * 64], F32)
        shh = sb.alloc([128, GP], F32)
        for b_ in range(NS):
            for bank_i, w in ((0, 512), (1, 8)):
                P.op("pe", lambda e, b_=b_, bank_i=bank_i, w=w: e.matmul(bank(bank_i, w), selc[:, b_, :], iqs32[:, bank_i * 512:bank_i * 512 + w], start=True, stop=True),
                     reads=[Bcc, Biqs], writes=[B_ps[bank_i]])
            P.op("act", lambda e: e.activation(out=iq_bc[:, 0:512], in_=bank(0, 512), func=AF.Copy), reads=[B_ps[0], Biqbc], writes=[Biqbc])
            P.op("act", lambda e: e.activation(out=iq_bc[:, 512:520], in_=bank(1, 8), func=AF.Copy), reads=[B_ps[1], Biqbc], writes=[Biqbc])
            for (j0, nj) in groups:
                gather(ikg, cache_ik_d, b_, j0, nj, Bikg, 512, 64)
                for h in range(8):
                    P.op("dve", lambda e, h=h, nj=nj: e.tensor_tensor(out=tmpc[:, 0:nj * 64].rearrange("p (j d) -> p j d", j=nj), in0=ikg[:, 0:nj, :],
                                                                  in1=iq_bc[:, h * 64:(h + 1) * 64].unsqueeze(1).to_broadcast([128, nj, 64]), op=ALU.mult),
                         reads=[Bikg, Biqbc, Btmpc], writes=[Btmpc])
                    P.op("dve", lambda e, nj=nj: e.tensor_reduce(out=shh[:, 0:nj], in_=tmpc[:, 0:nj * 64].rearrange("p (j d) -> p j d", j=nj), axis=AX.X, op=ALU.add),
                         reads=[Btmpc, Bshh], writes=[Bshh])
                    P.op("act", lambda e, nj=nj: e.activation(out=shh[:, 0:nj], in_=shh[:, 0:nj], func=AF.Relu), reads=[Bshh], writes=[Bshh])
                    if h == 0:
                        P.op("dve", lambda e, nj=nj, j0=j0, b_=b_: e.tensor_scalar(out=sc_km[:, b_, j0:j0 + nj], in0=shh[:, 0:nj], scalar1=iq_bc[:, 512:513], scalar2=None, op0=ALU.mult),
                             reads=[Bshh, Biqbc, Bsc], writes=[Bsc])
                    else:
                        P.op("dve", lambda e, nj=nj, j0=j0, b_=b_, h=h: e.scalar_tensor_tensor(out=sc_km[:, b_, j0:j0 + nj], in0=shh[:, 0:nj], scalar=iq_bc[:, 512 + h:513 + h],
                                                                                            in1=sc_km[:, b_, j0:j0 + nj], op0=ALU.mult, op1=ALU.add), reads=[Bshh, Biqbc, Bsc], writes=[Bsc])
            P.op("dve", lambda e, b_=b_: e.tensor_tensor(out=sc_km[:, b_, 64:65], in0=sc_km[:, b_, 64:65], in1=negp[:, 0:1], op=ALU.add), reads=[Bsc, Bcc], writes=[Bsc])
            P.dma("sp", lambda e, b_=b_: e.dma_start(out=scr_d[b_, :].rearrange("(p j) -> p j", j=NPG), in_=sc_km[:, b_, :]), reads=[Bsc], writes=[Bflat])
        P.barrier()
        sb.release(mC)
        flat = sb.alloc([NS, NPG * 128], F32)
        m8c = sb.alloc([NS, 8], F32)
        r4 = sb.alloc([NS, NS], F32)
        P.dma("sp", lambda e: e.dma_start(out=flat[:, :], in_=scr_d[:, :]), reads=[Bflat], writes=[Bflat])
        for r in range(TOPK // 8):
            P.op("dve", lambda e: e.max(out=m8c[:, :], in_=flat[:, :]), reads=[Bflat, Bm8c], writes=[Bm8c])
            if r < TOPK // 8 - 1:
                P.op("dve", lambda e: e.match_replace(out=flat[:, :], in_to_replace=m8c[:, :], in_values=flat[:, :], imm_value=-1e30), reads=[Bflat, Bm8c], writes=[Bflat])
        P.op("dve", lambda e: e.tensor_scalar(out=r4[:, :], in0=ident[0:NS, 0:NS], scalar1=m8c[:, 7:8], scalar2=None, op0=ALU.mult), reads=[Bm8c, B_const], writes=[Bthrc])
        P.op("pe", lambda e: e.matmul(bank(2, NS), onesc[0:NS, :], r4[:, :], start=True, stop=True), reads=[Bcc, Bthrc], writes=[B_ps[2]])
        P.op("act", lambda e: e.activation(out=thr_bc[:, :], in_=bank(2, NS), func=AF.Copy), reads=[B_ps[2], Bthrc], writes=[Bthrc])
        for b_ in range(NS):
            P.op("dve", lambda e, b_=b_: e.tensor_scalar(out=mk_km[:, b_, :], in0=sc_km[:, b_, :], scalar1=thr_bc[:, b_:b_ + 1], scalar2=None, op0=ALU.is_ge),
                 reads=[Bsc, Bthrc, Bmk], writes=[Bmk])
        P.barrier()
        sb.release(mC)
        kg = sb.alloc([128, GP, 256], F32)
        vg = sb.alloc([128, GP, 256], F32)
        vag = sb.alloc([128, GP, 4, 65], BF16)
        tmpc2 = sb.alloc([128, GP * 64], F32)
        S_km = sb.alloc([128, GP, 16], F32)
        P_km = sb.alloc([128, GP, 16], BF16)
        o_pad = sb.alloc([128, 4, 64], F32)
        rdc = sb.alloc([NS, 4], F32)
        Btmpc2 = Buf("tmpc2")
        P.op("dve", lambda e: e.memset(vag[:], 1.0), writes=[Bvag])
        P.op("dve", lambda e: e.memset(o_pad[:], 0.0), writes=[Bop])
        for b_ in range(NS):
            for half in range(2):
                P.op("pe", lambda e, b_=b_, half=half: e.matmul(bank(half, 512), selc[:, b_, :], qs32[:, half * 512:(half + 1) * 512], start=True, stop=True),
                     reads=[Bcc, Bqs], writes=[B_ps[half]])
                P.op("act", lambda e, half=half: e.activation(out=q_bc[:, half * 512:(half + 1) * 512], in_=bank(half, 512), func=AF.Copy), reads=[B_ps[half], Bqbc], writes=[Bqbc])
            for gi, (j0, nj) in enumerate(groups):
                gather(kg, cache_k_d, b_, j0, nj, Bkg, 0, 256)
                gather(vg, cache_v_d, b_, j0, nj, Bvg, 256, 256)
                P.op("act", lambda e, nj=nj: e.activation(out=vag[:, 0:nj, :, 0:64], in_=vg[:, 0:nj, :].rearrange("p j (h d) -> p j h d", h=4), func=AF.Copy),
                     reads=[Bvg, Bvag], writes=[Bvag])
                for h in range(16):
                    kvh = h // 4
                    P.op("dve", lambda e, h=h, kvh=kvh, nj=nj: e.tensor_tensor(out=tmpc2[:, 0:nj * 64].rearrange("p (j d) -> p j d", j=nj), in0=kg[:, 0:nj, kvh * 64:(kvh + 1) * 64],
                                                                           in1=q_bc[:, h * 64:(h + 1) * 64].unsqueeze(1).to_broadcast([128, nj, 64]), op=ALU.mult),
                         reads=[Bkg, Bqbc, Btmpc2], writes=[Btmpc2])
                    P.op("dve", lambda e, h=h, nj=nj: e.tensor_reduce(out=S_km[:, 0:nj, h], in_=tmpc2[:, 0:nj * 64].rearrange("p (j d) -> p j d", j=nj), axis=AX.X, op=ALU.add),
                         reads=[Btmpc2, BS], writes=[BS])
                P.op("act", lambda e, nj=nj: e.activation(out=S_km[:, 0:nj, :], in_=S_km[:, 0:nj, :], func=AF.Exp, scale=0.125), reads=[BS], writes=[BS])
                P.op("dve", lambda e, nj=nj, j0=j0, b_=b_: e.tensor_tensor(out=P_km[:, 0:nj, :], in0=S_km[:, 0:nj, :],
                                                                       in1=mk_km[:, b_, j0:j0 + nj].unsqueeze(2).to_broadcast([128, nj, 16]), op=ALU.mult),
                     reads=[BS, Bmk, BPk], writes=[BPk])
                for jj in range(nj):
                    for kvh in range(4):
                        first = (gi == 0 and jj == 0)
                        last = (gi == len(groups) - 1 and jj == nj - 1)
                        P.op("pe", lambda e, jj=jj, kvh=kvh, first=first, last=last: e.matmul(bank(4 + kvh, 65)[0:4, :], P_km[:, jj, kvh * 4:(kvh + 1) * 4], vag[:, jj, kvh, :],
                                                                                        start=first, stop=last), reads=[BPk, Bvag], writes=[B_ps[4 + kvh]])
            for kvh in range(4):
                P.op("dve", lambda e, kvh=kvh: e.reciprocal(out=rdc[:, kvh:kvh + 1], in_=bank(4 + kvh, 65)[0:4, 64:65]), reads=[B_ps[4 + kvh], Brdc], writes=[Brdc])
                P.op("dve", lambda e, kvh=kvh: e.tensor_scalar(out=o_pad[0:4, kvh, :], in0=bank(4 + kvh, 65)[0:4, 0:64], scalar1=rdc[:, kvh:kvh + 1], scalar2=None, op0=ALU.mult),
                     reads=[B_ps[4 + kvh], Brdc, Bop], writes=[Bop])
            for kvh in range(4):
                bt = nb2(0, 4)
                P.op("pe", lambda e, kvh=kvh, bt=bt: e.transpose(bank(bt, 128)[0:64, :], o_pad[:, kvh, :], ident[:, :]), reads=[Bop, B_const], writes=[B_ps[bt]])
                P.op("act", lambda e, kvh=kvh, bt=bt, b_=b_: e.activation(out=oTs[:, kvh * 4:(kvh + 1) * 4, b_], in_=bank(bt, 128)[0:64, 0:4], func=AF.Copy),
                     reads=[B_ps[bt], BoTs], writes=[BoTs])
        P.barrier()
        sb.release(mC)
        woh = sb.alloc([64, 16, D], BF16)
        Bwoh = Buf("woh")
        P.dma("pool", lambda e: e.dma_start(out=woh[:], in_=att_w_out_d[:, :].rearrange("(h p) n -> p h n", p=64)), writes=[Bwoh])
        for m in range(KC):
            b = nb2(0, 4)
            for h in range(16):
                P.op("pe", lambda e, m=m, h=h, b=b: e.matmul(bank(b, NS), woh[:, h, m * 128:(m + 1) * 128], oTs[:, h, :], start=(h == 0), stop=(h == 15)),
                     reads=[Bwoh, BoTs], writes=[B_ps[b]])
            P.op("dve", lambda e, m=m, b=b: e.scalar_tensor_tensor(out=x32[:, m, SEQ:SEQ + NS], in0=bank(b, NS), scalar=1.0 / ALPHA, in1=x32[:, m, SEQ:SEQ + NS],
                                                                 op0=ALU.mult, op1=ALU.add), reads=[B_ps[b], B_x32[m][4]], writes=[B_x32[m][4]])
        sb.release(mk)
        P.barrier()


    def rwkv_sample():
        mk = sb.mark()
        R0, R1 = 96, 128
        TOK = NCOL - 128
        muc = sb.alloc([128, 48], F32)
        gnc = sb.alloc([128, 16], F32)
        vec = sb.alloc([128, D], F32)
        Bvec = Buf("rwvec")
        selr = sb.alloc([128, NS, 128], F32)
        blk = sb.alloc([128, 128], F32)
        epsg = sb.alloc([128, 1], F32)
        xsh = sb.alloc([128, KC, NS], F32)
        xmp = sb.alloc([128, 6, KC, 128], BF16)
        tm = sb.alloc([128, 6, D], F32)
        wsl = sb.alloc([128, 1, KC, 512], BF16)
        w1s = sb.alloc([128, KC, 256], BF16)
        w2s = sb.alloc([128, 3, D], BF16)
        t1T = sb.alloc([128, 3, 128], BF16)
        fmv = sb.alloc([128, 6, KC, NS], F32)
        wk = sb.alloc([128, 3, D], F32)
        hs = sb.alloc([128, 2, 16], F32)
        bc = sb.alloc([128, 5, D], F32)
        S = sb.alloc([128, KC, 64], F32)
        tS = sb.alloc([128, KC, 64], F32)
        sa = sb.alloc([128, 2, KC], F32)
        zb = sb.alloc([128, KC, NS], BF16)
        Bk, Bxsh, Bxmp, Btm, Bw1, Bt1, Bfm, Bwk, Bhs, Bbc, BS, BtS, Bsa, Bzb = (Buf(n) for n in (
            "rwc", "xsh", "xmp", "tm", "w1s", "t1T", "fmv", "wk", "hs", "bc", "S", "tS", "sa", "zb"))
        Bwsl = [Buf("wsl") for _ in range(2)]
        stg = sb.alloc([128, 128], F32)
        Bstg = Buf("stg")
        for dst, src, rows in ((muc, rw_mu_c_d, 48), (gnc, rw_gn_c_d, 16)):
            P.op("dve", lambda e: e.memset(stg[:, :], 0.0), writes=[Bstg])
            P.dma("sp", lambda e, src=src, rows=rows: e.dma_start(out=stg[:rows, :], in_=src[:, :]), writes=[Bstg])
            P.op("pe", lambda e: e.transpose(bank(7, 128), stg[:, :], ident[:, :]), reads=[Bstg, B_const], writes=[B_ps[7]])
            P.op("dve", lambda e, dst=dst, rows=rows: e.tensor_copy(out=dst[:, 0:rows], in_=bank(7, rows)), reads=[B_ps[7], Bk], writes=[Bk])
        P.dma("sp", lambda e: e.dma_start(out=selr[:], in_=ssm_sel_d[:, :, :]), reads=[Bk], writes=[Bk])
        P.dma("sp", lambda e: e.dma_start(out=blk[:], in_=rw_blk_d[:, :]), reads=[Bk], writes=[Bk])
        P.op("dve", lambda e: e.memset(epsg[:], 64e-5), reads=[Bk], writes=[Bk])
        P.dma("pool", lambda e: e.dma_start(out=w1s[:, :, 0:64], in_=rw_w1_d[:, :].rearrange("(kc p) n -> p kc n", p=128)), writes=[Bw1])
        P.dma("pool", lambda e: e.dma_start(out=w1s[:, :, 64:128], in_=rw_a1_d[:, :].rearrange("(kc p) n -> p kc n", p=128)), reads=[Bw1], writes=[Bw1])
        P.dma("pool", lambda e: e.dma_start(out=w1s[:, :, 128:256], in_=rw_g1_d[:, :].rearrange("(kc p) n -> p kc n", p=128)), reads=[Bw1], writes=[Bw1])
        P.dma("pool", lambda e: e.dma_start(out=w2s[0:64, 0, :], in_=rw_w2_d[:, :]), reads=[Bw1], writes=[Bw1])
        P.dma("pool", lambda e: e.dma_start(out=w2s[0:64, 1, :], in_=rw_a2_d[:, :]), reads=[Bw1], writes=[Bw1])
        P.dma("pool", lambda e: e.dma_start(out=w2s[:, 2, :], in_=rw_g2_d[:, :]), reads=[Bw1], writes=[Bw1])
        P.op("dve", lambda e: e.memset(stg[:, :], 0.0), writes=[Bstg])
        for c in range(KC):
            bt = 4 + c % 4
            P.dma("sp", lambda e, c=c: e.dma_start(out=stg[0:NS, :], in_=st_shift_d[:, c * 128:(c + 1) * 128]), writes=[Bstg])
            P.op("pe", lambda e, bt=bt: e.transpose(bank(bt, 128), stg[:, :], ident[:, :]), reads=[Bstg, B_const], writes=[B_ps[bt]])
            P.op("dve", lambda e, c=c, bt=bt: e.tensor_tensor(out=xsh[:, c, :], in0=bank(bt, 128)[:, 0:NS], in1=x32[:, c, SEQ:SEQ + NS], op=ALU.subtract),
                 reads=[B_ps[bt], B_x32[c][4], Bxsh], writes=[Bxsh])
        P.op("dve", lambda e: e.memset(xmp[:], 0.0), writes=[Bxmp])
        P.op("dve", lambda e: e.memset(tm[:], 0.0), writes=[Btm])
        P.op("dve", lambda e: e.memset(wk[:], 0.0), writes=[Bwk])
        xs4 = x32[:, :, SEQ:SEQ + NS]
        for j in range(6):
            P.op("dve", lambda e, j=j: e.tensor_tensor(out=fmv[:, 4], in0=xsh[:, :, :], in1=muc[:, j * 8:(j + 1) * 8].unsqueeze(2).to_broadcast([128, KC, NS]), op=ALU.mult),
                 reads=[Bxsh, Bk, Bfm], writes=[Bfm])
            P.op("dve", lambda e, j=j: e.tensor_tensor(out=xmp[:, j, :, 124:128], in0=fmv[:, 4], in1=xs4, op=ALU.add),
                 reads=[Bfm, Bxmp] + [B_x32[c][4] for c in range(KC)], writes=[Bxmp])
        st3 = {"cnt": 0, "nw": 0}

        def nb3(lo, n):
            b = lo + st3["cnt"] % n
            st3["cnt"] += 1
            return b
        for ti_, (nm, jx) in enumerate((("r", 0), ("k", 2), ("v", 3))):
            for half in range(2):
                sl = 0
                P.dma("pool", lambda e, sl=sl, nm=nm, half=half: e.dma_start(out=wsl[:, sl, :, :], in_=rw_w_d[nm][:, half * 512:(half + 1) * 512].rearrange("(kc p) n -> p kc n", p=128)),
                      writes=[Bwsl[sl]])
                b = nb3(0, 4)
                for kc in range(KC):
                    P.op("pe", lambda e, kc=kc, jx=jx, sl=sl, b=b: e.matmul(bank(b, 512), xmp[:, jx, kc, :], wsl[:, sl, kc, :], start=(kc == 0), stop=(kc == KC - 1)),
                         reads=[Bxmp, Bwsl[sl]], writes=[B_ps[b]])
                P.op("act", lambda e, ti_=ti_, half=half, b=b: e.activation(out=tm[R0:R1, ti_, half * 512:(half + 1) * 512], in_=bank(b, 512)[R0:R1, :], func=AF.Copy),
                     reads=[B_ps[b], Btm], writes=[Btm])
        for li, (jx, c0, M, fn) in enumerate(((1, 0, 64, AF.Tanh), (4, 64, 64, AF.Copy), (5, 128, 128, AF.Sigmoid))):
            b = nb3(4, 4)
            for kc in range(KC):
                P.op("pe", lambda e, kc=kc, jx=jx, c0=c0, M=M, b=b: e.matmul(bank(b, 128)[0:M, :], w1s[:, kc, c0:c0 + M], xmp[:, jx, kc, :], start=(kc == 0), stop=(kc == KC - 1)),
                     reads=[Bw1, Bxmp], writes=[B_ps[b]])
            P.op("act", lambda e, li=li, M=M, fn=fn, b=b: e.activation(out=t1T[0:M, li, :], in_=bank(b, 128)[0:M, :], func=fn), reads=[B_ps[b], Bt1], writes=[Bt1])
            for half in range(2):
                b2 = nb3(0, 4)
                P.op("pe", lambda e, li=li, M=M, half=half, b2=b2: e.matmul(bank(b2, 512), t1T[0:M, li, :], w2s[0:M, li, half * 512:(half + 1) * 512], start=True, stop=True),
                     reads=[Bt1, Bw1], writes=[B_ps[b2]])
                P.op("act", lambda e, li=li, half=half, b2=b2: e.activation(out=tm[R0:R1, 3 + li, half * 512:(half + 1) * 512], in_=bank(b2, 512)[R0:R1, :], func=AF.Copy),
                     reads=[B_ps[b2], Btm], writes=[Btm])
        T = lambda k: tm[R0:R1, k, :]
        def V(k):
            P.dma("sp", lambda e, k=k: e.dma_start(out=vec[R0:R1, :], in_=rw_vec_bc_d[:, k, :]), reads=[Bvec], writes=[Bvec])
            return vec[R0:R1, :]
        Wk = lambda k: wk[R0:R1, k, :]
        h3 = lambda ap: ap.rearrange("p (h d) -> p h d", h=16)
        v_ = V(0)
        P.op("dve", lambda e, v_=v_: e.tensor_tensor(out=T(3), in0=T(3), in1=v_, op=ALU.add), reads=[Btm, Bvec], writes=[Btm])
        P.op("act", lambda e: e.activation(out=T(3), in_=T(3), func=AF.Exp, scale=-1.0), reads=[Btm], writes=[Btm])
        P.op("dve", lambda e: e.tensor_scalar(out=T(3), in0=T(3), scalar1=1.0, scalar2=None, op0=ALU.add), reads=[Btm], writes=[Btm])
        P.op("dve", lambda e: e.reciprocal(out=T(3), in_=T(3)), reads=[Btm], writes=[Btm])
        P.op("act", lambda e: e.activation(out=T(3), in_=T(3), func=AF.Exp, scale=-float(np.exp(-0.5))), reads=[Btm], writes=[Btm])
        v_ = V(1)
        P.op("dve", lambda e, v_=v_: e.tensor_tensor(out=T(4), in0=T(4), in1=v_, op=ALU.add), reads=[Btm, Bvec], writes=[Btm])
        P.op("act", lambda e: e.activation(out=T(4), in_=T(4), func=AF.Sigmoid), reads=[Btm], writes=[Btm])
        v_ = V(2)
        P.op("dve", lambda e, v_=v_: e.tensor_tensor(out=Wk(0), in0=T(1), in1=v_, op=ALU.mult), reads=[Btm, Bvec, Bwk], writes=[Bwk])
        P.op("dve", lambda e: e.tensor_tensor(out=Wk(1), in0=Wk(0), in1=Wk(0), op=ALU.mult), reads=[Bwk], writes=[Bwk])
        P.op("dve", lambda e: e.tensor_reduce(out=hs[R0:R1, 0, :], in_=h3(Wk(1)), axis=AX.X, op=ALU.add), reads=[Bwk, Bhs], writes=[Bhs])
        P.op("act", lambda e: e.activation(out=hs[R0:R1, 0, :], in_=hs[R0:R1, 0, :], func=AF.Sqrt), reads=[Bhs], writes=[Bhs])
        P.op("dve", lambda e: e.tensor_scalar(out=hs[R0:R1, 0, :], in0=hs[R0:R1, 0, :], scalar1=1e-12, scalar2=None, op0=ALU.max), reads=[Bhs], writes=[Bhs])
        P.op("dve", lambda e: e.reciprocal(out=hs[R0:R1, 0, :], in_=hs[R0:R1, 0, :]), reads=[Bhs], writes=[Bhs])
        P.op("dve", lambda e: e.tensor_tensor(out=h3(Wk(0)), in0=h3(Wk(0)), in1=hs[R0:R1, 0, :].unsqueeze(2).to_broadcast([32, 16, 64]), op=ALU.mult), reads=[Bwk, Bhs], writes=[Bwk])
        P.op("dve", lambda e: e.tensor_tensor(out=Wk(1), in0=Wk(0), in1=T(4), op=ALU.mult), reads=[Bwk, Btm], writes=[Bwk])
        v_ = V(3)
        P.op("dve", lambda e, v_=v_: e.scalar_tensor_tensor(out=Wk(2), in0=T(4), scalar=-1.0, in1=v_, op0=ALU.add, op1=ALU.mult), reads=[Btm, Bvec, Bwk], writes=[Bwk])
        P.op("dve", lambda e: e.scalar_tensor_tensor(out=Wk(2), in0=Wk(2), scalar=1.0, in1=T(1), op0=ALU.add, op1=ALU.mult), reads=[Bwk, Btm], writes=[Bwk])
        P.op("dve", lambda e: e.tensor_tensor(out=T(4), in0=T(0), in1=Wk(2), op=ALU.mult), reads=[Btm, Bwk], writes=[Btm])
        v_ = V(4)
        P.op("dve", lambda e, v_=v_: e.tensor_tensor(out=T(4), in0=T(4), in1=v_, op=ALU.mult), reads=[Btm, Bvec], writes=[Btm])
        P.op("dve", lambda e: e.tensor_reduce(out=hs[R0:R1, 1, :], in_=h3(T(4)), axis=AX.X, op=ALU.add), reads=[Btm, Bhs], writes=[Bhs])
        P.op("dve", lambda e: e.tensor_tensor(out=h3(T(4)), in0=h3(T(2)), in1=hs[R0:R1, 1, :].unsqueeze(2).to_broadcast([32, 16, 64]), op=ALU.mult), reads=[Btm, Bhs], writes=[Btm])
        def tm_to_fm(k_src, k_dst, src_tile, Bsrc):
            for c in range(KC):
                bt = nb3(4, 4)
                P.op("pe", lambda e, c=c, bt=bt: e.transpose(bank(bt, 128), src_tile[:, k_src, c * 128:(c + 1) * 128], ident[:, :]), reads=[Bsrc, B_const], writes=[B_ps[bt]])
                P.op("act", lambda e, c=c, bt=bt: e.activation(out=fmv[:, k_dst, c, :], in_=bank(bt, 128)[:, 124:128], func=AF.Copy), reads=[B_ps[bt], Bfm], writes=[Bfm])
        tm_to_fm(2, 0, tm, Btm)
        tm_to_fm(4, 2, tm, Btm)
        tm_to_fm(5, 3, tm, Btm)
        srcs = ((wk, 0, Bwk), (tm, 3, Btm), (wk, 1, Bwk), (wk, 2, Bwk), (tm, 0, Btm))
        for b_ in range(NS):
            for q, (tile_, k_, Bsrc) in enumerate(srcs):
                for half in range(2):
                    bb_ = nb3(0, 4)
                    P.op("pe", lambda e, b_=b_, tile_=tile_, k_=k_, half=half, bb_=bb_: e.matmul(bank(bb_, 512), selr[:, b_, :], tile_[:, k_, half * 512:(half + 1) * 512], start=True, stop=True),
                         reads=[Bk, Bsrc], writes=[B_ps[bb_]])
                    P.op("act", lambda e, q=q, half=half, bb_=bb_: e.activation(out=bc[:, q, half * 512:(half + 1) * 512], in_=bank(bb_, 512), func=AF.Copy), reads=[B_ps[bb_], Bbc], writes=[Bbc])
            P.dma("sp", lambda e, b_=b_: e.dma_start(out=S[:], in_=st_wkv_d[b_].rearrange("(q hp) i j -> (hp i) q j", hp=2)), reads=[BS], writes=[BS])

            def rowv(q, hp):
                return bc[hp * 64:(hp + 1) * 64, q, :].rearrange("p (pr two j) -> p pr two j", two=2, j=64)[:, :, hp, :]
            Sh = lambda t_, hp: t_[hp * 64:(hp + 1) * 64, :, :]
            for hp in range(2):
                P.op("dve", lambda e, hp=hp: e.tensor_tensor(out=Sh(tS, hp), in0=Sh(S, hp), in1=rowv(0, hp), op=ALU.mult), reads=[BS, Bbc, BtS], writes=[BtS])
            P.op("dve", lambda e: e.tensor_reduce(out=sa[:, 0, :], in_=tS[:, :, :], axis=AX.X, op=ALU.add), reads=[BtS, Bsa], writes=[Bsa])
            for hp in range(2):
                P.op("dve", lambda e, hp=hp: e.tensor_tensor(out=Sh(S, hp), in0=Sh(S, hp), in1=rowv(1, hp), op=ALU.mult), reads=[BS, Bbc], writes=[BS])
                P.op("dve", lambda e, hp=hp: e.tensor_tensor(out=Sh(tS, hp), in0=rowv(2, hp), in1=sa[hp * 64:(hp + 1) * 64, 0, :].unsqueeze(2).to_broadcast([64, KC, 64]), op=ALU.mult),
                     reads=[Bbc, Bsa, BtS], writes=[BtS])
            P.op("dve", lambda e: e.tensor_tensor(out=S[:], in0=S[:], in1=tS[:], op=ALU.subtract), reads=[BS, BtS], writes=[BS])
            for hp in range(2):
                P.op("dve", lambda e, hp=hp, b_=b_: e.tensor_tensor(out=Sh(tS, hp), in0=rowv(3, hp), in1=fmv[hp * 64:(hp + 1) * 64, 0, :, b_:b_ + 1].to_broadcast([64, KC, 64]), op=ALU.mult),
                     reads=[Bbc, Bfm, BtS], writes=[BtS])
            P.op("dve", lambda e: e.tensor_tensor(out=S[:], in0=S[:], in1=tS[:], op=ALU.add), reads=[BS, BtS], writes=[BS])
            P.dma("sp", lambda e, b_=b_: e.dma_start(out=wkv_s_d[b_].rearrange("(q hp) i j -> (hp i) q j", hp=2), in_=S[:]), reads=[BS])
            for hp in range(2):
                P.op("dve", lambda e, hp=hp: e.tensor_tensor(out=Sh(tS, hp), in0=Sh(S, hp), in1=rowv(4, hp), op=ALU.mult), reads=[BS, Bbc, BtS], writes=[BtS])
            P.op("dve", lambda e, b_=b_: e.tensor_reduce(out=fmv[:, 1, :, b_], in_=tS[:, :, :], axis=AX.X, op=ALU.add), reads=[BtS, Bfm], writes=[Bfm])
        yv = fmv[:, 1].rearrange("p c b -> p (c b)")
        t4 = fmv[:, 4].rearrange("p c b -> p (c b)")
        t5 = fmv[:, 5].rearrange("p c b -> p (c b)")
        P.op("dve", lambda e: e.tensor_tensor(out=t4, in0=yv, in1=yv, op=ALU.mult), reads=[Bfm], writes=[Bfm])
        P.op("pe", lambda e: e.matmul(bank(0, 32), blk[:, :], yv, start=True, stop=True), reads=[Bk, Bfm], writes=[B_ps[0]])
        P.op("pe", lambda e: e.matmul(bank(1, 32), blk[:, :], t4, start=True, stop=True), reads=[Bk, Bfm], writes=[B_ps[1]])
        P.op("act", lambda e: e.activation(out=t5, in_=bank(0, 32), func=AF.Copy), reads=[B_ps[0], Bfm], writes=[Bfm])
        P.op("dve", lambda e: e.tensor_tensor(out=t4, in0=t5, in1=t5, op=ALU.mult), reads=[Bfm], writes=[Bfm])
        P.op("dve", lambda e: e.tensor_tensor(out=t4, in0=bank(1, 32), in1=t4, op=ALU.subtract), reads=[B_ps[1], Bfm], writes=[Bfm])
        P.op("act", lambda e: e.activation(out=t4, in_=t4, func=AF.Sqrt, bias=epsg[:, 0:1], scale=1.0), reads=[Bfm, Bk], writes=[Bfm])
        P.op("dve", lambda e: e.reciprocal(out=t4, in_=t4), reads=[Bfm], writes=[Bfm])
        P.op("dve", lambda e: e.tensor_tensor(out=yv, in0=yv, in1=t5, op=ALU.subtract), reads=[Bfm], writes=[Bfm])
        P.op("dve", lambda e: e.tensor_tensor(out=yv, in0=yv, in1=t4, op=ALU.mult), reads=[Bfm], writes=[Bfm])
        P.op("dve", lambda e: e.tensor_tensor(out=fmv[:, 1], in0=fmv[:, 1], in1=gnc[:, 0:8].unsqueeze(2).to_broadcast([128, KC, NS]), op=ALU.mult), reads=[Bfm, Bk], writes=[Bfm])
        P.op("dve", lambda e: e.tensor_tensor(out=fmv[:, 1], in0=fmv[:, 1], in1=gnc[:, 8:16].unsqueeze(2).to_broadcast([128, KC, NS]), op=ALU.add), reads=[Bfm, Bk], writes=[Bfm])
        P.op("dve", lambda e: e.tensor_tensor(out=fmv[:, 1], in0=fmv[:, 1], in1=fmv[:, 2], op=ALU.add), reads=[Bfm], writes=[Bfm])
        P.op("dve", lambda e: e.tensor_tensor(out=zb[:], in0=fmv[:, 1], in1=fmv[:, 3], op=ALU.mult), reads=[Bfm, Bzb], writes=[Bzb])
        for half in range(2):
            sl = 0
            P.dma("pool", lambda e, sl=sl, half=half: e.dma_start(out=wsl[:, sl, :, :], in_=rw_w_d["o"][:, half * 512:(half + 1) * 512].rearrange("(kc p) n -> p kc n", p=128)),
                  writes=[Bwsl[sl]])
            for mm in range(4):
                m = half * 4 + mm
                b = nb3(0, 4)
                for kc in range(KC):
                    P.op("pe", lambda e, kc=kc, sl=sl, mm=mm, b=b: e.matmul(bank(b, NS), wsl[:, sl, kc, mm * 128:(mm + 1) * 128], zb[:, kc, :], start=(kc == 0), stop=(kc == KC - 1)),
                         reads=[Bwsl[sl], Bzb], writes=[B_ps[b]])
                P.op("dve", lambda e, m=m, b=b: e.scalar_tensor_tensor(out=x32[:, m, SEQ:SEQ + NS], in0=bank(b, NS), scalar=1.0 / ALPHA, in1=x32[:, m, SEQ:SEQ + NS],
                                                                     op0=ALU.mult, op1=ALU.add), reads=[B_ps[b], B_x32[m][4]], writes=[B_x32[m][4]])
        sb.release(mk)
        P.barrier()


    def rwkv_prompt():
        mk = sb.mark()
        TS = 16
        muc = sb.alloc([128, 48], F32)
        vec = sb.alloc([128, D], F32)
        mbd = sb.alloc([16, D], F32)
        epsg = sb.alloc([128, 1], F32)
        xlast = sb.alloc([128, KC, 1], F32)
        ST = sb.alloc([64, D], F32)
        STb = sb.alloc([64, D], BF16)
        tm = sb.alloc([128, 6, D], F32)
        w1s = sb.alloc([128, KC, 256], BF16)
        w2s = sb.alloc([128, 3, D], BF16)
        Bk, Bvec, Bxl, BST, BSTb, Btm, Bw1 = (Buf(n) for n in ("rwpc", "rwpvec", "xlast", "ST", "STb", "tmp_", "w1s"))
        Bscr = [Buf("scr_h") for _ in range(3)]
        Bscy = Buf("scr_y")
        stg = sb.alloc([128, 128], F32)
        Bstg = Buf("stg")
        P.op("dve", lambda e: e.memset(stg[:, :], 0.0), writes=[Bstg])
        P.dma("sp", lambda e: e.dma_start(out=stg[:48, :], in_=rw_mu_c_d[:, :]), writes=[Bstg])
        P.op("pe", lambda e: e.transpose(bank(7, 128), stg[:, :], ident[:, :]), reads=[Bstg, B_const], writes=[B_ps[7]])
        P.op("dve", lambda e: e.tensor_copy(out=muc[:, 0:48], in_=bank(7, 48)), reads=[B_ps[7], Bk], writes=[Bk])
        P.dma("sp", lambda e: e.dma_start(out=mbd[:], in_=rw_mbd_d[:, :]), reads=[Bk], writes=[Bk])
        P.op("dve", lambda e: e.memset(epsg[:], 64e-5), reads=[Bk], writes=[Bk])
        P.op("dve", lambda e: e.memset(xlast[:], 0.0), writes=[Bxl])
        P.op("dve", lambda e: e.memset(ST[:], 0.0), writes=[BST])
        P.op("dve", lambda e: e.memset(STb[:], 0.0), writes=[BSTb])
        P.dma("pool", lambda e: e.dma_start(out=w1s[:, :, 0:64], in_=rw_w1_d[:, :].rearrange("(kc p) n -> p kc n", p=128)), writes=[Bw1])
        P.dma("pool", lambda e: e.dma_start(out=w1s[:, :, 64:128], in_=rw_a1_d[:, :].rearrange("(kc p) n -> p kc n", p=128)), reads=[Bw1], writes=[Bw1])
        P.dma("pool", lambda e: e.dma_start(out=w1s[:, :, 128:256], in_=rw_g1_d[:, :].rearrange("(kc p) n -> p kc n", p=128)), reads=[Bw1], writes=[Bw1])
        P.dma("pool", lambda e: e.dma_start(out=w2s[0:64, 0, :], in_=rw_w2_d[:, :]), reads=[Bw1], writes=[Bw1])
        P.dma("pool", lambda e: e.dma_start(out=w2s[0:64, 1, :], in_=rw_a2_d[:, :]), reads=[Bw1], writes=[Bw1])
        P.dma("pool", lambda e: e.dma_start(out=w2s[:, 2, :], in_=rw_g2_d[:, :]), reads=[Bw1], writes=[Bw1])
        st3 = {"cnt": 0}

        def nb3(lo, n):
            b = lo + st3["cnt"] % n
            st3["cnt"] += 1
            return b

        def V(k):
            P.dma("sp", lambda e, k=k: e.dma_start(out=vec[:, :], in_=rw_vec128_d[:, k, :]), reads=[Bvec], writes=[Bvec])
            return vec[:, :]
        T = lambda k: tm[:, k, :]
        h3 = lambda ap: ap.rearrange("p (h d) -> p h d", h=16)
        mwork = sb.mark()

        import os
        NCHK = int(os.environ.get("KRWP_CH", SEQ // 128))
        NSTEP = int(os.environ.get("KRWP_STEPS", TS))
        for ch in range(NCHK):
            tk = ch * 128
            ti = tk // 512
            mA_ = sb.mark()
            dx = sb.alloc([128, KC, 128], F32)
            xmp = sb.alloc([128, 6, KC, 128], BF16)
            wsl = sb.alloc([128, KC, 512], BF16)
            t1T = sb.alloc([128, 3, 128], BF16)
            wk = sb.alloc([128, 3, D], F32)
            tmpx = wk[:, 0, :].rearrange("p (c t) -> p c t", c=KC)
            hs = sb.alloc([128, 2, 16], F32)
            wkb = sb.alloc([128, 3, D], BF16)
            Bdx, Bxmp, Bwsl, Bt1, Bwk, Bhs, Bwkb = (Buf(n) for n in ("dx", "xmp", "wsl", "t1T", "wk", "hs", "wkb"))
            Btx = Bwk
            Wk = lambda k, wk=wk: wk[:, k, :]
            xin = [B_x32[c][ti] for c in range(KC)]
            P.op("dve", lambda e, tk=tk, dx=dx: e.tensor_tensor(out=dx[:, :, 0:1], in0=xlast[:, :, :], in1=x32[:, :, tk:tk + 1], op=ALU.subtract), reads=[Bxl] + xin, writes=[Bdx])
            P.op("dve", lambda e, tk=tk, dx=dx: e.tensor_tensor(out=dx[:, :, 1:128], in0=x32[:, :, tk:tk + 127], in1=x32[:, :, tk + 1:tk + 128], op=ALU.subtract), reads=xin + [Bdx], writes=[Bdx])
            P.op("dve", lambda e, tk=tk: e.tensor_copy(out=xlast[:, :, :], in_=x32[:, :, tk + 127:tk + 128]), reads=xin + [Bdx], writes=[Bxl])
            for j in range(6):
                P.op("dve", lambda e, j=j, dx=dx, tmpx=tmpx: e.tensor_tensor(out=tmpx[:], in0=dx[:], in1=muc[:, j * 8:(j + 1) * 8].unsqueeze(2).to_broadcast([128, KC, 128]), op=ALU.mult),
                     reads=[Bdx, Bk, Btx], writes=[Btx])
                P.op("dve", lambda e, j=j, tk=tk, tmpx=tmpx, xmp=xmp: e.tensor_tensor(out=xmp[:, j], in0=tmpx[:], in1=x32[:, :, tk:tk + 128], op=ALU.add), reads=[Btx, Bxmp] + xin, writes=[Bxmp])
            for ti_, (nm, jx) in enumerate((("r", 0), ("k", 2), ("v", 3))):
                for half in range(2):
                    P.dma("pool", lambda e, nm=nm, half=half, wsl=wsl: e.dma_start(out=wsl[:], in_=rw_w_d[nm][:, half * 512:(half + 1) * 512].rearrange("(kc p) n -> p kc n", p=128)),
                          writes=[Bwsl])
                    b = nb3(0, 4)
                    for kc in range(KC):
                        P.op("pe", lambda e, kc=kc, jx=jx, b=b, xmp=xmp, wsl=wsl: e.matmul(bank(b, 512), xmp[:, jx, kc, :], wsl[:, kc, :], start=(kc == 0), stop=(kc == KC - 1)),
                             reads=[Bxmp, Bwsl], writes=[B_ps[b]])
                    P.op("act", lambda e, ti_=ti_, half=half, b=b: e.activation(out=tm[:, ti_, half * 512:(half + 1) * 512], in_=bank(b, 512), func=AF.Copy), reads=[B_ps[b], Btm], writes=[Btm])
            for li, (jx, c0, M, fn) in enumerate(((1, 0, 64, AF.Tanh), (4, 64, 64, AF.Copy), (5, 128, 128, AF.Sigmoid))):
                b = nb3(4, 4)
                for kc in range(KC):
                    P.op("pe", lambda e, kc=kc, jx=jx, c0=c0, M=M, b=b, xmp=xmp: e.matmul(bank(b, 128)[0:M, :], w1s[:, kc, c0:c0 + M], xmp[:, jx, kc, :], start=(kc == 0), stop=(kc == KC - 1)),
                         reads=[Bw1, Bxmp], writes=[B_ps[b]])
                P.op("act", lambda e, li=li, M=M, fn=fn, b=b, t1T=t1T: e.activation(out=t1T[0:M, li, :], in_=bank(b, 128)[0:M, :], func=fn), reads=[B_ps[b], Bt1], writes=[Bt1])
                for half in range(2):
                    b2 = nb3(0, 4)
                    P.op("pe", lambda e, li=li, M=M, half=half, b2=b2, t1T=t1T: e.matmul(bank(b2, 512), t1T[0:M, li, :], w2s[0:M, li, half * 512:(half + 1) * 512], start=True, stop=True),
                         reads=[Bt1, Bw1], writes=[B_ps[b2]])
                    P.op("act", lambda e, li=li, half=half, b2=b2: e.activation(out=tm[:, 3 + li, half * 512:(half + 1) * 512], in_=bank(b2, 512), func=AF.Copy), reads=[B_ps[b2], Btm], writes=[Btm])
            v_ = V(0)
            P.op("dve", lambda e, v_=v_: e.tensor_tensor(out=T(3), in0=T(3), in1=v_, op=ALU.add), reads=[Btm, Bvec], writes=[Btm])
            P.op("act", lambda e: e.activation(out=T(3), in_=T(3), func=AF.Exp, scale=-1.0), reads=[Btm], writes=[Btm])
            P.op("dve", lambda e: e.tensor_scalar(out=T(3), in0=T(3), scalar1=1.0, scalar2=None, op0=ALU.add), reads=[Btm], writes=[Btm])
            P.op("dve", lambda e: e.reciprocal(out=T(3), in_=T(3)), reads=[Btm], writes=[Btm])
            P.op("act", lambda e: e.activation(out=T(3), in_=T(3), func=AF.Exp, scale=-float(np.exp(-0.5))), reads=[Btm], writes=[Btm])
            v_ = V(1)
            P.op("dve", lambda e, v_=v_: e.tensor_tensor(out=T(4), in0=T(4), in1=v_, op=ALU.add), reads=[Btm, Bvec], writes=[Btm])
            P.op("act", lambda e: e.activation(out=T(4), in_=T(4), func=AF.Sigmoid), reads=[Btm], writes=[Btm])
            v_ = V(2)
            P.op("dve", lambda e, v_=v_, Wk=Wk: e.tensor_tensor(out=Wk(0), in0=T(1), in1=v_, op=ALU.mult), reads=[Btm, Bvec, Bwk], writes=[Bwk])
            P.op("dve", lambda e, Wk=Wk: e.tensor_tensor(out=Wk(1), in0=Wk(0), in1=Wk(0), op=ALU.mult), reads=[Bwk], writes=[Bwk])
            P.op("dve", lambda e, Wk=Wk, hs=hs: e.tensor_reduce(out=hs[:, 0, :], in_=h3(Wk(1)), axis=AX.X, op=ALU.add), reads=[Bwk, Bhs], writes=[Bhs])
            P.op("act", lambda e, hs=hs: e.activation(out=hs[:, 0, :], in_=hs[:, 0, :], func=AF.Sqrt), reads=[Bhs], writes=[Bhs])
            P.op("dve", lambda e, hs=hs: e.tensor_scalar(out=hs[:, 0, :], in0=hs[:, 0, :], scalar1=1e-12, scalar2=None, op0=ALU.max), reads=[Bhs], writes=[Bhs])
            P.op("dve", lambda e, hs=hs: e.reciprocal(out=hs[:, 0, :], in_=hs[:, 0, :]), reads=[Bhs], writes=[Bhs])
            P.op("dve", lambda e, Wk=Wk, hs=hs: e.tensor_tensor(out=h3(Wk(0)), in0=h3(Wk(0)), in1=hs[:, 0, :].unsqueeze(2).to_broadcast([128, 16, 64]), op=ALU.mult), reads=[Bwk, Bhs], writes=[Bwk])
            P.op("dve", lambda e, Wk=Wk: e.tensor_tensor(out=Wk(1), in0=Wk(0), in1=T(4), op=ALU.mult), reads=[Bwk, Btm], writes=[Bwk])
            v_ = V(3)
            P.op("dve", lambda e, v_=v_, Wk=Wk: e.scalar_tensor_tensor(out=Wk(2), in0=T(4), scalar=-1.0, in1=v_, op0=ALU.add, op1=ALU.mult), reads=[Btm, Bvec, Bwk], writes=[Bwk])
            P.op("dve", lambda e, Wk=Wk: e.scalar_tensor_tensor(out=Wk(2), in0=Wk(2), scalar=1.0, in1=T(1), op0=ALU.add, op1=ALU.mult), reads=[Bwk, Btm], writes=[Bwk])
            P.op("act", lambda e, Wk=Wk, wkb=wkb: e.activation(out=wkb[:, 0, :], in_=Wk(2), func=AF.Copy), reads=[Bwk, Bwkb], writes=[Bwkb])
            P.op("act", lambda e, Wk=Wk, wkb=wkb: e.activation(out=wkb[:, 1, :], in_=Wk(1), func=AF.Copy, scale=-1.0), reads=[Bwk, Bwkb], writes=[Bwkb])
            P.op("act", lambda e, wkb=wkb: e.activation(out=wkb[:, 2, :], in_=T(2), func=AF.Copy), reads=[Btm, Bwkb], writes=[Bwkb])
            for q in range(3):
                P.dma("sp", lambda e, q=q, tk=tk, wkb=wkb: e.dma_start(out=scr_h_d[q, tk:tk + 128, :], in_=wkb[:, q, :]), reads=[Bwkb, Bscr[q]], writes=[Bscr[q]])
            P.op("dve", lambda e, Wk=Wk: e.tensor_tensor(out=T(4), in0=T(0), in1=Wk(2), op=ALU.mult), reads=[Btm, Bwk], writes=[Btm])
            v_ = V(4)
            P.op("dve", lambda e, v_=v_: e.tensor_tensor(out=T(4), in0=T(4), in1=v_, op=ALU.mult), reads=[Btm, Bvec], writes=[Btm])
            P.op("dve", lambda e, hs=hs: e.tensor_reduce(out=hs[:, 1, :], in_=h3(T(4)), axis=AX.X, op=ALU.add), reads=[Btm, Bhs], writes=[Bhs])
            P.op("dve", lambda e, hs=hs: e.tensor_tensor(out=h3(T(4)), in0=h3(T(2)), in1=hs[:, 1, :].unsqueeze(2).to_broadcast([128, 16, 64]), op=ALU.mult), reads=[Btm, Bhs], writes=[Btm])
            m_hm = sb.mark()
            kkH = sb.alloc([64, 128, 16], BF16)
            rH = sb.alloc([64, 128, 16], BF16)
            dH = sb.alloc([64, 128, 16], F32)
            BkkH, BrH, BdH = Buf("kkH"), Buf("rH"), Buf("dH")
            for (src, dst, Bs, Bd) in ((wk[:, 0, :], kkH, Bwk, BkkH), (tm[:, 0, :], rH, Btm, BrH), (tm[:, 3, :], dH, Btm, BdH)):
                for h in range(16):
                    bt = nb3(4, 4)
                    P.op("pe", lambda e, src=src, h=h, bt=bt: e.transpose(bank(bt, 128)[0:64, :], src[:, h * 64:(h + 1) * 64], ident[:, :]), reads=[Bs, B_const], writes=[B_ps[bt]])
                    P.op("act", lambda e, dst=dst, h=h, bt=bt: e.activation(out=dst[:, :, h], in_=bank(bt, 128)[0:64, :], func=AF.Copy), reads=[B_ps[bt], Bd], writes=[Bd])
            P.barrier()
            mtop = sb.mark()
            sb.release(mA_)
            Hop = sb.alloc([16, 2, 3, TS, 64], BF16)
            SAb = sb.alloc([16, D], BF16)
            VBb = sb.alloc([16, D], BF16)
            Ym = sb.alloc([16, D], F32)
            yH = sb.alloc([16, 2, TS, 64], F32)
            assert sb.mark() <= m_hm, "scan scratch overlaps head-major operands"
            sb.release(mtop)
            BHop = [Buf("Hop") for _ in range(2)]
            BSAb, BVBb, BYm = Buf("SAb"), Buf("VBb"), Buf("Ym")
            ByH = [Buf("yH") for _ in range(2)]
            for sbk in range(128 // TS):
                sl = sbk % 2
                t0 = sbk * TS
                for q in range(3):
                    P.dma("sp", lambda e, q=q, sl=sl, tk=tk, t0=t0, Hop=Hop: e.dma_start(out=Hop[:, sl, q, :, :], in_=scr_h_d[q, tk + t0:tk + t0 + TS, :].rearrange("t (h j) -> h t j", h=16)),
                          reads=[Bscr[q], BHop[sl]], writes=[BHop[sl]])
                for tt in range(NSTEP):
                    t = t0 + tt
                    for half in range(2):
                        P.op("pe", lambda e, t=t, half=half, kkH=kkH: e.matmul(bank(half, 512)[0:16, :], kkH[:, t, :], STb[:, half * 512:(half + 1) * 512], start=True, stop=True),
                             reads=[BkkH, BSTb], writes=[B_ps[half]])
                    P.op("dve", lambda e, SAb=SAb: e.tensor_tensor(out=SAb[:, :], in0=psum[0:16, 0:D], in1=mbd[:, :], op=ALU.mult), reads=[B_ps[0], B_ps[1], Bk, BSAb], writes=[BSAb])
                    P.op("dve", lambda e, sl=sl, tt=tt, VBb=VBb, Hop=Hop: e.tensor_tensor(out=VBb[:, :].rearrange("p (h i) -> p h i", h=16), in0=mbd[:, :].rearrange("p (h i) -> p h i", h=16),
                                                                                in1=Hop[:, sl, 2, tt:tt + 1, :].to_broadcast([16, 16, 64]), op=ALU.mult), reads=[BHop[sl], Bk, BVBb], writes=[BVBb])
                    for half in range(2):
                        P.op("pe", lambda e, sl=sl, tt=tt, half=half, Hop=Hop, SAb=SAb: e.matmul(bank(2 + half, 512)[0:64, :], Hop[:, sl, 1, tt, :], SAb[:, half * 512:(half + 1) * 512], start=True, stop=False),
                             reads=[BHop[sl], BSAb], writes=[B_ps[2 + half]])
                        P.op("pe", lambda e, sl=sl, tt=tt, half=half, Hop=Hop, VBb=VBb: e.matmul(bank(2 + half, 512)[0:64, :], Hop[:, sl, 0, tt, :], VBb[:, half * 512:(half + 1) * 512], start=False, stop=True),
                             reads=[BHop[sl], BVBb], writes=[B_ps[2 + half]])
                    P.op("dve", lambda e, t=t, dH=dH: e.tensor_tensor(out=ST[:, :].rearrange("p (h i) -> p h i", h=16), in0=ST[:, :].rearrange("p (h i) -> p h i", h=16),
                                                                    in1=dH[:, t, :].unsqueeze(2).to_broadcast([64, 16, 64]), op=ALU.mult), reads=[BST, BdH], writes=[BST])
                    P.op("dve", lambda e: e.tensor_tensor(out=ST[:, :], in0=ST[:, :], in1=psum[0:64, 2 * 512:2 * 512 + D], op=ALU.add), reads=[BST, B_ps[2], B_ps[3]], writes=[BST])
                    P.op("act", lambda e: e.activation(out=STb[:, :], in_=ST[:, :], func=AF.Copy), reads=[BST, BSTb], writes=[BSTb])
                    for half in range(2):
                        P.op("pe", lambda e, t=t, half=half, rH=rH: e.matmul(bank(4 + half, 512)[0:16, :], rH[:, t, :], STb[:, half * 512:(half + 1) * 512], start=True, stop=True),
                             reads=[BrH, BSTb], writes=[B_ps[4 + half]])
                    P.op("dve", lambda e, Ym=Ym: e.tensor_tensor(out=Ym[:, :], in0=psum[0:16, 4 * 512:4 * 512 + D], in1=mbd[:, :], op=ALU.mult), reads=[B_ps[4], B_ps[5], Bk, BYm], writes=[BYm])
                    P.op("dve", lambda e, sl=sl, tt=tt, Ym=Ym, yH=yH: e.tensor_reduce(out=yH[:, sl, tt, :], in_=Ym[:, :].rearrange("p (h i) -> p i h", h=16), axis=AX.X, op=ALU.add),
                         reads=[BYm, ByH[sl]], writes=[ByH[sl]])
                P.dma("sp", lambda e, sl=sl, tk=tk, t0=t0, yH=yH: e.dma_start(out=scr_y_d[tk + t0:tk + t0 + TS, :].rearrange("t (h i) -> h t i", h=16), in_=yH[:, sl, :, :]),
                      reads=[ByH[sl], Bscy], writes=[Bscy])
            P.barrier()
            sb.release(mA_)
            ytm = sb.alloc([128, D], F32)
            sq = sb.alloc([128, D], F32)
            hs2 = sb.alloc([128, 2, 16], F32)
            zbf = sb.alloc([128, KC, 128], BF16)
            wsl2 = sb.alloc([128, KC, 512], BF16)
            Bytm, Bsq, Bhs2, Bwsl2 = Buf("ytm"), Buf("sq"), Buf("hs2"), Buf("wsl2")
            Bzbf = [Buf("zbf") for _ in range(KC)]
            P.dma("sp", lambda e, tk=tk, ytm=ytm: e.dma_start(out=ytm[:, :], in_=scr_y_d[tk:tk + 128, :]), reads=[Bscy], writes=[Bytm])
            P.op("dve", lambda e, ytm=ytm, hs2=hs2: e.tensor_reduce(out=hs2[:, 0, :], in_=h3(ytm[:, :]), axis=AX.X, op=ALU.add), reads=[Bytm, Bhs2], writes=[Bhs2])
            P.op("dve", lambda e, hs2=hs2: e.tensor_scalar(out=hs2[:, 0, :], in0=hs2[:, 0, :], scalar1=1.0 / 64, scalar2=None, op0=ALU.mult), reads=[Bhs2], writes=[Bhs2])
            P.op("dve", lambda e, ytm=ytm, hs2=hs2: e.tensor_tensor(out=h3(ytm[:, :]), in0=h3(ytm[:, :]), in1=hs2[:, 0, :].unsqueeze(2).to_broadcast([128, 16, 64]), op=ALU.subtract), reads=[Bytm, Bhs2], writes=[Bytm])
            P.op("dve", lambda e, ytm=ytm, sq=sq: e.tensor_tensor(out=sq[:, :], in0=ytm[:, :], in1=ytm[:, :], op=ALU.mult), reads=[Bytm, Bsq], writes=[Bsq])
            P.op("dve", lambda e, sq=sq, hs2=hs2: e.tensor_reduce(out=hs2[:, 1, :], in_=h3(sq[:, :]), axis=AX.X, op=ALU.add), reads=[Bsq, Bhs2], writes=[Bhs2])
            P.op("act", lambda e, hs2=hs2: e.activation(out=hs2[:, 1, :], in_=hs2[:, 1, :], func=AF.Sqrt, bias=epsg[:, 0:1], scale=1.0 / 64), reads=[Bhs2, Bk], writes=[Bhs2])
            P.op("dve", lambda e, hs2=hs2: e.reciprocal(out=hs2[:, 1, :], in_=hs2[:, 1, :]), reads=[Bhs2], writes=[Bhs2])
            P.op("dve", lambda e, ytm=ytm, hs2=hs2: e.tensor_tensor(out=h3(ytm[:, :]), in0=h3(ytm[:, :]), in1=hs2[:, 1, :].unsqueeze(2).to_broadcast([128, 16, 64]), op=ALU.mult), reads=[Bytm, Bhs2], writes=[Bytm])
            v_ = V(5)
            P.op("dve", lambda e, v_=v_, ytm=ytm: e.tensor_tensor(out=ytm[:, :], in0=ytm[:, :], in1=v_, op=ALU.mult), reads=[Bytm, Bvec], writes=[Bytm])
            v_ = V(6)
            P.op("dve", lambda e, v_=v_, ytm=ytm: e.tensor_tensor(out=ytm[:, :], in0=ytm[:, :], in1=v_, op=ALU.add), reads=[Bytm, Bvec], writes=[Bytm])
            P.op("dve", lambda e, ytm=ytm: e.tensor_tensor(out=ytm[:, :], in0=ytm[:, :], in1=T(4), op=ALU.add), reads=[Bytm, Btm], writes=[Bytm])
            P.op("dve", lambda e, ytm=ytm: e.tensor_tensor(out=ytm[:, :], in0=ytm[:, :], in1=T(5), op=ALU.mult), reads=[Bytm, Btm], writes=[Bytm])
            for c in range(KC):
                bt = nb3(4, 4)
                P.op("pe", lambda e, c=c, bt=bt, ytm=ytm: e.transpose(bank(bt, 128), ytm[:, c * 128:(c + 1) * 128], ident[:, :]), reads=[Bytm, B_const], writes=[B_ps[bt]])
                P.op("act", lambda e, c=c, bt=bt, zbf=zbf: e.activation(out=zbf[:, c, :], in_=bank(bt, 128), func=AF.Copy), reads=[B_ps[bt], Bzbf[c]], writes=[Bzbf[c]])
            for half in range(2):
                P.dma("pool", lambda e, half=half, wsl2=wsl2: e.dma_start(out=wsl2[:], in_=rw_w_d["o"][:, half * 512:(half + 1) * 512].rearrange("(kc p) n -> p kc n", p=128)), writes=[Bwsl2])
                for mm in range(4):
                    m = half * 4 + mm
                    b = nb3(0, 4)
                    for kc in range(KC):
                        P.op("pe", lambda e, kc=kc, mm=mm, b=b, wsl2=wsl2, zbf=zbf: e.matmul(bank(b, 128), wsl2[:, kc, mm * 128:(mm + 1) * 128], zbf[:, kc, :], start=(kc == 0), stop=(kc == KC - 1)),
                             reads=[Bwsl2, Bzbf[kc]], writes=[B_ps[b]])
                    P.op("dve", lambda e, m=m, b=b, tk=tk: e.scalar_tensor_tensor(out=x32[:, m, tk:tk + 128], in0=bank(b, 128), scalar=1.0 / ALPHA, in1=x32[:, m, tk:tk + 128],
                                                                                 op0=ALU.mult, op1=ALU.add), reads=[B_ps[b], B_x32[m][ti]], writes=[B_x32[m][ti]])
            P.barrier()
            sb.release(mwork)
        so = sb.alloc([64, 2, 64], F32)
        Bso = [Buf("so") for _ in range(2)]
        for h in range(16):
            bt = nb3(4, 4)
            sl = h % 2
            P.op("pe", lambda e, h=h, bt=bt: e.transpose(bank(bt, 64)[0:64, :], ST[:, h * 64:(h + 1) * 64], ident[0:64, 0:64]), reads=[BST, B_const], writes=[B_ps[bt]])
            P.op("act", lambda e, sl=sl, bt=bt: e.activation(out=so[:, sl, :], in_=bank(bt, 64)[0:64, :], func=AF.Copy), reads=[B_ps[bt], Bso[sl]], writes=[Bso[sl]])
            P.dma("sp", lambda e, h=h, sl=sl: e.dma_start(out=wkv_p_d[h], in_=so[:, sl, :]), reads=[Bso[sl]])
        sb.release(mk)
        P.barrier()

    def mixer_ln(i):
        mk = sb.mark()
        ln = alloc_ln()
        layer_norm(i * 3 + 1, ln)
        sb.release(mk)
        P.barrier()

    def store_fm_to_tm(dst_d, ntok, col0):
        mk = sb.mark()
        tout = sb.alloc([128, 2, D], F32)
        Bt = [Buf("tout0"), Buf("tout1")]
        k = 0
        for t0 in range(0, ntok, 128):
            n = min(128, ntok - t0)
            s = k % 2
            col = col0 + t0 + n - 128
            tis = sorted(set([min(col // 512, 4), min((col + 127) // 512, 4)]))
            for c in range(KC):
                b = (k * KC + c) % 8
                P.op("pe", lambda e, c=c, col=col, b=b: e.transpose(bank(b, 128), x32[:, c, col:col + 128], ident[:, :]),
                     reads=[B_x32[c][ti] for ti in tis] + [B_const], writes=[B_ps[b]])
                if c % 2 == 0:
                    P.op("dve", lambda e, c=c, s=s, b=b: e.tensor_copy(out=tout[:, s, c * 128:(c + 1) * 128], in_=bank(b, 128)),
                         reads=[B_ps[b]], writes=[Bt[s]])
                else:
                    P.op("act", lambda e, c=c, s=s, b=b: e.activation(out=tout[:, s, c * 128:(c + 1) * 128], in_=bank(b, 128), func=AF.Copy),
                         reads=[B_ps[b]], writes=[Bt[s]])
            P.dma("sp", lambda e, t0=t0, n=n, s=s: e.dma_start(out=dst_d[t0:t0 + n, :], in_=tout[128 - n:128, s, :]), reads=[Bt[s]])
            k += 1
        sb.release(mk)


    def store_last_cols(dst_d, col_end, nrows):
        mk = sb.mark()
        tout = sb.alloc([128, D], F32)
        Bt = Buf("tout_s")
        col = col_end - 128
        tis = sorted(set([min(col // 512, 4), min((col_end - 1) // 512, 4)]))
        for c in range(KC):
            b = c % 8
            P.op("pe", lambda e, c=c, col=col, b=b: e.transpose(bank(b, 128), x32[:, c, col:col + 128], ident[:, :]),
                 reads=[B_x32[c][ti] for ti in tis] + [B_const], writes=[B_ps[b]])
            P.op("act", lambda e, c=c, b=b: e.activation(out=tout[:, c * 128:(c + 1) * 128], in_=bank(b, 128), func=AF.Copy), reads=[B_ps[b], Bt], writes=[Bt])
        P.dma("sp", lambda e: e.dma_start(out=dst_d[:, :], in_=tout[128 - nrows:128, :]), reads=[Bt])
        sb.release(mk)
        P.barrier()

    if only_mixer == "rw":
        import os
        if os.environ.get("KRW", "sp") in ("s", "sp"):
            rwkv_sample()
        if os.environ.get("KRW", "sp") in ("p", "sp"):
            rwkv_prompt()
        mixer_ln(3)
    if only_mixer == "att":
        dsa()
        mixer_ln(2)
    if only_mixer == "ssm":
        if mamba_prompt() == "stop":
            P.emit()
            nc._k_in_names = in_names
            nc._k_out_names = out_names
            nc._k_dbg = dbg_names
            return nc
        mixer_ln(1)
    for i in range(DEPTH):
        if i >= stop_after:
            break
        ffn(i, 0)
        if i == 3:
            store_last_cols(sh_p_d, SEQ, 1)
            store_last_cols(sh_s_d, NCOL, NS)
        if i == 0 and "gm" in mixers:
            gmlp()
        if i == 1 and "ssm" in mixers:
            mamba_prompt()
        if i == 2 and "att" in mixers:
            dsa()
        if i == 3 and "rw" in mixers:
            rwkv_sample()
            if "rwp" in mixers:
                rwkv_prompt()
        mixer_ln(i)
        ffn(i, 1)
        ple(i)

    if dbg >= 1:
        store_fm_to_tm(yp_d, SEQ, 0)
    P.barrier()
    if dbg >= 5:
        store_fm_to_tm(ys_d, NS, SEQ)
    P.emit()
    nc._k_in_names = in_names
    nc._k_dbg = dbg_names
    nc._k_out_names = out_names
    return nc


_W_NAMES = ["ffn_w_up", "ffn_w_down", "ple_w_p", "ple_w_g", "gm_w_in", "gm_w_out"]


def make_in_maps(inp):
    f = lambda a: np.ascontiguousarray(a, dtype=np.float32)
    shared = {k: f(inp[k]) for k in _W_NAMES}
    shared["ln_g"] = f(inp["ln_g"]).reshape(DEPTH * 3 * KC, 128)
    shared["ln_b"] = f(inp["ln_b"]).reshape(DEPTH * 3 * KC, 128)
    shared["ple_b_g"] = f(inp["ple_b_g"]).reshape(DEPTH * KC, 128)
    shared["gm_lng_c"] = f(inp["gm_ln_g"]).reshape(16, 128)
    shared["gm_lnb_c"] = f(inp["gm_ln_b"]).reshape(16, 128)
    shared["gm_wT"] = f(np.transpose(inp["gm_ws"], (2, 0, 1)))
    shared["gm_bs_bc"] = f(np.broadcast_to(inp["gm_bs"][None], (128, 8, 128)))
    shared["gm_w00"] = f(np.broadcast_to(inp["gm_ws"][None, :, 0, 0], (128, 8)))
    shared["gm_bs0"] = f(np.broadcast_to(inp["gm_bs"][None, :, 0], (128, 8)))
    shared["gm_lng_bc"] = f(np.broadcast_to(inp["gm_ln_g"][None], (32, 2048)))
    shared["gm_lnb_bc"] = f(np.broadcast_to(inp["gm_ln_b"][None], (32, 2048)))
    shared["ssm_w_in"] = f(inp["ssm_w_in"])
    shared["ssm_w_out"] = f(inp["ssm_w_out"])
    shared["ssm_cw_c"] = f(inp["ssm_conv_w"]).reshape(96, 128)
    shared["ssm_cb_c"] = f(inp["ssm_conv_b"]).reshape(24, 128)
    shared["ssm_ng_c"] = f(inp["ssm_norm_g"]).reshape(16, 128)
    shared["ssm_hv_bc"] = f(np.broadcast_to(np.stack([inp["ssm_dt_bias"], inp["ssm_a_log"], inp["ssm_d"]])[None], (128, 3, 32)))
    shared["ssm_cw_bc"] = f(np.broadcast_to(inp["ssm_conv_w"][None], (32, 4, 3072)))
    shared["ssm_cb_bc"] = f(np.broadcast_to(inp["ssm_conv_b"][None], (32, 3072)))
    shared["ssm_d_c"] = f(np.repeat(inp["ssm_d"], 64)).reshape(16, 128)
    sel = np.zeros((128, NS, 128), np.float32)
    for b in range(NS):
        sel[124 + b, b, :] = 1.0
    shared["ssm_sel"] = sel
    shared["att_w_in"] = f(inp["att_w_in"])
    shared["att_w_out"] = f(inp["att_w_out"])
    shared["att_kn_bc"] = f(np.broadcast_to(np.stack([inp["att_kn_g"], inp["att_kn_b"]])[None], (128, 2, 64)))
    inv = (np.float32(500000.0) ** (-np.arange(8, dtype=np.float32) / np.float32(8))).astype(np.float32)
    pos = np.concatenate([np.arange(SEQ, dtype=np.float32), np.full((128,), 8192.0, np.float32)])
    ang = (pos[:, None] * inv[None, :]).astype(np.float32)
    cs = np.concatenate([np.cos(ang), np.sin(ang)], 1).astype(np.float32)
    shared["rope"] = f(cs.reshape(17, 128, 16).transpose(1, 0, 2))
    shared["cache_k"] = f(inp["cache_k"]).reshape(2560 * 128, 256)
    shared["cache_v"] = f(inp["cache_v"]).reshape(2560 * 128, 256)
    shared["cache_ik"] = f(inp["cache_idx_k"]).reshape(2560 * 128, 64)
    shared["piota"] = np.arange(128, dtype=np.float32).reshape(128, 1)
    shared["negp"] = np.where(np.arange(128) == 0, 0.0, -1e30).astype(np.float32).reshape(128, 1)
    shared["rw_mu_c"] = f(inp["rw_mu"]).reshape(48, 128)
    shared["rw_gn_c"] = f(np.concatenate([inp["rw_gn_g"].reshape(8, 128), inp["rw_gn_b"].reshape(8, 128)], 0))
    shared["rw_vec_bc"] = f(np.broadcast_to(np.stack([inp["rw_w0"], inp["rw_a0"], inp["rw_k_k"], inp["rw_k_a"], inp["rw_r_k"].reshape(-1)])[None], (32, 5, D)))
    shared["rw_vec128"] = f(np.broadcast_to(np.stack([inp["rw_w0"], inp["rw_a0"], inp["rw_k_k"], inp["rw_k_a"], inp["rw_r_k"].reshape(-1),
                                                      inp["rw_gn_g"], inp["rw_gn_b"]])[None], (128, 7, D)))
    shared["rw_mbd"] = np.repeat(np.eye(16, dtype=np.float32), 64, axis=1)
    blk = np.zeros((128, 128), np.float32); blk[:64, :64] = 1.0 / 64; blk[64:, 64:] = 1.0 / 64
    shared["rw_blk"] = blk
    for n in ("r", "k", "v", "o"):
        shared["rw_w_" + n] = f(inp["rw_w_" + n])
    for n in ("w1", "w2", "a1", "a2", "g1", "g2"):
        shared["rw_" + n] = f(inp["rw_" + n])
    shared["negm"] = np.where(np.tril(np.ones((128, 128), dtype=bool)), 0.0, -1e30).astype(np.float32)
    shared["m1"] = np.tril(np.ones((128, 128), dtype=np.float32), -1)
    shared["onesf"] = np.ones((128, 128), dtype=np.float32)
    shared["ident"] = np.eye(128, dtype=np.float32)
    shared["cmask"] = np.triu(np.ones((128, 128), dtype=np.float32))
    maps = []
    for c in range(NCORES):
        m = dict(shared)
        m["xp"] = f(inp["x_prompt"][c])
        m["xs"] = f(inp["x_sample"][NS * c:NS * (c + 1), 0])
        m["pp"] = f(inp["p_prompt"][:, c])
        m["psm"] = f(inp["p_sample"][:, NS * c:NS * (c + 1), 0])
        m["pt_bc"] = np.ascontiguousarray(np.broadcast_to(inp["page_table"][NS * c:NS * (c + 1)].reshape(1, NS * 64), (128, NS * 64)).astype(np.int32))
        m["st_shift"] = f(inp["state_rwkv_shift"][NS * c:NS * (c + 1)])
        m["st_wkv"] = f(inp["state_rwkv_wkv"][NS * c:NS * (c + 1)])
        m["st_conv"] = f(inp["state_ssm_conv"][NS * c:NS * (c + 1)])
        m["st_ssm"] = f(inp["state_ssm"][NS * c:NS * (c + 1)]).reshape(NS, 2048, 128)
        maps.append(m)
    return maps


def kernel(**inp):
    nc = build()
    maps = make_in_maps(inp)
    res = run_bass_kernel_spmd(nc, maps, core_ids=list(range(NCORES)))
    R = res.results
    cat = lambda k: np.concatenate([R[c][k] for c in range(NCORES)], 0)
    stk = lambda k: np.stack([R[c][k] for c in range(NCORES)], 0)
    return (stk("yp"), cat("ys")[:, None, :], cat("gmv")[:, None, :],
            stk("conv_p"), stk("ssm_p"), cat("conv_s"), cat("ssm_s"),
            stk("k_p").reshape(8, SEQ, 4, 64), stk("v_p").reshape(8, SEQ, 4, 64), stk("ik_p"),
            cat("k_s").reshape(32, 1, 4, 64), cat("v_s").reshape(32, 1, 4, 64), cat("ik_s")[:, None, :],
            cat("sh_p"), stk("wkv_p"), cat("sh_s"), cat("wkv_s"))
```

```python
import numpy as np
import concourse.bass as bass
import concourse.mybir as mybir
from concourse.bass_utils import run_bass_kernel_spmd

F32 = mybir.dt.float32
BF16 = mybir.dt.bfloat16
I32 = mybir.dt.int32
AF = mybir.ActivationFunctionType
ALU = mybir.AluOpType
AX = mybir.AxisListType

NCORES = 8
D = 1024
KC = 8
SEQ = 2048
NS = 4
NCOL = SEQ + NS
DEPTH = 4
DFF = 2816
NFF = 22
PLE = 256
ALPHA = (2 * DEPTH) ** 0.25
LN_EPS = 1e-5
TILES = [(0, 512), (512, 512), (1024, 512), (1536, 512), (2048, NS)]

SB_BASE = 16512
SB_END = 229376


class Buf:
    __slots__ = ("name", "w", "r")

    def __init__(self, name):
        self.name = name
        self.w = None
        self.r = {}


class Op:
    __slots__ = ("fn", "waits", "flag", "clock", "dma")

    def __init__(self, fn, waits, clock, dma):
        self.fn = fn
        self.waits = waits
        self.flag = False
        self.clock = clock
        self.dma = dma


ENGS = ("pe", "act", "dve", "pool", "sp")
SEM_LIMIT = 8000
N_DMA_SEMS = 10


class Prog:
    def __init__(self, nc):
        self.nc = nc
        self.ops = {e: [] for e in ENGS}
        self.clock = {e: {} for e in ENGS}
        self.dma_cnt = {}
        self.dma_rr = {q: 0 for q in ENGS}
        self.direct = {e: {} for e in ENGS}

    def _deps(self, eng, reads, writes):
        deps = set()
        for b in reads:
            if b.w is not None:
                deps.add(b.w)
        for b in writes:
            if b.w is not None and (b.w[0] != eng or eng != "pe"):
                deps.add(b.w)
            for k, s in b.r.items():
                if k != eng or eng != "pe":
                    deps.add((k, s))
        return deps

    def _add(self, eng, fn, deps, dma=None, force=False):
        clk = self.clock[eng]
        waits = []
        for (k, s) in sorted(deps, key=lambda t: (str(t[0]), t[1])):
            if clk.get(k, 0) >= s and not (force and isinstance(k, str) and self.direct[eng].get(k, 0) < s):
                continue
            if isinstance(k, str):
                self.direct[eng][k] = max(self.direct[eng].get(k, 0), s)
            waits.append((k, s))
            clk[k] = max(clk.get(k, 0), s)
            if isinstance(k, str):
                op = self.ops[k][s - 1]
                op.flag = True
                for kk, ss in op.clock.items():
                    if clk.get(kk, 0) < ss:
                        clk[kk] = ss
        seq = len(self.ops[eng]) + 1
        self.ops[eng].append(Op(fn, waits, dict(clk), dma))
        return seq

    def op(self, eng, fn, reads=(), writes=()):
        deps = self._deps(eng, reads, writes)
        seq = self._add(eng, fn, deps)
        for b in writes:
            b.w = (eng, seq)
            b.r = {}
        for b in reads:
            b.r[eng] = seq
        return seq

    def dma(self, q, fn, reads=(), writes=()):
        deps = self._deps(("dma", q, -1), reads, writes)
        j = self.dma_rr[q]
        self.dma_rr[q] = (j + 1) % N_DMA_SEMS
        key = ("dma", q, j)
        c = self.dma_cnt.get(key, 0) + 1
        self.dma_cnt[key] = c
        if c > 1:
            deps.add((key, c - 1))
        self._add(q, fn, deps, dma=(key, c))
        for b in writes:
            b.w = (key, c)
            b.r = {}
        for b in reads:
            b.r[key] = c

    def barrier(self):
        deps = set()
        for e in ENGS:
            for idx in range(len(self.ops[e]), 0, -1):
                o = self.ops[e][idx - 1]
                if o.fn is not None and o.dma is None:
                    deps.add((e, idx))
                    break
        for key, c in self.dma_cnt.items():
            deps.add((key, c))
        for e in ENGS:
            self._add(e, None, set(d for d in deps if d[0] != e), force=True)

    def emit(self):
        nc = self.nc
        import contextlib
        with contextlib.ExitStack() as st:
            sems = {}
            pref = {}
            for e in ENGS:
                cnt = 0
                p = []
                for o in self.ops[e]:
                    if o.flag:
                        cnt += 1
                    p.append(cnt)
                pref[e] = p
                nsem = max(1, (cnt + SEM_LIMIT - 1) // SEM_LIMIT)
                sems[e] = [st.enter_context(nc.semaphore(f"s_{e}_{i}")) for i in range(nsem)]
            dsem = {}
            for key in self.dma_cnt:
                dsem[key] = st.enter_context(nc.semaphore(f"d_{key[1]}_{key[2]}"))

            def resolve(k, s):
                if isinstance(k, str):
                    c = pref[k][s - 1]
                    return sems[k][(c - 1) // SEM_LIMIT], (c - 1) % SEM_LIMIT + 1
                return dsem[k], 16 * s

            block = st.enter_context(nc.Block())

            def run(e, name):
                p = pref[name]
                for i, o in enumerate(self.ops[name]):
                    for (k, s) in o.waits:
                        sem, val = resolve(k, s)
                        e.wait_ge(sem, val)
                    if o.fn is None:
                        continue
                    ins = o.fn(e)
                    if o.dma is not None:
                        ins.then_inc(dsem[o.dma[0]], 16)
                    elif o.flag:
                        c = p[i]
                        ins.then_inc(sems[name][(c - 1) // SEM_LIMIT], 1)

            if self.ops["pe"]:
                @block.tensor
                def _(e):
                    run(e, "pe")

            if self.ops["act"]:
                @block.scalar
                def _(e):
                    run(e, "act")

            if self.ops["dve"]:
                @block.vector
                def _(e):
                    run(e, "dve")

            if any(o.fn is not None for o in self.ops["pool"]):
                @block.gpsimd
                def _(e):
                    run(e, "pool")

            @block.sync
            def _(e):
                run(e, "sp")
                for key, c in self.dma_cnt.items():
                    e.wait_ge(dsem[key], 16 * c)


class SB:
    def __init__(self, nc):
        self.nc = nc
        self.off = SB_BASE
        self.n = 0

    def alloc(self, shape, dtype, parts=128):
        nbytes = int(np.prod(shape[1:])) * (2 if dtype == BF16 else 4)
        off = (self.off + 63) // 64 * 64
        assert off + nbytes <= SB_END, f"SBUF overflow: need {off + nbytes - SB_END} more bytes"
        self.n += 1
        t = self.nc.alloc_sbuf_tensor_at(f"t{self.n}", list(shape), dtype, offset=off)
        self.off = off + nbytes
        return t.ap()

    def mark(self):
        return self.off

    def release(self, m):
        self.off = m


def build(stop_after=99, only=None, dbg=99, mixers=("gm", "ssm", "att", "rw", "rwp"), only_mixer=None):
    nc = bass.Bass("TRN2", target_bir_lowering=False)
    P = Prog(nc)
    sb = SB(nc)

    in_names = []
    dbg_names = {}

    def din(name, shape, dt=F32):
        if only is not None and name not in only:
            return None
        in_names.append(name)
        return nc.dram_tensor(name, list(shape), dt, kind="ExternalInput").ap()

    out_names = []

    def dout(name, shape, dt=F32):
        out_names.append(name)
        return nc.dram_tensor(name, list(shape), dt, kind="ExternalOutput").ap()

    xp_d = din("xp", [SEQ, D])
    xs_d = din("xs", [NS, D])
    pp_d = din("pp", [DEPTH, SEQ, PLE])
    ps_d = din("psm", [DEPTH, NS, PLE])
    ln_g_d = din("ln_g", [DEPTH * 3 * KC, 128])
    ln_b_d = din("ln_b", [DEPTH * 3 * KC, 128])
    w_up_d = din("ffn_w_up", [DEPTH, 2, D, 2 * DFF])
    w_dn_d = din("ffn_w_down", [DEPTH, 2, DFF, D])
    ple_wp_d = din("ple_w_p", [DEPTH, PLE, D])
    ple_wg_d = din("ple_w_g", [DEPTH, D, D])
    ple_bg_d = din("ple_b_g", [DEPTH * KC, 128])
    gm_w_in_d = din("gm_w_in", [D, 4096])
    gm_w_out_d = din("gm_w_out", [2048, D])
    gm_lng_c_d = din("gm_lng_c", [16, 128])
    gm_lnb_c_d = din("gm_lnb_c", [16, 128])
    gm_wT_d = din("gm_wT", [128, 8, 128])
    gm_bs_bc_d = din("gm_bs_bc", [128, 8, 128])
    gm_w00_d = din("gm_w00", [128, 8])
    gm_bs0_d = din("gm_bs0", [128, 8])
    gm_lng_bc_d = din("gm_lng_bc", [32, 2048])
    gm_lnb_bc_d = din("gm_lnb_bc", [32, 2048])
    ssm_w_in_d = din("ssm_w_in", [D, 5152])
    ssm_w_out_d = din("ssm_w_out", [2048, D])
    ssm_cw_c_d = din("ssm_cw_c", [96, 128])
    ssm_cb_c_d = din("ssm_cb_c", [24, 128])
    ssm_ng_c_d = din("ssm_ng_c", [16, 128])
    ssm_hv_bc_d = din("ssm_hv_bc", [128, 3, 32])
    ssm_cw_bc_d = din("ssm_cw_bc", [32, 4, 3072])
    ssm_cb_bc_d = din("ssm_cb_bc", [32, 3072])
    ssm_d_c_d = din("ssm_d_c", [16, 128])
    ssm_sel_d = din("ssm_sel", [128, NS, 128])
    st_conv_d = din("st_conv", [NS, 3, 3072])
    st_ssm_d = din("st_ssm", [NS, 2048, 128])
    att_w_in_d = din("att_w_in", [D, 2120])
    att_w_out_d = din("att_w_out", [D, D])
    att_kn_bc_d = din("att_kn_bc", [128, 2, 64])
    cache_k_d = din("cache_k", [2560 * 128, 256])
    cache_v_d = din("cache_v", [2560 * 128, 256])
    cache_ik_d = din("cache_ik", [2560 * 128, 64])
    pt_bc_d = din("pt_bc", [128, NS * 64], I32)
    piota_d = din("piota", [128, 1])
    negp_d = din("negp", [128, 1])
    scr_d = nc.dram_tensor("scr", [NS, 65 * 128], F32, kind="Internal").ap()
    rw_mu_c_d = din("rw_mu_c", [48, 128])
    rw_gn_c_d = din("rw_gn_c", [16, 128])
    rw_vec_bc_d = din("rw_vec_bc", [32, 5, D])
    rw_blk_d = din("rw_blk", [128, 128])
    rw_w_d = {n: din("rw_w_" + n, [D, D]) for n in ("r", "k", "v", "o")}
    rw_w1_d = din("rw_w1", [D, 64])
    rw_w2_d = din("rw_w2", [64, D])
    rw_a1_d = din("rw_a1", [D, 64])
    rw_a2_d = din("rw_a2", [64, D])
    rw_g1_d = din("rw_g1", [D, 128])
    rw_g2_d = din("rw_g2", [128, D])
    rw_vec128_d = din("rw_vec128", [128, 7, D])
    rw_mbd_d = din("rw_mbd", [16, D])
    scr_h_d = nc.dram_tensor("scr_h", [3, SEQ, D], BF16, kind="Internal").ap()
    scr_y_d = nc.dram_tensor("scr_y", [SEQ, D], F32, kind="Internal").ap()
    st_shift_d = din("st_shift", [NS, D])
    st_wkv_d = din("st_wkv", [NS, 16, 64, 64])
    negm_d = din("negm", [128, 128])
    rope_d = din("rope", [128, 17, 16])
    m1_d = din("m1", [128, 128])
    onesf_d = din("onesf", [128, 128])
    ident_d = din("ident", [128, 128])
    cmask_d = din("cmask", [128, 128])

    yp_d = dout("yp", [SEQ, D])
    ys_d = dout("ys", [NS, D])
    gmv_d = dout("gmv", [NS, 2048])
    conv_p_d = dout("conv_p", [3, 3072])
    ssm_p_d = dout("ssm_p", [32, 64, 128])
    conv_s_d = dout("conv_s", [NS, 3, 3072])
    ssm_s_d = dout("ssm_s", [NS, 32, 64, 128])
    k_p_d = dout("k_p", [SEQ, 256])
    v_p_d = dout("v_p", [SEQ, 256])
    ik_p_d = dout("ik_p", [SEQ, 64])
    k_s_d = dout("k_s", [NS, 256])
    v_s_d = dout("v_s", [NS, 256])
    ik_s_d = dout("ik_s", [NS, 64])
    sh_p_d = dout("sh_p", [1, D])
    wkv_p_d = dout("wkv_p", [16, 64, 64])
    sh_s_d = dout("sh_s", [NS, D])
    wkv_s_d = dout("wkv_s", [NS, 16, 64, 64])

    x32 = sb.alloc([128, KC, NCOL], F32)
    xbf = sb.alloc([128, KC, NCOL], BF16)
    ident = sb.alloc([128, 128], F32)
    identb = sb.alloc([128, 128], BF16)
    ones_bf = sb.alloc([128, 128], BF16)
    lng = sb.alloc([128, DEPTH * 3 * KC], F32)
    lnb = sb.alloc([128, DEPTH * 3 * KC], F32)
    plebg = sb.alloc([128, DEPTH * KC], F32)
    epsc = sb.alloc([128, 1], F32)
    psum = nc.alloc_psum_tensor("psum", [128, 8 * 512], F32).ap()

    def bank(b, w=512, n=1):
        return psum[:, b * 512:b * 512 + w] if n == 1 else psum[:, b * 512:(b + n) * 512]

    B_x32 = [[Buf(f"x32_{m}_{t}") for t in range(5)] for m in range(KC)]
    B_xbf = [[Buf(f"xbf_{m}_{t}") for t in range(5)] for m in range(KC)]
    B_ps = [Buf(f"ps{b}") for b in range(8)]
    B_const = Buf("const")
    B_const2 = Buf("const2")
    B_cols = Buf("cols")

    P.dma("sp", lambda e: e.dma_start(out=ident[:], in_=ident_d[:, :]), writes=[B_const])
    P.op("act", lambda e: e.activation(out=identb[:], in_=ident[:], func=AF.Copy), reads=[B_const], writes=[B_const2])
    P.op("dve", lambda e: e.memset(ones_bf[:], 1.0 / D), reads=[B_const2], writes=[B_const2])
    P.op("dve", lambda e: e.memset(epsc[:], LN_EPS / (ALPHA * ALPHA)), reads=[B_const2], writes=[B_const2])

    def load_cols(dst, src_d, rows, stage, B_stage):
        for r0 in range(0, rows, 128):
            r = min(128, rows - r0)
            if r < 128:
                P.op("dve", lambda e: e.memset(stage[:, :], 0.0), writes=[B_stage])
            P.dma("sp", lambda e, r0=r0, r=r: e.dma_start(out=stage[:r, :], in_=src_d[r0:r0 + r, :]), writes=[B_stage])
            P.op("pe", lambda e: e.transpose(bank(7, 128), stage[:, :], ident[:, :]), reads=[B_stage, B_const], writes=[B_ps[7]])
            P.op("dve", lambda e, r0=r0, r=r: e.tensor_copy(out=dst[:, r0:r0 + r], in_=bank(7, r)), reads=[B_ps[7], B_cols], writes=[B_cols])

    m0 = sb.mark()
    stage = sb.alloc([128, 128], F32)
    B_stage = Buf("stage")
    if dbg >= 2:
        load_cols(lng, ln_g_d, DEPTH * 3 * KC, stage, B_stage)
        load_cols(lnb, ln_b_d, DEPTH * 3 * KC, stage, B_stage)
        load_cols(plebg, ple_bg_d, DEPTH * KC, stage, B_stage)

    def load_tm_to_fm(src_d, ntok, col0, feat_chunks, dst32, dstbf, Bd32, Bdbf, feat_w=128):
        F = feat_chunks * 128
        tin = sb.alloc([128, 2, F], F32)
        Bt = [Buf("tin0"), Buf("tin1")]
        k = 0
        for t0 in range(0, ntok, 128):
            n = min(128, ntok - t0)
            s = k % 2
            if n < 128:
                P.op("dve", lambda e, s=s: e.memset(tin[:, s, :], 0.0), writes=[Bt[s]])
            P.dma("sp", lambda e, t0=t0, n=n, s=s: e.dma_start(out=tin[:n, s, :], in_=src_d[t0:t0 + n, :]), writes=[Bt[s]])
            for c in range(feat_chunks):
                b = 4 + (k * feat_chunks + c) % 4
                P.op("pe", lambda e, s=s, c=c, b=b: e.transpose(bank(b, 128), tin[:, s, c * 128:(c + 1) * 128], ident[:, :]),
                     reads=[Bt[s], B_const], writes=[B_ps[b]])
                col = col0 + t0
                ti = min(col // 512, 4)
                wr = []
                if dst32 is not None:
                    wr.append(Bd32[c][ti])
                    P.op("dve", lambda e, c=c, col=col, n=n, b=b: e.tensor_copy(out=dst32[:, c, col:col + n], in_=bank(b, n)),
                         reads=[B_ps[b]], writes=[Bd32[c][ti]])
                if dst32 is not None:
                    P.op("act", lambda e, c=c, col=col, n=n: e.activation(out=dstbf[:, c, col:col + n], in_=dst32[:, c, col:col + n], func=AF.Copy),
                         reads=[Bd32[c][ti]], writes=[Bdbf[c][ti]])
                else:
                    P.op("act", lambda e, c=c, col=col, n=n, b=b: e.activation(out=dstbf[:, c, col:col + n], in_=bank(b, n), func=AF.Copy),
                         reads=[B_ps[b]], writes=[Bdbf[c][ti]])
            k += 1

    m1 = sb.mark()
    if dbg >= 3:
        load_tm_to_fm(xp_d, SEQ, 0, KC, x32, xbf, B_x32, B_xbf)
    if dbg >= 4:
        load_tm_to_fm(xs_d, NS, SEQ, KC, x32, xbf, B_x32, B_xbf)
    sb.release(m1)
    sb.release(m0)
    P.barrier()

    def layer_norm(gi, ln):
        zb, zsq, st = ln["zb"], ln["zsq"], ln["st"]
        for ti, (c0, w) in enumerate(TILES):
            for m in range(KC):
                P.op("act", lambda e, m=m, c0=c0, w=w: e.activation(out=zb[:, m, :w], in_=x32[:, m, c0:c0 + w], func=AF.Copy),
                     reads=[B_x32[m][ti]], writes=[ln["Bzb"][m]])
                P.op("act", lambda e, m=m, c0=c0, w=w: e.activation(out=zsq[:, m, :w], in_=x32[:, m, c0:c0 + w], func=AF.Square),
                     reads=[B_x32[m][ti]], writes=[ln["Bzsq"][m]])
            for m in range(KC):
                P.op("pe", lambda e, m=m, w=w: e.matmul(bank(6, w), ones_bf[:], zb[:, m, :w], start=(m == 0), stop=(m == KC - 1)),
                     reads=[ln["Bzb"][m], B_const2], writes=[B_ps[6]])
            for m in range(KC):
                P.op("pe", lambda e, m=m, w=w: e.matmul(bank(7, w), ones_bf[:], zsq[:, m, :w], start=(m == 0), stop=(m == KC - 1)),
                     reads=[ln["Bzsq"][m], B_const2], writes=[B_ps[7]])
            mean, var, rstd, mr = st[:, 0, :w], st[:, 1, :w], st[:, 2, :w], st[:, 3, :w]
            Bs = ln["Bst"]
            P.op("act", lambda e, w=w, mean=mean: e.activation(out=mean, in_=bank(6, w), func=AF.Copy), reads=[B_ps[6]], writes=[Bs[0]])
            P.op("dve", lambda e, var=var, mean=mean: e.tensor_tensor(out=var, in0=mean, in1=mean, op=ALU.mult), reads=[Bs[0]], writes=[Bs[1]])
            P.op("dve", lambda e, w=w, var=var: e.tensor_tensor(out=var, in0=bank(7, w), in1=var, op=ALU.subtract), reads=[B_ps[7], Bs[1]], writes=[Bs[1]])
            P.op("act", lambda e, var=var, rstd=rstd: e.activation(out=rstd, in_=var, func=AF.Sqrt, bias=epsc[:, 0:1], scale=1.0),
                 reads=[Bs[1], B_const2], writes=[Bs[2]])
            P.op("dve", lambda e, rstd=rstd: e.reciprocal(out=rstd, in_=rstd), reads=[Bs[2]], writes=[Bs[2]])
            P.op("dve", lambda e, mr=mr, mean=mean, rstd=rstd: e.tensor_tensor(out=mr, in0=mean, in1=rstd, op=ALU.mult), reads=[Bs[0], Bs[2]], writes=[Bs[3]])
            for m in range(KC):
                tmp = ln["tmp"][:, m % 2, :w]
                Bt = ln["Btmp"][m % 2]
                gcol = lng[:, gi * KC + m:gi * KC + m + 1]
                bcol = lnb[:, gi * KC + m:gi * KC + m + 1]
                P.op("dve", lambda e, tmp=tmp, m=m, c0=c0, w=w, rstd=rstd: e.tensor_tensor(out=tmp, in0=x32[:, m, c0:c0 + w], in1=rstd, op=ALU.mult),
                     reads=[B_x32[m][ti], Bs[2]], writes=[Bt])
                P.op("dve", lambda e, tmp=tmp, mr=mr: e.tensor_tensor(out=tmp, in0=tmp, in1=mr, op=ALU.subtract), reads=[Bt, Bs[3]], writes=[Bt])
                P.op("act", lambda e, tmp=tmp, m=m, c0=c0, w=w, gcol=gcol, bcol=bcol: e.activation(
                    out=x32[:, m, c0:c0 + w], in_=tmp, func=AF.Identity, bias=bcol, scale=gcol),
                    reads=[Bt, B_cols], writes=[B_x32[m][ti]])
                P.op("act", lambda e, tmp=tmp, m=m, c0=c0, w=w, gcol=gcol, bcol=bcol: e.activation(
                    out=xbf[:, m, c0:c0 + w], in_=tmp, func=AF.Identity, bias=bcol, scale=gcol),
                    reads=[Bt, B_cols], writes=[B_xbf[m][ti]])

    def alloc_ln():
        ln = {}
        ln["zb"] = sb.alloc([128, KC, 512], BF16)
        ln["zsq"] = sb.alloc([128, KC, 512], BF16)
        ln["st"] = sb.alloc([128, 4, 512], F32)
        ln["tmp"] = sb.alloc([128, 2, 512], F32)
        ln["Bzb"] = [Buf("zb") for _ in range(KC)]
        ln["Bzsq"] = [Buf("zsq") for _ in range(KC)]
        ln["Bst"] = [Buf("st") for _ in range(4)]
        ln["Btmp"] = [Buf("tmp") for _ in range(2)]
        return ln

    HALF = NFF // 2

    def ffn(i, j):
        mk = sb.mark()
        h = sb.alloc([128, HALF, NCOL], BF16)
        NWU = 3
        wu = sb.alloc([128, NWU, 2, KC, 256], BF16)
        wd = sb.alloc([128, 2, HALF, 256], BF16)
        sg = sb.alloc([128, 2, 512], F32)
        B_h = [[Buf("h") for _ in range(5)] for _ in range(HALF)]
        B_wu = [Buf("wu") for _ in range(NWU)]
        B_wd = [Buf("wd") for _ in range(2)]
        B_sg = [Buf("sg") for _ in range(2)]
        c_res = 0.5 / ALPHA
        nslab = 0
        ndslab = 0
        cnt = 0
        for g in range(2):
            chunks = list(range(g * HALF, (g + 1) * HALF))
            for s0 in range(0, HALF, 2):
                sl = chunks[s0:s0 + 2]
                ncol = len(sl) * 128
                s = nslab % NWU
                nslab += 1
                a0 = sl[0] * 128
                P.dma("pool", lambda e, s=s, a0=a0, ncol=ncol: e.dma_start(
                    out=wu[:, s, 0, :, :ncol], in_=w_up_d[i, j, :, a0:a0 + ncol].rearrange("(kc p) n -> p kc n", p=128)), writes=[B_wu[s]])
                P.dma("pool", lambda e, s=s, a0=a0, ncol=ncol: e.dma_start(
                    out=wu[:, s, 1, :, :ncol], in_=w_up_d[i, j, :, DFF + a0:DFF + a0 + ncol].rearrange("(kc p) n -> p kc n", p=128)), writes=[B_wu[s]])
                for li, jj in enumerate(sl):
                    hj = jj - g * HALF
                    for ti, (c0, w) in enumerate(TILES):
                        ba = (cnt % 2) * 2
                        bb = ba + 1
                        sgi = cnt % 2
                        cnt += 1
                        for kc in range(KC):
                            P.op("pe", lambda e, s=s, li=li, kc=kc, c0=c0, w=w, ba=ba: e.matmul(
                                bank(ba, w), wu[:, s, 0, kc, li * 128:(li + 1) * 128], xbf[:, kc, c0:c0 + w], start=(kc == 0), stop=(kc == KC - 1)),
                                reads=[B_wu[s], B_xbf[kc][ti]], writes=[B_ps[ba]])
                        for kc in range(KC):
                            P.op("pe", lambda e, s=s, li=li, kc=kc, c0=c0, w=w, bb=bb: e.matmul(
                                bank(bb, w), wu[:, s, 1, kc, li * 128:(li + 1) * 128], xbf[:, kc, c0:c0 + w], start=(kc == 0), stop=(kc == KC - 1)),
                                reads=[B_wu[s], B_xbf[kc][ti]], writes=[B_ps[bb]])
                        P.op("act", lambda e, sgi=sgi, w=w, ba=ba: e.activation(out=sg[:, sgi, :w], in_=bank(ba, w), func=AF.Silu),
                             reads=[B_ps[ba]], writes=[B_sg[sgi]])
                        P.op("dve", lambda e, sgi=sgi, w=w, bb=bb, hj=hj, c0=c0: e.tensor_tensor(
                            out=h[:, hj, c0:c0 + w], in0=sg[:, sgi, :w], in1=bank(bb, w), op=ALU.mult),
                            reads=[B_sg[sgi], B_ps[bb]], writes=[B_h[hj][ti]])
            for dq in range(4):
                s = ndslab % 2
                ndslab += 1
                r0 = g * HALF * 128
                P.dma("pool", lambda e, s=s, r0=r0, dq=dq: e.dma_start(
                    out=wd[:, s, :, :], in_=w_dn_d[i, j, r0:r0 + HALF * 128, dq * 256:(dq + 1) * 256].rearrange("(c p) n -> p c n", p=128)),
                    writes=[B_wd[s]])
                for mi in range(2):
                    m = dq * 2 + mi
                    for ti, (c0, w) in enumerate(TILES):
                        b = 4 + (cnt % 2)
                        cnt += 1
                        for hj in range(HALF):
                            P.op("pe", lambda e, s=s, hj=hj, mi=mi, c0=c0, w=w, b=b: e.matmul(
                                bank(b, w), wd[:, s, hj, mi * 128:(mi + 1) * 128], h[:, hj, c0:c0 + w], start=(hj == 0), stop=(hj == HALF - 1)),
                                reads=[B_wd[s], B_h[hj][ti]], writes=[B_ps[b]])
                        P.op("dve", lambda e, m=m, c0=c0, w=w, b=b: e.scalar_tensor_tensor(
                            out=x32[:, m, c0:c0 + w], in0=bank(b, w), scalar=c_res, in1=x32[:, m, c0:c0 + w], op0=ALU.mult, op1=ALU.add),
                            reads=[B_ps[b], B_x32[m][ti]], writes=[B_x32[m][ti]])
        sb.release(mk)
        P.barrier()
        mk = sb.mark()
        ln = alloc_ln()
        layer_norm(i * 3 + 2 * j, ln)
        sb.release(mk)
        P.barrier()

    def ple(i):
        mk = sb.mark()
        pT = sb.alloc([128, 2, NCOL], BF16)
        B_pT = [[Buf("pT") for _ in range(5)] for _ in range(2)]
        m1 = sb.mark()
        load_tm_to_fm(pp_d[i], SEQ, 0, 2, None, pT, None, B_pT)
        load_tm_to_fm(ps_d[i], NS, SEQ, 2, None, pT, None, B_pT)
        sb.release(m1)
        wg = sb.alloc([128, KC, D], BF16)
        wp = sb.alloc([128, 2, D], BF16)
        sgt = sb.alloc([128, 2, 512], F32)
        B_wg, B_wp = Buf("wg"), Buf("wp")
        B_sg = [Buf("sg") for _ in range(2)]
        P.barrier()
        P.dma("pool", lambda e: e.dma_start(out=wg[:], in_=ple_wg_d[i].rearrange("(kc p) n -> p kc n", p=128)), writes=[B_wg])
        P.dma("pool", lambda e: e.dma_start(out=wp[:], in_=ple_wp_d[i].rearrange("(kc p) n -> p kc n", p=128)), writes=[B_wp])
        cnt = 0
        for m in range(KC):
            for ti, (c0, w) in enumerate(TILES):
                ba = (cnt % 2) * 2
                bb = ba + 1
                si = cnt % 2
                cnt += 1
                for kc in range(KC):
                    P.op("pe", lambda e, kc=kc, m=m, c0=c0, w=w, ba=ba: e.matmul(
                        bank(ba, w), wg[:, kc, m * 128:(m + 1) * 128], xbf[:, kc, c0:c0 + w], start=(kc == 0), stop=(kc == KC - 1)),
                        reads=[B_wg, B_xbf[kc][ti]], writes=[B_ps[ba]])
                for kc in range(2):
                    P.op("pe", lambda e, kc=kc, m=m, c0=c0, w=w, bb=bb: e.matmul(
                        bank(bb, w), wp[:, kc, m * 128:(m + 1) * 128], pT[:, kc, c0:c0 + w], start=(kc == 0), stop=(kc == 1)),
                        reads=[B_wp, B_pT[kc][ti]], writes=[B_ps[bb]])
                bcol = plebg[:, i * KC + m:i * KC + m + 1]
                P.op("act", lambda e, si=si, w=w, ba=ba, bcol=bcol: e.activation(out=sgt[:, si, :w], in_=bank(ba, w), func=AF.Sigmoid, bias=bcol, scale=1.0),
                     reads=[B_ps[ba], B_cols], writes=[B_sg[si]])
                P.op("dve", lambda e, si=si, w=w, bb=bb: e.tensor_tensor(out=sgt[:, si, :w], in0=sgt[:, si, :w], in1=bank(bb, w), op=ALU.mult),
                     reads=[B_sg[si], B_ps[bb]], writes=[B_sg[si]])
                P.op("dve", lambda e, si=si, m=m, c0=c0, w=w: e.tensor_tensor(out=x32[:, m, c0:c0 + w], in0=x32[:, m, c0:c0 + w], in1=sgt[:, si, :w], op=ALU.add),
                     reads=[B_sg[si], B_x32[m][ti]], writes=[B_x32[m][ti]])
        for m in range(KC):
            for ti, (c0, w) in enumerate(TILES):
                P.op("act", lambda e, m=m, c0=c0, w=w: e.activation(out=xbf[:, m, c0:c0 + w], in_=x32[:, m, c0:c0 + w], func=AF.Copy),
                     reads=[B_x32[m][ti]], writes=[B_xbf[m][ti]])
        sb.release(mk)
        P.barrier()


    def gmlp():
        GELU = AF.Gelu_apprx_tanh
        mk = sb.mark()
        gmg = sb.alloc([128, 16], F32)
        gmb = sb.alloc([128, 16], F32)
        wT = sb.alloc([128, 8, 128], BF16)
        Cb = sb.alloc([128, 16, 128], F32)
        eps1 = sb.alloc([128, 1], F32)
        As = sb.alloc([128, 16], F32)
        Cs = sb.alloc([128, 16], F32)
        B_gc = Buf("gm_consts")
        mt = sb.mark()
        stage = sb.alloc([128, 128], F32)
        wT32 = sb.alloc([128, 8, 128], F32)
        cm = sb.alloc([128, 128], F32)
        bsbc = sb.alloc([128, 8, 128], F32)
        w00c = sb.alloc([128, 8], F32)
        bs0c = sb.alloc([128, 8], F32)
        ones1 = sb.alloc([128, 128], BF16)
        B_stage, B_t = Buf("stage"), Buf("gmtmp")
        for dst, src in ((gmg, gm_lng_c_d), (gmb, gm_lnb_c_d)):
            P.op("dve", lambda e: e.memset(stage[:, :], 0.0), writes=[B_stage])
            P.dma("sp", lambda e, src=src: e.dma_start(out=stage[:16, :], in_=src[:, :]), writes=[B_stage])
            P.op("pe", lambda e: e.transpose(bank(7, 128), stage[:, :], ident[:, :]), reads=[B_stage, B_const], writes=[B_ps[7]])
            P.op("dve", lambda e, dst=dst: e.tensor_copy(out=dst[:, :], in_=bank(7, 16)), reads=[B_ps[7], B_gc], writes=[B_gc])
        P.dma("sp", lambda e: e.dma_start(out=wT32[:], in_=gm_wT_d[:, :, :]), writes=[B_t])
        P.dma("sp", lambda e: e.dma_start(out=cm[:], in_=cmask_d[:, :]), writes=[B_t])
        P.dma("sp", lambda e: e.dma_start(out=bsbc[:], in_=gm_bs_bc_d[:, :, :]), writes=[B_t])
        P.dma("sp", lambda e: e.dma_start(out=w00c[:], in_=gm_w00_d[:, :]), writes=[B_t])
        P.dma("sp", lambda e: e.dma_start(out=bs0c[:], in_=gm_bs0_d[:, :]), writes=[B_t])
        P.op("dve", lambda e: e.memset(ones1[:], 1.0), reads=[B_t], writes=[B_t])
        P.op("dve", lambda e: e.memset(eps1[:], LN_EPS), reads=[B_gc], writes=[B_gc])
        P.op("dve", lambda e: e.tensor_tensor(out=wT[:], in0=wT32[:], in1=cm[:, :].unsqueeze(1).to_broadcast([128, 8, 128]), op=ALU.mult),
             reads=[B_t, B_gc], writes=[B_gc])
        for hb in range(2):
            P.op("pe", lambda e, hb=hb: e.matmul(bank(6, 512), ones1[:], wT[:, hb * 4:(hb + 1) * 4, :], start=True, stop=True),
                 reads=[B_t, B_gc], writes=[B_ps[6]])
            for gg in range(4):
                g = hb * 4 + gg
                for hh in range(2):
                    fc = 2 * g + hh
                    P.op("dve", lambda e, gg=gg, g=g, fc=fc: e.scalar_tensor_tensor(
                        out=Cb[:, fc, :], in0=bank(6, 512)[:, gg * 128:(gg + 1) * 128], scalar=gmb[:, fc:fc + 1], in1=bsbc[:, g, :], op0=ALU.mult, op1=ALU.add),
                        reads=[B_ps[6], B_t, B_gc], writes=[B_gc])
        w00x = w00c[:, :].unsqueeze(2).to_broadcast([128, 8, 2])
        bs0x = bs0c[:, :].unsqueeze(2).to_broadcast([128, 8, 2])
        v3 = lambda t: t[:, :].rearrange("p (g h) -> p g h", h=2)
        P.op("dve", lambda e: e.tensor_tensor(out=v3(As), in0=v3(gmg), in1=w00x, op=ALU.mult), reads=[B_t, B_gc], writes=[B_gc])
        P.op("dve", lambda e: e.tensor_tensor(out=v3(Cs), in0=v3(gmb), in1=w00x, op=ALU.mult), reads=[B_t, B_gc], writes=[B_gc])
        P.op("dve", lambda e: e.tensor_tensor(out=v3(Cs), in0=v3(Cs), in1=bs0x, op=ALU.add), reads=[B_t, B_gc], writes=[B_gc])
        P.barrier()
        sb.release(mt)

        wins = sb.alloc([128, 3, KC, 256], BF16)
        wos = sb.alloc([128, 2, 16, 128], BF16)
        u = sb.alloc([128, 16, 512], BF16)
        v32 = sb.alloc([128, 4, 2048], F32)
        vbf = sb.alloc([128, 4, 2048], BF16)
        st6 = sb.alloc([128, 4, 4, 6], F32)
        mv = sb.alloc([128, 4, 2], F32)
        rs = sb.alloc([128, 4], F32)
        tmp = sb.alloc([128, 2, 512], F32)
        B_wins = [Buf("wins") for _ in range(3)]
        B_wos = [Buf("wos") for _ in range(2)]
        B_u = [Buf("u") for _ in range(16)]
        B_v32 = [Buf("v32") for _ in range(4)]
        B_vbf = [Buf("vbf") for _ in range(4)]
        B_st = [Buf("st") for _ in range(4)]
        B_tmp = [Buf("tmp") for _ in range(2)]
        state = {"nsl": 0, "nwo": 0, "cnt": 0}

        def load_win(col0):
            sl = state["nsl"] % 3
            state["nsl"] += 1
            P.dma("pool", lambda e, sl=sl, col0=col0: e.dma_start(
                out=wins[:, sl, :, :], in_=gm_w_in_d[:, col0:col0 + 256].rearrange("(kc p) n -> p kc n", p=128)), writes=[B_wins[sl]])
            return sl

        def v_chunk(ch, tok):
            for q in range(4):
                P.op("dve", lambda e, ch=ch, q=q: e.bn_stats(out=st6[:, ch, q, :], in_=v32[:, ch, q * 512:(q + 1) * 512]),
                     reads=[B_v32[ch]], writes=[B_st[ch]])
            P.op("dve", lambda e, ch=ch: e.bn_aggr(out=mv[:, ch, :], in_=st6[:, ch, :, :].rearrange("p a b -> p (a b)")),
                 reads=[B_st[ch]], writes=[B_st[ch]])
            P.op("act", lambda e, ch=ch: e.activation(out=rs[:, ch:ch + 1], in_=mv[:, ch, 1:2], func=AF.Sqrt, bias=eps1[:, 0:1], scale=1.0),
                 reads=[B_st[ch], B_gc], writes=[B_st[ch]])
            P.op("dve", lambda e, ch=ch: e.reciprocal(out=rs[:, ch:ch + 1], in_=rs[:, ch:ch + 1]), reads=[B_st[ch]], writes=[B_st[ch]])

        def v_slabs(chunks):
            for slb in range(8):
                sl = load_win(2048 + slb * 256)
                for (ch, tok) in chunks:
                    b = state["cnt"] % 2
                    state["cnt"] += 1
                    for kc in range(KC):
                        P.op("pe", lambda e, kc=kc, tok=tok, sl=sl, b=b: e.matmul(
                            bank(b, 256), xbf[:, kc, tok:tok + 128], wins[:, sl, kc, :], start=(kc == 0), stop=(kc == KC - 1)),
                            reads=[B_wins[sl]] + [B_xbf[kc][t] for t in range(5)], writes=[B_ps[b]])
                    P.op("act", lambda e, ch=ch, slb=slb, b=b: e.activation(out=v32[:, ch, slb * 256:(slb + 1) * 256], in_=bank(b, 256), func=GELU),
                         reads=[B_ps[b]], writes=[B_v32[ch]])

        def u_slabs(c0, w, udst, B_ud):
            for slb in range(8):
                sl = load_win(slb * 256)
                for li in range(2):
                    fc = slb * 2 + li
                    b = 2 + state["cnt"] % 2
                    state["cnt"] += 1
                    for kc in range(KC):
                        P.op("pe", lambda e, kc=kc, sl=sl, li=li, b=b: e.matmul(
                            bank(b, w), wins[:, sl, kc, li * 128:(li + 1) * 128], xbf[:, kc, c0:c0 + w], start=(kc == 0), stop=(kc == KC - 1)),
                            reads=[B_wins[sl]] + [B_xbf[kc][t] for t in range(5)], writes=[B_ps[b]])
                    P.op("act", lambda e, fc=fc, b=b: e.activation(out=udst[:, fc, :w], in_=bank(b, w), func=GELU),
                         reads=[B_ps[b]], writes=[B_ud[fc]])

        def out_proj(c0, w, ti, gsrc, B_g):
            for m in range(KC):
                so = state["nwo"] % 2
                state["nwo"] += 1
                P.dma("pool", lambda e, so=so, m=m: e.dma_start(
                    out=wos[:, so, :, :], in_=gm_w_out_d[:, m * 128:(m + 1) * 128].rearrange("(fc p) n -> p fc n", p=128)), writes=[B_wos[so]])
                b = state["cnt"] % 2
                state["cnt"] += 1
                for fc in range(16):
                    P.op("pe", lambda e, so=so, fc=fc, b=b: e.matmul(
                        bank(b, w), wos[:, so, fc, :], gsrc[:, fc, :w], start=(fc == 0), stop=(fc == 15)),
                        reads=[B_wos[so], B_g[fc]], writes=[B_ps[b]])
                P.op("dve", lambda e, m=m, b=b: e.scalar_tensor_tensor(
                    out=x32[:, m, c0:c0 + w], in0=bank(b, w), scalar=1.0 / ALPHA, in1=x32[:, m, c0:c0 + w], op0=ALU.mult, op1=ALU.add),
                    reads=[B_ps[b], B_x32[m][ti]], writes=[B_x32[m][ti]])

        for ti in range(4):
            c0 = 512 * ti
            v_slabs([(ch, c0 + 128 * ch) for ch in range(4)])
            for ch in range(4):
                v_chunk(ch, c0 + 128 * ch)
                P.op("dve", lambda e, ch=ch: e.tensor_scalar(out=vbf[:, ch, :], in0=v32[:, ch, :], scalar1=mv[:, ch, 0:1], scalar2=rs[:, ch:ch + 1],
                                                              op0=ALU.subtract, op1=ALU.mult),
                     reads=[B_v32[ch], B_st[ch]], writes=[B_vbf[ch]])
            u_slabs(c0, 512, u, B_u)
            for g in range(8):
                pb = 4 + 2 * (g % 2)
                for hh in range(2):
                    fc = 2 * g + hh
                    for ch in range(4):
                        o0 = (pb + hh) * 512 + ch * 128
                        P.op("pe", lambda e, o0=o0, ch=ch, fc=fc, g=g: e.matmul(
                            psum[:, o0:o0 + 128], vbf[:, ch, fc * 128:(fc + 1) * 128], wT[:, g, :], start=True, stop=True),
                            reads=[B_vbf[ch], B_gc], writes=[B_ps[pb + hh]])
                    tt = tmp[:, hh, :]
                    P.op("dve", lambda e, tt=tt, pb=pb, hh=hh, fc=fc: e.scalar_tensor_tensor(
                        out=tt.rearrange("p (c t) -> p c t", c=4), in0=bank(pb + hh, 512).rearrange("p (c t) -> p c t", c=4),
                        scalar=gmg[:, fc:fc + 1], in1=Cb[:, fc:fc + 1, :].to_broadcast([128, 4, 128]), op0=ALU.mult, op1=ALU.add),
                        reads=[B_ps[pb + hh], B_gc], writes=[B_tmp[hh]])
                    P.op("dve", lambda e, tt=tt, fc=fc: e.tensor_tensor(out=u[:, fc, :], in0=tt, in1=u[:, fc, :], op=ALU.mult),
                         reads=[B_tmp[hh], B_u[fc]], writes=[B_u[fc]])
            out_proj(c0, 512, ti, u, B_u)

        TOK = NCOL - 128
        v_slabs([(0, TOK)])
        v_chunk(0, TOK)
        P.op("dve", lambda e: e.tensor_scalar(out=v32[:, 0, :], in0=v32[:, 0, :], scalar1=mv[:, 0, 0:1], scalar2=rs[:, 0:1],
                                              op0=ALU.subtract, op1=ALU.mult), reads=[B_v32[0], B_st[0]], writes=[B_v32[0]])
        P.dma("sp", lambda e: e.dma_start(out=v32[96:128, 2, :], in_=gm_lng_bc_d[:, :]), writes=[B_v32[2]])
        P.dma("sp", lambda e: e.dma_start(out=v32[96:128, 3, :], in_=gm_lnb_bc_d[:, :]), writes=[B_v32[3]])
        P.op("dve", lambda e: e.tensor_tensor(out=v32[96:128, 1, :], in0=v32[96:128, 0, :], in1=v32[96:128, 2, :], op=ALU.mult),
             reads=[B_v32[0], B_v32[2]], writes=[B_v32[1]])
        P.op("dve", lambda e: e.tensor_tensor(out=v32[96:128, 1, :], in0=v32[96:128, 1, :], in1=v32[96:128, 3, :], op=ALU.add),
             reads=[B_v32[1], B_v32[3]], writes=[B_v32[1]])
        P.dma("sp", lambda e: e.dma_start(out=gmv_d[:, :], in_=v32[124:128, 1, :]), reads=[B_v32[1]])
        us = vbf[:, 1, :].rearrange("p (f t) -> p f t", f=16)[:, :, 0:NS]
        B_us = [Buf("us") for _ in range(16)]
        u_slabs(SEQ, NS, us, B_us)
        ms = tmp[:, 0, :64].rearrange("p (f t) -> p f t", f=16)
        for fc in range(16):
            b = 4 + fc % 4
            P.op("pe", lambda e, fc=fc, b=b: e.transpose(bank(b, 128), v32[:, 0, fc * 128:(fc + 1) * 128], ident[:, :]),
                 reads=[B_v32[0], B_const], writes=[B_ps[b]])
            P.op("act", lambda e, fc=fc, b=b: e.activation(out=ms[:, fc, :], in_=bank(b, 128)[:, 124:128], func=AF.Identity,
                                                           bias=Cs[:, fc:fc + 1], scale=As[:, fc:fc + 1]),
                 reads=[B_ps[b], B_gc], writes=[B_tmp[0]])
            P.op("dve", lambda e, fc=fc: e.tensor_tensor(out=us[:, fc, :], in0=ms[:, fc, :], in1=us[:, fc, :], op=ALU.mult),
                 reads=[B_tmp[0], B_us[fc]], writes=[B_us[fc]])
        out_proj(SEQ, NS, 4, us, B_us)
        sb.release(mk)
        P.barrier()


    def mamba_prompt():
        NCH = 2
        TW = NCH * 128
        mk = sb.mark()
        cw = sb.alloc([128, 4, 24], F32)
        cbias = sb.alloc([128, 24], F32)
        ngc = sb.alloc([128, 16], F32)
        hv = sb.alloc([128, 3, 32], F32)
        M1 = sb.alloc([128, 128], F32)
        M2 = sb.alloc([128, 128], F32)
        onesf = sb.alloc([128, 128], F32)
        eps1 = sb.alloc([128, 1], F32)
        one1 = sb.alloc([128, 1], F32)
        wdt = sb.alloc([128, KC, 32], BF16)
        B_c = Buf("ssm_consts")
        B_carry = [Buf("carry") for _ in range(24)]
        B_HT32, B_HTbf = Buf("HT32"), Buf("HTbf")
        mt = sb.mark()
        stage = sb.alloc([128, 128], F32)
        B_stage = Buf("stage")
        cw_flat = cw[:, :, :].rearrange("p a b -> p (a b)")
        for dst, src, rows in ((cw_flat, ssm_cw_c_d, 96), (cbias, ssm_cb_c_d, 24), (ngc, ssm_ng_c_d, 16)):
            P.op("dve", lambda e: e.memset(stage[:, :], 0.0), writes=[B_stage])
            P.dma("sp", lambda e, src=src, rows=rows: e.dma_start(out=stage[:rows, :], in_=src[:, :]), writes=[B_stage])
            P.op("pe", lambda e: e.transpose(bank(7, 128), stage[:, :], ident[:, :]), reads=[B_stage, B_const], writes=[B_ps[7]])
            P.op("dve", lambda e, dst=dst, rows=rows: e.tensor_copy(out=dst[:, 0:rows], in_=bank(7, rows)), reads=[B_ps[7], B_c], writes=[B_c])
        P.dma("sp", lambda e: e.dma_start(out=hv[:], in_=ssm_hv_bc_d[:, :, :]), reads=[B_c], writes=[B_c])
        P.op("act", lambda e: e.activation(out=hv[:, 1, :], in_=hv[:, 1, :], func=AF.Exp), reads=[B_c], writes=[B_c])
        P.op("dve", lambda e: e.tensor_scalar(out=hv[:, 1, :], in0=hv[:, 1, :], scalar1=-1.0, scalar2=None, op0=ALU.mult), reads=[B_c], writes=[B_c])
        P.dma("sp", lambda e: e.dma_start(out=M1[:], in_=m1_d[:, :]), reads=[B_c], writes=[B_c])
        P.dma("sp", lambda e: e.dma_start(out=M2[:], in_=cmask_d[:, :]), reads=[B_c], writes=[B_c])
        P.dma("sp", lambda e: e.dma_start(out=onesf[:], in_=onesf_d[:, :]), reads=[B_c], writes=[B_c])
        P.dma("pool", lambda e: e.dma_start(out=wdt[:], in_=ssm_w_in_d[:, 5120:5152].rearrange("(kc p) n -> p kc n", p=128)), reads=[B_c], writes=[B_c])
        P.op("dve", lambda e: e.memset(eps1[:], LN_EPS), reads=[B_c], writes=[B_c])
        P.op("dve", lambda e: e.memset(one1[:], 1.0), reads=[B_c], writes=[B_c])
        P.barrier()
        sb.release(mt)

        mwork = sb.mark()
        carry = sb.alloc([128, 24, 3], F32)
        HT32 = sb.alloc([128, 2048], F32)
        HTbf = sb.alloc([128, 2048], BF16)
        P.op("dve", lambda e: e.memset(carry[:], 0.0), writes=B_carry)
        P.op("dve", lambda e: e.memset(HT32[:], 0.0), writes=[B_HT32])
        P.op("dve", lambda e: e.memset(HTbf[:], 0.0), writes=[B_HTbf])
        wins = sb.alloc([128, 3, KC, 256], BF16)
        wos = sb.alloc([128, 2, 16, 128], BF16)
        stg = sb.alloc([128, 2, TW + 3], F32)
        acc = sb.alloc([128, 2, TW], F32)
        xs_tm = sb.alloc([128, NCH, 2048], BF16)
        BT = sb.alloc([128, 4, TW], BF16)
        CT = sb.alloc([128, 4, TW], BF16)
        B_tm = sb.alloc([128, NCH, 512], BF16)
        sz = sb.alloc([128, NCH, 2048], BF16)
        dt = sb.alloc([128, NCH, 32], F32)
        a_tm = sb.alloc([128, NCH, 32], F32)
        small = sb.alloc([128, 4, 32], F32)
        rb = sb.alloc([128, 2, 512], F32)
        seg = sb.alloc([128, 2, 512], F32)
        eab = sb.alloc([128, 2, 512], F32)
        LT = sb.alloc([128, 2, 512], BF16)
        Csc = sb.alloc([128, 2, 512], BF16)
        cbm = sb.alloc([128, 4, 128], F32)
        dcy = sb.alloc([128, 32], F32)
        xdt = sb.alloc([128, 2048], BF16)
        xdtd = sb.alloc([128, 2048], BF16)
        y32 = sb.alloc([128, 2048], F32)
        junk = sb.alloc([128, 512], BF16)
        ssq = sb.alloc([128, 8], F32)
        ynT = sb.alloc([128, 16, TW], BF16)
        Bw = [Buf("wins") for _ in range(3)]
        Bwo = [Buf("wos") for _ in range(2)]
        Bstg = [Buf("stg") for _ in range(2)]
        Bacc = [Buf("acc") for _ in range(2)]
        Bxs = [Buf("xs_tm") for _ in range(NCH)]
        BBT, BCT = [Buf("BT") for _ in range(4)], [Buf("CT") for _ in range(4)]
        BBtm = [Buf("B_tm") for _ in range(NCH)]
        Bsz = [Buf("sz") for _ in range(NCH)]
        Bdt = [Buf("dt") for _ in range(NCH)]
        Bsm = Buf("small")
        Brb, Bseg, Beab, BLT, BCsc = ([Buf(n) for _ in range(2)] for n in ("rb", "seg", "eab", "LT", "Csc"))
        Bcbm, Bdcy, Bxdt, Bxdtd, By32, Bssq = Buf("cbm"), Buf("dcy"), Buf("xdt"), Buf("xdtd"), Buf("y32"), Buf("ssq")
        BynT = [Buf("ynT") for _ in range(16)]
        state = {"nsl": 0, "nwo": 0, "cnt": 0}
        allx = lambda kc: [B_xbf[kc][t] for t in range(5)]
        dbg_names.update(dt=dt.name, a_tm=a_tm.name, HT32=HT32.name, xs_tm=xs_tm.name, sz=sz.name, y32=y32.name, BT=BT.name, CT=CT.name,
                         small=small.name, dcy=dcy.name, ynT=ynT.name, xdt=xdt.name, cbm=cbm.name, LT=LT.name, seg=seg.name, eab=eab.name, B_tm=B_tm.name)

        def load_win(col0, ncol=256):
            sl = state["nsl"] % 3
            state["nsl"] += 1
            P.dma("pool", lambda e, sl=sl, col0=col0, ncol=ncol: e.dma_start(
                out=wins[:, sl, :, :ncol], in_=ssm_w_in_d[:, col0:col0 + ncol].rearrange("(kc p) n -> p kc n", p=128)), writes=[Bw[sl]])
            return sl

        def nb(lo=0, n=4):
            b = lo + state["cnt"] % n
            state["cnt"] += 1
            return b

        for tile in range(SEQ // TW):
            tok0 = tile * TW
            for slb in range(12):
                sl = load_win(2048 + slb * 256)
                for li in range(2):
                    fcx = slb * 2 + li
                    si = fcx % 2
                    b = nb(0, 4)
                    for kc in range(KC):
                        P.op("pe", lambda e, kc=kc, sl=sl, li=li, b=b, tok0=tok0: e.matmul(
                            bank(b, TW), wins[:, sl, kc, li * 128:(li + 1) * 128], xbf[:, kc, tok0:tok0 + TW], start=(kc == 0), stop=(kc == KC - 1)),
                            reads=[Bw[sl]] + allx(kc), writes=[B_ps[b]])
                    P.op("act", lambda e, si=si, b=b: e.activation(out=stg[:, si, 3:3 + TW], in_=bank(b, TW), func=AF.Copy), reads=[B_ps[b]], writes=[Bstg[si]])
                    P.op("dve", lambda e, si=si, fcx=fcx: e.tensor_copy(out=stg[:, si, 0:3], in_=carry[:, fcx, :]), reads=[B_carry[fcx], Bstg[si]], writes=[Bstg[si]])
                    P.op("dve", lambda e, si=si, fcx=fcx: e.tensor_copy(out=carry[:, fcx, :], in_=stg[:, si, TW:TW + 3]), reads=[Bstg[si]], writes=[B_carry[fcx]])
                    P.op("act", lambda e, si=si, fcx=fcx: e.activation(out=acc[:, si, :], in_=stg[:, si, 3:3 + TW], func=AF.Identity,
                                                                      bias=cbias[:, fcx:fcx + 1], scale=cw[:, 3, fcx:fcx + 1]),
                         reads=[Bstg[si], B_c], writes=[Bacc[si]])
                    for j in range(3):
                        P.op("dve", lambda e, si=si, fcx=fcx, j=j: e.scalar_tensor_tensor(
                            out=acc[:, si, :], in0=stg[:, si, j:j + TW], scalar=cw[:, j, fcx:fcx + 1], in1=acc[:, si, :], op0=ALU.mult, op1=ALU.add),
                            reads=[Bstg[si], Bacc[si], B_c], writes=[Bacc[si]])
                    if fcx < 16:
                        P.op("act", lambda e, si=si: e.activation(out=acc[:, si, :], in_=acc[:, si, :], func=AF.Silu), reads=[Bacc[si]], writes=[Bacc[si]])
                        for ch in range(NCH):
                            bt = nb(4, 4)
                            P.op("pe", lambda e, si=si, ch=ch, bt=bt: e.transpose(bank(bt, 128), acc[:, si, ch * 128:(ch + 1) * 128], ident[:, :]),
                                 reads=[Bacc[si], B_const], writes=[B_ps[bt]])
                            P.op("act", lambda e, ch=ch, fcx=fcx, bt=bt: e.activation(out=xs_tm[:, ch, fcx * 128:(fcx + 1) * 128], in_=bank(bt, 128), func=AF.Copy),
                                 reads=[B_ps[bt]], writes=[Bxs[ch]])
                    elif fcx < 20:
                        g = fcx - 16
                        P.op("act", lambda e, si=si: e.activation(out=acc[:, si, :], in_=acc[:, si, :], func=AF.Silu), reads=[Bacc[si]], writes=[Bacc[si]])
                        P.op("dve", lambda e, si=si, g=g: e.tensor_copy(out=BT[:, g, :], in_=acc[:, si, :]), reads=[Bacc[si]], writes=[BBT[g]])
                        for ch in range(NCH):
                            bt = nb(4, 4)
                            P.op("pe", lambda e, si=si, ch=ch, bt=bt: e.transpose(bank(bt, 128), acc[:, si, ch * 128:(ch + 1) * 128], ident[:, :]),
                                 reads=[Bacc[si], B_const], writes=[B_ps[bt]])
                            P.op("act", lambda e, ch=ch, g=g, bt=bt: e.activation(out=B_tm[:, ch, g * 128:(g + 1) * 128], in_=bank(bt, 128), func=AF.Copy),
                                 reads=[B_ps[bt]], writes=[BBtm[ch]])
                    else:
                        g = fcx - 20
                        P.op("act", lambda e, si=si, g=g: e.activation(out=CT[:, g, :], in_=acc[:, si, :], func=AF.Silu), reads=[Bacc[si]], writes=[BCT[g]])
            for slb in range(8):
                sl = load_win(slb * 256)
                for ch in range(NCH):
                    b = nb(0, 4)
                    tk = tok0 + ch * 128
                    for kc in range(KC):
                        P.op("pe", lambda e, kc=kc, sl=sl, tk=tk, b=b: e.matmul(
                            bank(b, 256), xbf[:, kc, tk:tk + 128], wins[:, sl, kc, :], start=(kc == 0), stop=(kc == KC - 1)),
                            reads=[Bw[sl]] + allx(kc), writes=[B_ps[b]])
                    P.op("act", lambda e, ch=ch, slb=slb, b=b: e.activation(out=sz[:, ch, slb * 256:(slb + 1) * 256], in_=bank(b, 256), func=AF.Silu),
                         reads=[B_ps[b]], writes=[Bsz[ch]])
            for ch in range(NCH):
                b = nb(0, 4)
                tk = tok0 + ch * 128
                for kc in range(KC):
                    P.op("pe", lambda e, kc=kc, tk=tk, b=b: e.matmul(bank(b, 32), xbf[:, kc, tk:tk + 128], wdt[:, kc, :], start=(kc == 0), stop=(kc == KC - 1)),
                         reads=[B_c] + allx(kc), writes=[B_ps[b]])
                P.op("dve", lambda e, ch=ch, b=b: e.tensor_tensor(out=dt[:, ch, :], in0=bank(b, 32), in1=hv[:, 0, :], op=ALU.add), reads=[B_ps[b], B_c], writes=[Bdt[ch]])
                P.op("act", lambda e, ch=ch: e.activation(out=dt[:, ch, :], in_=dt[:, ch, :], func=AF.Exp), reads=[Bdt[ch]], writes=[Bdt[ch]])
                P.op("act", lambda e, ch=ch: e.activation(out=dt[:, ch, :], in_=dt[:, ch, :], func=AF.Ln, bias=one1[:, 0:1], scale=1.0), reads=[Bdt[ch], B_c], writes=[Bdt[ch]])
                P.op("dve", lambda e, ch=ch: e.tensor_tensor(out=a_tm[:, ch, :], in0=dt[:, ch, :], in1=hv[:, 1, :], op=ALU.mult), reads=[Bdt[ch], B_c], writes=[Bdt[ch]])
            for ch in range(NCH):
                cs = slice(ch * 128, (ch + 1) * 128)
                for g in range(4):
                    P.op("pe", lambda e, g=g, cs=cs: e.matmul(bank(0, 512)[:, g * 128:(g + 1) * 128], BT[:, g, cs], CT[:, g, cs], start=True, stop=True),
                         reads=[BBT[g], BCT[g]], writes=[B_ps[0]])
                P.op("dve", lambda e: e.tensor_tensor(out=cbm[:], in0=bank(0, 512).rearrange("p (g t) -> p g t", g=4),
                                                      in1=M2[:, :].unsqueeze(1).to_broadcast([128, 4, 128]), op=ALU.mult),
                     reads=[B_ps[0], B_c], writes=[Bcbm])
                P.op("pe", lambda e, ch=ch: e.matmul(bank(3, 64)[:, 0:32], M2[:, :], a_tm[:, ch, :], start=True, stop=True), reads=[B_c, Bdt[ch]], writes=[B_ps[3]])
                P.op("pe", lambda e, ch=ch: e.matmul(bank(3, 64)[:, 32:64], onesf[:, :], a_tm[:, ch, :], start=True, stop=True), reads=[B_c, Bdt[ch]], writes=[B_ps[3]])
                P.op("act", lambda e: e.activation(out=small[:, 0, :], in_=bank(3, 64)[:, 32:64], func=AF.Copy), reads=[B_ps[3]], writes=[Bsm])
                P.op("dve", lambda e: e.tensor_tensor(out=small[:, 1, :], in0=small[:, 0, :], in1=bank(3, 64)[:, 0:32], op=ALU.subtract), reads=[B_ps[3], Bsm], writes=[Bsm])
                P.op("act", lambda e: e.activation(out=small[:, 1, :], in_=small[:, 1, :], func=AF.Exp), reads=[Bsm], writes=[Bsm])
                P.op("dve", lambda e, ch=ch: e.tensor_tensor(out=small[:, 2, :], in0=small[:, 1, :], in1=dt[:, ch, :], op=ALU.mult), reads=[Bsm, Bdt[ch]], writes=[Bsm])
                xv = xs_tm[:, ch, :].rearrange("p (h q) -> p h q", h=32)
                P.op("dve", lambda e, ch=ch, xv=xv: e.tensor_tensor(out=xdt[:, :].rearrange("p (h q) -> p h q", h=32), in0=xv,
                                                                    in1=dt[:, ch, :].unsqueeze(2).to_broadcast([128, 32, 64]), op=ALU.mult),
                     reads=[Bxs[ch], Bdt[ch]], writes=[Bxdt])
                P.op("dve", lambda e, xv=xv: e.tensor_tensor(out=xdtd[:, :].rearrange("p (h q) -> p h q", h=32), in0=xv,
                                                             in1=small[:, 2, :].unsqueeze(2).to_broadcast([128, 32, 64]), op=ALU.mult),
                     reads=[Bxs[ch], Bsm], writes=[Bxdtd])
                for bq in range(8):
                    pi = bq % 2
                    g = bq // 2
                    h0 = bq * 4
                    P.op("dve", lambda e, pi=pi, ch=ch, h0=h0: e.tensor_tensor(
                        out=rb[:, pi, :].rearrange("p (h t) -> p h t", h=4), in0=M2[:, :].unsqueeze(1).to_broadcast([128, 4, 128]),
                        in1=a_tm[:, ch, h0:h0 + 4].unsqueeze(2).to_broadcast([128, 4, 128]), op=ALU.mult),
                        reads=[B_c, Bdt[ch]], writes=[Brb[pi]])
                    P.op("pe", lambda e, pi=pi: e.matmul(bank(1, 512), M1[:, :], rb[:, pi, :], start=True, stop=True), reads=[B_c, Brb[pi]], writes=[B_ps[1]])
                    P.op("pe", lambda e, pi=pi: e.matmul(bank(2, 512), onesf[:, :], rb[:, pi, :], start=True, stop=True), reads=[B_c, Brb[pi]], writes=[B_ps[2]])
                    P.op("act", lambda e, pi=pi: e.activation(out=seg[:, pi, :], in_=bank(1, 512), func=AF.Exp), reads=[B_ps[1]], writes=[Bseg[pi]])
                    P.op("act", lambda e, pi=pi: e.activation(out=eab[:, pi, :], in_=bank(2, 512), func=AF.Exp), reads=[B_ps[2]], writes=[Beab[pi]])
                    P.op("dve", lambda e, pi=pi, g=g: e.tensor_tensor(
                        out=LT[:, pi, :].rearrange("p (h t) -> p h t", h=4), in0=seg[:, pi, :].rearrange("p (h t) -> p h t", h=4),
                        in1=cbm[:, g:g + 1, :].to_broadcast([128, 4, 128]), op=ALU.mult), reads=[Bseg[pi], Bcbm], writes=[BLT[pi]])
                    P.op("dve", lambda e, pi=pi, g=g, cs=cs: e.tensor_tensor(
                        out=Csc[:, pi, :].rearrange("p (h t) -> p h t", h=4), in0=eab[:, pi, :].rearrange("p (h t) -> p h t", h=4),
                        in1=CT[:, g:g + 1, cs].to_broadcast([128, 4, 128]), op=ALU.mult), reads=[Beab[pi], BCT[g]], writes=[BCsc[pi]])
                    P.op("dve", lambda e, pi=pi, h0=h0: e.tensor_copy(out=dcy[:, h0:h0 + 4], in_=eab[:, pi, :].rearrange("p (h t) -> p h t", h=4)[:, :, 127]),
                         reads=[Beab[pi]], writes=[Bdcy])
                    for hh in range(4):
                        h = h0 + hh
                        yb = 4 + h // 8
                        o0 = yb * 512 + (h % 8) * 64
                        P.op("pe", lambda e, pi=pi, hh=hh, h=h, o0=o0: e.matmul(
                            psum[:, o0:o0 + 64], LT[:, pi, hh * 128:(hh + 1) * 128], xdt[:, h * 64:(h + 1) * 64], start=True, stop=False),
                            reads=[BLT[pi], Bxdt], writes=[B_ps[yb]])
                        P.op("pe", lambda e, pi=pi, hh=hh, h=h, o0=o0: e.matmul(
                            psum[:, o0:o0 + 64], Csc[:, pi, hh * 128:(hh + 1) * 128], HTbf[:, h * 64:(h + 1) * 64], start=False, stop=True),
                            reads=[BCsc[pi], B_HTbf], writes=[B_ps[yb]])
                P.op("dve", lambda e, xv=xv: e.tensor_tensor(out=y32[:, :].rearrange("p (h q) -> p h q", h=32), in0=xv,
                                                             in1=hv[:, 2, :].unsqueeze(2).to_broadcast([128, 32, 64]), op=ALU.mult),
                     reads=[Bxs[ch], B_c], writes=[By32])
                for q in range(4):
                    P.op("dve", lambda e, q=q: e.tensor_tensor(out=y32[:, q * 512:(q + 1) * 512], in0=y32[:, q * 512:(q + 1) * 512], in1=bank(4 + q, 512), op=ALU.add),
                         reads=[By32, B_ps[4 + q]], writes=[By32])
                P.op("dve", lambda e, ch=ch: e.tensor_tensor(out=y32[:, :], in0=y32[:, :], in1=sz[:, ch, :], op=ALU.mult), reads=[By32, Bsz[ch]], writes=[By32])
                for q in range(4):
                    P.op("act", lambda e, q=q: e.activation(out=junk[:, :], in_=y32[:, q * 512:(q + 1) * 512], func=AF.Square, accum_out=ssq[:, q:q + 1]),
                         reads=[By32], writes=[Bssq])
                P.op("act", lambda e: e.activation(out=ssq[:, 4:8], in_=ssq[:, 0:4], func=AF.Sqrt, bias=eps1[:, 0:1], scale=1.0 / 512), reads=[Bssq, B_c], writes=[Bssq])
                P.op("dve", lambda e: e.reciprocal(out=ssq[:, 4:8], in_=ssq[:, 4:8]), reads=[Bssq], writes=[Bssq])
                P.op("dve", lambda e: e.tensor_tensor(out=y32[:, :].rearrange("p (g q) -> p g q", g=4), in0=y32[:, :].rearrange("p (g q) -> p g q", g=4),
                                                      in1=ssq[:, 4:8].unsqueeze(2).to_broadcast([128, 4, 512]), op=ALU.mult), reads=[By32, Bssq], writes=[By32])
                for g in range(4):
                    P.op("pe", lambda e, g=g, ch=ch: e.matmul(bank(4 + g, 512), B_tm[:, ch, g * 128:(g + 1) * 128], xdtd[:, g * 512:(g + 1) * 512], start=True, stop=True),
                         reads=[BBtm[ch], Bxdtd, By32], writes=[B_ps[4 + g]])
                P.op("dve", lambda e: e.tensor_tensor(out=HT32[:, :].rearrange("p (h q) -> p h q", h=32), in0=HT32[:, :].rearrange("p (h q) -> p h q", h=32),
                                                      in1=dcy[:, :].unsqueeze(2).to_broadcast([128, 32, 64]), op=ALU.mult), reads=[B_HT32, Bdcy, B_HTbf], writes=[B_HT32])
                for g in range(4):
                    P.op("dve", lambda e, g=g: e.tensor_tensor(out=HT32[:, g * 512:(g + 1) * 512], in0=HT32[:, g * 512:(g + 1) * 512], in1=bank(4 + g, 512), op=ALU.add),
                         reads=[B_HT32, B_ps[4 + g]], writes=[B_HT32])
                P.op("act", lambda e: e.activation(out=HTbf[:, :], in_=HT32[:, :], func=AF.Copy), reads=[B_HT32], writes=[B_HTbf])
                for fc in range(16):
                    bt = nb(0, 4)
                    P.op("pe", lambda e, fc=fc, bt=bt: e.transpose(bank(bt, 128), y32[:, fc * 128:(fc + 1) * 128], ident[:, :]), reads=[By32, B_const], writes=[B_ps[bt]])
                    P.op("act", lambda e, fc=fc, bt=bt, cs=cs: e.activation(out=ynT[:, fc, cs], in_=bank(bt, 128), func=AF.Identity, scale=ngc[:, fc:fc + 1], bias=0.0),
                         reads=[B_ps[bt], B_c], writes=[BynT[fc]])
            ti = tok0 // 512
            for m in range(KC):
                so = state["nwo"] % 2
                state["nwo"] += 1
                P.dma("pool", lambda e, so=so, m=m: e.dma_start(
                    out=wos[:, so, :, :], in_=ssm_w_out_d[:, m * 128:(m + 1) * 128].rearrange("(fc p) n -> p fc n", p=128)), writes=[Bwo[so]])
                b = nb(0, 4)
                for fc in range(16):
                    P.op("pe", lambda e, so=so, fc=fc, b=b: e.matmul(bank(b, TW), wos[:, so, fc, :], ynT[:, fc, :], start=(fc == 0), stop=(fc == 15)),
                         reads=[Bwo[so], BynT[fc]], writes=[B_ps[b]])
                P.op("dve", lambda e, m=m, b=b, tok0=tok0: e.scalar_tensor_tensor(
                    out=x32[:, m, tok0:tok0 + TW], in0=bank(b, TW), scalar=1.0 / ALPHA, in1=x32[:, m, tok0:tok0 + TW], op0=ALU.mult, op1=ALU.add),
                    reads=[B_ps[b], B_x32[m][ti]], writes=[B_x32[m][ti]])
        import os
        if os.environ.get("KDBG_SSM"):
            return "stop"
        for j in range(3):
            P.dma("sp", lambda e, j=j: e.dma_start(out=conv_p_d[j:j + 1, :].rearrange("o (f p) -> p (o f)", p=128), in_=carry[:, :, j],
                                                   allow_slow_non_contiguous=True), reads=B_carry)
        hout = sb.alloc([128, 2, 128], F32)
        Bho = [Buf("hout") for _ in range(2)]
        for r in range(16):
            bt = nb(0, 4)
            so = r % 2
            P.op("pe", lambda e, r=r, bt=bt: e.transpose(bank(bt, 128), HT32[:, r * 128:(r + 1) * 128], ident[:, :]), reads=[B_HT32, B_const], writes=[B_ps[bt]])
            P.op("act", lambda e, so=so, bt=bt: e.activation(out=hout[:, so, :], in_=bank(bt, 128), func=AF.Copy), reads=[B_ps[bt]], writes=[Bho[so]])
            P.dma("sp", lambda e, r=r, so=so: e.dma_start(out=ssm_p_d[2 * r:2 * r + 2, :, :].rearrange("h p n -> (h p) n"), in_=hout[:, so, :]), reads=[Bho[so]])

        P.barrier()
        sb.release(mwork)
        TOK = NCOL - 128
        R0, R1 = 96, 128
        wins2 = sb.alloc([128, 3, KC, 256], BF16)
        wos2 = sb.alloc([128, 2, 16, 128], BF16)
        proj = sb.alloc([128, 5152], F32)
        cst = sb.alloc([128, 3072], F32)
        cwb = sb.alloc([128, 3072], F32)
        act_s = sb.alloc([128, 3072], F32)
        tmx = sb.alloc([128, 2048], F32)
        sm = sb.alloc([128, 4, 32], F32)
        sel = sb.alloc([128, NS, 128], F32)
        dsc = sb.alloc([128, 16], F32)
        fm = sb.alloc([128, 5, 16, NS], F32)
        bcB = sb.alloc([128, 512], F32)
        bcC = sb.alloc([128, 512], F32)
        hst = sb.alloc([128, 16, 128], F32)
        jk = sb.alloc([128, 128], F32)
        ysq = sb.alloc([128, 16, NS], BF16)
        rst = sb.alloc([128, 4, NS], F32)
        ynb = sb.alloc([128, 16, NS], BF16)
        Bw2 = [Buf("wins2") for _ in range(3)]
        Bwo2 = [Buf("wos2") for _ in range(2)]
        Bproj, Bcst, Bcw, Bact, Btmx, Bsm2, Bsel, Bfm = (Buf(n) for n in ("proj", "cst", "cwb", "act_s", "tmx", "sm", "sel", "fm"))
        BbcB, BbcC, Bhst, Bjk, Bysq, Brst, Bynb = (Buf(n) for n in ("bcB", "bcC", "hst", "jk", "ysq", "rst", "ynb"))
        state["nsl"] = 0
        state["nwo"] = 0

        def load_win2(col0, ncol=256):
            sl = state["nsl"] % 3
            state["nsl"] += 1
            P.dma("pool", lambda e, sl=sl, col0=col0, ncol=ncol: e.dma_start(
                out=wins2[:, sl, :, :ncol], in_=ssm_w_in_d[:, col0:col0 + ncol].rearrange("(kc p) n -> p kc n", p=128)), writes=[Bw2[sl]])
            return sl
        stage2 = sb.alloc([128, 128], F32)
        Bst2 = Buf("stage2")
        P.op("dve", lambda e: e.memset(stage2[:, :], 0.0), writes=[Bst2])
        P.dma("sp", lambda e: e.dma_start(out=stage2[:16, :], in_=ssm_d_c_d[:, :]), writes=[Bst2])
        P.op("pe", lambda e: e.transpose(bank(7, 128), stage2[:, :], ident[:, :]), reads=[Bst2, B_const], writes=[B_ps[7]])
        P.op("dve", lambda e: e.tensor_copy(out=dsc[:, :], in_=bank(7, 16)), reads=[B_ps[7]], writes=[Bsel])
        P.dma("sp", lambda e: e.dma_start(out=sel[:], in_=ssm_sel_d[:, :, :]), reads=[Bsel], writes=[Bsel])
        P.op("dve", lambda e: e.memset(cst[:], 0.0), writes=[Bcst])
        P.op("dve", lambda e: e.memset(proj[:], 0.0), writes=[Bproj])
        P.op("dve", lambda e: e.memset(act_s[:], 0.0), writes=[Bact])
        col = 0
        while col < 5152:
            ncol = min(256, 5152 - col)
            sl = load_win2(col, ncol)
            b = nb(0, 4)
            for kc in range(KC):
                P.op("pe", lambda e, kc=kc, sl=sl, b=b, ncol=ncol: e.matmul(bank(b, ncol), xbf[:, kc, TOK:TOK + 128], wins2[:, sl, kc, :ncol], start=(kc == 0), stop=(kc == KC - 1)),
                     reads=[Bw2[sl]] + allx(kc), writes=[B_ps[b]])
            P.op("act", lambda e, b=b, col=col, ncol=ncol: e.activation(out=proj[R0:R1, col:col + ncol], in_=bank(b, ncol)[R0:R1, :], func=AF.Copy),
                 reads=[B_ps[b]], writes=[Bproj])
            col += ncol
        xbc = proj[R0:R1, 2048:5120]
        P.dma("sp", lambda e: e.dma_start(out=cwb[R0:R1, :], in_=ssm_cw_bc_d[:, 3, :]), writes=[Bcw])
        P.op("dve", lambda e: e.tensor_tensor(out=act_s[R0:R1, :], in0=xbc, in1=cwb[R0:R1, :], op=ALU.mult), reads=[Bproj, Bcw], writes=[Bact])
        P.dma("sp", lambda e: e.dma_start(out=cwb[R0:R1, :], in_=ssm_cb_bc_d[:, :]), reads=[Bcw], writes=[Bcw])
        P.op("dve", lambda e: e.tensor_tensor(out=act_s[R0:R1, :], in0=act_s[R0:R1, :], in1=cwb[R0:R1, :], op=ALU.add), reads=[Bact, Bcw], writes=[Bact])
        for j in range(3):
            P.dma("sp", lambda e, j=j: e.dma_start(out=cwb[R0:R1, :], in_=ssm_cw_bc_d[:, j, :]), reads=[Bcw], writes=[Bcw])
            P.dma("sp", lambda e, j=j: e.dma_start(out=cst[124:128, :], in_=st_conv_d[:, j, :]), reads=[Bcst], writes=[Bcst])
            P.op("dve", lambda e: e.tensor_tensor(out=cwb[R0:R1, :], in0=cwb[R0:R1, :], in1=cst[R0:R1, :], op=ALU.mult), reads=[Bcw, Bcst], writes=[Bcw])
            P.op("dve", lambda e: e.tensor_tensor(out=act_s[R0:R1, :], in0=act_s[R0:R1, :], in1=cwb[R0:R1, :], op=ALU.add), reads=[Bact, Bcw], writes=[Bact])
        P.op("act", lambda e: e.activation(out=act_s[R0:R1, :], in_=act_s[R0:R1, :], func=AF.Silu), reads=[Bact], writes=[Bact])
        P.dma("sp", lambda e: e.dma_start(out=conv_s_d[:, 0:2, :], in_=st_conv_d[:, 1:3, :]))
        P.dma("sp", lambda e: e.dma_start(out=conv_s_d[:, 2, :], in_=proj[124:128, 2048:5120]), reads=[Bproj])
        P.op("dve", lambda e: e.tensor_tensor(out=sm[R0:R1, 0, :], in0=proj[R0:R1, 5120:5152], in1=hv[R0:R1, 0, :], op=ALU.add), reads=[Bproj, B_c], writes=[Bsm2])
        P.op("act", lambda e: e.activation(out=sm[R0:R1, 0, :], in_=sm[R0:R1, 0, :], func=AF.Exp), reads=[Bsm2], writes=[Bsm2])
        P.op("act", lambda e: e.activation(out=sm[R0:R1, 0, :], in_=sm[R0:R1, 0, :], func=AF.Ln, bias=one1[R0:R1, 0:1], scale=1.0), reads=[Bsm2, B_c], writes=[Bsm2])
        P.op("dve", lambda e: e.tensor_tensor(out=sm[R0:R1, 1, :], in0=sm[R0:R1, 0, :], in1=hv[R0:R1, 1, :], op=ALU.mult), reads=[Bsm2, B_c], writes=[Bsm2])
        P.op("act", lambda e: e.activation(out=sm[R0:R1, 1, :], in_=sm[R0:R1, 1, :], func=AF.Exp), reads=[Bsm2], writes=[Bsm2])
        P.op("dve", lambda e: e.memset(tmx[:], 0.0), writes=[Btmx])
        v3 = lambda ap: ap.rearrange("p (h q) -> p h q", h=32)

        def to_fm(src, k, Bsrc):
            for fc in range(16):
                bt = nb(4, 4)
                P.op("pe", lambda e, fc=fc, bt=bt: e.transpose(bank(bt, 128), src[:, fc * 128:(fc + 1) * 128], ident[:, :]), reads=[Bsrc, B_const], writes=[B_ps[bt]])
                P.op("act", lambda e, fc=fc, bt=bt: e.activation(out=fm[:, k, fc, :], in_=bank(bt, 128)[:, 124:128], func=AF.Copy), reads=[B_ps[bt], Bfm], writes=[Bfm])
        P.op("dve", lambda e: e.tensor_tensor(out=v3(tmx[R0:R1, :]), in0=v3(act_s[R0:R1, 0:2048]), in1=sm[R0:R1, 0, :].unsqueeze(2).to_broadcast([32, 32, 64]), op=ALU.mult),
             reads=[Bact, Bsm2, Btmx], writes=[Btmx])
        to_fm(tmx[:, :], 0, Btmx)
        P.op("dve", lambda e: e.tensor_copy(out=v3(tmx[R0:R1, :]), in_=sm[R0:R1, 1, :].unsqueeze(2).to_broadcast([32, 32, 64])), reads=[Bsm2, Btmx], writes=[Btmx])
        to_fm(tmx[:, :], 1, Btmx)
        to_fm(act_s[:, 0:2048], 2, Bact)
        P.op("act", lambda e: e.activation(out=tmx[R0:R1, :], in_=proj[R0:R1, 0:2048], func=AF.Silu), reads=[Bproj, Btmx], writes=[Btmx])
        to_fm(tmx[:, :], 3, Btmx)
        for b_ in range(NS):
            P.dma("sp", lambda e, b_=b_: e.dma_start(out=hst[:], in_=st_ssm_d[b_].rearrange("(fc p) n -> p fc n", p=128)), writes=[Bhst])
            P.op("pe", lambda e, b_=b_: e.matmul(bank(0, 512), sel[:, b_, :], act_s[:, 2048:2560], start=True, stop=True), reads=[Bsel, Bact], writes=[B_ps[0]])
            P.op("pe", lambda e, b_=b_: e.matmul(bank(1, 512), sel[:, b_, :], act_s[:, 2560:3072], start=True, stop=True), reads=[Bsel, Bact], writes=[B_ps[1]])
            P.op("act", lambda e: e.activation(out=bcB[:], in_=bank(0, 512), func=AF.Copy), reads=[B_ps[0]], writes=[BbcB])
            P.op("act", lambda e: e.activation(out=bcC[:], in_=bank(1, 512), func=AF.Copy), reads=[B_ps[1]], writes=[BbcC])
            for fc in range(16):
                g = fc // 4
                P.op("dve", lambda e, fc=fc, b_=b_: e.tensor_scalar(out=hst[:, fc, :], in0=hst[:, fc, :], scalar1=fm[:, 1, fc, b_:b_ + 1], scalar2=None, op0=ALU.mult),
                     reads=[Bhst, Bfm], writes=[Bhst])
                P.op("dve", lambda e, fc=fc, b_=b_, g=g: e.scalar_tensor_tensor(out=hst[:, fc, :], in0=bcB[:, g * 128:(g + 1) * 128], scalar=fm[:, 0, fc, b_:b_ + 1],
                                                                              in1=hst[:, fc, :], op0=ALU.mult, op1=ALU.add),
                     reads=[Bhst, Bfm, BbcB], writes=[Bhst])
                P.op("dve", lambda e, fc=fc, b_=b_, g=g: e.scalar_tensor_tensor(out=jk[:, :], in0=hst[:, fc, :], scalar=1.0, in1=bcC[:, g * 128:(g + 1) * 128],
                                                                              op0=ALU.mult, op1=ALU.mult, accum_out=fm[:, 4, fc, b_:b_ + 1]),
                     reads=[Bhst, BbcC, Bfm, Bjk], writes=[Bjk, Bfm])
            P.dma("sp", lambda e, b_=b_: e.dma_start(out=ssm_s_d[b_].rearrange("h q n -> (h q) n").rearrange("(fc p) n -> p fc n", p=128), in_=hst[:]), reads=[Bhst])
        P.op("dve", lambda e: e.tensor_tensor(out=fm[:, 2], in0=fm[:, 2], in1=dsc[:, :].unsqueeze(2).to_broadcast([128, 16, NS]), op=ALU.mult), reads=[Bfm, Bsel], writes=[Bfm])
        P.op("dve", lambda e: e.tensor_tensor(out=fm[:, 4], in0=fm[:, 4], in1=fm[:, 2], op=ALU.add), reads=[Bfm], writes=[Bfm])
        P.op("dve", lambda e: e.tensor_tensor(out=fm[:, 4], in0=fm[:, 4], in1=fm[:, 3], op=ALU.mult), reads=[Bfm], writes=[Bfm])
        P.op("act", lambda e: e.activation(out=ysq[:], in_=fm[:, 4], func=AF.Square), reads=[Bfm], writes=[Bysq])
        ones_bf1 = sb.alloc([128, 128], BF16)
        P.op("dve", lambda e: e.memset(ones_bf1[:], 1.0), writes=[Bsel])
        for g in range(4):
            for q in range(4):
                fc = g * 4 + q
                P.op("pe", lambda e, g=g, q=q, fc=fc: e.matmul(bank(2, 16)[:, g * NS:(g + 1) * NS], ones_bf1[:], ysq[:, fc, :], start=(q == 0), stop=(q == 3)),
                     reads=[Bysq, Bsel], writes=[B_ps[2]])
        P.op("act", lambda e: e.activation(out=rst[:].rearrange("p g b -> p (g b)"), in_=bank(2, 16), func=AF.Sqrt, bias=eps1[:, 0:1], scale=1.0 / 512), reads=[B_ps[2], B_c], writes=[Brst])
        P.op("dve", lambda e: e.reciprocal(out=rst[:], in_=rst[:]), reads=[Brst], writes=[Brst])
        for g in range(4):
            P.op("dve", lambda e, g=g: e.tensor_tensor(out=fm[:, 4, g * 4:(g + 1) * 4, :], in0=fm[:, 4, g * 4:(g + 1) * 4, :],
                                                       in1=rst[:, g:g + 1, :].to_broadcast([128, 4, NS]), op=ALU.mult), reads=[Bfm, Brst], writes=[Bfm])
        P.op("dve", lambda e: e.tensor_tensor(out=ynb[:], in0=fm[:, 4], in1=ngc[:, :].unsqueeze(2).to_broadcast([128, 16, NS]), op=ALU.mult), reads=[Bfm, B_c], writes=[Bynb])
        for m in range(KC):
            so = state["nwo"] % 2
            state["nwo"] += 1
            P.dma("pool", lambda e, so=so, m=m: e.dma_start(
                out=wos2[:, so, :, :], in_=ssm_w_out_d[:, m * 128:(m + 1) * 128].rearrange("(fc p) n -> p fc n", p=128)), writes=[Bwo2[so]])
            b = nb(0, 2)
            for fc in range(16):
                P.op("pe", lambda e, so=so, fc=fc, b=b: e.matmul(bank(b, NS), wos2[:, so, fc, :], ynb[:, fc, :], start=(fc == 0), stop=(fc == 15)),
                     reads=[Bwo2[so], Bynb], writes=[B_ps[b]])
            P.op("dve", lambda e, m=m, b=b: e.scalar_tensor_tensor(
                out=x32[:, m, SEQ:SEQ + NS], in0=bank(b, NS), scalar=1.0 / ALPHA, in1=x32[:, m, SEQ:SEQ + NS], op0=ALU.mult, op1=ALU.add),
                reads=[B_ps[b], B_x32[m][4]], writes=[B_x32[m][4]])
        sb.release(mk)
        P.barrier()


    def dsa():
        mk = sb.mark()
        rope = sb.alloc([128, 17, 16], F32)
        eps1 = sb.alloc([128, 1], F32)
        rt = sb.alloc([128, 2, 5, 128], F32)
        kT = sb.alloc([64, 4, SEQ], BF16)
        ikT = sb.alloc([64, SEQ], BF16)
        vaug = sb.alloc([128, 16, 4, 65], BF16)
        snew = sb.alloc([128, NS, 576], BF16)
        mA = sb.mark()
        wkv = sb.alloc([128, KC, 512], BF16)
        wik = sb.alloc([128, KC, 64], BF16)
        knb = sb.alloc([128, 2, 64], F32)
        kv32 = sb.alloc([128, 2, 512], F32)
        ik32 = sb.alloc([128, 2, 64], F32)
        st6 = sb.alloc([128, 2, 8], F32)
        BkT = [Buf("kT") for _ in range(16)]
        Bva = [Buf("vaug") for _ in range(16)]
        P.op("dve", lambda e: e.memset(vaug[:], 1.0), writes=Bva)
        Bsnew = Buf("snew")
        P.op("dve", lambda e: e.memset(snew[:], 0.0), writes=[Bsnew])
        Bc = Buf("att_consts")
        Bkv = [Buf("kv32") for _ in range(2)]
        Bik = [Buf("ik32") for _ in range(2)]
        Brt = [Buf("rt") for _ in range(2)]
        P.dma("pool", lambda e: e.dma_start(out=wkv[:], in_=att_w_in_d[:, 1024:1536].rearrange("(kc p) n -> p kc n", p=128)), writes=[Bc])
        P.dma("pool", lambda e: e.dma_start(out=wik[:], in_=att_w_in_d[:, 2048:2112].rearrange("(kc p) n -> p kc n", p=128)), reads=[Bc], writes=[Bc])
        P.dma("sp", lambda e: e.dma_start(out=knb[:], in_=att_kn_bc_d[:, :, :]), reads=[Bc], writes=[Bc])
        P.dma("sp", lambda e: e.dma_start(out=rope[:], in_=rope_d[:, :, :]), reads=[Bc], writes=[Bc])
        P.op("dve", lambda e: e.memset(eps1[:], LN_EPS), reads=[Bc], writes=[Bc])
        allx = lambda kc: [B_xbf[kc][t] for t in range(5)]

        def rope_apply(xh, nh, c, si, Bx):
            cos = rope[:, c:c + 1, 0:8].to_broadcast([128, nh, 8])
            sin = rope[:, c:c + 1, 8:16].to_broadcast([128, nh, 8])
            x1, x2 = xh[:, :, 0:8], xh[:, :, 8:16]
            t = [rt[:, si, k, 0:nh * 8].rearrange("p (h d) -> p h d", h=nh) for k in range(5)]
            P.op("dve", lambda e: e.tensor_tensor(out=t[0], in0=x1, in1=cos, op=ALU.mult), reads=[Bx, Bc], writes=[Brt[si]])
            P.op("dve", lambda e: e.tensor_tensor(out=t[1], in0=x2, in1=sin, op=ALU.mult), reads=[Bx, Bc, Brt[si]], writes=[Brt[si]])
            P.op("dve", lambda e: e.tensor_tensor(out=t[2], in0=x2, in1=cos, op=ALU.mult), reads=[Bx, Bc, Brt[si]], writes=[Brt[si]])
            P.op("dve", lambda e: e.tensor_tensor(out=t[3], in0=x1, in1=sin, op=ALU.mult), reads=[Bx, Bc, Brt[si]], writes=[Brt[si]])
            P.op("dve", lambda e: e.tensor_tensor(out=x1, in0=t[0], in1=t[1], op=ALU.subtract), reads=[Brt[si], Bx], writes=[Bx])
            P.op("dve", lambda e: e.tensor_tensor(out=x2, in0=t[2], in1=t[3], op=ALU.add), reads=[Brt[si], Bx], writes=[Bx])

        for ch in range(17):
            si = ch % 2
            tk = ch * 128 if ch < 16 else NCOL - 128
            b0, b1 = 2 * si, 2 * si + 1
            for kc in range(KC):
                P.op("pe", lambda e, kc=kc, tk=tk, b0=b0: e.matmul(bank(b0, 512), xbf[:, kc, tk:tk + 128], wkv[:, kc, :], start=(kc == 0), stop=(kc == KC - 1)),
                     reads=[Bc] + allx(kc), writes=[B_ps[b0]])
            for kc in range(KC):
                P.op("pe", lambda e, kc=kc, tk=tk, b1=b1: e.matmul(bank(b1, 64), xbf[:, kc, tk:tk + 128], wik[:, kc, :], start=(kc == 0), stop=(kc == KC - 1)),
                     reads=[Bc] + allx(kc), writes=[B_ps[b1]])
            P.op("act", lambda e, si=si, b0=b0: e.activation(out=kv32[:, si, :], in_=bank(b0, 512), func=AF.Copy), reads=[B_ps[b0]], writes=[Bkv[si]])
            P.op("act", lambda e, si=si, b1=b1: e.activation(out=ik32[:, si, :], in_=bank(b1, 64), func=AF.Copy), reads=[B_ps[b1]], writes=[Bik[si]])
            P.op("dve", lambda e, si=si: e.bn_stats(out=st6[:, si, 0:6], in_=ik32[:, si, :]), reads=[Bik[si]], writes=[Brt[si]])
            P.op("dve", lambda e, si=si: e.bn_aggr(out=st6[:, si, 6:8], in_=st6[:, si, 0:6]), reads=[Brt[si]], writes=[Brt[si]])
            P.op("act", lambda e, si=si: e.activation(out=st6[:, si, 7:8], in_=st6[:, si, 7:8], func=AF.Sqrt, bias=eps1[:, 0:1], scale=1.0), reads=[Brt[si], Bc], writes=[Brt[si]])
            P.op("dve", lambda e, si=si: e.reciprocal(out=st6[:, si, 7:8], in_=st6[:, si, 7:8]), reads=[Brt[si]], writes=[Brt[si]])
            P.op("dve", lambda e, si=si: e.tensor_scalar(out=ik32[:, si, :], in0=ik32[:, si, :], scalar1=st6[:, si, 6:7], scalar2=st6[:, si, 7:8],
                                                         op0=ALU.subtract, op1=ALU.mult), reads=[Bik[si], Brt[si]], writes=[Bik[si]])
            P.op("dve", lambda e, si=si: e.tensor_tensor(out=ik32[:, si, :], in0=ik32[:, si, :], in1=knb[:, 0, :], op=ALU.mult), reads=[Bik[si], Bc], writes=[Bik[si]])
            P.op("dve", lambda e, si=si: e.tensor_tensor(out=ik32[:, si, :], in0=ik32[:, si, :], in1=knb[:, 1, :], op=ALU.add), reads=[Bik[si], Bc], writes=[Bik[si]])
            c = min(ch, 16)
            rope_apply(kv32[:, si, 0:256].rearrange("p (h d) -> p h d", h=4), 4, c, si, Bkv[si])
            rope_apply(ik32[:, si, :].rearrange("p (h d) -> p h d", h=1), 1, c, si, Bik[si])
            if ch < 16:
                for hk in range(5):
                    bt = 4 + hk % 4
                    src = kv32[:, si, hk * 64:(hk + 1) * 64] if hk < 4 else ik32[:, si, :]
                    P.op("pe", lambda e, src=src, bt=bt: e.transpose(bank(bt, 128)[0:64, :], src, ident[:, :]),
                         reads=[Bkv[si] if hk < 4 else Bik[si], B_const], writes=[B_ps[bt]])
                    dst = kT[:, hk, tk:tk + 128] if hk < 4 else ikT[:, tk:tk + 128]
                    P.op("act", lambda e, dst=dst, bt=bt: e.activation(out=dst, in_=bank(bt, 128)[0:64, :], func=AF.Copy), reads=[B_ps[bt], BkT[ch]], writes=[BkT[ch]])
                P.op("act", lambda e, si=si, ch=ch: e.activation(out=vaug[:, ch, :, 0:64], in_=kv32[:, si, 256:512].rearrange("p (h d) -> p h d", h=4), func=AF.Copy),
                     reads=[Bkv[si], Bva[ch]], writes=[Bva[ch]])
                P.dma("sp", lambda e, si=si, tk=tk: e.dma_start(out=k_p_d[tk:tk + 128, :], in_=kv32[:, si, 0:256]), reads=[Bkv[si]])
                P.dma("sp", lambda e, si=si, tk=tk: e.dma_start(out=v_p_d[tk:tk + 128, :], in_=kv32[:, si, 256:512]), reads=[Bkv[si]])
                P.dma("sp", lambda e, si=si, tk=tk: e.dma_start(out=ik_p_d[tk:tk + 128, :], in_=ik32[:, si, :]), reads=[Bik[si]])
            else:
                P.dma("sp", lambda e, si=si: e.dma_start(out=k_s_d[:, :], in_=kv32[124:128, si, 0:256]), reads=[Bkv[si]])
                P.dma("sp", lambda e, si=si: e.dma_start(out=v_s_d[:, :], in_=kv32[124:128, si, 256:512]), reads=[Bkv[si]])
                P.dma("sp", lambda e, si=si: e.dma_start(out=ik_s_d[:, :], in_=ik32[124:128, si, :]), reads=[Bik[si]])
                for b_ in range(NS):
                    P.dma("pool", lambda e, si=si, b_=b_: e.dma_start(out=snew[0:1, b_, 0:512], in_=kv32[124 + b_:125 + b_, si, :]), reads=[Bkv[si], Bsnew], writes=[Bsnew])
                    P.dma("pool", lambda e, si=si, b_=b_: e.dma_start(out=snew[0:1, b_, 512:576], in_=ik32[124 + b_:125 + b_, si, :]), reads=[Bik[si], Bsnew], writes=[Bsnew])

        P.barrier()
        sb.release(mA)
        mB = sb.mark()
        TOPK = 256
        wq = sb.alloc([128, KC, 1024], BF16)
        wiq = sb.alloc([128, KC, 520], BF16)
        wo_s = sb.alloc([128, 1, KC, 128], BF16)
        negm = sb.alloc([128, 128], F32)
        q32 = sb.alloc([128, 1024], F32)
        iq32 = sb.alloc([128, 520], F32)
        qT = sb.alloc([64, 16, 128], BF16)
        iqT = sb.alloc([64, 8, 128], BF16)
        score = sb.alloc([128, SEQ], F32)
        work = sb.alloc([128, SEQ], F32)
        m8 = sb.alloc([128, 8], F32)
        thr = sb.alloc([128, 1], F32)
        maskT = sb.alloc([128, 16, 128], BF16)
        Pt = sb.alloc([128, 2, 512], BF16)
        rden = sb.alloc([128, 16], F32)
        o32 = sb.alloc([128, 1024], F32)
        oT = sb.alloc([128, KC, 128], BF16)
        Bw2 = Buf("attw")
        Bwo = [Buf("wo_s") for _ in range(2)]
        Bq32, Biq32, BqT, BiqT, Bscore, Bwork, Bm8, Bthr, BmaskT, Brden, Bo32 = (Buf(n) for n in (
            "q32", "iq32", "qT", "iqT", "score", "work", "m8", "thr", "maskT", "rden", "o32"))
        BPt = [Buf("Pt") for _ in range(2)]
        BoT = [Buf("oT") for _ in range(KC)]
        P.dma("pool", lambda e: e.dma_start(out=wq[:], in_=att_w_in_d[:, 0:1024].rearrange("(kc p) n -> p kc n", p=128)), writes=[Bw2])
        P.dma("pool", lambda e: e.dma_start(out=wiq[:, :, 0:512], in_=att_w_in_d[:, 1536:2048].rearrange("(kc p) n -> p kc n", p=128)), reads=[Bw2], writes=[Bw2])
        P.dma("pool", lambda e: e.dma_start(out=wiq[:, :, 512:520], in_=att_w_in_d[:, 2112:2120].rearrange("(kc p) n -> p kc n", p=128)), reads=[Bw2], writes=[Bw2])
        P.dma("sp", lambda e: e.dma_start(out=negm[:], in_=negm_d[:, :]), reads=[Bw2], writes=[Bw2])
        st2 = {"cnt": 0, "nwo": 0}

        def nb2(lo, n):
            b = lo + st2["cnt"] % n
            st2["cnt"] += 1
            return b

        for qi in range(16):
            tk = qi * 128
            nk = qi + 1
            W = nk * 128
            ti = tk // 512
            for half in range(2):
                b = nb2(0, 4)
                for kc in range(KC):
                    P.op("pe", lambda e, kc=kc, tk=tk, half=half, b=b: e.matmul(bank(b, 512), xbf[:, kc, tk:tk + 128], wq[:, kc, half * 512:(half + 1) * 512],
                                                                            start=(kc == 0), stop=(kc == KC - 1)), reads=[Bw2] + allx(kc), writes=[B_ps[b]])
                P.op("act", lambda e, half=half, b=b: e.activation(out=q32[:, half * 512:(half + 1) * 512], in_=bank(b, 512), func=AF.Copy), reads=[B_ps[b]], writes=[Bq32])
            b = nb2(0, 4)
            for kc in range(KC):
                P.op("pe", lambda e, kc=kc, tk=tk, b=b: e.matmul(bank(b, 512), xbf[:, kc, tk:tk + 128], wiq[:, kc, 0:512], start=(kc == 0), stop=(kc == KC - 1)),
                     reads=[Bw2] + allx(kc), writes=[B_ps[b]])
            P.op("act", lambda e, b=b: e.activation(out=iq32[:, 0:512], in_=bank(b, 512), func=AF.Copy), reads=[B_ps[b]], writes=[Biq32])
            b = nb2(0, 4)
            for kc in range(KC):
                P.op("pe", lambda e, kc=kc, tk=tk, b=b: e.matmul(bank(b, 8), xbf[:, kc, tk:tk + 128], wiq[:, kc, 512:520], start=(kc == 0), stop=(kc == KC - 1)),
                     reads=[Bw2] + allx(kc), writes=[B_ps[b]])
            P.op("act", lambda e, b=b: e.activation(out=iq32[:, 512:520], in_=bank(b, 8), func=AF.Copy, scale=float(8 ** -0.5 * 64 ** -0.5)), reads=[B_ps[b], Biq32], writes=[Biq32])
            rope_apply(q32[:, :].rearrange("p (h d) -> p h d", h=16), 16, qi, 0, Bq32)
            rope_apply(iq32[:, 0:512].rearrange("p (h d) -> p h d", h=8), 8, qi, 1, Biq32)
            for h in range(24):
                bt = 4 + h % 4
                src = q32[:, h * 64:(h + 1) * 64] if h < 16 else iq32[:, (h - 16) * 64:(h - 15) * 64]
                dst = qT[:, h, :] if h < 16 else iqT[:, h - 16, :]
                P.op("pe", lambda e, src=src, bt=bt: e.transpose(bank(bt, 128)[0:64, :], src, ident[:, :]), reads=[Bq32 if h < 16 else Biq32, B_const], writes=[B_ps[bt]])
                P.op("act", lambda e, dst=dst, bt=bt: e.activation(out=dst, in_=bank(bt, 128)[0:64, :], func=AF.Copy),
                     reads=[B_ps[bt], BqT if h < 16 else BiqT], writes=[BqT if h < 16 else BiqT])
            for h in range(8):
                b0 = 4 * (h % 2)
                for c4 in range((W + 511) // 512):
                    w = min(512, W - c4 * 512)
                    P.op("pe", lambda e, h=h, c4=c4, w=w, b0=b0: e.matmul(bank(b0 + c4, w), iqT[:, h, :], ikT[:, c4 * 512:c4 * 512 + w], start=True, stop=True),
                         reads=[BiqT] + BkT[c4 * 4:c4 * 4 + 4], writes=[B_ps[b0 + c4]])
                P.op("act", lambda e, b0=b0, W=W: e.activation(out=work[:, 0:W], in_=psum[:, b0 * 512:b0 * 512 + W], func=AF.Relu),
                     reads=[B_ps[b0 + c] for c in range(4)] + [Bwork], writes=[Bwork])
                if h == 0:
                    P.op("dve", lambda e, W=W: e.tensor_scalar(out=score[:, 0:W], in0=work[:, 0:W], scalar1=iq32[:, 512:513], scalar2=None, op0=ALU.mult),
                         reads=[Bwork, Biq32, Bscore], writes=[Bscore])
                else:
                    P.op("dve", lambda e, W=W, h=h: e.scalar_tensor_tensor(out=score[:, 0:W], in0=work[:, 0:W], scalar=iq32[:, 512 + h:513 + h], in1=score[:, 0:W],
                                                                           op0=ALU.mult, op1=ALU.add), reads=[Bwork, Biq32, Bscore], writes=[Bscore])
            P.op("dve", lambda e, tk=tk: e.tensor_tensor(out=score[:, tk:tk + 128], in0=score[:, tk:tk + 128], in1=negm[:, :], op=ALU.add), reads=[Bscore, Bw2], writes=[Bscore])
            if W <= TOPK:
                P.op("dve", lambda e: e.memset(thr[:], -1e29), reads=[Bthr], writes=[Bthr])
            else:
                P.op("act", lambda e, W=W: e.activation(out=work[:, 0:W], in_=score[:, 0:W], func=AF.Copy), reads=[Bscore, Bwork], writes=[Bwork])
                for r in range(TOPK // 8):
                    P.op("dve", lambda e, W=W: e.max(out=m8[:, :], in_=work[:, 0:W]), reads=[Bwork, Bm8], writes=[Bm8])
                    if r < TOPK // 8 - 1:
                        P.op("dve", lambda e, W=W: e.match_replace(out=work[:, 0:W], in_to_replace=m8[:, :], in_values=work[:, 0:W], imm_value=-1e30),
                             reads=[Bwork, Bm8], writes=[Bwork])
                P.op("dve", lambda e: e.tensor_scalar(out=thr[:], in0=m8[:, 7:8], scalar1=-1e29, scalar2=None, op0=ALU.max), reads=[Bm8, Bthr], writes=[Bthr])
            P.op("dve", lambda e, W=W: e.tensor_scalar(out=work[:, 0:W], in0=score[:, 0:W], scalar1=thr[:, 0:1], scalar2=None, op0=ALU.is_ge),
                 reads=[Bscore, Bthr, Bwork], writes=[Bwork])
            for kc in range(nk):
                bt = 4 + kc % 4
                P.op("pe", lambda e, kc=kc, bt=bt: e.transpose(bank(bt, 128), work[:, kc * 128:(kc + 1) * 128], ident[:, :]), reads=[Bwork, B_const], writes=[B_ps[bt]])
                P.op("act", lambda e, kc=kc, bt=bt: e.activation(out=maskT[:, kc, :], in_=bank(bt, 128), func=AF.Copy), reads=[B_ps[bt], BmaskT], writes=[BmaskT])
            for kvh in range(4):
                for kc in range(nk):
                    sb_ = nb2(0, 2)
                    pi = kc % 2
                    P.op("pe", lambda e, kvh=kvh, kc=kc, sb_=sb_: e.matmul(bank(sb_, 512), kT[:, kvh, kc * 128:(kc + 1) * 128],
                                                                          qT[:, kvh * 4:(kvh + 1) * 4, :].rearrange("p h q -> p (h q)"), start=True, stop=True),
                         reads=[BkT[kc], BqT], writes=[B_ps[sb_]])
                    P.op("act", lambda e, pi=pi, sb_=sb_: e.activation(out=Pt[:, pi, :], in_=bank(sb_, 512), func=AF.Exp, scale=0.125), reads=[B_ps[sb_], BPt[pi]], writes=[BPt[pi]])
                    P.op("dve", lambda e, pi=pi, kc=kc: e.tensor_tensor(out=Pt[:, pi, :].rearrange("p (h q) -> p h q", h=4), in0=Pt[:, pi, :].rearrange("p (h q) -> p h q", h=4),
                                                                        in1=maskT[:, kc:kc + 1, :].to_broadcast([128, 4, 128]), op=ALU.mult), reads=[BPt[pi], BmaskT], writes=[BPt[pi]])
                    for hq in range(4):
                        P.op("pe", lambda e, pi=pi, hq=hq, kc=kc, kvh=kvh, nk=nk: e.matmul(bank(4 + hq, 65), Pt[:, pi, hq * 128:(hq + 1) * 128], vaug[:, kc, kvh, :],
                                                                                     start=(kc == 0), stop=(kc == nk - 1)), reads=[BPt[pi], Bva[kc]], writes=[B_ps[4 + hq]])
                for hq in range(4):
                    h = kvh * 4 + hq
                    P.op("dve", lambda e, h=h, hq=hq: e.reciprocal(out=rden[:, h:h + 1], in_=bank(4 + hq, 65)[:, 64:65]), reads=[B_ps[4 + hq], Brden], writes=[Brden])
                    P.op("dve", lambda e, h=h, hq=hq: e.tensor_scalar(out=o32[:, h * 64:(h + 1) * 64], in0=bank(4 + hq, 65)[:, 0:64], scalar1=rden[:, h:h + 1], scalar2=None, op0=ALU.mult),
                         reads=[B_ps[4 + hq], Brden, Bo32], writes=[Bo32])
            for c in range(KC):
                bt = nb2(0, 4)
                P.op("pe", lambda e, c=c, bt=bt: e.transpose(bank(bt, 128), o32[:, c * 128:(c + 1) * 128], ident[:, :]), reads=[Bo32, B_const], writes=[B_ps[bt]])
                P.op("act", lambda e, c=c, bt=bt: e.activation(out=oT[:, c, :], in_=bank(bt, 128), func=AF.Copy), reads=[B_ps[bt]], writes=[BoT[c]])
            for m in range(KC):
                so = 0
                P.dma("pool", lambda e, so=so, m=m: e.dma_start(out=wo_s[:, so, :, :], in_=att_w_out_d[:, m * 128:(m + 1) * 128].rearrange("(kc p) n -> p kc n", p=128)), writes=[Bwo[so]])
                b = nb2(0, 4)
                for c in range(KC):
                    P.op("pe", lambda e, so=so, c=c, b=b: e.matmul(bank(b, 128), wo_s[:, so, c, :], oT[:, c, :], start=(c == 0), stop=(c == KC - 1)),
                         reads=[Bwo[so], BoT[c]], writes=[B_ps[b]])
                P.op("dve", lambda e, m=m, b=b, tk=tk: e.scalar_tensor_tensor(out=x32[:, m, tk:tk + 128], in0=bank(b, 128), scalar=1.0 / ALPHA, in1=x32[:, m, tk:tk + 128],
                                                                             op0=ALU.mult, op1=ALU.add), reads=[B_ps[b], B_x32[m][ti]], writes=[B_x32[m][ti]])

        P.barrier()
        sb.release(mB)
        NPG = 65
        GP = 16
        selc = sb.alloc([128, NS, 128], F32)
        onesc = sb.alloc([128, 128], F32)
        negp = sb.alloc([128, 1], F32)
        pio = sb.alloc([128, 1], F32)
        ptb = sb.alloc([128, NS * 64], I32)
        idx = sb.alloc([128, NS * 64], I32)
        qs32 = sb.alloc([128, 1024], F32)
        iqs32 = sb.alloc([128, 520], F32)
        q_bc = sb.alloc([128, 1024], F32)
        iq_bc = sb.alloc([128, 520], F32)
        sc_km = sb.alloc([128, NS, NPG], F32)
        mk_km = sb.alloc([128, NS, NPG], F32)
        thr_bc = sb.alloc([128, NS], F32)
        oTs = sb.alloc([64, 16, NS], BF16)
        Bcw, Bcc, Bidx, Bqs, Biqs, Bqbc, Biqbc, Bikg, Bkg, Bvg, Bvag, Btmpc, Bshh, Bsc, Bmk, Bflat, Bm8c, Bthrc, BS, BPk, Bop, Brdc, BoTs = (
            Buf(n) for n in ("wq_c", "cconst", "idx", "qs32", "iqs32", "q_bc", "iq_bc", "ikg", "kg", "vg", "vag", "tmpc", "shh", "sc_km", "mk_km",
                             "flat", "m8c", "thr_bc", "S_km", "P_km", "o_pad", "rdc", "oTs"))
        mC = sb.mark()
        wq_c = sb.alloc([128, KC, 1024], BF16)
        wiq_c = sb.alloc([128, KC, 520], BF16)
        P.dma("pool", lambda e: e.dma_start(out=wq_c[:], in_=att_w_in_d[:, 0:1024].rearrange("(kc p) n -> p kc n", p=128)), writes=[Bcw])
        P.dma("pool", lambda e: e.dma_start(out=wiq_c[:, :, 0:512], in_=att_w_in_d[:, 1536:2048].rearrange("(kc p) n -> p kc n", p=128)), reads=[Bcw], writes=[Bcw])
        P.dma("pool", lambda e: e.dma_start(out=wiq_c[:, :, 512:520], in_=att_w_in_d[:, 2112:2120].rearrange("(kc p) n -> p kc n", p=128)), reads=[Bcw], writes=[Bcw])
        for dst, src in ((selc, ssm_sel_d), (onesc, onesf_d), (negp, negp_d), (pio, piota_d), (ptb, pt_bc_d)):
            P.dma("sp", lambda e, dst=dst, src=src: e.dma_start(out=dst[:], in_=src), reads=[Bcc], writes=[Bcc])
        P.op("dve", lambda e: e.tensor_scalar(out=idx[:], in0=ptb[:], scalar1=128.0, scalar2=pio[:, 0:1], op0=ALU.mult, op1=ALU.add), reads=[Bcc], writes=[Bidx])
        TOKS = NCOL - 128
        for half in range(2):
            b = nb2(0, 4)
            for kc in range(KC):
                P.op("pe", lambda e, kc=kc, half=half, b=b: e.matmul(bank(b, 512), xbf[:, kc, TOKS:TOKS + 128], wq_c[:, kc, half * 512:(half + 1) * 512],
                                                                  start=(kc == 0), stop=(kc == KC - 1)), reads=[Bcw] + allx(kc), writes=[B_ps[b]])
            P.op("act", lambda e, half=half, b=b: e.activation(out=qs32[:, half * 512:(half + 1) * 512], in_=bank(b, 512), func=AF.Copy), reads=[B_ps[b], Bqs], writes=[Bqs])
        b = nb2(0, 4)
        for kc in range(KC):
            P.op("pe", lambda e, kc=kc, b=b: e.matmul(bank(b, 512), xbf[:, kc, TOKS:TOKS + 128], wiq_c[:, kc, 0:512], start=(kc == 0), stop=(kc == KC - 1)),
                 reads=[Bcw] + allx(kc), writes=[B_ps[b]])
        P.op("act", lambda e, b=b: e.activation(out=iqs32[:, 0:512], in_=bank(b, 512), func=AF.Copy), reads=[B_ps[b], Biqs], writes=[Biqs])
        b = nb2(0, 4)
        for kc in range(KC):
            P.op("pe", lambda e, kc=kc, b=b: e.matmul(bank(b, 8), xbf[:, kc, TOKS:TOKS + 128], wiq_c[:, kc, 512:520], start=(kc == 0), stop=(kc == KC - 1)),
                 reads=[Bcw] + allx(kc), writes=[B_ps[b]])
        P.op("act", lambda e, b=b: e.activation(out=iqs32[:, 512:520], in_=bank(b, 8), func=AF.Copy, scale=float(8 ** -0.5 * 64 ** -0.5)), reads=[B_ps[b], Biqs], writes=[Biqs])
        rope_apply(qs32[:, :].rearrange("p (h d) -> p h d", h=16), 16, 16, 0, Bqs)
        rope_apply(iqs32[:, 0:512].rearrange("p (h d) -> p h d", h=8), 8, 16, 1, Biqs)

        def gather(dst, cache_d, b_, j0, nj, Bd, src_off, width):
            for jj in range(nj):
                j = j0 + jj
                if j < 64:
                    c = b_ * 64 + j
                    P.dma("pool", lambda e, jj=jj, c=c: e.indirect_dma_start(
                        out=dst[:, jj, :], out_offset=None, in_=cache_d[:, :], in_offset=bass.IndirectOffsetOnAxis(ap=idx[:, c:c + 1], axis=0)),
                        reads=[Bidx, Bd], writes=[Bd])
                else:
                    P.op("act", lambda e, jj=jj: e.activation(out=dst[:, jj, :], in_=snew[:, b_, src_off:src_off + width], func=AF.Copy), reads=[Bsnew, Bd], writes=[Bd])

        groups = [(0, 16), (16, 16), (32, 16), (48, 16), (64, 1)]
        P.barrier()
        sb.release(mC)
        ikg = sb.alloc([128, GP, 64], F32)
        tmpc = sb.alloc([128, GP * 64], F32)
        shh = sb.alloc([128, GP], F32)
        for b_ in range(NS):
            for bank_i, w in ((0, 512), (1, 8)):
                P.op("pe", lambda e, b_=b_, bank_i=bank_i, w=w: e.matmul(bank(bank_i, w), selc[:, b_, :], iqs32[:, bank_i * 512:bank_i * 512 + w], start=True, stop=True),
                     reads=[Bcc, Biqs], writes=[B_ps[bank_i]])
            P.op("act", lambda e: e.activation(out=iq_bc[:, 0:512], in_=bank(0, 512), func=AF.Copy), reads=[B_ps[0], Biqbc], writes=[Biqbc])
            P.op("act", lambda e: e.activation(out=iq_bc[:, 512:520], in_=bank(1, 8), func=AF.Copy), reads=[B_ps[1], Biqbc], writes=[Biqbc])
            for (j0, nj) in groups:
                gather(ikg, cache_ik_d, b_, j0, nj, Bikg, 512, 64)
                for h in range(8):
                    P.op("dve", lambda e, h=h, nj=nj: e.tensor_tensor(out=tmpc[:, 0:nj * 64].rearrange("p (j d) -> p j d", j=nj), in0=ikg[:, 0:nj, :],
                                                                  in1=iq_bc[:, h * 64:(h + 1) * 64].unsqueeze(1).to_broadcast([128, nj, 64]), op=ALU.mult),
                         reads=[Bikg, Biqbc, Btmpc], writes=[Btmpc])
                    P.op("dve", lambda e, nj=nj: e.tensor_reduce(out=shh[:, 0:nj], in_=tmpc[:, 0:nj * 64].rearrange("p (j d) -> p j d", j=nj), axis=AX.X, op=ALU.add),
                         reads=[Btmpc, Bshh], writes=[Bshh])
                    P.op("act", lambda e, nj=nj: e.activation(out=shh[:, 0:nj], in_=shh[:, 0:nj], func=AF.Relu), reads=[Bshh], writes=[Bshh])
                    if h == 0:
                        P.op("dve", lambda e, nj=nj, j0=j0, b_=b_: e.tensor_scalar(out=sc_km[:, b_, j0:j0 + nj], in0=shh[:, 0:nj], scalar1=iq_bc[:, 512:513], scalar2=None, op0=ALU.mult),
                             reads=[Bshh, Biqbc, Bsc], writes=[Bsc])
                    else:
                        P.op("dve", lambda e, nj=nj, j0=j0, b_=b_, h=h: e.scalar_tensor_tensor(out=sc_km[:, b_, j0:j0 + nj], in0=shh[:, 0:nj], scalar=iq_bc[:, 512 + h:513 + h],
                                                                                            in1=sc_km[:, b_, j0:j0 + nj], op0=ALU.mult, op1=ALU.add), reads=[Bshh, Biqbc, Bsc], writes=[Bsc])
            P.op("dve", lambda e, b_=b_: e.tensor_tensor(out=sc_km[:, b_, 64:65], in0=sc_km[:, b_, 64:65], in1=negp[:, 0:1], op=ALU.add), reads=[Bsc, Bcc], writes=[Bsc])
            P.dma("sp", lambda e, b_=b_: e.dma_start(out=scr_d[b_, :].rearrange("(p j) -> p j", j=NPG), in_=sc_km[:, b_, :]), reads=[Bsc], writes=[Bflat])
        P.barrier()
        sb.release(mC)
        flat = sb.alloc([NS, NPG * 128], F32)
        m8c = sb.alloc([NS, 8], F32)
        r4 = sb.alloc([NS, NS], F32)
        P.dma("sp", lambda e: e.dma_start(out=flat[:, :], in_=scr_d[:, :]), reads=[Bflat], writes=[Bflat])
        for r in range(TOPK // 8):
            P.op("dve", lambda e: e.max(out=m8c[:, :], in_=flat[:, :]), reads=[Bflat, Bm8c], writes=[Bm8c])
            if r < TOPK // 8 - 1:
                P.op("dve", lambda e: e.match_replace(out=flat[:, :], in_to_replace=m8c[:, :], in_values=flat[:, :], imm_value=-1e30), reads=[Bflat, Bm8c], writes=[Bflat])
        P.op("dve", lambda e: e.tensor_scalar(out=r4[:, :], in0=ident[0:NS, 0:NS], scalar1=m8c[:, 7:8], scalar2=None, op0=ALU.mult), reads=[Bm8c, B_const], writes=[Bthrc])
        P.op("pe", lambda e: e.matmul(bank(2, NS), onesc[0:NS, :], r4[:, :], start=True, stop=True), reads=[Bcc, Bthrc], writes=[B_ps[2]])
        P.op("act", lambda e: e.activation(out=thr_bc[:, :], in_=bank(2, NS), func=AF.Copy), reads=[B_ps[2], Bthrc], writes=[Bthrc])
        for b_ in range(NS):
            P.op("dve", lambda e, b_=b_: e.tensor_scalar(out=mk_km[:, b_, :], in0=sc_km[:, b_, :], scalar1=thr_bc[:, b_:b_ + 1], scalar2=None, op0=ALU.is_ge),
                 reads=[Bsc, Bthrc, Bmk], writes=[Bmk])
        P.barrier()
        sb.release(mC)
        kg = sb.alloc([128, GP, 256], F32)
        vg = sb.alloc([128, GP, 256], F32)
        vag = sb.alloc([128, GP, 4, 65], BF16)
        tmpc2 = sb.alloc([128, GP * 64], F32)
        S_km = sb.alloc([128, GP, 16], F32)
        P_km = sb.alloc([128, GP, 16], BF16)
        o_pad = sb.alloc([128, 4, 64], F32)
        rdc = sb.alloc([NS, 4], F32)
        Btmpc2 = Buf("tmpc2")
        P.op("dve", lambda e: e.memset(vag[:], 1.0), writes=[Bvag])
        P.op("dve", lambda e: e.memset(o_pad[:], 0.0), writes=[Bop])
        for b_ in range(NS):
            for half in range(2):
                P.op("pe", lambda e, b_=b_, half=half: e.matmul(bank(half, 512), selc[:, b_, :], qs32[:, half * 512:(half + 1) * 512], start=True, stop=True),
                     reads=[Bcc, Bqs], writes=[B_ps[half]])
                P.op("act", lambda e, half=half: e.activation(out=q_bc[:, half * 512:(half + 1) * 512], in_=bank(half, 512), func=AF.Copy), reads=[B_ps[half], Bqbc], writes=[Bqbc])
            for gi, (j0, nj) in enumerate(groups):
                gather(kg, cache_k_d, b_, j0, nj, Bkg, 0, 256)
                gather(vg, cache_v_d, b_, j0, nj, Bvg, 256, 256)
                P.op("act", lambda e, nj=nj: e.activation(out=vag[:, 0:nj, :, 0:64], in_=vg[:, 0:nj, :].rearrange("p j (h d) -> p j h d", h=4), func=AF.Copy),
                     reads=[Bvg, Bvag], writes=[Bvag])
                for h in range(16):
                    kvh = h // 4
                    P.op("dve", lambda e, h=h, kvh=kvh, nj=nj: e.tensor_tensor(out=tmpc2[:, 0:nj * 64].rearrange("p (j d) -> p j d", j=nj), in0=kg[:, 0:nj, kvh * 64:(kvh + 1) * 64],
                                                                           in1=q_bc[:, h * 64:(h + 1) * 64].unsqueeze(1).to_broadcast([128, nj, 64]), op=ALU.mult),
                         reads=[Bkg, Bqbc, Btmpc2], writes=[Btmpc2])
                    P.op("dve", lambda e, h=h, nj=nj: e.tensor_reduce(out=S_km[:, 0:nj, h], in_=tmpc2[:, 0:nj * 64].rearrange("p (j d) -> p j d", j=nj), axis=AX.X, op=ALU.add),
                         reads=[Btmpc2, BS], writes=[BS])
                P.op("act", lambda e, nj=nj: e.activation(out=S_km[:, 0:nj, :], in_=S_km[:, 0:nj, :], func=AF.Exp, scale=0.125), reads=[BS], writes=[BS])
                P.op("dve", lambda e, nj=nj, j0=j0, b_=b_: e.tensor_tensor(out=P_km[:, 0:nj, :], in0=S_km[:, 0:nj, :],
                                                                       in1=mk_km[:, b_, j0:j0 + nj].unsqueeze(2).to_broadcast([128, nj, 16]), op=ALU.mult),
                     reads=[BS, Bmk, BPk], writes=[BPk])
                for jj in range(nj):
                    for kvh in range(4):
                        first = (gi == 0 and jj == 0)
                        last = (gi == len(groups) - 1 and jj == nj - 1)
                        P.op("pe", lambda e, jj=jj, kvh=kvh, first=first, last=last: e.matmul(bank(4 + kvh, 65)[0:4, :], P_km[:, jj, kvh * 4:(kvh + 1) * 4], vag[:, jj, kvh, :],
                                                                                        start=first, stop=last), reads=[BPk, Bvag], writes=[B_ps[4 + kvh]])
            for kvh in range(4):
                P.op("dve", lambda e, kvh=kvh: e.reciprocal(out=rdc[:, kvh:kvh + 1], in_=bank(4 + kvh, 65)[0:4, 64:65]), reads=[B_ps[4 + kvh], Brdc], writes=[Brdc])
                P.op("dve", lambda e, kvh=kvh: e.tensor_scalar(out=o_pad[0:4, kvh, :], in0=bank(4 + kvh, 65)[0:4, 0:64], scalar1=rdc[:, kvh:kvh + 1], scalar2=None, op0=ALU.mult),
                     reads=[B_ps[4 + kvh], Brdc, Bop], writes=[Bop])
            for kvh in range(4):
                bt = nb2(0, 4)
                P.op("pe", lambda e, kvh=kvh, bt=bt: e.transpose(bank(bt, 128)[0:64, :], o_pad[:, kvh, :], ident[:, :]), reads=[Bop, B_const], writes=[B_ps[bt]])
                P.op("act", lambda e, kvh=kvh, bt=bt, b_=b_: e.activation(out=oTs[:, kvh * 4:(kvh + 1) * 4, b_], in_=bank(bt, 128)[0:64, 0:4], func=AF.Copy),
                     reads=[B_ps[bt], BoTs], writes=[BoTs])
        P.barrier()
        sb.release(mC)
        woh = sb.alloc([64, 16, D], BF16)
        Bwoh = Buf("woh")
        P.dma("pool", lambda e: e.dma_start(out=woh[:], in_=att_w_out_d[:, :].rearrange("(h p) n -> p h n", p=64)), writes=[Bwoh])
        for m in range(KC):
            b = nb2(0, 4)
            for h in range(16):
                P.op("pe", lambda e, m=m, h=h, b=b: e.matmul(bank(b, NS), woh[:, h, m * 128:(m + 1) * 128], oTs[:, h, :], start=(h == 0), stop=(h == 15)),
                     reads=[Bwoh, BoTs], writes=[B_ps[b]])
            P.op("dve", lambda e, m=m, b=b: e.scalar_tensor_tensor(out=x32[:, m, SEQ:SEQ + NS], in0=bank(b, NS), scalar=1.0 / ALPHA, in1=x32[:, m, SEQ:SEQ + NS],
                                                                 op0=ALU.mult, op1=ALU.add), reads=[B_ps[b], B_x32[m][4]], writes=[B_x32[m][4]])
        sb.release(mk)
        P.barrier()


    def rwkv_sample():
        mk = sb.mark()
        R0, R1 = 96, 128
        TOK = NCOL - 128
        muc = sb.alloc([128, 48], F32)
        gnc = sb.alloc([128, 16], F32)
        vec = sb.alloc([128, D], F32)
        Bvec = Buf("rwvec")
        selr = sb.alloc([128, NS, 128], F32)
        blk = sb.alloc([128, 128], F32)
        epsg = sb.alloc([128, 1], F32)
        xsh = sb.alloc([128, KC, NS], F32)
        xmp = sb.alloc([128, 6, KC, 128], BF16)
        tm = sb.alloc([128, 6, D], F32)
        wsl = sb.alloc([128, 1, KC, 512], BF16)
        w1s = sb.alloc([128, KC, 256], BF16)
        w2s = sb.alloc([128, 3, D], BF16)
        t1T = sb.alloc([128, 3, 128], BF16)
        fmv = sb.alloc([128, 6, KC, NS], F32)
        wk = sb.alloc([128, 3, D], F32)
        hs = sb.alloc([128, 2, 16], F32)
        bc = sb.alloc([128, 5, D], F32)
        S = sb.alloc([128, KC, 64], F32)
        tS = sb.alloc([128, KC, 64], F32)
        sa = sb.alloc([128, 2, KC], F32)
        zb = sb.alloc([128, KC, NS], BF16)
        Bk, Bxsh, Bxmp, Btm, Bw1, Bt1, Bfm, Bwk, Bhs, Bbc, BS, BtS, Bsa, Bzb = (Buf(n) for n in (
            "rwc", "xsh", "xmp", "tm", "w1s", "t1T", "fmv", "wk", "hs", "bc", "S", "tS", "sa", "zb"))
        Bwsl = [Buf("wsl") for _ in range(2)]
        stg = sb.alloc([128, 128], F32)
        Bstg = Buf("stg")
        for dst, src, rows in ((muc, rw_mu_c_d, 48), (gnc, rw_gn_c_d, 16)):
            P.op("dve", lambda e: e.memset(stg[:, :], 0.0), writes=[Bstg])
            P.dma("sp", lambda e, src=src, rows=rows: e.dma_start(out=stg[:rows, :], in_=src[:, :]), writes=[Bstg])
            P.op("pe", lambda e: e.transpose(bank(7, 128), stg[:, :], ident[:, :]), reads=[Bstg, B_const], writes=[B_ps[7]])
            P.op("dve", lambda e, dst=dst, rows=rows: e.tensor_copy(out=dst[:, 0:rows], in_=bank(7, rows)), reads=[B_ps[7], Bk], writes=[Bk])
        P.dma("sp", lambda e: e.dma_start(out=selr[:], in_=ssm_sel_d[:, :, :]), reads=[Bk], writes=[Bk])
        P.dma("sp", lambda e: e.dma_start(out=blk[:], in_=rw_blk_d[:, :]), reads=[Bk], writes=[Bk])
        P.op("dve", lambda e: e.memset(epsg[:], 64e-5), reads=[Bk], writes=[Bk])
        P.dma("pool", lambda e: e.dma_start(out=w1s[:, :, 0:64], in_=rw_w1_d[:, :].rearrange("(kc p) n -> p kc n", p=128)), writes=[Bw1])
        P.dma("pool", lambda e: e.dma_start(out=w1s[:, :, 64:128], in_=rw_a1_d[:, :].rearrange("(kc p) n -> p kc n", p=128)), reads=[Bw1], writes=[Bw1])
        P.dma("pool", lambda e: e.dma_start(out=w1s[:, :, 128:256], in_=rw_g1_d[:, :].rearrange("(kc p) n -> p kc n", p=128)), reads=[Bw1], writes=[Bw1])
        P.dma("pool", lambda e: e.dma_start(out=w2s[0:64, 0, :], in_=rw_w2_d[:, :]), reads=[Bw1], writes=[Bw1])
        P.dma("pool", lambda e: e.dma_start(out=w2s[0:64, 1, :], in_=rw_a2_d[:, :]), reads=[Bw1], writes=[Bw1])
        P.dma("pool", lambda e: e.dma_start(out=w2s[:, 2, :], in_=rw_g2_d[:, :]), reads=[Bw1], writes=[Bw1])
        P.op("dve", lambda e: e.memset(stg[:, :], 0.0), writes=[Bstg])
        for c in range(KC):
            bt = 4 + c % 4
            P.dma("sp", lambda e, c=c: e.dma_start(out=stg[0:NS, :], in_=st_shift_d[:, c * 128:(c + 1) * 128]), writes=[Bstg])
            P.op("pe", lambda e, bt=bt: e.transpose(bank(bt, 128), stg[:, :], ident[:, :]), reads=[Bstg, B_const], writes=[B_ps[bt]])
            P.op("dve", lambda e, c=c, bt=bt: e.tensor_tensor(out=xsh[:, c, :], in0=bank(bt, 128)[:, 0:NS], in1=x32[:, c, SEQ:SEQ + NS], op=ALU.subtract),
                 reads=[B_ps[bt], B_x32[c][4], Bxsh], writes=[Bxsh])
        P.op("dve", lambda e: e.memset(xmp[:], 0.0), writes=[Bxmp])
        P.op("dve", lambda e: e.memset(tm[:], 0.0), writes=[Btm])
        P.op("dve", lambda e: e.memset(wk[:], 0.0), writes=[Bwk])
        xs4 = x32[:, :, SEQ:SEQ + NS]
        for j in range(6):
            P.op("dve", lambda e, j=j: e.tensor_tensor(out=fmv[:, 4], in0=xsh[:, :, :], in1=muc[:, j * 8:(j + 1) * 8].unsqueeze(2).to_broadcast([128, KC, NS]), op=ALU.mult),
                 reads=[Bxsh, Bk, Bfm], writes=[Bfm])
            P.op("dve", lambda e, j=j: e.tensor_tensor(out=xmp[:, j, :, 124:128], in0=fmv[:, 4], in1=xs4, op=ALU.add),
                 reads=[Bfm, Bxmp] + [B_x32[c][4] for c in range(KC)], writes=[Bxmp])
        st3 = {"cnt": 0, "nw": 0}

        def nb3(lo, n):
            b = lo + st3["cnt"] % n
            st3["cnt"] += 1
            return b
        for ti_, (nm, jx) in enumerate((("r", 0), ("k", 2), ("v", 3))):
            for half in range(2):
                sl = 0
                P.dma("pool", lambda e, sl=sl, nm=nm, half=half: e.dma_start(out=wsl[:, sl, :, :], in_=rw_w_d[nm][:, half * 512:(half + 1) * 512].rearrange("(kc p) n -> p kc n", p=128)),
                      writes=[Bwsl[sl]])
                b = nb3(0, 4)
                for kc in range(KC):
                    P.op("pe", lambda e, kc=kc, jx=jx, sl=sl, b=b: e.matmul(bank(b, 512), xmp[:, jx, kc, :], wsl[:, sl, kc, :], start=(kc == 0), stop=(kc == KC - 1)),
                         reads=[Bxmp, Bwsl[sl]], writes=[B_ps[b]])
                P.op("act", lambda e, ti_=ti_, half=half, b=b: e.activation(out=tm[R0:R1, ti_, half * 512:(half + 1) * 512], in_=bank(b, 512)[R0:R1, :], func=AF.Copy),
                     reads=[B_ps[b], Btm], writes=[Btm])
        for li, (jx, c0, M, fn) in enumerate(((1, 0, 64, AF.Tanh), (4, 64, 64, AF.Copy), (5, 128, 128, AF.Sigmoid))):
            b = nb3(4, 4)
            for kc in range(KC):
                P.op("pe", lambda e, kc=kc, jx=jx, c0=c0, M=M, b=b: e.matmul(bank(b, 128)[0:M, :], w1s[:, kc, c0:c0 + M], xmp[:, jx, kc, :], start=(kc == 0), stop=(kc == KC - 1)),
                     reads=[Bw1, Bxmp], writes=[B_ps[b]])
            P.op("act", lambda e, li=li, M=M, fn=fn, b=b: e.activation(out=t1T[0:M, li, :], in_=bank(b, 128)[0:M, :], func=fn), reads=[B_ps[b], Bt1], writes=[Bt1])
            for half in range(2):
                b2 = nb3(0, 4)
                P.op("pe", lambda e, li=li, M=M, half=half, b2=b2: e.matmul(bank(b2, 512), t1T[0:M, li, :], w2s[0:M, li, half * 512:(half + 1) * 512], start=True, stop=True),
                     reads=[Bt1, Bw1], writes=[B_ps[b2]])
                P.op("act", lambda e, li=li, half=half, b2=b2: e.activation(out=tm[R0:R1, 3 + li, half * 512:(half + 1) * 512], in_=bank(b2, 512)[R0:R1, :], func=AF.Copy),
                     reads=[B_ps[b2], Btm], writes=[Btm])
        T = lambda k: tm[R0:R1, k, :]
        def V(k):
            P.dma("sp", lambda e, k=k: e.dma_start(out=vec[R0:R1, :], in_=rw_vec_bc_d[:, k, :]), reads=[Bvec], writes=[Bvec])
            return vec[R0:R1, :]
        Wk = lambda k: wk[R0:R1, k, :]
        h3 = lambda ap: ap.rearrange("p (h d) -> p h d", h=16)
        v_ = V(0)
        P.op("dve", lambda e, v_=v_: e.tensor_tensor(out=T(3), in0=T(3), in1=v_, op=ALU.add), reads=[Btm, Bvec], writes=[Btm])
        P.op("act", lambda e: e.activation(out=T(3), in_=T(3), func=AF.Exp, scale=-1.0), reads=[Btm], writes=[Btm])
        P.op("dve", lambda e: e.tensor_scalar(out=T(3), in0=T(3), scalar1=1.0, scalar2=None, op0=ALU.add), reads=[Btm], writes=[Btm])
        P.op("dve", lambda e: e.reciprocal(out=T(3), in_=T(3)), reads=[Btm], writes=[Btm])
        P.op("act", lambda e: e.activation(out=T(3), in_=T(3), func=AF.Exp, scale=-float(np.exp(-0.5))), reads=[Btm], writes=[Btm])
        v_ = V(1)
        P.op("dve", lambda e, v_=v_: e.tensor_tensor(out=T(4), in0=T(4), in1=v_, op=ALU.add), reads=[Btm, Bvec], writes=[Btm])
        P.op("act", lambda e: e.activation(out=T(4), in_=T(4), func=AF.Sigmoid), reads=[Btm], writes=[Btm])
        v_ = V(2)
        P.op("dve", lambda e, v_=v_: e.tensor_tensor(out=Wk(0), in0=T(1), in1=v_, op=ALU.mult), reads=[Btm, Bvec, Bwk], writes=[Bwk])
        P.op("dve", lambda e: e.tensor_tensor(out=Wk(1), in0=Wk(0), in1=Wk(0), op=ALU.mult), reads=[Bwk], writes=[Bwk])
        P.op("dve", lambda e: e.tensor_reduce(out=hs[R0:R1, 0, :], in_=h3(Wk(1)), axis=AX.X, op=ALU.add), reads=[Bwk, Bhs], writes=[Bhs])
        P.op("act", lambda e: e.activation(out=hs[R0:R1, 0, :], in_=hs[R0:R1, 0, :], func=AF.Sqrt), reads=[Bhs], writes=[Bhs])
        P.op("dve", lambda e: e.tensor_scalar(out=hs[R0:R1, 0, :], in0=hs[R0:R1, 0, :], scalar1=1e-12, scalar2=None, op0=ALU.max), reads=[Bhs], writes=[Bhs])
        P.op("dve", lambda e: e.reciprocal(out=hs[R0:R1, 0, :], in_=hs[R0:R1, 0, :]), reads=[Bhs], writes=[Bhs])
        P.op("dve", lambda e: e.tensor_tensor(out=h3(Wk(0)), in0=h3(Wk(0)), in1=hs[R0:R1, 0, :].unsqueeze(2).to_broadcast([32, 16, 64]), op=ALU.mult), reads=[Bwk, Bhs], writes=[Bwk])
        P.op("dve", lambda e: e.tensor_tensor(out=Wk(1), in0=Wk(0), in1=T(4), op=ALU.mult), reads=[Bwk, Btm], writes=[Bwk])
        v_ = V(3)
        P.op("dve", lambda e, v_=v_: e.scalar_tensor_tensor(out=Wk(2), in0=T(4), scalar=-1.0, in1=v_, op0=ALU.add, op1=ALU.mult), reads=[Btm, Bvec, Bwk], writes=[Bwk])
        P.op("dve", lambda e: e.scalar_tensor_tensor(out=Wk(2), in0=Wk(2), scalar=1.0, in1=T(1), op0=ALU.add, op1=ALU.mult), reads=[Bwk, Btm], writes=[Bwk])
        P.op("dve", lambda e: e.tensor_tensor(out=T(4), in0=T(0), in1=Wk(2), op=ALU.mult), reads=[Btm, Bwk], writes=[Btm])
        v_ = V(4)
        P.op("dve", lambda e, v_=v_: e.tensor_tensor(out=T(4), in0=T(4), in1=v_, op=ALU.mult), reads=[Btm, Bvec], writes=[Btm])
        P.op("dve", lambda e: e.tensor_reduce(out=hs[R0:R1, 1, :], in_=h3(T(4)), axis=AX.X, op=ALU.add), reads=[Btm, Bhs], writes=[Bhs])
        P.op("dve", lambda e: e.tensor_tensor(out=h3(T(4)), in0=h3(T(2)), in1=hs[R0:R1, 1, :].unsqueeze(2).to_broadcast([32, 16, 64]), op=ALU.mult), reads=[Btm, Bhs], writes=[Btm])
        def tm_to_fm(k_src, k_dst, src_tile, Bsrc):
            for c in range(KC):
                bt = nb3(4, 4)
                P.op("pe", lambda e, c=c, bt=bt: e.transpose(bank(bt, 128), src_tile[:, k_src, c * 128:(c + 1) * 128], ident[:, :]), reads=[Bsrc, B_const], writes=[B_ps[bt]])
                P.op("act", lambda e, c=c, bt=bt: e.activation(out=fmv[:, k_dst, c, :], in_=bank(bt, 128)[:, 124:128], func=AF.Copy), reads=[B_ps[bt], Bfm], writes=[Bfm])
        tm_to_fm(2, 0, tm, Btm)
        tm_to_fm(4, 2, tm, Btm)
        tm_to_fm(5, 3, tm, Btm)
        srcs = ((wk, 0, Bwk), (tm, 3, Btm), (wk, 1, Bwk), (wk, 2, Bwk), (tm, 0, Btm))
        for b_ in range(NS):
            for q, (tile_, k_, Bsrc) in enumerate(srcs):
                for half in range(2):
                    bb_ = nb3(0, 4)
                    P.op("pe", lambda e, b_=b_, tile_=tile_, k_=k_, half=half, bb_=bb_: e.matmul(bank(bb_, 512), selr[:, b_, :], tile_[:, k_, half * 512:(half + 1) * 512], start=True, stop=True),
                         reads=[Bk, Bsrc], writes=[B_ps[bb_]])
                    P.op("act", lambda e, q=q, half=half, bb_=bb_: e.activation(out=bc[:, q, half * 512:(half + 1) * 512], in_=bank(bb_, 512), func=AF.Copy), reads=[B_ps[bb_], Bbc], writes=[Bbc])
            P.dma("sp", lambda e, b_=b_: e.dma_start(out=S[:], in_=st_wkv_d[b_].rearrange("(q hp) i j -> (hp i) q j", hp=2)), reads=[BS], writes=[BS])

            def rowv(q, hp):
                return bc[hp * 64:(hp + 1) * 64, q, :].rearrange("p (pr two j) -> p pr two j", two=2, j=64)[:, :, hp, :]
            Sh = lambda t_, hp: t_[hp * 64:(hp + 1) * 64, :, :]
            for hp in range(2):
                P.op("dve", lambda e, hp=hp: e.tensor_tensor(out=Sh(tS, hp), in0=Sh(S, hp), in1=rowv(0, hp), op=ALU.mult), reads=[BS, Bbc, BtS], writes=[BtS])
            P.op("dve", lambda e: e.tensor_reduce(out=sa[:, 0, :], in_=tS[:, :, :], axis=AX.X, op=ALU.add), reads=[BtS, Bsa], writes=[Bsa])
            for hp in range(2):
                P.op("dve", lambda e, hp=hp: e.tensor_tensor(out=Sh(S, hp), in0=Sh(S, hp), in1=rowv(1, hp), op=ALU.mult), reads=[BS, Bbc], writes=[BS])
                P.op("dve", lambda e, hp=hp: e.tensor_tensor(out=Sh(tS, hp), in0=rowv(2, hp), in1=sa[hp * 64:(hp + 1) * 64, 0, :].unsqueeze(2).to_broadcast([64, KC, 64]), op=ALU.mult),
                     reads=[Bbc, Bsa, BtS], writes=[BtS])
            P.op("dve", lambda e: e.tensor_tensor(out=S[:], in0=S[:], in1=tS[:], op=ALU.subtract), reads=[BS, BtS], writes=[BS])
            for hp in range(2):
                P.op("dve", lambda e, hp=hp, b_=b_: e.tensor_tensor(out=Sh(tS, hp), in0=rowv(3, hp), in1=fmv[hp * 64:(hp + 1) * 64, 0, :, b_:b_ + 1].to_broadcast([64, KC, 64]), op=ALU.mult),
                     reads=[Bbc, Bfm, BtS], writes=[BtS])
            P.op("dve", lambda e: e.tensor_tensor(out=S[:], in0=S[:], in1=tS[:], op=ALU.add), reads=[BS, BtS], writes=[BS])
            P.dma("sp", lambda e, b_=b_: e.dma_start(out=wkv_s_d[b_].rearrange("(q hp) i j -> (hp i) q j", hp=2), in_=S[:]), reads=[BS])
            for hp in range(2):
                P.op("dve", lambda e, hp=hp: e.tensor_tensor(out=Sh(tS, hp), in0=Sh(S, hp), in1=rowv(4, hp), op=ALU.mult), reads=[BS, Bbc, BtS], writes=[BtS])
            P.op("dve", lambda e, b_=b_: e.tensor_reduce(out=fmv[:, 1, :, b_], in_=tS[:, :, :], axis=AX.X, op=ALU.add), reads=[BtS, Bfm], writes=[Bfm])
        yv = fmv[:, 1].rearrange("p c b -> p (c b)")
        t4 = fmv[:, 4].rearrange("p c b -> p (c b)")
        t5 = fmv[:, 5].rearrange("p c b -> p (c b)")
        P.op("dve", lambda e: e.tensor_tensor(out=t4, in0=yv, in1=yv, op=ALU.mult), reads=[Bfm], writes=[Bfm])
        P.op("pe", lambda e: e.matmul(bank(0, 32), blk[:, :], yv, start=True, stop=True), reads=[Bk, Bfm], writes=[B_ps[0]])
        P.op("pe", lambda e: e.matmul(bank(1, 32), blk[:, :], t4, start=True, stop=True), reads=[Bk, Bfm], writes=[B_ps[1]])
        P.op("act", lambda e: e.activation(out=t5, in_=bank(0, 32), func=AF.Copy), reads=[B_ps[0], Bfm], writes=[Bfm])
        P.op("dve", lambda e: e.tensor_tensor(out=t4, in0=t5, in1=t5, op=ALU.mult), reads=[Bfm], writes=[Bfm])
        P.op("dve", lambda e: e.tensor_tensor(out=t4, in0=bank(1, 32), in1=t4, op=ALU.subtract), reads=[B_ps[1], Bfm], writes=[Bfm])
        P.op("act", lambda e: e.activation(out=t4, in_=t4, func=AF.Sqrt, bias=epsg[:, 0:1], scale=1.0), reads=[Bfm, Bk], writes=[Bfm])
        P.op("dve", lambda e: e.reciprocal(out=t4, in_=t4), reads=[Bfm], writes=[Bfm])
        P.op("dve", lambda e: e.tensor_tensor(out=yv, in0=yv, in1=t5, op=ALU.subtract), reads=[Bfm], writes=[Bfm])
        P.op("dve", lambda e: e.tensor_tensor(out=yv, in0=yv, in1=t4, op=ALU.mult), reads=[Bfm], writes=[Bfm])
        P.op("dve", lambda e: e.tensor_tensor(out=fmv[:, 1], in0=fmv[:, 1], in1=gnc[:, 0:8].unsqueeze(2).to_broadcast([128, KC, NS]), op=ALU.mult), reads=[Bfm, Bk], writes=[Bfm])
        P.op("dve", lambda e: e.tensor_tensor(out=fmv[:, 1], in0=fmv[:, 1], in1=gnc[:, 8:16].unsqueeze(2).to_broadcast([128, KC, NS]), op=ALU.add), reads=[Bfm, Bk], writes=[Bfm])
        P.op("dve", lambda e: e.tensor_tensor(out=fmv[:, 1], in0=fmv[:, 1], in1=fmv[:, 2], op=ALU.add), reads=[Bfm], writes=[Bfm])
        P.op("dve", lambda e: e.tensor_tensor(out=zb[:], in0=fmv[:, 1], in1=fmv[:, 3], op=ALU.mult), reads=[Bfm, Bzb], writes=[Bzb])
        for half in range(2):
            sl = 0
            P.dma("pool", lambda e, sl=sl, half=half: e.dma_start(out=wsl[:, sl, :, :], in_=rw_w_d["o"][:, half * 512:(half + 1) * 512].rearrange("(kc p) n -> p kc n", p=128)),
                  writes=[Bwsl[sl]])
            for mm in range(4):
                m = half * 4 + mm
                b = nb3(0, 4)
                for kc in range(KC):
                    P.op("pe", lambda e, kc=kc, sl=sl, mm=mm, b=b: e.matmul(bank(b, NS), wsl[:, sl, kc, mm * 128:(mm + 1) * 128], zb[:, kc, :], start=(kc == 0), stop=(kc == KC - 1)),
                         reads=[Bwsl[sl], Bzb], writes=[B_ps[b]])
                P.op("dve", lambda e, m=m, b=b: e.scalar_tensor_tensor(out=x32[:, m, SEQ:SEQ + NS], in0=bank(b, NS), scalar=1.0 / ALPHA, in1=x32[:, m, SEQ:SEQ + NS],
                                                                     op0=ALU.mult, op1=ALU.add), reads=[B_ps[b], B_x32[m][4]], writes=[B_x32[m][4]])
        sb.release(mk)
        P.barrier()


    def rwkv_prompt():
        mk = sb.mark()
        TS = 16
        muc = sb.alloc([128, 48], F32)
        vec = sb.alloc([128, D], F32)
        mbd = sb.alloc([16, D], F32)
        epsg = sb.alloc([128, 1], F32)
        xlast = sb.alloc([128, KC, 1], F32)
        ST = sb.alloc([64, D], F32)
        STb = sb.alloc([64, D], BF16)
        tm = sb.alloc([128, 6, D], F32)
        w1s = sb.alloc([128, KC, 256], BF16)
        w2s = sb.alloc([128, 3, D], BF16)
        Bk, Bvec, Bxl, BST, BSTb, Btm, Bw1 = (Buf(n) for n in ("rwpc", "rwpvec", "xlast", "ST", "STb", "tmp_", "w1s"))
        Bscr = [Buf("scr_h") for _ in range(3)]
        Bscy = Buf("scr_y")
        stg = sb.alloc([128, 128], F32)
        Bstg = Buf("stg")
        P.op("dve", lambda e: e.memset(stg[:, :], 0.0), writes=[Bstg])
        P.dma("sp", lambda e: e.dma_start(out=stg[:48, :], in_=rw_mu_c_d[:, :]), writes=[Bstg])
        P.op("pe", lambda e: e.transpose(bank(7, 128), stg[:, :], ident[:, :]), reads=[Bstg, B_const], writes=[B_ps[7]])
        P.op("dve", lambda e: e.tensor_copy(out=muc[:, 0:48], in_=bank(7, 48)), reads=[B_ps[7], Bk], writes=[Bk])
        P.dma("sp", lambda e: e.dma_start(out=mbd[:], in_=rw_mbd_d[:, :]), reads=[Bk], writes=[Bk])
        P.op("dve", lambda e: e.memset(epsg[:], 64e-5), reads=[Bk], writes=[Bk])
        P.op("dve", lambda e: e.memset(xlast[:], 0.0), writes=[Bxl])
        P.op("dve", lambda e: e.memset(ST[:], 0.0), writes=[BST])
        P.op("dve", lambda e: e.memset(STb[:], 0.0), writes=[BSTb])
        P.dma("pool", lambda e: e.dma_start(out=w1s[:, :, 0:64], in_=rw_w1_d[:, :].rearrange("(kc p) n -> p kc n", p=128)), writes=[Bw1])
        P.dma("pool", lambda e: e.dma_start(out=w1s[:, :, 64:128], in_=rw_a1_d[:, :].rearrange("(kc p) n -> p kc n", p=128)), reads=[Bw1], writes=[Bw1])
        P.dma("pool", lambda e: e.dma_start(out=w1s[:, :, 128:256], in_=rw_g1_d[:, :].rearrange("(kc p) n -> p kc n", p=128)), reads=[Bw1], writes=[Bw1])
        P.dma("pool", lambda e: e.dma_start(out=w2s[0:64, 0, :], in_=rw_w2_d[:, :]), reads=[Bw1], writes=[Bw1])
        P.dma("pool", lambda e: e.dma_start(out=w2s[0:64, 1, :], in_=rw_a2_d[:, :]), reads=[Bw1], writes=[Bw1])
        P.dma("pool", lambda e: e.dma_start(out=w2s[:, 2, :], in_=rw_g2_d[:, :]), reads=[Bw1], writes=[Bw1])
        st3 = {"cnt": 0}

        def nb3(lo, n):
            b = lo + st3["cnt"] % n
            st3["cnt"] += 1
            return b

        def V(k):
            P.dma("sp", lambda e, k=k: e.dma_start(out=vec[:, :], in_=rw_vec128_d[:, k, :]), reads=[Bvec], writes=[Bvec])
            return vec[:, :]
        T = lambda k: tm[:, k, :]
        h3 = lambda ap: ap.rearrange("p (h d) -> p h d", h=16)
        mwork = sb.mark()

        import os
        NCHK = int(os.environ.get("KRWP_CH", SEQ // 128))
        NSTEP = int(os.environ.get("KRWP_STEPS", TS))
        for ch in range(NCHK):
            tk = ch * 128
            ti = tk // 512
            mA_ = sb.mark()
            dx = sb.alloc([128, KC, 128], F32)
            xmp = sb.alloc([128, 6, KC, 128], BF16)
            wsl = sb.alloc([128, KC, 512], BF16)
            t1T = sb.alloc([128, 3, 128], BF16)
            wk = sb.alloc([128, 3, D], F32)
            tmpx = wk[:, 0, :].rearrange("p (c t) -> p c t", c=KC)
            hs = sb.alloc([128, 2, 16], F32)
            wkb = sb.alloc([128, 3, D], BF16)
            Bdx, Bxmp, Bwsl, Bt1, Bwk, Bhs, Bwkb = (Buf(n) for n in ("dx", "xmp", "wsl", "t1T", "wk", "hs", "wkb"))
            Btx = Bwk
            Wk = lambda k, wk=wk: wk[:, k, :]
            xin = [B_x32[c][ti] for c in range(KC)]
            P.op("dve", lambda e, tk=tk, dx=dx: e.tensor_tensor(out=dx[:, :, 0:1], in0=xlast[:, :, :], in1=x32[:, :, tk:tk + 1], op=ALU.subtract), reads=[Bxl] + xin, writes=[Bdx])
            P.op("dve", lambda e, tk=tk, dx=dx: e.tensor_tensor(out=dx[:, :, 1:128], in0=x32[:, :, tk:tk + 127], in1=x32[:, :, tk + 1:tk + 128], op=ALU.subtract), reads=xin + [Bdx], writes=[Bdx])
            P.op("dve", lambda e, tk=tk: e.tensor_copy(out=xlast[:, :, :], in_=x32[:, :, tk + 127:tk + 128]), reads=xin + [Bdx], writes=[Bxl])
            for j in range(6):
                P.op("dve", lambda e, j=j, dx=dx, tmpx=tmpx: e.tensor_tensor(out=tmpx[:], in0=dx[:], in1=muc[:, j * 8:(j + 1) * 8].unsqueeze(2).to_broadcast([128, KC, 128]), op=ALU.mult),
                     reads=[Bdx, Bk, Btx], writes=[Btx])
                P.op("dve", lambda e, j=j, tk=tk, tmpx=tmpx, xmp=xmp: e.tensor_tensor(out=xmp[:, j], in0=tmpx[:], in1=x32[:, :, tk:tk + 128], op=ALU.add), reads=[Btx, Bxmp] + xin, writes=[Bxmp])
            for ti_, (nm, jx) in enumerate((("r", 0), ("k", 2), ("v", 3))):
                for half in range(2):
                    P.dma("pool", lambda e, nm=nm, half=half, wsl=wsl: e.dma_start(out=wsl[:], in_=rw_w_d[nm][:, half * 512:(half + 1) * 512].rearrange("(kc p) n -> p kc n", p=128)),
                          writes=[Bwsl])
                    b = nb3(0, 4)
                    for kc in range(KC):
                        P.op("pe", lambda e, kc=kc, jx=jx, b=b, xmp=xmp, wsl=wsl: e.matmul(bank(b, 512), xmp[:, jx, kc, :], wsl[:, kc, :], start=(kc == 0), stop=(kc == KC - 1)),
                             reads=[Bxmp, Bwsl], writes=[B_ps[b]])
                    P.op("act", lambda e, ti_=ti_, half=half, b=b: e.activation(out=tm[:, ti_, half * 512:(half + 1) * 512], in_=bank(b, 512), func=AF.Copy), reads=[B_ps[b], Btm], writes=[Btm])
            for li, (jx, c0, M, fn) in enumerate(((1, 0, 64, AF.Tanh), (4, 64, 64, AF.Copy), (5, 128, 128, AF.Sigmoid))):
                b = nb3(4, 4)
                for kc in range(KC):
                    P.op("pe", lambda e, kc=kc, jx=jx, c0=c0, M=M, b=b, xmp=xmp: e.matmul(bank(b, 128)[0:M, :], w1s[:, kc, c0:c0 + M], xmp[:, jx, kc, :], start=(kc == 0), stop=(kc == KC - 1)),
                         reads=[Bw1, Bxmp], writes=[B_ps[b]])
                P.op("act", lambda e, li=li, M=M, fn=fn, b=b, t1T=t1T: e.activation(out=t1T[0:M, li, :], in_=bank(b, 128)[0:M, :], func=fn), reads=[B_ps[b], Bt1], writes=[Bt1])
                for half in range(2):
                    b2 = nb3(0, 4)
                    P.op("pe", lambda e, li=li, M=M, half=half, b2=b2, t1T=t1T: e.matmul(bank(b2, 512), t1T[0:M, li, :], w2s[0:M, li, half * 512:(half + 1) * 512], start=True, stop=True),
                         reads=[Bt1, Bw1], writes=[B_ps[b2]])
                    P.op("act", lambda e, li=li, half=half, b2=b2: e.activation(out=tm[:, 3 + li, half * 512:(half + 1) * 512], in_=bank(b2, 512), func=AF.Copy), reads=[B_ps[b2], Btm], writes=[Btm])
            v_ = V(0)
            P.op("dve", lambda e, v_=v_: e.tensor_tensor(out=T(3), in0=T(3), in1=v_, op=ALU.add), reads=[Btm, Bvec], writes=[Btm])
            P.op("act", lambda e: e.activation(out=T(3), in_=T(3), func=AF.Exp, scale=-1.0), reads=[Btm], writes=[Btm])
            P.op("dve", lambda e: e.tensor_scalar(out=T(3), in0=T(3), scalar1=1.0, scalar2=None, op0=ALU.add), reads=[Btm], writes=[Btm])
            P.op("dve", lambda e: e.reciprocal(out=T(3), in_=T(3)), reads=[Btm], writes=[Btm])
            P.op("act", lambda e: e.activation(out=T(3), in_=T(3), func=AF.Exp, scale=-float(np.exp(-0.5))), reads=[Btm], writes=[Btm])
            v_ = V(1)
            P.op("dve", lambda e, v_=v_: e.tensor_tensor(out=T(4), in0=T(4), in1=v_, op=ALU.add), reads=[Btm, Bvec], writes=[Btm])
            P.op("act", lambda e: e.activation(out=T(4), in_=T(4), func=AF.Sigmoid), reads=[Btm], writes=[Btm])
            v_ = V(2)
            P.op("dve", lambda e, v_=v_, Wk=Wk: e.tensor_tensor(out=Wk(0), in0=T(1), in1=v_, op=ALU.mult), reads=[Btm, Bvec, Bwk], writes=[Bwk])
            P.op("dve", lambda e, Wk=Wk: e.tensor_tensor(out=Wk(1), in0=Wk(0), in1=Wk(0), op=ALU.mult), reads=[Bwk], writes=[Bwk])
            P.op("dve", lambda e, Wk=Wk, hs=hs: e.tensor_reduce(out=hs[:, 0, :], in_=h3(Wk(1)), axis=AX.X, op=ALU.add), reads=[Bwk, Bhs], writes=[Bhs])
            P.op("act", lambda e, hs=hs: e.activation(out=hs[:, 0, :], in_=hs[:, 0, :], func=AF.Sqrt), reads=[Bhs], writes=[Bhs])
            P.op("dve", lambda e, hs=hs: e.tensor_scalar(out=hs[:, 0, :], in0=hs[:, 0, :], scalar1=1e-12, scalar2=None, op0=ALU.max), reads=[Bhs], writes=[Bhs])
            P.op("dve", lambda e, hs=hs: e.reciprocal(out=hs[:, 0, :], in_=hs[:, 0, :]), reads=[Bhs], writes=[Bhs])
            P.op("dve", lambda e, Wk=Wk, hs=hs: e.tensor_tensor(out=h3(Wk(0)), in0=h3(Wk(0)), in1=hs[:, 0, :].unsqueeze(2).to_broadcast([128, 16, 64]), op=ALU.mult), reads=[Bwk, Bhs], writes=[Bwk])
            P.op("dve", lambda e, Wk=Wk: e.tensor_tensor(out=Wk(1), in0=Wk(0), in1=T(4), op=ALU.mult), reads=[Bwk, Btm], writes=[Bwk])
            v_ = V(3)
            P.op("dve", lambda e, v_=v_, Wk=Wk: e.scalar_tensor_tensor(out=Wk(2), in0=T(4), scalar=-1.0, in1=v_, op0=ALU.add, op1=ALU.mult), reads=[Btm, Bvec, Bwk], writes=[Bwk])
            P.op("dve", lambda e, Wk=Wk: e.scalar_tensor_tensor(out=Wk(2), in0=Wk(2), scalar=1.0, in1=T(1), op0=ALU.add, op1=ALU.mult), reads=[Bwk, Btm], writes=[Bwk])
            P.op("act", lambda e, Wk=Wk, wkb=wkb: e.activation(out=wkb[:, 0, :], in_=Wk(2), func=AF.Copy), reads=[Bwk, Bwkb], writes=[Bwkb])
            P.op("act", lambda e, Wk=Wk, wkb=wkb: e.activation(out=wkb[:, 1, :], in_=Wk(1), func=AF.Copy, scale=-1.0), reads=[Bwk, Bwkb], writes=[Bwkb])
            P.op("act", lambda e, wkb=wkb: e.activation(out=wkb[:, 2, :], in_=T(2), func=AF.Copy), reads=[Btm, Bwkb], writes=[Bwkb])
            for q in range(3):
                P.dma("sp", lambda e, q=q, tk=tk, wkb=wkb: e.dma_start(out=scr_h_d[q, tk:tk + 128, :], in_=wkb[:, q, :]), reads=[Bwkb, Bscr[q]], writes=[Bscr[q]])
            P.op("dve", lambda e, Wk=Wk: e.tensor_tensor(out=T(4), in0=T(0), in1=Wk(2), op=ALU.mult), reads=[Btm, Bwk], writes=[Btm])
            v_ = V(4)
            P.op("dve", lambda e, v_=v_: e.tensor_tensor(out=T(4), in0=T(4), in1=v_, op=ALU.mult), reads=[Btm, Bvec], writes=[Btm])
            P.op("dve", lambda e, hs=hs: e.tensor_reduce(out=hs[:, 1, :], in_=h3(T(4)), axis=AX.X, op=ALU.add), reads=[Btm, Bhs], writes=[Bhs])
            P.op("dve", lambda e, hs=hs: e.tensor_tensor(out=h3(T(4)), in0=h3(T(2)), in1=hs[:, 1, :].unsqueeze(2).to_broadcast([128, 16, 64]), op=ALU.mult), reads=[Btm, Bhs], writes=[Btm])
            m_hm = sb.mark()
            kkH = sb.alloc([64, 128, 16], BF16)
            rH = sb.alloc([64, 128, 16], BF16)
            dH = sb.alloc([64, 128, 16], F32)
            BkkH, BrH, BdH, ByFM = Buf("kkH"), Buf("rH"), Buf("dH"), Buf("yFM")
            for (src, dst, Bs, Bd) in ((wk[:, 0, :], kkH, Bwk, BkkH), (tm[:, 0, :], rH, Btm, BrH), (tm[:, 3, :], dH, Btm, BdH)):
                for h in range(16):
                    bt = nb3(4, 4)
                    P.op("pe", lambda e, src=src, h=h, bt=bt: e.transpose(bank(bt, 128)[0:64, :], src[:, h * 64:(h + 1) * 64], ident[:, :]), reads=[Bs, B_const], writes=[B_ps[bt]])
                    P.op("act", lambda e, dst=dst, h=h, bt=bt: e.activation(out=dst[:, :, h], in_=bank(bt, 128)[0:64, :], func=AF.Copy), reads=[B_ps[bt], Bd], writes=[Bd])
            P.barrier()
            mtop = sb.mark()
            sb.release(mA_)
            yFM = sb.alloc([128, KC, 128], F32)
            Hop = sb.alloc([16, 2, 3, TS, 64], BF16)
            SAb = sb.alloc([16, D], BF16)
            VBb = sb.alloc([16, D], BF16)
            STd = sb.alloc([64, D], F32)
            assert sb.mark() <= m_hm, "scan scratch overlaps head-major operands"
            sb.release(mtop)
            BHop = [Buf("Hop") for _ in range(2)]
            BSAb, BVBb, BSTd = Buf("SAb"), Buf("VBb"), Buf("STd")
            for sbk in range(128 // TS):
                sl = sbk % 2
                t0 = sbk * TS
                for q in range(3):
                    P.dma("sp", lambda e, q=q, sl=sl, tk=tk, t0=t0, Hop=Hop: e.dma_start(out=Hop[:, sl, q, :, :], in_=scr_h_d[q, tk + t0:tk + t0 + TS, :].rearrange("t (h j) -> h t j", h=16)),
                          reads=[Bscr[q], BHop[sl]], writes=[BHop[sl]])
                for tt in range(NSTEP):
                    t = t0 + tt
                    for half in range(2):
                        P.op("pe", lambda e, t=t, half=half, kkH=kkH: e.matmul(bank(half, 512)[0:16, :], kkH[:, t, :], STb[:, half * 512:(half + 1) * 512], start=True, stop=True),
                             reads=[BkkH, BSTb], writes=[B_ps[half]])
                    P.op("dve", lambda e, SAb=SAb: e.tensor_tensor(out=SAb[:, :], in0=psum[0:16, 0:D], in1=mbd[:, :], op=ALU.mult), reads=[B_ps[0], B_ps[1], Bk, BSAb], writes=[BSAb])
                    P.op("pool", lambda e, sl=sl, tt=tt, VBb=VBb, Hop=Hop: e.tensor_tensor(out=VBb[:, :].rearrange("p (h i) -> p h i", h=16), in0=mbd[:, :].rearrange("p (h i) -> p h i", h=16),
                                                                                in1=Hop[:, sl, 2, tt:tt + 1, :].to_broadcast([16, 16, 64]), op=ALU.mult), reads=[BHop[sl], Bk, BVBb], writes=[BVBb])
                    for half in range(2):
                        P.op("pe", lambda e, sl=sl, tt=tt, half=half, Hop=Hop, SAb=SAb: e.matmul(bank(2 + half, 512)[0:64, :], Hop[:, sl, 1, tt, :], SAb[:, half * 512:(half + 1) * 512], start=True, stop=False),
                             reads=[BHop[sl], BSAb], writes=[B_ps[2 + half]])
                        P.op("pe", lambda e, sl=sl, tt=tt, half=half, Hop=Hop, VBb=VBb: e.matmul(bank(2 + half, 512)[0:64, :], Hop[:, sl, 0, tt, :], VBb[:, half * 512:(half + 1) * 512], start=False, stop=True),
                             reads=[BHop[sl], BVBb], writes=[B_ps[2 + half]])
                    P.op("pool", lambda e, t=t, dH=dH, STd=STd: e.tensor_tensor(out=STd[:, :].rearrange("p (h i) -> p h i", h=16), in0=ST[:, :].rearrange("p (h i) -> p h i", h=16),
                                                                             in1=dH[:, t, :].unsqueeze(2).to_broadcast([64, 16, 64]), op=ALU.mult), reads=[BST, BdH, BSTd], writes=[BSTd])
                    P.op("dve", lambda e, STd=STd: e.tensor_tensor(out=STb[:, :], in0=STd[:, :], in1=psum[0:64, 2 * 512:2 * 512 + D], op=ALU.add), reads=[BSTd, B_ps[2], B_ps[3], BSTb], writes=[BSTb])
                    P.op("dve", lambda e, STd=STd: e.tensor_tensor(out=ST[:, :], in0=STd[:, :], in1=psum[0:64, 2 * 512:2 * 512 + D], op=ALU.add), reads=[BSTd, B_ps[2], B_ps[3], BST], writes=[BST])
                    yb = 4 + t % 2
                    for c in range(KC):
                        P.op("pe", lambda e, t=t, c=c, yb=yb, rH=rH: e.matmul(bank(yb, 16)[:, 2 * c:2 * c + 2], STb[:, c * 128:(c + 1) * 128], rH[:, t, 2 * c:2 * c + 2], start=True, stop=True),
                             reads=[BrH, BSTb], writes=[B_ps[yb]])
                    for hp in range(2):
                        P.op("dve", lambda e, t=t, hp=hp, yb=yb, yFM=yFM: e.tensor_copy(out=yFM[hp * 64:(hp + 1) * 64, :, t],
                                                                                    in_=bank(yb, 16)[hp * 64:(hp + 1) * 64, :].rearrange("p (c two) -> p c two", two=2)[:, :, hp]),
                             reads=[B_ps[yb], ByFM], writes=[ByFM])
            P.barrier()
            sb.release(mA_)
            _keep_yFM = sb.alloc([128, KC, 128], F32)
            ytm = sb.alloc([128, D], F32)
            sq = sb.alloc([128, D], F32)
            hs2 = sb.alloc([128, 2, 16], F32)
            zbf = sb.alloc([128, KC, 128], BF16)
            wsl2 = sb.alloc([128, KC, 512], BF16)
            Bytm, Bsq, Bhs2, Bwsl2 = Buf("ytm"), Buf("sq"), Buf("hs2"), Buf("wsl2")
            Bzbf = [Buf("zbf") for _ in range(KC)]
            for c in range(KC):
                bt = nb3(4, 4)
                P.op("pe", lambda e, c=c, bt=bt, yFM=yFM: e.transpose(bank(bt, 128), yFM[:, c, :], ident[:, :]), reads=[ByFM, B_const], writes=[B_ps[bt]])
                P.op("act", lambda e, c=c, bt=bt, ytm=ytm: e.activation(out=ytm[:, c * 128:(c + 1) * 128], in_=bank(bt, 128), func=AF.Copy), reads=[B_ps[bt], Bytm], writes=[Bytm])
            P.op("dve", lambda e, ytm=ytm, hs2=hs2: e.tensor_reduce(out=hs2[:, 0, :], in_=h3(ytm[:, :]), axis=AX.X, op=ALU.add), reads=[Bytm, Bhs2], writes=[Bhs2])
            P.op("dve", lambda e, hs2=hs2: e.tensor_scalar(out=hs2[:, 0, :], in0=hs2[:, 0, :], scalar1=1.0 / 64, scalar2=None, op0=ALU.mult), reads=[Bhs2], writes=[Bhs2])
            P.op("dve", lambda e, ytm=ytm, hs2=hs2: e.tensor_tensor(out=h3(ytm[:, :]), in0=h3(ytm[:, :]), in1=hs2[:, 0, :].unsqueeze(2).to_broadcast([128, 16, 64]), op=ALU.subtract), reads=[Bytm, Bhs2], writes=[Bytm])
            P.op("dve", lambda e, ytm=ytm, sq=sq: e.tensor_tensor(out=sq[:, :], in0=ytm[:, :], in1=ytm[:, :], op=ALU.mult), reads=[Bytm, Bsq], writes=[Bsq])
            P.op("dve", lambda e, sq=sq, hs2=hs2: e.tensor_reduce(out=hs2[:, 1, :], in_=h3(sq[:, :]), axis=AX.X, op=ALU.add), reads=[Bsq, Bhs2], writes=[Bhs2])
            P.op("act", lambda e, hs2=hs2: e.activation(out=hs2[:, 1, :], in_=hs2[:, 1, :], func=AF.Sqrt, bias=epsg[:, 0:1], scale=1.0 / 64), reads=[Bhs2, Bk], writes=[Bhs2])
            P.op("dve", lambda e, hs2=hs2: e.reciprocal(out=hs2[:, 1, :], in_=hs2[:, 1, :]), reads=[Bhs2], writes=[Bhs2])
            P.op("dve", lambda e, ytm=ytm, hs2=hs2: e.tensor_tensor(out=h3(ytm[:, :]), in0=h3(ytm[:, :]), in1=hs2[:, 1, :].unsqueeze(2).to_broadcast([128, 16, 64]), op=ALU.mult), reads=[Bytm, Bhs2], writes=[Bytm])
            v_ = V(5)
            P.op("dve", lambda e, v_=v_, ytm=ytm: e.tensor_tensor(out=ytm[:, :], in0=ytm[:, :], in1=v_, op=ALU.mult), reads=[Bytm, Bvec], writes=[Bytm])
            v_ = V(6)
            P.op("dve", lambda e, v_=v_, ytm=ytm: e.tensor_tensor(out=ytm[:, :], in0=ytm[:, :], in1=v_, op=ALU.add), reads=[Bytm, Bvec], writes=[Bytm])
            P.op("dve", lambda e, ytm=ytm: e.tensor_tensor(out=ytm[:, :], in0=ytm[:, :], in1=T(4), op=ALU.add), reads=[Bytm, Btm], writes=[Bytm])
            P.op("dve", lambda e, ytm=ytm: e.tensor_tensor(out=ytm[:, :], in0=ytm[:, :], in1=T(5), op=ALU.mult), reads=[Bytm, Btm], writes=[Bytm])
            for c in range(KC):
                bt = nb3(4, 4)
                P.op("pe", lambda e, c=c, bt=bt, ytm=ytm: e.transpose(bank(bt, 128), ytm[:, c * 128:(c + 1) * 128], ident[:, :]), reads=[Bytm, B_const], writes=[B_ps[bt]])
                P.op("act", lambda e, c=c, bt=bt, zbf=zbf: e.activation(out=zbf[:, c, :], in_=bank(bt, 128), func=AF.Copy), reads=[B_ps[bt], Bzbf[c]], writes=[Bzbf[c]])
            for half in range(2):
                P.dma("pool", lambda e, half=half, wsl2=wsl2: e.dma_start(out=wsl2[:], in_=rw_w_d["o"][:, half * 512:(half + 1) * 512].rearrange("(kc p) n -> p kc n", p=128)), writes=[Bwsl2])
                for mm in range(4):
                    m = half * 4 + mm
                    b = nb3(0, 4)
                    for kc in range(KC):
                        P.op("pe", lambda e, kc=kc, mm=mm, b=b, wsl2=wsl2, zbf=zbf: e.matmul(bank(b, 128), wsl2[:, kc, mm * 128:(mm + 1) * 128], zbf[:, kc, :], start=(kc == 0), stop=(kc == KC - 1)),
                             reads=[Bwsl2, Bzbf[kc]], writes=[B_ps[b]])
                    P.op("dve", lambda e, m=m, b=b, tk=tk: e.scalar_tensor_tensor(out=x32[:, m, tk:tk + 128], in0=bank(b, 128), scalar=1.0 / ALPHA, in1=x32[:, m, tk:tk + 128],
                                                                                 op0=ALU.mult, op1=ALU.add), reads=[B_ps[b], B_x32[m][ti]], writes=[B_x32[m][ti]])
            P.barrier()
            sb.release(mwork)
        so = sb.alloc([64, 2, 64], F32)
        Bso = [Buf("so") for _ in range(2)]
        for h in range(16):
            bt = nb3(4, 4)
            sl = h % 2
            P.op("pe", lambda e, h=h, bt=bt: e.transpose(bank(bt, 64)[0:64, :], ST[:, h * 64:(h + 1) * 64], ident[0:64, 0:64]), reads=[BST, B_const], writes=[B_ps[bt]])
            P.op("act", lambda e, sl=sl, bt=bt: e.activation(out=so[:, sl, :], in_=bank(bt, 64)[0:64, :], func=AF.Copy), reads=[B_ps[bt], Bso[sl]], writes=[Bso[sl]])
            P.dma("sp", lambda e, h=h, sl=sl: e.dma_start(out=wkv_p_d[h], in_=so[:, sl, :]), reads=[Bso[sl]])
        sb.release(mk)
        P.barrier()

    def mixer_ln(i):
        mk = sb.mark()
        ln = alloc_ln()
        layer_norm(i * 3 + 1, ln)
        sb.release(mk)
        P.barrier()

    def store_fm_to_tm(dst_d, ntok, col0):
        mk = sb.mark()
        tout = sb.alloc([128, 2, D], F32)
        Bt = [Buf("tout0"), Buf("tout1")]
        k = 0
        for t0 in range(0, ntok, 128):
            n = min(128, ntok - t0)
            s = k % 2
            col = col0 + t0 + n - 128
            tis = sorted(set([min(col // 512, 4), min((col + 127) // 512, 4)]))
            for c in range(KC):
                b = (k * KC + c) % 8
                P.op("pe", lambda e, c=c, col=col, b=b: e.transpose(bank(b, 128), x32[:, c, col:col + 128], ident[:, :]),
                     reads=[B_x32[c][ti] for ti in tis] + [B_const], writes=[B_ps[b]])
                if c % 2 == 0:
                    P.op("dve", lambda e, c=c, s=s, b=b: e.tensor_copy(out=tout[:, s, c * 128:(c + 1) * 128], in_=bank(b, 128)),
                         reads=[B_ps[b]], writes=[Bt[s]])
                else:
                    P.op("act", lambda e, c=c, s=s, b=b: e.activation(out=tout[:, s, c * 128:(c + 1) * 128], in_=bank(b, 128), func=AF.Copy),
                         reads=[B_ps[b]], writes=[Bt[s]])
            P.dma("sp", lambda e, t0=t0, n=n, s=s: e.dma_start(out=dst_d[t0:t0 + n, :], in_=tout[128 - n:128, s, :]), reads=[Bt[s]])
            k += 1
        sb.release(mk)


    def store_last_cols(dst_d, col_end, nrows):
        mk = sb.mark()
        tout = sb.alloc([128, D], F32)
        Bt = Buf("tout_s")
        col = col_end - 128
        tis = sorted(set([min(col // 512, 4), min((col_end - 1) // 512, 4)]))
        for c in range(KC):
            b = c % 8
            P.op("pe", lambda e, c=c, col=col, b=b: e.transpose(bank(b, 128), x32[:, c, col:col + 128], ident[:, :]),
                 reads=[B_x32[c][ti] for ti in tis] + [B_const], writes=[B_ps[b]])
            P.op("act", lambda e, c=c, b=b: e.activation(out=tout[:, c * 128:(c + 1) * 128], in_=bank(b, 128), func=AF.Copy), reads=[B_ps[b], Bt], writes=[Bt])
        P.dma("sp", lambda e: e.dma_start(out=dst_d[:, :], in_=tout[128 - nrows:128, :]), reads=[Bt])
        sb.release(mk)
        P.barrier()

    if only_mixer == "rw":
        import os
        if os.environ.get("KRW", "sp") in ("s", "sp"):
            rwkv_sample()
        if os.environ.get("KRW", "sp") in ("p", "sp"):
            rwkv_prompt()
        mixer_ln(3)
    if only_mixer == "att":
        dsa()
        mixer_ln(2)
    if only_mixer == "ssm":
        if mamba_prompt() == "stop":
            P.emit()
            nc._k_in_names = in_names
            nc._k_out_names = out_names
            nc._k_dbg = dbg_names
            return nc
        mixer_ln(1)
    for i in range(DEPTH):
        if i >= stop_after:
            break
        ffn(i, 0)
        if i == 3:
            store_last_cols(sh_p_d, SEQ, 1)
            store_last_cols(sh_s_d, NCOL, NS)
        if i == 0 and "gm" in mixers:
            gmlp()
        if i == 1 and "ssm" in mixers:
            mamba_prompt()
        if i == 2 and "att" in mixers:
            dsa()
        if i == 3 and "rw" in mixers:
            rwkv_sample()
            if "rwp" in mixers:
                rwkv_prompt()
        mixer_ln(i)
        ffn(i, 1)
        ple(i)

    if dbg >= 1:
        store_fm_to_tm(yp_d, SEQ, 0)
    P.barrier()
    if dbg >= 5:
        store_fm_to_tm(ys_d, NS, SEQ)
    P.emit()
    nc._k_in_names = in_names
    nc._k_dbg = dbg_names
    nc._k_out_names = out_names
    return nc


_W_NAMES = ["ffn_w_up", "ffn_w_down", "ple_w_p", "ple_w_g", "gm_w_in", "gm_w_out"]


def make_in_maps(inp):
    f = lambda a: np.ascontiguousarray(a, dtype=np.float32)
    shared = {k: f(inp[k]) for k in _W_NAMES}
    shared["ln_g"] = f(inp["ln_g"]).reshape(DEPTH * 3 * KC, 128)
    shared["ln_b"] = f(inp["ln_b"]).reshape(DEPTH * 3 * KC, 128)
    shared["ple_b_g"] = f(inp["ple_b_g"]).reshape(DEPTH * KC, 128)
    shared["gm_lng_c"] = f(inp["gm_ln_g"]).reshape(16, 128)
    shared["gm_lnb_c"] = f(inp["gm_ln_b"]).reshape(16, 128)
    shared["gm_wT"] = f(np.transpose(inp["gm_ws"], (2, 0, 1)))
    shared["gm_bs_bc"] = f(np.broadcast_to(inp["gm_bs"][None], (128, 8, 128)))
    shared["gm_w00"] = f(np.broadcast_to(inp["gm_ws"][None, :, 0, 0], (128, 8)))
    shared["gm_bs0"] = f(np.broadcast_to(inp["gm_bs"][None, :, 0], (128, 8)))
    shared["gm_lng_bc"] = f(np.broadcast_to(inp["gm_ln_g"][None], (32, 2048)))
    shared["gm_lnb_bc"] = f(np.broadcast_to(inp["gm_ln_b"][None], (32, 2048)))
    shared["ssm_w_in"] = f(inp["ssm_w_in"])
    shared["ssm_w_out"] = f(inp["ssm_w_out"])
    shared["ssm_cw_c"] = f(inp["ssm_conv_w"]).reshape(96, 128)
    shared["ssm_cb_c"] = f(inp["ssm_conv_b"]).reshape(24, 128)
    shared["ssm_ng_c"] = f(inp["ssm_norm_g"]).reshape(16, 128)
    shared["ssm_hv_bc"] = f(np.broadcast_to(np.stack([inp["ssm_dt_bias"], inp["ssm_a_log"], inp["ssm_d"]])[None], (128, 3, 32)))
    shared["ssm_cw_bc"] = f(np.broadcast_to(inp["ssm_conv_w"][None], (32, 4, 3072)))
    shared["ssm_cb_bc"] = f(np.broadcast_to(inp["ssm_conv_b"][None], (32, 3072)))
    shared["ssm_d_c"] = f(np.repeat(inp["ssm_d"], 64)).reshape(16, 128)
    sel = np.zeros((128, NS, 128), np.float32)
    for b in range(NS):
        sel[124 + b, b, :] = 1.0
    shared["ssm_sel"] = sel
    shared["att_w_in"] = f(inp["att_w_in"])
    shared["att_w_out"] = f(inp["att_w_out"])
    shared["att_kn_bc"] = f(np.broadcast_to(np.stack([inp["att_kn_g"], inp["att_kn_b"]])[None], (128, 2, 64)))
    inv = (np.float32(500000.0) ** (-np.arange(8, dtype=np.float32) / np.float32(8))).astype(np.float32)
    pos = np.concatenate([np.arange(SEQ, dtype=np.float32), np.full((128,), 8192.0, np.float32)])
    ang = (pos[:, None] * inv[None, :]).astype(np.float32)
    cs = np.concatenate([np.cos(ang), np.sin(ang)], 1).astype(np.float32)
    shared["rope"] = f(cs.reshape(17, 128, 16).transpose(1, 0, 2))
    shared["cache_k"] = f(inp["cache_k"]).reshape(2560 * 128, 256)
    shared["cache_v"] = f(inp["cache_v"]).reshape(2560 * 128, 256)
    shared["cache_ik"] = f(inp["cache_idx_k"]).reshape(2560 * 128, 64)
    shared["piota"] = np.arange(128, dtype=np.float32).reshape(128, 1)
    shared["negp"] = np.where(np.arange(128) == 0, 0.0, -1e30).astype(np.float32).reshape(128, 1)
    shared["rw_mu_c"] = f(inp["rw_mu"]).reshape(48, 128)
    shared["rw_gn_c"] = f(np.concatenate([inp["rw_gn_g"].reshape(8, 128), inp["rw_gn_b"].reshape(8, 128)], 0))
    shared["rw_vec_bc"] = f(np.broadcast_to(np.stack([inp["rw_w0"], inp["rw_a0"], inp["rw_k_k"], inp["rw_k_a"], inp["rw_r_k"].reshape(-1)])[None], (32, 5, D)))
    shared["rw_vec128"] = f(np.broadcast_to(np.stack([inp["rw_w0"], inp["rw_a0"], inp["rw_k_k"], inp["rw_k_a"], inp["rw_r_k"].reshape(-1),
                                                      inp["rw_gn_g"], inp["rw_gn_b"]])[None], (128, 7, D)))
    shared["rw_mbd"] = np.repeat(np.eye(16, dtype=np.float32), 64, axis=1)
    blk = np.zeros((128, 128), np.float32); blk[:64, :64] = 1.0 / 64; blk[64:, 64:] = 1.0 / 64
    shared["rw_blk"] = blk
    for n in ("r", "k", "v", "o"):
        shared["rw_w_" + n] = f(inp["rw_w_" + n])
    for n in ("w1", "w2", "a1", "a2", "g1", "g2"):
        shared["rw_" + n] = f(inp["rw_" + n])
    shared["negm"] = np.where(np.tril(np.ones((128, 128), dtype=bool)), 0.0, -1e30).astype(np.float32)
    shared["m1"] = np.tril(np.ones((128, 128), dtype=np.float32), -1)
    shared["onesf"] = np.ones((128, 128), dtype=np.float32)
    shared["ident"] = np.eye(128, dtype=np.float32)
    shared["cmask"] = np.triu(np.ones((128, 128), dtype=np.float32))
    maps = []
    for c in range(NCORES):
        m = dict(shared)
        m["xp"] = f(inp["x_prompt"][c])
        m["xs"] = f(inp["x_sample"][NS * c:NS * (c + 1), 0])
        m["pp"] = f(inp["p_prompt"][:, c])
        m["psm"] = f(inp["p_sample"][:, NS * c:NS * (c + 1), 0])
        m["pt_bc"] = np.ascontiguousarray(np.broadcast_to(inp["page_table"][NS * c:NS * (c + 1)].reshape(1, NS * 64), (128, NS * 64)).astype(np.int32))
        m["st_shift"] = f(inp["state_rwkv_shift"][NS * c:NS * (c + 1)])
        m["st_wkv"] = f(inp["state_rwkv_wkv"][NS * c:NS * (c + 1)])
        m["st_conv"] = f(inp["state_ssm_conv"][NS * c:NS * (c + 1)])
        m["st_ssm"] = f(inp["state_ssm"][NS * c:NS * (c + 1)]).reshape(NS, 2048, 128)
        maps.append(m)
    return maps


def kernel(**inp):
    nc = build()
    maps = make_in_maps(inp)
    res = run_bass_kernel_spmd(nc, maps, core_ids=list(range(NCORES)))
    R = res.results
    cat = lambda k: np.concatenate([R[c][k] for c in range(NCORES)], 0)
    stk = lambda k: np.stack([R[c][k] for c in range(NCORES)], 0)
    return (stk("yp"), cat("ys")[:, None, :], cat("gmv")[:, None, :],
            stk("conv_p"), stk("ssm_p"), cat("conv_s"), cat("ssm_s"),
            stk("k_p").reshape(8, SEQ, 4, 64), stk("v_p").reshape(8, SEQ, 4, 64), stk("ik_p"),
            cat("k_s").reshape(32, 1, 4, 64), cat("v_s").reshape(32, 1, 4, 64), cat("ik_s")[:, None, :],
            cat("sh_p"), stk("wkv_p"), cat("sh_s"), cat("wkv_s"))
```

```python
import numpy as np
import concourse.bass as bass
import concourse.mybir as mybir
from concourse.bass_utils import run_bass_kernel_spmd

F32 = mybir.dt.float32
BF16 = mybir.dt.bfloat16
I32 = mybir.dt.int32
AF = mybir.ActivationFunctionType
ALU = mybir.AluOpType
AX = mybir.AxisListType

NCORES = 8
D = 1024
KC = 8
SEQ = 2048
NS = 4
NCOL = SEQ + NS
DEPTH = 4
DFF = 2816
NFF = 22
PLE = 256
ALPHA = (2 * DEPTH) ** 0.25
LN_EPS = 1e-5
TILES = [(0, 512), (512, 512), (1024, 512), (1536, 512), (2048, NS)]

SB_BASE = 16512
SB_END = 229376


class Buf:
    __slots__ = ("name", "w", "r")

    def __init__(self, name):
        self.name = name
        self.w = None
        self.r = {}


class Op:
    __slots__ = ("fn", "waits", "flag", "clock", "dma")

    def __init__(self, fn, waits, clock, dma):
        self.fn = fn
        self.waits = waits
        self.flag = False
        self.clock = clock
        self.dma = dma


ENGS = ("pe", "act", "dve", "pool", "sp")
SEM_LIMIT = 8000
N_DMA_SEMS = 10


class Prog:
    def __init__(self, nc):
        self.nc = nc
        self.ops = {e: [] for e in ENGS}
        self.clock = {e: {} for e in ENGS}
        self.dma_cnt = {}
        self.dma_rr = {q: 0 for q in ENGS}
        self.direct = {e: {} for e in ENGS}

    def _deps(self, eng, reads, writes):
        deps = set()
        for b in reads:
            if b.w is not None:
                deps.add(b.w)
        for b in writes:
            if b.w is not None and (b.w[0] != eng or eng != "pe"):
                deps.add(b.w)
            for k, s in b.r.items():
                if k != eng or eng != "pe":
                    deps.add((k, s))
        return deps

    def _add(self, eng, fn, deps, dma=None, force=False):
        clk = self.clock[eng]
        waits = []
        for (k, s) in sorted(deps, key=lambda t: (str(t[0]), t[1])):
            if clk.get(k, 0) >= s and not (force and isinstance(k, str) and self.direct[eng].get(k, 0) < s):
                continue
            if isinstance(k, str):
                self.direct[eng][k] = max(self.direct[eng].get(k, 0), s)
            waits.append((k, s))
            clk[k] = max(clk.get(k, 0), s)
            if isinstance(k, str):
                op = self.ops[k][s - 1]
                op.flag = True
                for kk, ss in op.clock.items():
                    if clk.get(kk, 0) < ss:
                        clk[kk] = ss
        seq = len(self.ops[eng]) + 1
        self.ops[eng].append(Op(fn, waits, dict(clk), dma))
        return seq

    def op(self, eng, fn, reads=(), writes=()):
        deps = self._deps(eng, reads, writes)
        seq = self._add(eng, fn, deps)
        for b in writes:
            b.w = (eng, seq)
            b.r = {}
        for b in reads:
            b.r[eng] = seq
        return seq

    def dma(self, q, fn, reads=(), writes=()):
        deps = self._deps(("dma", q, -1), reads, writes)
        j = self.dma_rr[q]
        self.dma_rr[q] = (j + 1) % N_DMA_SEMS
        key = ("dma", q, j)
        c = self.dma_cnt.get(key, 0) + 1
        self.dma_cnt[key] = c
        if c > 1:
            deps.add((key, c - 1))
        self._add(q, fn, deps, dma=(key, c))
        for b in writes:
            b.w = (key, c)
            b.r = {}
        for b in reads:
            b.r[key] = c

    def barrier(self):
        deps = set()
        for e in ENGS:
            for idx in range(len(self.ops[e]), 0, -1):
                o = self.ops[e][idx - 1]
                if o.fn is not None and o.dma is None:
                    deps.add((e, idx))
                    break
        for key, c in self.dma_cnt.items():
            deps.add((key, c))
        for e in ENGS:
            self._add(e, None, set(d for d in deps if d[0] != e), force=True)

    def emit(self):
        nc = self.nc
        import contextlib
        with contextlib.ExitStack() as st:
            sems = {}
            pref = {}
            for e in ENGS:
                cnt = 0
                p = []
                for o in self.ops[e]:
                    if o.flag:
                        cnt += 1
                    p.append(cnt)
                pref[e] = p
                nsem = max(1, (cnt + SEM_LIMIT - 1) // SEM_LIMIT)
                sems[e] = [st.enter_context(nc.semaphore(f"s_{e}_{i}")) for i in range(nsem)]
            dsem = {}
            for key in self.dma_cnt:
                dsem[key] = st.enter_context(nc.semaphore(f"d_{key[1]}_{key[2]}"))

            def resolve(k, s):
                if isinstance(k, str):
                    c = pref[k][s - 1]
                    return sems[k][(c - 1) // SEM_LIMIT], (c - 1) % SEM_LIMIT + 1
                return dsem[k], 16 * s

            block = st.enter_context(nc.Block())

            def run(e, name):
                p = pref[name]
                for i, o in enumerate(self.ops[name]):
                    for (k, s) in o.waits:
                        sem, val = resolve(k, s)
                        e.wait_ge(sem, val)
                    if o.fn is None:
                        continue
                    ins = o.fn(e)
                    if o.dma is not None:
                        ins.then_inc(dsem[o.dma[0]], 16)
                    elif o.flag:
                        c = p[i]
                        ins.then_inc(sems[name][(c - 1) // SEM_LIMIT], 1)

            if self.ops["pe"]:
                @block.tensor
                def _(e):
                    run(e, "pe")

            if self.ops["act"]:
                @block.scalar
                def _(e):
                    run(e, "act")

            if self.ops["dve"]:
                @block.vector
                def _(e):
                    run(e, "dve")

            if any(o.fn is not None for o in self.ops["pool"]):
                @block.gpsimd
                def _(e):
                    run(e, "pool")

            @block.sync
            def _(e):
                run(e, "sp")
                for key, c in self.dma_cnt.items():
                    e.wait_ge(dsem[key], 16 * c)


class SB:
    def __init__(self, nc):
        self.nc = nc
        self.off = SB_BASE
        self.n = 0

    def alloc(self, shape, dtype, parts=128):
        nbytes = int(np.prod(shape[1:])) * (2 if dtype == BF16 else 4)
        off = (self.off + 63) // 64 * 64
        assert off + nbytes <= SB_END, f"SBUF overflow: need {off + nbytes - SB_END} more bytes"
        self.n += 1
        t = self.nc.alloc_sbuf_tensor_at(f"t{self.n}", list(shape), dtype, offset=off)
        self.off = off + nbytes
        return t.ap()

    def mark(self):
        return self.off

    def release(self, m):
        self.off = m


def build(stop_after=99, only=None, dbg=99, mixers=("gm", "ssm", "att", "rw", "rwp"), only_mixer=None):
    nc = bass.Bass("TRN2", target_bir_lowering=False)
    P = Prog(nc)
    sb = SB(nc)

    in_names = []
    dbg_names = {}

    def din(name, shape, dt=F32):
        if only is not None and name not in only:
            return None
        in_names.append(name)
        return nc.dram_tensor(name, list(shape), dt, kind="ExternalInput").ap()

    out_names = []

    def dout(name, shape, dt=F32):
        out_names.append(name)
        return nc.dram_tensor(name, list(shape), dt, kind="ExternalOutput").ap()

    xp_d = din("xp", [SEQ, D])
    xs_d = din("xs", [NS, D])
    pp_d = din("pp", [DEPTH, SEQ, PLE])
    ps_d = din("psm", [DEPTH, NS, PLE])
    ln_g_d = din("ln_g", [DEPTH * 3 * KC, 128])
    ln_b_d = din("ln_b", [DEPTH * 3 * KC, 128])
    w_up_d = din("ffn_w_up", [DEPTH, 2, D, 2 * DFF])
    w_dn_d = din("ffn_w_down", [DEPTH, 2, DFF, D])
    ple_wp_d = din("ple_w_p", [DEPTH, PLE, D])
    ple_wg_d = din("ple_w_g", [DEPTH, D, D])
    ple_bg_d = din("ple_b_g", [DEPTH * KC, 128])
    gm_w_in_d = din("gm_w_in", [D, 4096])
    gm_w_out_d = din("gm_w_out", [2048, D])
    gm_lng_c_d = din("gm_lng_c", [16, 128])
    gm_lnb_c_d = din("gm_lnb_c", [16, 128])
    gm_wT_d = din("gm_wT", [128, 8, 128])
    gm_bs_bc_d = din("gm_bs_bc", [128, 8, 128])
    gm_w00_d = din("gm_w00", [128, 8])
    gm_bs0_d = din("gm_bs0", [128, 8])
    gm_lng_bc_d = din("gm_lng_bc", [32, 2048])
    gm_lnb_bc_d = din("gm_lnb_bc", [32, 2048])
    ssm_w_in_d = din("ssm_w_in", [D, 5152])
    ssm_w_out_d = din("ssm_w_out", [2048, D])
    ssm_cw_c_d = din("ssm_cw_c", [96, 128])
    ssm_cb_c_d = din("ssm_cb_c", [24, 128])
    ssm_ng_c_d = din("ssm_ng_c", [16, 128])
    ssm_hv_bc_d = din("ssm_hv_bc", [128, 3, 32])
    ssm_cw_bc_d = din("ssm_cw_bc", [32, 4, 3072])
    ssm_cb_bc_d = din("ssm_cb_bc", [32, 3072])
    ssm_d_c_d = din("ssm_d_c", [16, 128])
    ssm_sel_d = din("ssm_sel", [128, NS, 128])
    st_conv_d = din("st_conv", [NS, 3, 3072])
    st_ssm_d = din("st_ssm", [NS, 2048, 128])
    att_w_in_d = din("att_w_in", [D, 2120])
    att_w_out_d = din("att_w_out", [D, D])
    att_kn_bc_d = din("att_kn_bc", [128, 2, 64])
    cache_k_d = din("cache_k", [2560 * 128, 256])
    cache_v_d = din("cache_v", [2560 * 128, 256])
    cache_ik_d = din("cache_ik", [2560 * 128, 64])
    pt_bc_d = din("pt_bc", [128, NS * 64], I32)
    piota_d = din("piota", [128, 1])
    negp_d = din("negp", [128, 1])
    scr_d = nc.dram_tensor("scr", [NS, 65 * 128], F32, kind="Internal").ap()
    rw_mu_c_d = din("rw_mu_c", [48, 128])
    rw_gn_c_d = din("rw_gn_c", [16, 128])
    rw_vec_bc_d = din("rw_vec_bc", [32, 5, D])
    rw_blk_d = din("rw_blk", [128, 128])
    rw_w_d = {n: din("rw_w_" + n, [D, D]) for n in ("r", "k", "v", "o")}
    rw_w1_d = din("rw_w1", [D, 64])
    rw_w2_d = din("rw_w2", [64, D])
    rw_a1_d = din("rw_a1", [D, 64])
    rw_a2_d = din("rw_a2", [64, D])
    rw_g1_d = din("rw_g1", [D, 128])
    rw_g2_d = din("rw_g2", [128, D])
    rw_vec128_d = din("rw_vec128", [128, 7, D])
    rw_mbd_d = din("rw_mbd", [16, D])
    scr_h_d = nc.dram_tensor("scr_h", [3, SEQ, D], BF16, kind="Internal").ap()
    scr_y_d = nc.dram_tensor("scr_y", [SEQ, D], F32, kind="Internal").ap()
    st_shift_d = din("st_shift", [NS, D])
    st_wkv_d = din("st_wkv", [NS, 16, 64, 64])
    negm_d = din("negm", [128, 128])
    rope_d = din("rope", [128, 17, 16])
    m1_d = din("m1", [128, 128])
    onesf_d = din("onesf", [128, 128])
    ident_d = din("ident", [128, 128])
    cmask_d = din("cmask", [128, 128])

    yp_d = dout("yp", [SEQ, D])
    ys_d = dout("ys", [NS, D])
    gmv_d = dout("gmv", [NS, 2048])
    conv_p_d = dout("conv_p", [3, 3072])
    ssm_p_d = dout("ssm_p", [32, 64, 128])
    conv_s_d = dout("conv_s", [NS, 3, 3072])
    ssm_s_d = dout("ssm_s", [NS, 32, 64, 128])
    k_p_d = dout("k_p", [SEQ, 256])
    v_p_d = dout("v_p", [SEQ, 256])
    ik_p_d = dout("ik_p", [SEQ, 64])
    k_s_d = dout("k_s", [NS, 256])
    v_s_d = dout("v_s", [NS, 256])
    ik_s_d = dout("ik_s", [NS, 64])
    sh_p_d = dout("sh_p", [1, D])
    wkv_p_d = dout("wkv_p", [16, 64, 64])
    sh_s_d = dout("sh_s", [NS, D])
    wkv_s_d = dout("wkv_s", [NS, 16, 64, 64])

    x32 = sb.alloc([128, KC, NCOL], F32)
    xbf = sb.alloc([128, KC, NCOL], BF16)
    ident = sb.alloc([128, 128], F32)
    identb = sb.alloc([128, 128], BF16)
    ones_bf = sb.alloc([128, 128], BF16)
    lng = sb.alloc([128, DEPTH * 3 * KC], F32)
    lnb = sb.alloc([128, DEPTH * 3 * KC], F32)
    plebg = sb.alloc([128, DEPTH * KC], F32)
    epsc = sb.alloc([128, 1], F32)
    psum = nc.alloc_psum_tensor("psum", [128, 8 * 512], F32).ap()

    def bank(b, w=512, n=1):
        return psum[:, b * 512:b * 512 + w] if n == 1 else psum[:, b * 512:(b + n) * 512]

    B_x32 = [[Buf(f"x32_{m}_{t}") for t in range(5)] for m in range(KC)]
    B_xbf = [[Buf(f"xbf_{m}_{t}") for t in range(5)] for m in range(KC)]
    B_ps = [Buf(f"ps{b}") for b in range(8)]
    B_const = Buf("const")
    B_const2 = Buf("const2")
    B_cols = Buf("cols")

    P.dma("sp", lambda e: e.dma_start(out=ident[:], in_=ident_d[:, :]), writes=[B_const])
    P.op("act", lambda e: e.activation(out=identb[:], in_=ident[:], func=AF.Copy), reads=[B_const], writes=[B_const2])
    P.op("dve", lambda e: e.memset(ones_bf[:], 1.0 / D), reads=[B_const2], writes=[B_const2])
    P.op("dve", lambda e: e.memset(epsc[:], LN_EPS / (ALPHA * ALPHA)), reads=[B_const2], writes=[B_const2])

    def load_cols(dst, src_d, rows, stage, B_stage):
        for r0 in range(0, rows, 128):
            r = min(128, rows - r0)
            if r < 128:
                P.op("dve", lambda e: e.memset(stage[:, :], 0.0), writes=[B_stage])
            P.dma("sp", lambda e, r0=r0, r=r: e.dma_start(out=stage[:r, :], in_=src_d[r0:r0 + r, :]), writes=[B_stage])
            P.op("pe", lambda e: e.transpose(bank(7, 128), stage[:, :], ident[:, :]), reads=[B_stage, B_const], writes=[B_ps[7]])
            P.op("dve", lambda e, r0=r0, r=r: e.tensor_copy(out=dst[:, r0:r0 + r], in_=bank(7, r)), reads=[B_ps[7], B_cols], writes=[B_cols])

    m0 = sb.mark()
    stage = sb.alloc([128, 128], F32)
    B_stage = Buf("stage")
    if dbg >= 2:
        load_cols(lng, ln_g_d, DEPTH * 3 * KC, stage, B_stage)
        load_cols(lnb, ln_b_d, DEPTH * 3 * KC, stage, B_stage)
        load_cols(plebg, ple_bg_d, DEPTH * KC, stage, B_stage)

    def load_tm_to_fm(src_d, ntok, col0, feat_chunks, dst32, dstbf, Bd32, Bdbf, feat_w=128):
        F = feat_chunks * 128
        tin = sb.alloc([128, 2, F], F32)
        Bt = [Buf("tin0"), Buf("tin1")]
        k = 0
        for t0 in range(0, ntok, 128):
            n = min(128, ntok - t0)
            s = k % 2
            if n < 128:
                P.op("dve", lambda e, s=s: e.memset(tin[:, s, :], 0.0), writes=[Bt[s]])
            P.dma("sp", lambda e, t0=t0, n=n, s=s: e.dma_start(out=tin[:n, s, :], in_=src_d[t0:t0 + n, :]), writes=[Bt[s]])
            for c in range(feat_chunks):
                b = 4 + (k * feat_chunks + c) % 4
                P.op("pe", lambda e, s=s, c=c, b=b: e.transpose(bank(b, 128), tin[:, s, c * 128:(c + 1) * 128], ident[:, :]),
                     reads=[Bt[s], B_const], writes=[B_ps[b]])
                col = col0 + t0
                ti = min(col // 512, 4)
                wr = []
                if dst32 is not None:
                    wr.append(Bd32[c][ti])
                    P.op("dve", lambda e, c=c, col=col, n=n, b=b: e.tensor_copy(out=dst32[:, c, col:col + n], in_=bank(b, n)),
                         reads=[B_ps[b]], writes=[Bd32[c][ti]])
                if dst32 is not None:
                    P.op("act", lambda e, c=c, col=col, n=n: e.activation(out=dstbf[:, c, col:col + n], in_=dst32[:, c, col:col + n], func=AF.Copy),
                         reads=[Bd32[c][ti]], writes=[Bdbf[c][ti]])
                else:
                    P.op("act", lambda e, c=c, col=col, n=n, b=b: e.activation(out=dstbf[:, c, col:col + n], in_=bank(b, n), func=AF.Copy),
                         reads=[B_ps[b]], writes=[Bdbf[c][ti]])
            k += 1

    m1 = sb.mark()
    if dbg >= 3:
        load_tm_to_fm(xp_d, SEQ, 0, KC, x32, xbf, B_x32, B_xbf)
    if dbg >= 4:
        load_tm_to_fm(xs_d, NS, SEQ, KC, x32, xbf, B_x32, B_xbf)
    sb.release(m1)
    sb.release(m0)
    P.barrier()

    def layer_norm(gi, ln):
        zb, zsq, st = ln["zb"], ln["zsq"], ln["st"]
        for ti, (c0, w) in enumerate(TILES):
            for m in range(KC):
                P.op("act", lambda e, m=m, c0=c0, w=w: e.activation(out=zb[:, m, :w], in_=x32[:, m, c0:c0 + w], func=AF.Copy),
                     reads=[B_x32[m][ti]], writes=[ln["Bzb"][m]])
                P.op("act", lambda e, m=m, c0=c0, w=w: e.activation(out=zsq[:, m, :w], in_=x32[:, m, c0:c0 + w], func=AF.Square),
                     reads=[B_x32[m][ti]], writes=[ln["Bzsq"][m]])
            for m in range(KC):
                P.op("pe", lambda e, m=m, w=w: e.matmul(bank(6, w), ones_bf[:], zb[:, m, :w], start=(m == 0), stop=(m == KC - 1)),
                     reads=[ln["Bzb"][m], B_const2], writes=[B_ps[6]])
            for m in range(KC):
                P.op("pe", lambda e, m=m, w=w: e.matmul(bank(7, w), ones_bf[:], zsq[:, m, :w], start=(m == 0), stop=(m == KC - 1)),
                     reads=[ln["Bzsq"][m], B_const2], writes=[B_ps[7]])
            mean, var, rstd, mr = st[:, 0, :w], st[:, 1, :w], st[:, 2, :w], st[:, 3, :w]
            Bs = ln["Bst"]
            P.op("act", lambda e, w=w, mean=mean: e.activation(out=mean, in_=bank(6, w), func=AF.Copy), reads=[B_ps[6]], writes=[Bs[0]])
            P.op("dve", lambda e, var=var, mean=mean: e.tensor_tensor(out=var, in0=mean, in1=mean, op=ALU.mult), reads=[Bs[0]], writes=[Bs[1]])
            P.op("dve", lambda e, w=w, var=var: e.tensor_tensor(out=var, in0=bank(7, w), in1=var, op=ALU.subtract), reads=[B_ps[7], Bs[1]], writes=[Bs[1]])
            P.op("act", lambda e, var=var, rstd=rstd: e.activation(out=rstd, in_=var, func=AF.Sqrt, bias=epsc[:, 0:1], scale=1.0),
                 reads=[Bs[1], B_const2], writes=[Bs[2]])
            P.op("dve", lambda e, rstd=rstd: e.reciprocal(out=rstd, in_=rstd), reads=[Bs[2]], writes=[Bs[2]])
            P.op("dve", lambda e, mr=mr, mean=mean, rstd=rstd: e.tensor_tensor(out=mr, in0=mean, in1=rstd, op=ALU.mult), reads=[Bs[0], Bs[2]], writes=[Bs[3]])
            for m in range(KC):
                tmp = ln["tmp"][:, m % 2, :w]
                Bt = ln["Btmp"][m % 2]
                gcol = lng[:, gi * KC + m:gi * KC + m + 1]
                bcol = lnb[:, gi * KC + m:gi * KC + m + 1]
                P.op("dve", lambda e, tmp=tmp, m=m, c0=c0, w=w, rstd=rstd: e.tensor_tensor(out=tmp, in0=x32[:, m, c0:c0 + w], in1=rstd, op=ALU.mult),
                     reads=[B_x32[m][ti], Bs[2]], writes=[Bt])
                P.op("dve", lambda e, tmp=tmp, mr=mr: e.tensor_tensor(out=tmp, in0=tmp, in1=mr, op=ALU.subtract), reads=[Bt, Bs[3]], writes=[Bt])
                P.op("act", lambda e, tmp=tmp, m=m, c0=c0, w=w, gcol=gcol, bcol=bcol: e.activation(
                    out=x32[:, m, c0:c0 + w], in_=tmp, func=AF.Identity, bias=bcol, scale=gcol),
                    reads=[Bt, B_cols], writes=[B_x32[m][ti]])
                P.op("act", lambda e, tmp=tmp, m=m, c0=c0, w=w, gcol=gcol, bcol=bcol: e.activation(
                    out=xbf[:, m, c0:c0 + w], in_=tmp, func=AF.Identity, bias=bcol, scale=gcol),
                    reads=[Bt, B_cols], writes=[B_xbf[m][ti]])

    def alloc_ln():
        ln = {}
        ln["zb"] = sb.alloc([128, KC, 512], BF16)
        ln["zsq"] = sb.alloc([128, KC, 512], BF16)
        ln["st"] = sb.alloc([128, 4, 512], F32)
        ln["tmp"] = sb.alloc([128, 2, 512], F32)
        ln["Bzb"] = [Buf("zb") for _ in range(KC)]
        ln["Bzsq"] = [Buf("zsq") for _ in range(KC)]
        ln["Bst"] = [Buf("st") for _ in range(4)]
        ln["Btmp"] = [Buf("tmp") for _ in range(2)]
        return ln

    HALF = NFF // 2

    def ffn(i, j):
        mk = sb.mark()
        h = sb.alloc([128, HALF, NCOL], BF16)
        NWU = 3
        wu = sb.alloc([128, NWU, 2, KC, 256], BF16)
        wd = sb.alloc([128, 2, HALF, 256], BF16)
        sg = sb.alloc([128, 2, 512], F32)
        B_h = [[Buf("h") for _ in range(5)] for _ in range(HALF)]
        B_wu = [Buf("wu") for _ in range(NWU)]
        B_wd = [Buf("wd") for _ in range(2)]
        B_sg = [Buf("sg") for _ in range(2)]
        c_res = 0.5 / ALPHA
        nslab = 0
        ndslab = 0
        cnt = 0
        for g in range(2):
            chunks = list(range(g * HALF, (g + 1) * HALF))
            for s0 in range(0, HALF, 2):
                sl = chunks[s0:s0 + 2]
                ncol = len(sl) * 128
                s = nslab % NWU
                nslab += 1
                a0 = sl[0] * 128
                P.dma("pool", lambda e, s=s, a0=a0, ncol=ncol: e.dma_start(
                    out=wu[:, s, 0, :, :ncol], in_=w_up_d[i, j, :, a0:a0 + ncol].rearrange("(kc p) n -> p kc n", p=128)), writes=[B_wu[s]])
                P.dma("pool", lambda e, s=s, a0=a0, ncol=ncol: e.dma_start(
                    out=wu[:, s, 1, :, :ncol], in_=w_up_d[i, j, :, DFF + a0:DFF + a0 + ncol].rearrange("(kc p) n -> p kc n", p=128)), writes=[B_wu[s]])
                for li, jj in enumerate(sl):
                    hj = jj - g * HALF
                    for ti, (c0, w) in enumerate(TILES):
                        ba = (cnt % 2) * 2
                        bb = ba + 1
                        sgi = cnt % 2
                        cnt += 1
                        for kc in range(KC):
                            P.op("pe", lambda e, s=s, li=li, kc=kc, c0=c0, w=w, ba=ba: e.matmul(
                                bank(ba, w), wu[:, s, 0, kc, li * 128:(li + 1) * 128], xbf[:, kc, c0:c0 + w], start=(kc == 0), stop=(kc == KC - 1)),
                                reads=[B_wu[s], B_xbf[kc][ti]], writes=[B_ps[ba]])
                        for kc in range(KC):
                            P.op("pe", lambda e, s=s, li=li, kc=kc, c0=c0, w=w, bb=bb: e.matmul(
                                bank(bb, w), wu[:, s, 1, kc, li * 128:(li + 1) * 128], xbf[:, kc, c0:c0 + w], start=(kc == 0), stop=(kc == KC - 1)),
                                reads=[B_wu[s], B_xbf[kc][ti]], writes=[B_ps[bb]])
                        P.op("act", lambda e, sgi=sgi, w=w, ba=ba: e.activation(out=sg[:, sgi, :w], in_=bank(ba, w), func=AF.Silu),
                             reads=[B_ps[ba]], writes=[B_sg[sgi]])
                        P.op("dve", lambda e, sgi=sgi, w=w, bb=bb, hj=hj, c0=c0: e.tensor_tensor(
                            out=h[:, hj, c0:c0 + w], in0=sg[:, sgi, :w], in1=bank(bb, w), op=ALU.mult),
                            reads=[B_sg[sgi], B_ps[bb]], writes=[B_h[hj][ti]])
            for dq in range(4):
                s = ndslab % 2
                ndslab += 1
                r0 = g * HALF * 128
                P.dma("pool", lambda e, s=s, r0=r0, dq=dq: e.dma_start(
                    out=wd[:, s, :, :], in_=w_dn_d[i, j, r0:r0 + HALF * 128, dq * 256:(dq + 1) * 256].rearrange("(c p) n -> p c n", p=128)),
                    writes=[B_wd[s]])
                for mi in range(2):
                    m = dq * 2 + mi
                    for ti, (c0, w) in enumerate(TILES):
                        b = 4 + (cnt % 2)
                        cnt += 1
                        for hj in range(HALF):
                            P.op("pe", lambda e, s=s, hj=hj, mi=mi, c0=c0, w=w, b=b: e.matmul(
                                bank(b, w), wd[:, s, hj, mi * 128:(mi + 1) * 128], h[:, hj, c0:c0 + w], start=(hj == 0), stop=(hj == HALF - 1)),
                                reads=[B_wd[s], B_h[hj][ti]], writes=[B_ps[b]])
                        P.op("dve", lambda e, m=m, c0=c0, w=w, b=b: e.scalar_tensor_tensor(
                            out=x32[:, m, c0:c0 + w], in0=bank(b, w), scalar=c_res, in1=x32[:, m, c0:c0 + w], op0=ALU.mult, op1=ALU.add),
                            reads=[B_ps[b], B_x32[m][ti]], writes=[B_x32[m][ti]])
        sb.release(mk)
        P.barrier()
        mk = sb.mark()
        ln = alloc_ln()
        layer_norm(i * 3 + 2 * j, ln)
        sb.release(mk)
        P.barrier()

    def ple(i):
        mk = sb.mark()
        pT = sb.alloc([128, 2, NCOL], BF16)
        B_pT = [[Buf("pT") for _ in range(5)] for _ in range(2)]
        m1 = sb.mark()
        load_tm_to_fm(pp_d[i], SEQ, 0, 2, None, pT, None, B_pT)
        load_tm_to_fm(ps_d[i], NS, SEQ, 2, None, pT, None, B_pT)
        sb.release(m1)
        wg = sb.alloc([128, KC, D], BF16)
        wp = sb.alloc([128, 2, D], BF16)
        sgt = sb.alloc([128, 2, 512], F32)
        B_wg, B_wp = Buf("wg"), Buf("wp")
        B_sg = [Buf("sg") for _ in range(2)]
        P.barrier()
        P.dma("pool", lambda e: e.dma_start(out=wg[:], in_=ple_wg_d[i].rearrange("(kc p) n -> p kc n", p=128)), writes=[B_wg])
        P.dma("pool", lambda e: e.dma_start(out=wp[:], in_=ple_wp_d[i].rearrange("(kc p) n -> p kc n", p=128)), writes=[B_wp])
        cnt = 0
        for m in range(KC):
            for ti, (c0, w) in enumerate(TILES):
                ba = (cnt % 2) * 2
                bb = ba + 1
                si = cnt % 2
                cnt += 1
                for kc in range(KC):
                    P.op("pe", lambda e, kc=kc, m=m, c0=c0, w=w, ba=ba: e.matmul(
                        bank(ba, w), wg[:, kc, m * 128:(m + 1) * 128], xbf[:, kc, c0:c0 + w], start=(kc == 0), stop=(kc == KC - 1)),
                        reads=[B_wg, B_xbf[kc][ti]], writes=[B_ps[ba]])
                for kc in range(2):
                    P.op("pe", lambda e, kc=kc, m=m, c0=c0, w=w, bb=bb: e.matmul(
                        bank(bb, w), wp[:, kc, m * 128:(m + 1) * 128], pT[:, kc, c0:c0 + w], start=(kc == 0), stop=(kc == 1)),
                        reads=[B_wp, B_pT[kc][ti]], writes=[B_ps[bb]])
                bcol = plebg[:, i * KC + m:i * KC + m + 1]
                P.op("act", lambda e, si=si, w=w, ba=ba, bcol=bcol: e.activation(out=sgt[:, si, :w], in_=bank(ba, w), func=AF.Sigmoid, bias=bcol, scale=1.0),
                     reads=[B_ps[ba], B_cols], writes=[B_sg[si]])
                P.op("dve", lambda e, si=si, w=w, bb=bb: e.tensor_tensor(out=sgt[:, si, :w], in0=sgt[:, si, :w], in1=bank(bb, w), op=ALU.mult),
                     reads=[B_sg[si], B_ps[bb]], writes=[B_sg[si]])
                P.op("dve", lambda e, si=si, m=m, c0=c0, w=w: e.tensor_tensor(out=x32[:, m, c0:c0 + w], in0=x32[:, m, c0:c0 + w], in1=sgt[:, si, :w], op=ALU.add),
                     reads=[B_sg[si], B_x32[m][ti]], writes=[B_x32[m][ti]])
        for m in range(KC):
            for ti, (c0, w) in enumerate(TILES):
                P.op("act", lambda e, m=m, c0=c0, w=w: e.activation(out=xbf[:, m, c0:c0 + w], in_=x32[:, m, c0:c0 + w], func=AF.Copy),
                     reads=[B_x32[m][ti]], writes=[B_xbf[m][ti]])
        sb.release(mk)
        P.barrier()


    def gmlp():
        GELU = AF.Gelu_apprx_tanh
        mk = sb.mark()
        gmg = sb.alloc([128, 16], F32)
        gmb = sb.alloc([128, 16], F32)
        wT = sb.alloc([128, 8, 128], BF16)
        Cb = sb.alloc([128, 16, 128], F32)
        eps1 = sb.alloc([128, 1], F32)
        As = sb.alloc([128, 16], F32)
        Cs = sb.alloc([128, 16], F32)
        B_gc = Buf("gm_consts")
        mt = sb.mark()
        stage = sb.alloc([128, 128], F32)
        wT32 = sb.alloc([128, 8, 128], F32)
        cm = sb.alloc([128, 128], F32)
        bsbc = sb.alloc([128, 8, 128], F32)
        w00c = sb.alloc([128, 8], F32)
        bs0c = sb.alloc([128, 8], F32)
        ones1 = sb.alloc([128, 128], BF16)
        B_stage, B_t = Buf("stage"), Buf("gmtmp")
        for dst, src in ((gmg, gm_lng_c_d), (gmb, gm_lnb_c_d)):
            P.op("dve", lambda e: e.memset(stage[:, :], 0.0), writes=[B_stage])
            P.dma("sp", lambda e, src=src: e.dma_start(out=stage[:16, :], in_=src[:, :]), writes=[B_stage])
            P.op("pe", lambda e: e.transpose(bank(7, 128), stage[:, :], ident[:, :]), reads=[B_stage, B_const], writes=[B_ps[7]])
            P.op("dve", lambda e, dst=dst: e.tensor_copy(out=dst[:, :], in_=bank(7, 16)), reads=[B_ps[7], B_gc], writes=[B_gc])
        P.dma("sp", lambda e: e.dma_start(out=wT32[:], in_=gm_wT_d[:, :, :]), writes=[B_t])
        P.dma("sp", lambda e: e.dma_start(out=cm[:], in_=cmask_d[:, :]), writes=[B_t])
        P.dma("sp", lambda e: e.dma_start(out=bsbc[:], in_=gm_bs_bc_d[:, :, :]), writes=[B_t])
        P.dma("sp", lambda e: e.dma_start(out=w00c[:], in_=gm_w00_d[:, :]), writes=[B_t])
        P.dma("sp", lambda e: e.dma_start(out=bs0c[:], in_=gm_bs0_d[:, :]), writes=[B_t])
        P.op("dve", lambda e: e.memset(ones1[:], 1.0), reads=[B_t], writes=[B_t])
        P.op("dve", lambda e: e.memset(eps1[:], LN_EPS), reads=[B_gc], writes=[B_gc])
        P.op("dve", lambda e: e.tensor_tensor(out=wT[:], in0=wT32[:], in1=cm[:, :].unsqueeze(1).to_broadcast([128, 8, 128]), op=ALU.mult),
             reads=[B_t, B_gc], writes=[B_gc])
        for hb in range(2):
            P.op("pe", lambda e, hb=hb: e.matmul(bank(6, 512), ones1[:], wT[:, hb * 4:(hb + 1) * 4, :], start=True, stop=True),
                 reads=[B_t, B_gc], writes=[B_ps[6]])
            for gg in range(4):
                g = hb * 4 + gg
                for hh in range(2):
                    fc = 2 * g + hh
                    P.op("dve", lambda e, gg=gg, g=g, fc=fc: e.scalar_tensor_tensor(
                        out=Cb[:, fc, :], in0=bank(6, 512)[:, gg * 128:(gg + 1) * 128], scalar=gmb[:, fc:fc + 1], in1=bsbc[:, g, :], op0=ALU.mult, op1=ALU.add),
                        reads=[B_ps[6], B_t, B_gc], writes=[B_gc])
        w00x = w00c[:, :].unsqueeze(2).to_broadcast([128, 8, 2])
        bs0x = bs0c[:, :].unsqueeze(2).to_broadcast([128, 8, 2])
        v3 = lambda t: t[:, :].rearrange("p (g h) -> p g h", h=2)
        P.op("dve", lambda e: e.tensor_tensor(out=v3(As), in0=v3(gmg), in1=w00x, op=ALU.mult), reads=[B_t, B_gc], writes=[B_gc])
        P.op("dve", lambda e: e.tensor_tensor(out=v3(Cs), in0=v3(gmb), in1=w00x, op=ALU.mult), reads=[B_t, B_gc], writes=[B_gc])
        P.op("dve", lambda e: e.tensor_tensor(out=v3(Cs), in0=v3(Cs), in1=bs0x, op=ALU.add), reads=[B_t, B_gc], writes=[B_gc])
        P.barrier()
        sb.release(mt)

        wins = sb.alloc([128, 3, KC, 256], BF16)
        wos = sb.alloc([128, 2, 16, 128], BF16)
        u = sb.alloc([128, 16, 512], BF16)
        v32 = sb.alloc([128, 4, 2048], F32)
        vbf = sb.alloc([128, 4, 2048], BF16)
        st6 = sb.alloc([128, 4, 4, 6], F32)
        mv = sb.alloc([128, 4, 2], F32)
        rs = sb.alloc([128, 4], F32)
        tmp = sb.alloc([128, 2, 512], F32)
        B_wins = [Buf("wins") for _ in range(3)]
        B_wos = [Buf("wos") for _ in range(2)]
        B_u = [Buf("u") for _ in range(16)]
        B_v32 = [Buf("v32") for _ in range(4)]
        B_vbf = [Buf("vbf") for _ in range(4)]
        B_st = [Buf("st") for _ in range(4)]
        B_tmp = [Buf("tmp") for _ in range(2)]
        state = {"nsl": 0, "nwo": 0, "cnt": 0}

        def load_win(col0):
            sl = state["nsl"] % 3
            state["nsl"] += 1
            P.dma("pool", lambda e, sl=sl, col0=col0: e.dma_start(
                out=wins[:, sl, :, :], in_=gm_w_in_d[:, col0:col0 + 256].rearrange("(kc p) n -> p kc n", p=128)), writes=[B_wins[sl]])
            return sl

        def v_chunk(ch, tok):
            for q in range(4):
                P.op("dve", lambda e, ch=ch, q=q: e.bn_stats(out=st6[:, ch, q, :], in_=v32[:, ch, q * 512:(q + 1) * 512]),
                     reads=[B_v32[ch]], writes=[B_st[ch]])
            P.op("dve", lambda e, ch=ch: e.bn_aggr(out=mv[:, ch, :], in_=st6[:, ch, :, :].rearrange("p a b -> p (a b)")),
                 reads=[B_st[ch]], writes=[B_st[ch]])
            P.op("act", lambda e, ch=ch: e.activation(out=rs[:, ch:ch + 1], in_=mv[:, ch, 1:2], func=AF.Sqrt, bias=eps1[:, 0:1], scale=1.0),
                 reads=[B_st[ch], B_gc], writes=[B_st[ch]])
            P.op("dve", lambda e, ch=ch: e.reciprocal(out=rs[:, ch:ch + 1], in_=rs[:, ch:ch + 1]), reads=[B_st[ch]], writes=[B_st[ch]])

        def v_slabs(chunks):
            for slb in range(8):
                sl = load_win(2048 + slb * 256)
                for (ch, tok) in chunks:
                    b = state["cnt"] % 2
                    state["cnt"] += 1
                    for kc in range(KC):
                        P.op("pe", lambda e, kc=kc, tok=tok, sl=sl, b=b: e.matmul(
                            bank(b, 256), xbf[:, kc, tok:tok + 128], wins[:, sl, kc, :], start=(kc == 0), stop=(kc == KC - 1)),
                            reads=[B_wins[sl]] + [B_xbf[kc][t] for t in range(5)], writes=[B_ps[b]])
                    P.op("act", lambda e, ch=ch, slb=slb, b=b: e.activation(out=v32[:, ch, slb * 256:(slb + 1) * 256], in_=bank(b, 256), func=GELU),
                         reads=[B_ps[b]], writes=[B_v32[ch]])

        def u_slabs(c0, w, udst, B_ud):
            for slb in range(8):
                sl = load_win(slb * 256)
                for li in range(2):
                    fc = slb * 2 + li
                    b = 2 + state["cnt"] % 2
                    state["cnt"] += 1
                    for kc in range(KC):
                        P.op("pe", lambda e, kc=kc, sl=sl, li=li, b=b: e.matmul(
                            bank(b, w), wins[:, sl, kc, li * 128:(li + 1) * 128], xbf[:, kc, c0:c0 + w], start=(kc == 0), stop=(kc == KC - 1)),
                            reads=[B_wins[sl]] + [B_xbf[kc][t] for t in range(5)], writes=[B_ps[b]])
                    P.op("act", lambda e, fc=fc, b=b: e.activation(out=udst[:, fc, :w], in_=bank(b, w), func=GELU),
                         reads=[B_ps[b]], writes=[B_ud[fc]])

        def out_proj(c0, w, ti, gsrc, B_g):
            for m in range(KC):
                so = state["nwo"] % 2
                state["nwo"] += 1
                P.dma("pool", lambda e, so=so, m=m: e.dma_start(
                    out=wos[:, so, :, :], in_=gm_w_out_d[:, m * 128:(m + 1) * 128].rearrange("(fc p) n -> p fc n", p=128)), writes=[B_wos[so]])
                b = state["cnt"] % 2
                state["cnt"] += 1
                for fc in range(16):
                    P.op("pe", lambda e, so=so, fc=fc, b=b: e.matmul(
                        bank(b, w), wos[:, so, fc, :], gsrc[:, fc, :w], start=(fc == 0), stop=(fc == 15)),
                        reads=[B_wos[so], B_g[fc]], writes=[B_ps[b]])
                P.op("dve", lambda e, m=m, b=b: e.scalar_tensor_tensor(
                    out=x32[:, m, c0:c0 + w], in0=bank(b, w), scalar=1.0 / ALPHA, in1=x32[:, m, c0:c0 + w], op0=ALU.mult, op1=ALU.add),
                    reads=[B_ps[b], B_x32[m][ti]], writes=[B_x32[m][ti]])

        for ti in range(4):
            c0 = 512 * ti
            v_slabs([(ch, c0 + 128 * ch) for ch in range(4)])
            for ch in range(4):
                v_chunk(ch, c0 + 128 * ch)
                P.op("dve", lambda e, ch=ch: e.tensor_scalar(out=vbf[:, ch, :], in0=v32[:, ch, :], scalar1=mv[:, ch, 0:1], scalar2=rs[:, ch:ch + 1],
                                                              op0=ALU.subtract, op1=ALU.mult),
                     reads=[B_v32[ch], B_st[ch]], writes=[B_vbf[ch]])
            u_slabs(c0, 512, u, B_u)
            for g in range(8):
                pb = 4 + 2 * (g % 2)
                for hh in range(2):
                    fc = 2 * g + hh
                    for ch in range(4):
                        o0 = (pb + hh) * 512 + ch * 128
                        P.op("pe", lambda e, o0=o0, ch=ch, fc=fc, g=g: e.matmul(
                            psum[:, o0:o0 + 128], vbf[:, ch, fc * 128:(fc + 1) * 128], wT[:, g, :], start=True, stop=True),
                            reads=[B_vbf[ch], B_gc], writes=[B_ps[pb + hh]])
                    tt = tmp[:, hh, :]
                    P.op("dve", lambda e, tt=tt, pb=pb, hh=hh, fc=fc: e.scalar_tensor_tensor(
                        out=tt.rearrange("p (c t) -> p c t", c=4), in0=bank(pb + hh, 512).rearrange("p (c t) -> p c t", c=4),
                        scalar=gmg[:, fc:fc + 1], in1=Cb[:, fc:fc + 1, :].to_broadcast([128, 4, 128]), op0=ALU.mult, op1=ALU.add),
                        reads=[B_ps[pb + hh], B_gc], writes=[B_tmp[hh]])
                    P.op("dve", lambda e, tt=tt, fc=fc: e.tensor_tensor(out=u[:, fc, :], in0=tt, in1=u[:, fc, :], op=ALU.mult),
                         reads=[B_tmp[hh], B_u[fc]], writes=[B_u[fc]])
            out_proj(c0, 512, ti, u, B_u)

        TOK = NCOL - 128
        v_slabs([(0, TOK)])
        v_chunk(0, TOK)
        P.op("dve", lambda e: e.tensor_scalar(out=v32[:, 0, :], in0=v32[:, 0, :], scalar1=mv[:, 0, 0:1], scalar2=rs[:, 0:1],
                                              op0=ALU.subtract, op1=ALU.mult), reads=[B_v32[0], B_st[0]], writes=[B_v32[0]])
        P.dma("sp", lambda e: e.dma_start(out=v32[96:128, 2, :], in_=gm_lng_bc_d[:, :]), writes=[B_v32[2]])
        P.dma("sp", lambda e: e.dma_start(out=v32[96:128, 3, :], in_=gm_lnb_bc_d[:, :]), writes=[B_v32[3]])
        P.op("dve", lambda e: e.tensor_tensor(out=v32[96:128, 1, :], in0=v32[96:128, 0, :], in1=v32[96:128, 2, :], op=ALU.mult),
             reads=[B_v32[0], B_v32[2]], writes=[B_v32[1]])
        P.op("dve", lambda e: e.tensor_tensor(out=v32[96:128, 1, :], in0=v32[96:128, 1, :], in1=v32[96:128, 3, :], op=ALU.add),
             reads=[B_v32[1], B_v32[3]], writes=[B_v32[1]])
        P.dma("sp", lambda e: e.dma_start(out=gmv_d[:, :], in_=v32[124:128, 1, :]), reads=[B_v32[1]])
        us = vbf[:, 1, :].rearrange("p (f t) -> p f t", f=16)[:, :, 0:NS]
        B_us = [Buf("us") for _ in range(16)]
        u_slabs(SEQ, NS, us, B_us)
        ms = tmp[:, 0, :64].rearrange("p (f t) -> p f t", f=16)
        for fc in range(16):
            b = 4 + fc % 4
            P.op("pe", lambda e, fc=fc, b=b: e.transpose(bank(b, 128), v32[:, 0, fc * 128:(fc + 1) * 128], ident[:, :]),
                 reads=[B_v32[0], B_const], writes=[B_ps[b]])
            P.op("act", lambda e, fc=fc, b=b: e.activation(out=ms[:, fc, :], in_=bank(b, 128)[:, 124:128], func=AF.Identity,
                                                           bias=Cs[:, fc:fc + 1], scale=As[:, fc:fc + 1]),
                 reads=[B_ps[b], B_gc], writes=[B_tmp[0]])
            P.op("dve", lambda e, fc=fc: e.tensor_tensor(out=us[:, fc, :], in0=ms[:, fc, :], in1=us[:, fc, :], op=ALU.mult),
                 reads=[B_tmp[0], B_us[fc]], writes=[B_us[fc]])
        out_proj(SEQ, NS, 4, us, B_us)
        sb.release(mk)
        P.barrier()


    def mamba_prompt():
        NCH = 2
        TW = NCH * 128
        mk = sb.mark()
        cw = sb.alloc([128, 4, 24], F32)
        cbias = sb.alloc([128, 24], F32)
        ngc = sb.alloc([128, 16], F32)
        hv = sb.alloc([128, 3, 32], F32)
        M1 = sb.alloc([128, 128], F32)
        M2 = sb.alloc([128, 128], F32)
        onesf = sb.alloc([128, 128], F32)
        eps1 = sb.alloc([128, 1], F32)
        one1 = sb.alloc([128, 1], F32)
        wdt = sb.alloc([128, KC, 32], BF16)
        B_c = Buf("ssm_consts")
        B_carry = [Buf("carry") for _ in range(24)]
        B_HT32, B_HTbf = Buf("HT32"), Buf("HTbf")
        mt = sb.mark()
        stage = sb.alloc([128, 128], F32)
        B_stage = Buf("stage")
        cw_flat = cw[:, :, :].rearrange("p a b -> p (a b)")
        for dst, src, rows in ((cw_flat, ssm_cw_c_d, 96), (cbias, ssm_cb_c_d, 24), (ngc, ssm_ng_c_d, 16)):
            P.op("dve", lambda e: e.memset(stage[:, :], 0.0), writes=[B_stage])
            P.dma("sp", lambda e, src=src, rows=rows: e.dma_start(out=stage[:rows, :], in_=src[:, :]), writes=[B_stage])
            P.op("pe", lambda e: e.transpose(bank(7, 128), stage[:, :], ident[:, :]), reads=[B_stage, B_const], writes=[B_ps[7]])
            P.op("dve", lambda e, dst=dst, rows=rows: e.tensor_copy(out=dst[:, 0:rows], in_=bank(7, rows)), reads=[B_ps[7], B_c], writes=[B_c])
        P.dma("sp", lambda e: e.dma_start(out=hv[:], in_=ssm_hv_bc_d[:, :, :]), reads=[B_c], writes=[B_c])
        P.op("act", lambda e: e.activation(out=hv[:, 1, :], in_=hv[:, 1, :], func=AF.Exp), reads=[B_c], writes=[B_c])
        P.op("dve", lambda e: e.tensor_scalar(out=hv[:, 1, :], in0=hv[:, 1, :], scalar1=-1.0, scalar2=None, op0=ALU.mult), reads=[B_c], writes=[B_c])
        P.dma("sp", lambda e: e.dma_start(out=M1[:], in_=m1_d[:, :]), reads=[B_c], writes=[B_c])
        P.dma("sp", lambda e: e.dma_start(out=M2[:], in_=cmask_d[:, :]), reads=[B_c], writes=[B_c])
        P.dma("sp", lambda e: e.dma_start(out=onesf[:], in_=onesf_d[:, :]), reads=[B_c], writes=[B_c])
        P.dma("pool", lambda e: e.dma_start(out=wdt[:], in_=ssm_w_in_d[:, 5120:5152].rearrange("(kc p) n -> p kc n", p=128)), reads=[B_c], writes=[B_c])
        P.op("dve", lambda e: e.memset(eps1[:], LN_EPS), reads=[B_c], writes=[B_c])
        P.op("dve", lambda e: e.memset(one1[:], 1.0), reads=[B_c], writes=[B_c])
        P.barrier()
        sb.release(mt)

        mwork = sb.mark()
        carry = sb.alloc([128, 24, 3], F32)
        HT32 = sb.alloc([128, 2048], F32)
        HTbf = sb.alloc([128, 2048], BF16)
        P.op("dve", lambda e: e.memset(carry[:], 0.0), writes=B_carry)
        P.op("dve", lambda e: e.memset(HT32[:], 0.0), writes=[B_HT32])
        P.op("dve", lambda e: e.memset(HTbf[:], 0.0), writes=[B_HTbf])
        wins = sb.alloc([128, 3, KC, 256], BF16)
        wos = sb.alloc([128, 2, 16, 128], BF16)
        stg = sb.alloc([128, 2, TW + 3], F32)
        acc = sb.alloc([128, 2, TW], F32)
        xs_tm = sb.alloc([128, NCH, 2048], BF16)
        BT = sb.alloc([128, 4, TW], BF16)
        CT = sb.alloc([128, 4, TW], BF16)
        B_tm = sb.alloc([128, NCH, 512], BF16)
        sz = sb.alloc([128, NCH, 2048], BF16)
        dt = sb.alloc([128, NCH, 32], F32)
        a_tm = sb.alloc([128, NCH, 32], F32)
        small = sb.alloc([128, 4, 32], F32)
        rb = sb.alloc([128, 2, 512], F32)
        seg = sb.alloc([128, 2, 512], F32)
        eab = sb.alloc([128, 2, 512], F32)
        LT = sb.alloc([128, 2, 512], BF16)
        Csc = sb.alloc([128, 2, 512], BF16)
        cbm = sb.alloc([128, 4, 128], F32)
        dcy = sb.alloc([128, 32], F32)
        xdt = sb.alloc([128, 2048], BF16)
        xdtd = sb.alloc([128, 2048], BF16)
        y32 = sb.alloc([128, 2048], F32)
        junk = sb.alloc([128, 512], BF16)
        ssq = sb.alloc([128, 8], F32)
        ynT = sb.alloc([128, 16, TW], BF16)
        Bw = [Buf("wins") for _ in range(3)]
        Bwo = [Buf("wos") for _ in range(2)]
        Bstg = [Buf("stg") for _ in range(2)]
        Bacc = [Buf("acc") for _ in range(2)]
        Bxs = [Buf("xs_tm") for _ in range(NCH)]
        BBT, BCT = [Buf("BT") for _ in range(4)], [Buf("CT") for _ in range(4)]
        BBtm = [Buf("B_tm") for _ in range(NCH)]
        Bsz = [Buf("sz") for _ in range(NCH)]
        Bdt = [Buf("dt") for _ in range(NCH)]
        Bsm = Buf("small")
        Brb, Bseg, Beab, BLT, BCsc = ([Buf(n) for _ in range(2)] for n in ("rb", "seg", "eab", "LT", "Csc"))
        Bcbm, Bdcy, Bxdt, Bxdtd, By32, Bssq = Buf("cbm"), Buf("dcy"), Buf("xdt"), Buf("xdtd"), Buf("y32"), Buf("ssq")
        BynT = [Buf("ynT") for _ in range(16)]
        state = {"nsl": 0, "nwo": 0, "cnt": 0}
        allx = lambda kc: [B_xbf[kc][t] for t in range(5)]
        dbg_names.update(dt=dt.name, a_tm=a_tm.name, HT32=HT32.name, xs_tm=xs_tm.name, sz=sz.name, y32=y32.name, BT=BT.name, CT=CT.name,
                         small=small.name, dcy=dcy.name, ynT=ynT.name, xdt=xdt.name, cbm=cbm.name, LT=LT.name, seg=seg.name, eab=eab.name, B_tm=B_tm.name)

        def load_win(col0, ncol=256):
            sl = state["nsl"] % 3
            state["nsl"] += 1
            P.dma("pool", lambda e, sl=sl, col0=col0, ncol=ncol: e.dma_start(
                out=wins[:, sl, :, :ncol], in_=ssm_w_in_d[:, col0:col0 + ncol].rearrange("(kc p) n -> p kc n", p=128)), writes=[Bw[sl]])
            return sl

        def nb(lo=0, n=4):
            b = lo + state["cnt"] % n
            state["cnt"] += 1
            return b

        for tile in range(SEQ // TW):
            tok0 = tile * TW
            for slb in range(12):
                sl = load_win(2048 + slb * 256)
                for li in range(2):
                    fcx = slb * 2 + li
                    si = fcx % 2
                    b = nb(0, 4)
                    for kc in range(KC):
                        P.op("pe", lambda e, kc=kc, sl=sl, li=li, b=b, tok0=tok0: e.matmul(
                            bank(b, TW), wins[:, sl, kc, li * 128:(li + 1) * 128], xbf[:, kc, tok0:tok0 + TW], start=(kc == 0), stop=(kc == KC - 1)),
                            reads=[Bw[sl]] + allx(kc), writes=[B_ps[b]])
                    P.op("act", lambda e, si=si, b=b: e.activation(out=stg[:, si, 3:3 + TW], in_=bank(b, TW), func=AF.Copy), reads=[B_ps[b]], writes=[Bstg[si]])
                    P.op("dve", lambda e, si=si, fcx=fcx: e.tensor_copy(out=stg[:, si, 0:3], in_=carry[:, fcx, :]), reads=[B_carry[fcx], Bstg[si]], writes=[Bstg[si]])
                    P.op("dve", lambda e, si=si, fcx=fcx: e.tensor_copy(out=carry[:, fcx, :], in_=stg[:, si, TW:TW + 3]), reads=[Bstg[si]], writes=[B_carry[fcx]])
                    P.op("act", lambda e, si=si, fcx=fcx: e.activation(out=acc[:, si, :], in_=stg[:, si, 3:3 + TW], func=AF.Identity,
                                                                      bias=cbias[:, fcx:fcx + 1], scale=cw[:, 3, fcx:fcx + 1]),
                         reads=[Bstg[si], B_c], writes=[Bacc[si]])
                    for j in range(3):
                        P.op("dve", lambda e, si=si, fcx=fcx, j=j: e.scalar_tensor_tensor(
                            out=acc[:, si, :], in0=stg[:, si, j:j + TW], scalar=cw[:, j, fcx:fcx + 1], in1=acc[:, si, :], op0=ALU.mult, op1=ALU.add),
                            reads=[Bstg[si], Bacc[si], B_c], writes=[Bacc[si]])
                    if fcx < 16:
                        P.op("act", lambda e, si=si: e.activation(out=acc[:, si, :], in_=acc[:, si, :], func=AF.Silu), reads=[Bacc[si]], writes=[Bacc[si]])
                        for ch in range(NCH):
                            bt = nb(4, 4)
                            P.op("pe", lambda e, si=si, ch=ch, bt=bt: e.transpose(bank(bt, 128), acc[:, si, ch * 128:(ch + 1) * 128], ident[:, :]),
                                 reads=[Bacc[si], B_const], writes=[B_ps[bt]])
                            P.op("act", lambda e, ch=ch, fcx=fcx, bt=bt: e.activation(out=xs_tm[:, ch, fcx * 128:(fcx + 1) * 128], in_=bank(bt, 128), func=AF.Copy),
                                 reads=[B_ps[bt]], writes=[Bxs[ch]])
                    elif fcx < 20:
                        g = fcx - 16
                        P.op("act", lambda e, si=si: e.activation(out=acc[:, si, :], in_=acc[:, si, :], func=AF.Silu), reads=[Bacc[si]], writes=[Bacc[si]])
                        P.op("dve", lambda e, si=si, g=g: e.tensor_copy(out=BT[:, g, :], in_=acc[:, si, :]), reads=[Bacc[si]], writes=[BBT[g]])
                        for ch in range(NCH):
                            bt = nb(4, 4)
                            P.op("pe", lambda e, si=si, ch=ch, bt=bt: e.transpose(bank(bt, 128), acc[:, si, ch * 128:(ch + 1) * 128], ident[:, :]),
                                 reads=[Bacc[si], B_const], writes=[B_ps[bt]])
                            P.op("act", lambda e, ch=ch, g=g, bt=bt: e.activation(out=B_tm[:, ch, g * 128:(g + 1) * 128], in_=bank(bt, 128), func=AF.Copy),
                                 reads=[B_ps[bt]], writes=[BBtm[ch]])
                    else:
                        g = fcx - 20
                        P.op("act", lambda e, si=si, g=g: e.activation(out=CT[:, g, :], in_=acc[:, si, :], func=AF.Silu), reads=[Bacc[si]], writes=[BCT[g]])
            for slb in range(8):
                sl = load_win(slb * 256)
                for ch in range(NCH):
                    b = nb(0, 4)
                    tk = tok0 + ch * 128
                    for kc in range(KC):
                        P.op("pe", lambda e, kc=kc, sl=sl, tk=tk, b=b: e.matmul(
                            bank(b, 256), xbf[:, kc, tk:tk + 128], wins[:, sl, kc, :], start=(kc == 0), stop=(kc == KC - 1)),
                            reads=[Bw[sl]] + allx(kc), writes=[B_ps[b]])
                    P.op("act", lambda e, ch=ch, slb=slb, b=b: e.activation(out=sz[:, ch, slb * 256:(slb + 1) * 256], in_=bank(b, 256), func=AF.Silu),
                         reads=[B_ps[b]], writes=[Bsz[ch]])
            for ch in range(NCH):
                b = nb(0, 4)
                tk = tok0 + ch * 128
                for kc in range(KC):
                    P.op("pe", lambda e, kc=kc, tk=tk, b=b: e.matmul(bank(b, 32), xbf[:, kc, tk:tk + 128], wdt[:, kc, :], start=(kc == 0), stop=(kc == KC - 1)),
                         reads=[B_c] + allx(kc), writes=[B_ps[b]])
                P.op("dve", lambda e, ch=ch, b=b: e.tensor_tensor(out=dt[:, ch, :], in0=bank(b, 32), in1=hv[:, 0, :], op=ALU.add), reads=[B_ps[b], B_c], writes=[Bdt[ch]])
                P.op("act", lambda e, ch=ch: e.activation(out=dt[:, ch, :], in_=dt[:, ch, :], func=AF.Exp), reads=[Bdt[ch]], writes=[Bdt[ch]])
                P.op("act", lambda e, ch=ch: e.activation(out=dt[:, ch, :], in_=dt[:, ch, :], func=AF.Ln, bias=one1[:, 0:1], scale=1.0), reads=[Bdt[ch], B_c], writes=[Bdt[ch]])
                P.op("dve", lambda e, ch=ch: e.tensor_tensor(out=a_tm[:, ch, :], in0=dt[:, ch, :], in1=hv[:, 1, :], op=ALU.mult), reads=[Bdt[ch], B_c], writes=[Bdt[ch]])
            for ch in range(NCH):
                cs = slice(ch * 128, (ch + 1) * 128)
                for g in range(4):
                    P.op("pe", lambda e, g=g, cs=cs: e.matmul(bank(0, 512)[:, g * 128:(g + 1) * 128], BT[:, g, cs], CT[:, g, cs], start=True, stop=True),
                         reads=[BBT[g], BCT[g]], writes=[B_ps[0]])
                P.op("dve", lambda e: e.tensor_tensor(out=cbm[:], in0=bank(0, 512).rearrange("p (g t) -> p g t", g=4),
                                                      in1=M2[:, :].unsqueeze(1).to_broadcast([128, 4, 128]), op=ALU.mult),
                     reads=[B_ps[0], B_c], writes=[Bcbm])
                P.op("pe", lambda e, ch=ch: e.matmul(bank(3, 64)[:, 0:32], M2[:, :], a_tm[:, ch, :], start=True, stop=True), reads=[B_c, Bdt[ch]], writes=[B_ps[3]])
                P.op("pe", lambda e, ch=ch: e.matmul(bank(3, 64)[:, 32:64], onesf[:, :], a_tm[:, ch, :], start=True, stop=True), reads=[B_c, Bdt[ch]], writes=[B_ps[3]])
                P.op("act", lambda e: e.activation(out=small[:, 0, :], in_=bank(3, 64)[:, 32:64], func=AF.Copy), reads=[B_ps[3]], writes=[Bsm])
                P.op("dve", lambda e: e.tensor_tensor(out=small[:, 1, :], in0=small[:, 0, :], in1=bank(3, 64)[:, 0:32], op=ALU.subtract), reads=[B_ps[3], Bsm], writes=[Bsm])
                P.op("act", lambda e: e.activation(out=small[:, 1, :], in_=small[:, 1, :], func=AF.Exp), reads=[Bsm], writes=[Bsm])
                P.op("dve", lambda e, ch=ch: e.tensor_tensor(out=small[:, 2, :], in0=small[:, 1, :], in1=dt[:, ch, :], op=ALU.mult), reads=[Bsm, Bdt[ch]], writes=[Bsm])
                xv = xs_tm[:, ch, :].rearrange("p (h q) -> p h q", h=32)
                P.op("dve", lambda e, ch=ch, xv=xv: e.tensor_tensor(out=xdt[:, :].rearrange("p (h q) -> p h q", h=32), in0=xv,
                                                                    in1=dt[:, ch, :].unsqueeze(2).to_broadcast([128, 32, 64]), op=ALU.mult),
                     reads=[Bxs[ch], Bdt[ch]], writes=[Bxdt])
                P.op("dve", lambda e, xv=xv: e.tensor_tensor(out=xdtd[:, :].rearrange("p (h q) -> p h q", h=32), in0=xv,
                                                             in1=small[:, 2, :].unsqueeze(2).to_broadcast([128, 32, 64]), op=ALU.mult),
                     reads=[Bxs[ch], Bsm], writes=[Bxdtd])
                for bq in range(8):
                    pi = bq % 2
                    g = bq // 2
                    h0 = bq * 4
                    P.op("dve", lambda e, pi=pi, ch=ch, h0=h0: e.tensor_tensor(
                        out=rb[:, pi, :].rearrange("p (h t) -> p h t", h=4), in0=M2[:, :].unsqueeze(1).to_broadcast([128, 4, 128]),
                        in1=a_tm[:, ch, h0:h0 + 4].unsqueeze(2).to_broadcast([128, 4, 128]), op=ALU.mult),
                        reads=[B_c, Bdt[ch]], writes=[Brb[pi]])
                    P.op("pe", lambda e, pi=pi: e.matmul(bank(1, 512), M1[:, :], rb[:, pi, :], start=True, stop=True), reads=[B_c, Brb[pi]], writes=[B_ps[1]])
                    P.op("pe", lambda e, pi=pi: e.matmul(bank(2, 512), onesf[:, :], rb[:, pi, :], start=True, stop=True), reads=[B_c, Brb[pi]], writes=[B_ps[2]])
                    P.op("act", lambda e, pi=pi: e.activation(out=seg[:, pi, :], in_=bank(1, 512), func=AF.Exp), reads=[B_ps[1]], writes=[Bseg[pi]])
                    P.op("act", lambda e, pi=pi: e.activation(out=eab[:, pi, :], in_=bank(2, 512), func=AF.Exp), reads=[B_ps[2]], writes=[Beab[pi]])
                    P.op("dve", lambda e, pi=pi, g=g: e.tensor_tensor(
                        out=LT[:, pi, :].rearrange("p (h t) -> p h t", h=4), in0=seg[:, pi, :].rearrange("p (h t) -> p h t", h=4),
                        in1=cbm[:, g:g + 1, :].to_broadcast([128, 4, 128]), op=ALU.mult), reads=[Bseg[pi], Bcbm], writes=[BLT[pi]])
                    P.op("dve", lambda e, pi=pi, g=g, cs=cs: e.tensor_tensor(
                        out=Csc[:, pi, :].rearrange("p (h t) -> p h t", h=4), in0=eab[:, pi, :].rearrange("p (h t) -> p h t", h=4),
                        in1=CT[:, g:g + 1, cs].to_broadcast([128, 4, 128]), op=ALU.mult), reads=[Beab[pi], BCT[g]], writes=[BCsc[pi]])
                    P.op("dve", lambda e, pi=pi, h0=h0: e.tensor_copy(out=dcy[:, h0:h0 + 4], in_=eab[:, pi, :].rearrange("p (h t) -> p h t", h=4)[:, :, 127]),
                         reads=[Beab[pi]], writes=[Bdcy])
                    for hh in range(4):
                        h = h0 + hh
                        yb = 4 + h // 8
                        o0 = yb * 512 + (h % 8) * 64
                        P.op("pe", lambda e, pi=pi, hh=hh, h=h, o0=o0: e.matmul(
                            psum[:, o0:o0 + 64], LT[:, pi, hh * 128:(hh + 1) * 128], xdt[:, h * 64:(h + 1) * 64], start=True, stop=False),
                            reads=[BLT[pi], Bxdt], writes=[B_ps[yb]])
                        P.op("pe", lambda e, pi=pi, hh=hh, h=h, o0=o0: e.matmul(
                            psum[:, o0:o0 + 64], Csc[:, pi, hh * 128:(hh + 1) * 128], HTbf[:, h * 64:(h + 1) * 64], start=False, stop=True),
                            reads=[BCsc[pi], B_HTbf], writes=[B_ps[yb]])
                P.op("dve", lambda e, xv=xv: e.tensor_tensor(out=y32[:, :].rearrange("p (h q) -> p h q", h=32), in0=xv,
                                                             in1=hv[:, 2, :].unsqueeze(2).to_broadcast([128, 32, 64]), op=ALU.mult),
                     reads=[Bxs[ch], B_c], writes=[By32])
                for q in range(4):
                    P.op("dve", lambda e, q=q: e.tensor_tensor(out=y32[:, q * 512:(q + 1) * 512], in0=y32[:, q * 512:(q + 1) * 512], in1=bank(4 + q, 512), op=ALU.add),
                         reads=[By32, B_ps[4 + q]], writes=[By32])
                P.op("dve", lambda e, ch=ch: e.tensor_tensor(out=y32[:, :], in0=y32[:, :], in1=sz[:, ch, :], op=ALU.mult), reads=[By32, Bsz[ch]], writes=[By32])
                for q in range(4):
                    P.op("act", lambda e, q=q: e.activation(out=junk[:, :], in_=y32[:, q * 512:(q + 1) * 512], func=AF.Square, accum_out=ssq[:, q:q + 1]),
                         reads=[By32], writes=[Bssq])
                P.op("act", lambda e: e.activation(out=ssq[:, 4:8], in_=ssq[:, 0:4], func=AF.Sqrt, bias=eps1[:, 0:1], scale=1.0 / 512), reads=[Bssq, B_c], writes=[Bssq])
                P.op("dve", lambda e: e.reciprocal(out=ssq[:, 4:8], in_=ssq[:, 4:8]), reads=[Bssq], writes=[Bssq])
                P.op("dve", lambda e: e.tensor_tensor(out=y32[:, :].rearrange("p (g q) -> p g q", g=4), in0=y32[:, :].rearrange("p (g q) -> p g q", g=4),
                                                      in1=ssq[:, 4:8].unsqueeze(2).to_broadcast([128, 4, 512]), op=ALU.mult), reads=[By32, Bssq], writes=[By32])
                for g in range(4):
                    P.op("pe", lambda e, g=g, ch=ch: e.matmul(bank(4 + g, 512), B_tm[:, ch, g * 128:(g + 1) * 128], xdtd[:, g * 512:(g + 1) * 512], start=True, stop=True),
                         reads=[BBtm[ch], Bxdtd, By32], writes=[B_ps[4 + g]])
                P.op("dve", lambda e: e.tensor_tensor(out=HT32[:, :].rearrange("p (h q) -> p h q", h=32), in0=HT32[:, :].rearrange("p (h q) -> p h q", h=32),
                                                      in1=dcy[:, :].unsqueeze(2).to_broadcast([128, 32, 64]), op=ALU.mult), reads=[B_HT32, Bdcy, B_HTbf], writes=[B_HT32])
                for g in range(4):
                    P.op("dve", lambda e, g=g: e.tensor_tensor(out=HT32[:, g * 512:(g + 1) * 512], in0=HT32[:, g * 512:(g + 1) * 512], in1=bank(4 + g, 512), op=ALU.add),
                         reads=[B_HT32, B_ps[4 + g]], writes=[B_HT32])
                P.op("act", lambda e: e.activation(out=HTbf[:, :], in_=HT32[:, :], func=AF.Copy), reads=[B_HT32], writes=[B_HTbf])
                for fc in range(16):
                    bt = nb(0, 4)
                    P.op("pe", lambda e, fc=fc, bt=bt: e.transpose(bank(bt, 128), y32[:, fc * 128:(fc + 1) * 128], ident[:, :]), reads=[By32, B_const], writes=[B_ps[bt]])
                    P.op("act", lambda e, fc=fc, bt=bt, cs=cs: e.activation(out=ynT[:, fc, cs], in_=bank(bt, 128), func=AF.Identity, scale=ngc[:, fc:fc + 1], bias=0.0),
                         reads=[B_ps[bt], B_c], writes=[BynT[fc]])
            ti = tok0 // 512
            for m in range(KC):
                so = state["nwo"] % 2
                state["nwo"] += 1
                P.dma("pool", lambda e, so=so, m=m: e.dma_start(
                    out=wos[:, so, :, :], in_=ssm_w_out_d[:, m * 128:(m + 1) * 128].rearrange("(fc p) n -> p fc n", p=128)), writes=[Bwo[so]])
                b = nb(0, 4)
                for fc in range(16):
                    P.op("pe", lambda e, so=so, fc=fc, b=b: e.matmul(bank(b, TW), wos[:, so, fc, :], ynT[:, fc, :], start=(fc == 0), stop=(fc == 15)),
                         reads=[Bwo[so], BynT[fc]], writes=[B_ps[b]])
                P.op("dve", lambda e, m=m, b=b, tok0=tok0: e.scalar_tensor_tensor(
                    out=x32[:, m, tok0:tok0 + TW], in0=bank(b, TW), scalar=1.0 / ALPHA, in1=x32[:, m, tok0:tok0 + TW], op0=ALU.mult, op1=ALU.add),
                    reads=[B_ps[b], B_x32[m][ti]], writes=[B_x32[m][ti]])
        import os
        if os.environ.get("KDBG_SSM"):
            return "stop"
        for j in range(3):
            P.dma("sp", lambda e, j=j: e.dma_start(out=conv_p_d[j:j + 1, :].rearrange("o (f p) -> p (o f)", p=128), in_=carry[:, :, j],
                                                   allow_slow_non_contiguous=True), reads=B_carry)
        hout = sb.alloc([128, 2, 128], F32)
        Bho = [Buf("hout") for _ in range(2)]
        for r in range(16):
            bt = nb(0, 4)
            so = r % 2
            P.op("pe", lambda e, r=r, bt=bt: e.transpose(bank(bt, 128), HT32[:, r * 128:(r + 1) * 128], ident[:, :]), reads=[B_HT32, B_const], writes=[B_ps[bt]])
            P.op("act", lambda e, so=so, bt=bt: e.activation(out=hout[:, so, :], in_=bank(bt, 128), func=AF.Copy), reads=[B_ps[bt]], writes=[Bho[so]])
            P.dma("sp", lambda e, r=r, so=so: e.dma_start(out=ssm_p_d[2 * r:2 * r + 2, :, :].rearrange("h p n -> (h p) n"), in_=hout[:, so, :]), reads=[Bho[so]])

        P.barrier()
        sb.release(mwork)
        TOK = NCOL - 128
        R0, R1 = 96, 128
        wins2 = sb.alloc([128, 3, KC, 256], BF16)
        wos2 = sb.alloc([128, 2, 16, 128], BF16)
        proj = sb.alloc([128, 5152], F32)
        cst = sb.alloc([128, 3072], F32)
        cwb = sb.alloc([128, 3072], F32)
        act_s = sb.alloc([128, 3072], F32)
        tmx = sb.alloc([128, 2048], F32)
        sm = sb.alloc([128, 4, 32], F32)
        sel = sb.alloc([128, NS, 128], F32)
        dsc = sb.alloc([128, 16], F32)
        fm = sb.alloc([128, 5, 16, NS], F32)
        bcB = sb.alloc([128, 512], F32)
        bcC = sb.alloc([128, 512], F32)
        hst = sb.alloc([128, 16, 128], F32)
        jk = sb.alloc([128, 128], F32)
        ysq = sb.alloc([128, 16, NS], BF16)
        rst = sb.alloc([128, 4, NS], F32)
        ynb = sb.alloc([128, 16, NS], BF16)
        Bw2 = [Buf("wins2") for _ in range(3)]
        Bwo2 = [Buf("wos2") for _ in range(2)]
        Bproj, Bcst, Bcw, Bact, Btmx, Bsm2, Bsel, Bfm = (Buf(n) for n in ("proj", "cst", "cwb", "act_s", "tmx", "sm", "sel", "fm"))
        BbcB, BbcC, Bhst, Bjk, Bysq, Brst, Bynb = (Buf(n) for n in ("bcB", "bcC", "hst", "jk", "ysq", "rst", "ynb"))
        state["nsl"] = 0
        state["nwo"] = 0

        def load_win2(col0, ncol=256):
            sl = state["nsl"] % 3
            state["nsl"] += 1
            P.dma("pool", lambda e, sl=sl, col0=col0, ncol=ncol: e.dma_start(
                out=wins2[:, sl, :, :ncol], in_=ssm_w_in_d[:, col0:col0 + ncol].rearrange("(kc p) n -> p kc n", p=128)), writes=[Bw2[sl]])
            return sl
        stage2 = sb.alloc([128, 128], F32)
        Bst2 = Buf("stage2")
        P.op("dve", lambda e: e.memset(stage2[:, :], 0.0), writes=[Bst2])
        P.dma("sp", lambda e: e.dma_start(out=stage2[:16, :], in_=ssm_d_c_d[:, :]), writes=[Bst2])
        P.op("pe", lambda e: e.transpose(bank(7, 128), stage2[:, :], ident[:, :]), reads=[Bst2, B_const], writes=[B_ps[7]])
        P.op("dve", lambda e: e.tensor_copy(out=dsc[:, :], in_=bank(7, 16)), reads=[B_ps[7]], writes=[Bsel])
        P.dma("sp", lambda e: e.dma_start(out=sel[:], in_=ssm_sel_d[:, :, :]), reads=[Bsel], writes=[Bsel])
        P.op("dve", lambda e: e.memset(cst[:], 0.0), writes=[Bcst])
        P.op("dve", lambda e: e.memset(proj[:], 0.0), writes=[Bproj])
        P.op("dve", lambda e: e.memset(act_s[:], 0.0), writes=[Bact])
        col = 0
        while col < 5152:
            ncol = min(256, 5152 - col)
            sl = load_win2(col, ncol)
            b = nb(0, 4)
            for kc in range(KC):
                P.op("pe", lambda e, kc=kc, sl=sl, b=b, ncol=ncol: e.matmul(bank(b, ncol), xbf[:, kc, TOK:TOK + 128], wins2[:, sl, kc, :ncol], start=(kc == 0), stop=(kc == KC - 1)),
                     reads=[Bw2[sl]] + allx(kc), writes=[B_ps[b]])
            P.op("act", lambda e, b=b, col=col, ncol=ncol: e.activation(out=proj[R0:R1, col:col + ncol], in_=bank(b, ncol)[R0:R1, :], func=AF.Copy),
                 reads=[B_ps[b]], writes=[Bproj])
            col += ncol
        xbc = proj[R0:R1, 2048:5120]
        P.dma("sp", lambda e: e.dma_start(out=cwb[R0:R1, :], in_=ssm_cw_bc_d[:, 3, :]), writes=[Bcw])
        P.op("dve", lambda e: e.tensor_tensor(out=act_s[R0:R1, :], in0=xbc, in1=cwb[R0:R1, :], op=ALU.mult), reads=[Bproj, Bcw], writes=[Bact])
        P.dma("sp", lambda e: e.dma_start(out=cwb[R0:R1, :], in_=ssm_cb_bc_d[:, :]), reads=[Bcw], writes=[Bcw])
        P.op("dve", lambda e: e.tensor_tensor(out=act_s[R0:R1, :], in0=act_s[R0:R1, :], in1=cwb[R0:R1, :], op=ALU.add), reads=[Bact, Bcw], writes=[Bact])
        for j in range(3):
            P.dma("sp", lambda e, j=j: e.dma_start(out=cwb[R0:R1, :], in_=ssm_cw_bc_d[:, j, :]), reads=[Bcw], writes=[Bcw])
            P.dma("sp", lambda e, j=j: e.dma_start(out=cst[124:128, :], in_=st_conv_d[:, j, :]), reads=[Bcst], writes=[Bcst])
            P.op("dve", lambda e: e.tensor_tensor(out=cwb[R0:R1, :], in0=cwb[R0:R1, :], in1=cst[R0:R1, :], op=ALU.mult), reads=[Bcw, Bcst], writes=[Bcw])
            P.op("dve", lambda e: e.tensor_tensor(out=act_s[R0:R1, :], in0=act_s[R0:R1, :], in1=cwb[R0:R1, :], op=ALU.add), reads=[Bact, Bcw], writes=[Bact])
        P.op("act", lambda e: e.activation(out=act_s[R0:R1, :], in_=act_s[R0:R1, :], func=AF.Silu), reads=[Bact], writes=[Bact])
        P.dma("sp", lambda e: e.dma_start(out=conv_s_d[:, 0:2, :], in_=st_conv_d[:, 1:3, :]))
        P.dma("sp", lambda e: e.dma_start(out=conv_s_d[:, 2, :], in_=proj[124:128, 2048:5120]), reads=[Bproj])
        P.op("dve", lambda e: e.tensor_tensor(out=sm[R0:R1, 0, :], in0=proj[R0:R1, 5120:5152], in1=hv[R0:R1, 0, :], op=ALU.add), reads=[Bproj, B_c], writes=[Bsm2])
        P.op("act", lambda e: e.activation(out=sm[R0:R1, 0, :], in_=sm[R0:R1, 0, :], func=AF.Exp), reads=[Bsm2], writes=[Bsm2])
        P.op("act", lambda e: e.activation(out=sm[R0:R1, 0, :], in_=sm[R0:R1, 0, :], func=AF.Ln, bias=one1[R0:R1, 0:1], scale=1.0), reads=[Bsm2, B_c], writes=[Bsm2])
        P.op("dve", lambda e: e.tensor_tensor(out=sm[R0:R1, 1, :], in0=sm[R0:R1, 0, :], in1=hv[R0:R1, 1, :], op=ALU.mult), reads=[Bsm2, B_c], writes=[Bsm2])
        P.op("act", lambda e: e.activation(out=sm[R0:R1, 1, :], in_=sm[R0:R1, 1, :], func=AF.Exp), reads=[Bsm2], writes=[Bsm2])
        P.op("dve", lambda e: e.memset(tmx[:], 0.0), writes=[Btmx])
        v3 = lambda ap: ap.rearrange("p (h q) -> p h q", h=32)

        def to_fm(src, k, Bsrc):
            for fc in range(16):
                bt = nb(4, 4)
                P.op("pe", lambda e, fc=fc, bt=bt: e.transpose(bank(bt, 128), src[:, fc * 128:(fc + 1) * 128], ident[:, :]), reads=[Bsrc, B_const], writes=[B_ps[bt]])
                P.op("act", lambda e, fc=fc, bt=bt: e.activation(out=fm[:, k, fc, :], in_=bank(bt, 128)[:, 124:128], func=AF.Copy), reads=[B_ps[bt], Bfm], writes=[Bfm])
        P.op("dve", lambda e: e.tensor_tensor(out=v3(tmx[R0:R1, :]), in0=v3(act_s[R0:R1, 0:2048]), in1=sm[R0:R1, 0, :].unsqueeze(2).to_broadcast([32, 32, 64]), op=ALU.mult),
             reads=[Bact, Bsm2, Btmx], writes=[Btmx])
        to_fm(tmx[:, :], 0, Btmx)
        P.op("dve", lambda e: e.tensor_copy(out=v3(tmx[R0:R1, :]), in_=sm[R0:R1, 1, :].unsqueeze(2).to_broadcast([32, 32, 64])), reads=[Bsm2, Btmx], writes=[Btmx])
        to_fm(tmx[:, :], 1, Btmx)
        to_fm(act_s[:, 0:2048], 2, Bact)
        P.op("act", lambda e: e.activation(out=tmx[R0:R1, :], in_=proj[R0:R1, 0:2048], func=AF.Silu), reads=[Bproj, Btmx], writes=[Btmx])
        to_fm(tmx[:, :], 3, Btmx)
        for b_ in range(NS):
            P.dma("sp", lambda e, b_=b_: e.dma_start(out=hst[:], in_=st_ssm_d[b_].rearrange("(fc p) n -> p fc n", p=128)), writes=[Bhst])
            P.op("pe", lambda e, b_=b_: e.matmul(bank(0, 512), sel[:, b_, :], act_s[:, 2048:2560], start=True, stop=True), reads=[Bsel, Bact], writes=[B_ps[0]])
            P.op("pe", lambda e, b_=b_: e.matmul(bank(1, 512), sel[:, b_, :], act_s[:, 2560:3072], start=True, stop=True), reads=[Bsel, Bact], writes=[B_ps[1]])
            P.op("act", lambda e: e.activation(out=bcB[:], in_=bank(0, 512), func=AF.Copy), reads=[B_ps[0]], writes=[BbcB])
            P.op("act", lambda e: e.activation(out=bcC[:], in_=bank(1, 512), func=AF.Copy), reads=[B_ps[1]], writes=[BbcC])
            for fc in range(16):
                g = fc // 4
                P.op("dve", lambda e, fc=fc, b_=b_: e.tensor_scalar(out=hst[:, fc, :], in0=hst[:, fc, :], scalar1=fm[:, 1, fc, b_:b_ + 1], scalar2=None, op0=ALU.mult),
                     reads=[Bhst, Bfm], writes=[Bhst])
                P.op("dve", lambda e, fc=fc, b_=b_, g=g: e.scalar_tensor_tensor(out=hst[:, fc, :], in0=bcB[:, g * 128:(g + 1) * 128], scalar=fm[:, 0, fc, b_:b_ + 1],
                                                                              in1=hst[:, fc, :], op0=ALU.mult, op1=ALU.add),
                     reads=[Bhst, Bfm, BbcB], writes=[Bhst])
                P.op("dve", lambda e, fc=fc, b_=b_, g=g: e.scalar_tensor_tensor(out=jk[:, :], in0=hst[:, fc, :], scalar=1.0, in1=bcC[:, g * 128:(g + 1) * 128],
                                                                              op0=ALU.mult, op1=ALU.mult, accum_out=fm[:, 4, fc, b_:b_ + 1]),
                     reads=[Bhst, BbcC, Bfm, Bjk], writes=[Bjk, Bfm])
            P.dma("sp", lambda e, b_=b_: e.dma_start(out=ssm_s_d[b_].rearrange("h q n -> (h q) n").rearrange("(fc p) n -> p fc n", p=128), in_=hst[:]), reads=[Bhst])
        P.op("dve", lambda e: e.tensor_tensor(out=fm[:, 2], in0=fm[:, 2], in1=dsc[:, :].unsqueeze(2).to_broadcast([128, 16, NS]), op=ALU.mult), reads=[Bfm, Bsel], writes=[Bfm])
        P.op("dve", lambda e: e.tensor_tensor(out=fm[:, 4], in0=fm[:, 4], in1=fm[:, 2], op=ALU.add), reads=[Bfm], writes=[Bfm])
        P.op("dve", lambda e: e.tensor_tensor(out=fm[:, 4], in0=fm[:, 4], in1=fm[:, 3], op=ALU.mult), reads=[Bfm], writes=[Bfm])
        P.op("act", lambda e: e.activation(out=ysq[:], in_=fm[:, 4], func=AF.Square), reads=[Bfm], writes=[Bysq])
        ones_bf1 = sb.alloc([128, 128], BF16)
        P.op("dve", lambda e: e.memset(ones_bf1[:], 1.0), writes=[Bsel])
        for g in range(4):
            for q in range(4):
                fc = g * 4 + q
                P.op("pe", lambda e, g=g, q=q, fc=fc: e.matmul(bank(2, 16)[:, g * NS:(g + 1) * NS], ones_bf1[:], ysq[:, fc, :], start=(q == 0), stop=(q == 3)),
                     reads=[Bysq, Bsel], writes=[B_ps[2]])
        P.op("act", lambda e: e.activation(out=rst[:].rearrange("p g b -> p (g b)"), in_=bank(2, 16), func=AF.Sqrt, bias=eps1[:, 0:1], scale=1.0 / 512), reads=[B_ps[2], B_c], writes=[Brst])
        P.op("dve", lambda e: e.reciprocal(out=rst[:], in_=rst[:]), reads=[Brst], writes=[Brst])
        for g in range(4):
            P.op("dve", lambda e, g=g: e.tensor_tensor(out=fm[:, 4, g * 4:(g + 1) * 4, :], in0=fm[:, 4, g * 4:(g + 1) * 4, :],
                                                       in1=rst[:, g:g + 1, :].to_broadcast([128, 4, NS]), op=ALU.mult), reads=[Bfm, Brst], writes=[Bfm])
        P.op("dve", lambda e: e.tensor_tensor(out=ynb[:], in0=fm[:, 4], in1=ngc[:, :].unsqueeze(2).to_broadcast([128, 16, NS]), op=ALU.mult), reads=[Bfm, B_c], writes=[Bynb])
        for m in range(KC):
            so = state["nwo"] % 2
            state["nwo"] += 1
            P.dma("pool", lambda e, so=so, m=m: e.dma_start(
                out=wos2[:, so, :, :], in_=ssm_w_out_d[:, m * 128:(m + 1) * 128].rearrange("(fc p) n -> p fc n", p=128)), writes=[Bwo2[so]])
            b = nb(0, 2)
            for fc in range(16):
                P.op("pe", lambda e, so=so, fc=fc, b=b: e.matmul(bank(b, NS), wos2[:, so, fc, :], ynb[:, fc, :], start=(fc == 0), stop=(fc == 15)),
                     reads=[Bwo2[so], Bynb], writes=[B_ps[b]])
            P.op("dve", lambda e, m=m, b=b: e.scalar_tensor_tensor(
                out=x32[:, m, SEQ:SEQ + NS], in0=bank(b, NS), scalar=1.0 / ALPHA, in1=x32[:, m, SEQ:SEQ + NS], op0=ALU.mult, op1=ALU.add),
                reads=[B_ps[b], B_x32[m][4]], writes=[B_x32[m][4]])
        sb.release(mk)
        P.barrier()


    def dsa():
        mk = sb.mark()
        rope = sb.alloc([128, 17, 16], F32)
        eps1 = sb.alloc([128, 1], F32)
        rt = sb.alloc([128, 2, 5, 128], F32)
        kT = sb.alloc([64, 4, SEQ], BF16)
        ikT = sb.alloc([64, SEQ], BF16)
        vaug = sb.alloc([128, 16, 4, 65], BF16)
        snew = sb.alloc([128, NS, 576], BF16)
        mA = sb.mark()
        wkv = sb.alloc([128, KC, 512], BF16)
        wik = sb.alloc([128, KC, 64], BF16)
        knb = sb.alloc([128, 2, 64], F32)
        kv32 = sb.alloc([128, 2, 512], F32)
        ik32 = sb.alloc([128, 2, 64], F32)
        st6 = sb.alloc([128, 2, 8], F32)
        BkT = [Buf("kT") for _ in range(16)]
        Bva = [Buf("vaug") for _ in range(16)]
        P.op("dve", lambda e: e.memset(vaug[:], 1.0), writes=Bva)
        Bsnew = Buf("snew")
        P.op("dve", lambda e: e.memset(snew[:], 0.0), writes=[Bsnew])
        Bc = Buf("att_consts")
        Bkv = [Buf("kv32") for _ in range(2)]
        Bik = [Buf("ik32") for _ in range(2)]
        Brt = [Buf("rt") for _ in range(2)]
        P.dma("pool", lambda e: e.dma_start(out=wkv[:], in_=att_w_in_d[:, 1024:1536].rearrange("(kc p) n -> p kc n", p=128)), writes=[Bc])
        P.dma("pool", lambda e: e.dma_start(out=wik[:], in_=att_w_in_d[:, 2048:2112].rearrange("(kc p) n -> p kc n", p=128)), reads=[Bc], writes=[Bc])
        P.dma("sp", lambda e: e.dma_start(out=knb[:], in_=att_kn_bc_d[:, :, :]), reads=[Bc], writes=[Bc])
        P.dma("sp", lambda e: e.dma_start(out=rope[:], in_=rope_d[:, :, :]), reads=[Bc], writes=[Bc])
        P.op("dve", lambda e: e.memset(eps1[:], LN_EPS), reads=[Bc], writes=[Bc])
        allx = lambda kc: [B_xbf[kc][t] for t in range(5)]

        def rope_apply(xh, nh, c, si, Bx):
            cos = rope[:, c:c + 1, 0:8].to_broadcast([128, nh, 8])
            sin = rope[:, c:c + 1, 8:16].to_broadcast([128, nh, 8])
            x1, x2 = xh[:, :, 0:8], xh[:, :, 8:16]
            t = [rt[:, si, k, 0:nh * 8].rearrange("p (h d) -> p h d", h=nh) for k in range(5)]
            P.op("dve", lambda e: e.tensor_tensor(out=t[0], in0=x1, in1=cos, op=ALU.mult), reads=[Bx, Bc], writes=[Brt[si]])
            P.op("dve", lambda e: e.tensor_tensor(out=t[1], in0=x2, in1=sin, op=ALU.mult), reads=[Bx, Bc, Brt[si]], writes=[Brt[si]])
            P.op("dve", lambda e: e.tensor_tensor(out=t[2], in0=x2, in1=cos, op=ALU.mult), reads=[Bx, Bc, Brt[si]], writes=[Brt[si]])
            P.op("dve", lambda e: e.tensor_tensor(out=t[3], in0=x1, in1=sin, op=ALU.mult), reads=[Bx, Bc, Brt[si]], writes=[Brt[si]])
            P.op("dve", lambda e: e.tensor_tensor(out=x1, in0=t[0], in1=t[1], op=ALU.subtract), reads=[Brt[si], Bx], writes=[Bx])
            P.op("dve", lambda e: e.tensor_tensor(out=x2, in0=t[2], in1=t[3], op=ALU.add), reads=[Brt[si], Bx], writes=[Bx])

        for ch in range(17):
            si = ch % 2
            tk = ch * 128 if ch < 16 else NCOL - 128
            b0, b1 = 2 * si, 2 * si + 1
            for kc in range(KC):
                P.op("pe", lambda e, kc=kc, tk=tk, b0=b0: e.matmul(bank(b0, 512), xbf[:, kc, tk:tk + 128], wkv[:, kc, :], start=(kc == 0), stop=(kc == KC - 1)),
                     reads=[Bc] + allx(kc), writes=[B_ps[b0]])
            for kc in range(KC):
                P.op("pe", lambda e, kc=kc, tk=tk, b1=b1: e.matmul(bank(b1, 64), xbf[:, kc, tk:tk + 128], wik[:, kc, :], start=(kc == 0), stop=(kc == KC - 1)),
                     reads=[Bc] + allx(kc), writes=[B_ps[b1]])
            P.op("act", lambda e, si=si, b0=b0: e.activation(out=kv32[:, si, :], in_=bank(b0, 512), func=AF.Copy), reads=[B_ps[b0]], writes=[Bkv[si]])
            P.op("act", lambda e, si=si, b1=b1: e.activation(out=ik32[:, si, :], in_=bank(b1, 64), func=AF.Copy), reads=[B_ps[b1]], writes=[Bik[si]])
            P.op("dve", lambda e, si=si: e.bn_stats(out=st6[:, si, 0:6], in_=ik32[:, si, :]), reads=[Bik[si]], writes=[Brt[si]])
            P.op("dve", lambda e, si=si: e.bn_aggr(out=st6[:, si, 6:8], in_=st6[:, si, 0:6]), reads=[Brt[si]], writes=[Brt[si]])
            P.op("act", lambda e, si=si: e.activation(out=st6[:, si, 7:8], in_=st6[:, si, 7:8], func=AF.Sqrt, bias=eps1[:, 0:1], scale=1.0), reads=[Brt[si], Bc], writes=[Brt[si]])
            P.op("dve", lambda e, si=si: e.reciprocal(out=st6[:, si, 7:8], in_=st6[:, si, 7:8]), reads=[Brt[si]], writes=[Brt[si]])
            P.op("dve", lambda e, si=si: e.tensor_scalar(out=ik32[:, si, :], in0=ik32[:, si, :], scalar1=st6[:, si, 6:7], scalar2=st6[:, si, 7:8],
                                                         op0=ALU.subtract, op1=ALU.mult), reads=[Bik[si], Brt[si]], writes=[Bik[si]])
            P.op("dve", lambda e, si=si: e.tensor_tensor(out=ik32[:, si, :], in0=ik32[:, si, :], in1=knb[:, 0, :], op=ALU.mult), reads=[Bik[si], Bc], writes=[Bik[si]])
            P.op("dve", lambda e, si=si: e.tensor_tensor(out=ik32[:, si, :], in0=ik32[:, si, :], in1=knb[:, 1, :], op=ALU.add), reads=[Bik[si], Bc], writes=[Bik[si]])
            c = min(ch, 16)
            rope_apply(kv32[:, si, 0:256].rearrange("p (h d) -> p h d", h=4), 4, c, si, Bkv[si])
            rope_apply(ik32[:, si, :].rearrange("p (h d) -> p h d", h=1), 1, c, si, Bik[si])
            if ch < 16:
                for hk in range(5):
                    bt = 4 + hk % 4
                    src = kv32[:, si, hk * 64:(hk + 1) * 64] if hk < 4 else ik32[:, si, :]
                    P.op("pe", lambda e, src=src, bt=bt: e.transpose(bank(bt, 128)[0:64, :], src, ident[:, :]),
                         reads=[Bkv[si] if hk < 4 else Bik[si], B_const], writes=[B_ps[bt]])
                    dst = kT[:, hk, tk:tk + 128] if hk < 4 else ikT[:, tk:tk + 128]
                    P.op("act", lambda e, dst=dst, bt=bt: e.activation(out=dst, in_=bank(bt, 128)[0:64, :], func=AF.Copy), reads=[B_ps[bt], BkT[ch]], writes=[BkT[ch]])
                P.op("act", lambda e, si=si, ch=ch: e.activation(out=vaug[:, ch, :, 0:64], in_=kv32[:, si, 256:512].rearrange("p (h d) -> p h d", h=4), func=AF.Copy),
                     reads=[Bkv[si], Bva[ch]], writes=[Bva[ch]])
                P.dma("sp", lambda e, si=si, tk=tk: e.dma_start(out=k_p_d[tk:tk + 128, :], in_=kv32[:, si, 0:256]), reads=[Bkv[si]])
                P.dma("sp", lambda e, si=si, tk=tk: e.dma_start(out=v_p_d[tk:tk + 128, :], in_=kv32[:, si, 256:512]), reads=[Bkv[si]])
                P.dma("sp", lambda e, si=si, tk=tk: e.dma_start(out=ik_p_d[tk:tk + 128, :], in_=ik32[:, si, :]), reads=[Bik[si]])
            else:
                P.dma("sp", lambda e, si=si: e.dma_start(out=k_s_d[:, :], in_=kv32[124:128, si, 0:256]), reads=[Bkv[si]])
                P.dma("sp", lambda e, si=si: e.dma_start(out=v_s_d[:, :], in_=kv32[124:128, si, 256:512]), reads=[Bkv[si]])
                P.dma("sp", lambda e, si=si: e.dma_start(out=ik_s_d[:, :], in_=ik32[124:128, si, :]), reads=[Bik[si]])
                for b_ in range(NS):
                    P.dma("pool", lambda e, si=si, b_=b_: e.dma_start(out=snew[0:1, b_, 0:512], in_=kv32[124 + b_:125 + b_, si, :]), reads=[Bkv[si], Bsnew], writes=[Bsnew])
                    P.dma("pool", lambda e, si=si, b_=b_: e.dma_start(out=snew[0:1, b_, 512:576], in_=ik32[124 + b_:125 + b_, si, :]), reads=[Bik[si], Bsnew], writes=[Bsnew])

        P.barrier()
        sb.release(mA)
        mB = sb.mark()
        TOPK = 256
        wq = sb.alloc([128, KC, 1024], BF16)
        wiq = sb.alloc([128, KC, 520], BF16)
        wo_s = sb.alloc([128, 1, KC, 128], BF16)
        negm = sb.alloc([128, 128], F32)
        q32 = sb.alloc([128, 1024], F32)
        iq32 = sb.alloc([128, 520], F32)
        qT = sb.alloc([64, 16, 128], BF16)
        iqT = sb.alloc([64, 8, 128], BF16)
        score = sb.alloc([128, SEQ], F32)
        work = sb.alloc([128, SEQ], F32)
        m8 = sb.alloc([128, 8], F32)
        thr = sb.alloc([128, 1], F32)
        maskT = sb.alloc([128, 16, 128], BF16)
        Pt = sb.alloc([128, 2, 512], BF16)
        rden = sb.alloc([128, 16], F32)
        o32 = sb.alloc([128, 1024], F32)
        oT = sb.alloc([128, KC, 128], BF16)
        Bw2 = Buf("attw")
        Bwo = [Buf("wo_s") for _ in range(2)]
        Bq32, Biq32, BqT, BiqT, Bscore, Bwork, Bm8, Bthr, BmaskT, Brden, Bo32 = (Buf(n) for n in (
            "q32", "iq32", "qT", "iqT", "score", "work", "m8", "thr", "maskT", "rden", "o32"))
        BPt = [Buf("Pt") for _ in range(2)]
        BoT = [Buf("oT") for _ in range(KC)]
        P.dma("pool", lambda e: e.dma_start(out=wq[:], in_=att_w_in_d[:, 0:1024].rearrange("(kc p) n -> p kc n", p=128)), writes=[Bw2])
        P.dma("pool", lambda e: e.dma_start(out=wiq[:, :, 0:512], in_=att_w_in_d[:, 1536:2048].rearrange("(kc p) n -> p kc n", p=128)), reads=[Bw2], writes=[Bw2])
        P.dma("pool", lambda e: e.dma_start(out=wiq[:, :, 512:520], in_=att_w_in_d[:, 2112:2120].rearrange("(kc p) n -> p kc n", p=128)), reads=[Bw2], writes=[Bw2])
        P.dma("sp", lambda e: e.dma_start(out=negm[:], in_=negm_d[:, :]), reads=[Bw2], writes=[Bw2])
        st2 = {"cnt": 0, "nwo": 0}

        def nb2(lo, n):
            b = lo + st2["cnt"] % n
            st2["cnt"] += 1
            return b

        for qi in range(16):
            tk = qi * 128
            nk = qi + 1
            W = nk * 128
            ti = tk // 512
            for half in range(2):
                b = nb2(0, 4)
                for kc in range(KC):
                    P.op("pe", lambda e, kc=kc, tk=tk, half=half, b=b: e.matmul(bank(b, 512), xbf[:, kc, tk:tk + 128], wq[:, kc, half * 512:(half + 1) * 512],
                                                                            start=(kc == 0), stop=(kc == KC - 1)), reads=[Bw2] + allx(kc), writes=[B_ps[b]])
                P.op("act", lambda e, half=half, b=b: e.activation(out=q32[:, half * 512:(half + 1) * 512], in_=bank(b, 512), func=AF.Copy), reads=[B_ps[b]], writes=[Bq32])
            b = nb2(0, 4)
            for kc in range(KC):
                P.op("pe", lambda e, kc=kc, tk=tk, b=b: e.matmul(bank(b, 512), xbf[:, kc, tk:tk + 128], wiq[:, kc, 0:512], start=(kc == 0), stop=(kc == KC - 1)),
                     reads=[Bw2] + allx(kc), writes=[B_ps[b]])
            P.op("act", lambda e, b=b: e.activation(out=iq32[:, 0:512], in_=bank(b, 512), func=AF.Copy), reads=[B_ps[b]], writes=[Biq32])
            b = nb2(0, 4)
            for kc in range(KC):
                P.op("pe", lambda e, kc=kc, tk=tk, b=b: e.matmul(bank(b, 8), xbf[:, kc, tk:tk + 128], wiq[:, kc, 512:520], start=(kc == 0), stop=(kc == KC - 1)),
                     reads=[Bw2] + allx(kc), writes=[B_ps[b]])
            P.op("act", lambda e, b=b: e.activation(out=iq32[:, 512:520], in_=bank(b, 8), func=AF.Copy, scale=float(8 ** -0.5 * 64 ** -0.5)), reads=[B_ps[b], Biq32], writes=[Biq32])
            rope_apply(q32[:, :].rearrange("p (h d) -> p h d", h=16), 16, qi, 0, Bq32)
            rope_apply(iq32[:, 0:512].rearrange("p (h d) -> p h d", h=8), 8, qi, 1, Biq32)
            for h in range(24):
                bt = 4 + h % 4
                src = q32[:, h * 64:(h + 1) * 64] if h < 16 else iq32[:, (h - 16) * 64:(h - 15) * 64]
                dst = qT[:, h, :] if h < 16 else iqT[:, h - 16, :]
                P.op("pe", lambda e, src=src, bt=bt: e.transpose(bank(bt, 128)[0:64, :], src, ident[:, :]), reads=[Bq32 if h < 16 else Biq32, B_const], writes=[B_ps[bt]])
                P.op("act", lambda e, dst=dst, bt=bt: e.activation(out=dst, in_=bank(bt, 128)[0:64, :], func=AF.Copy),
                     reads=[B_ps[bt], BqT if h < 16 else BiqT], writes=[BqT if h < 16 else BiqT])
            for h in range(8):
                b0 = 4 * (h % 2)
                for c4 in range((W + 511) // 512):
                    w = min(512, W - c4 * 512)
                    P.op("pe", lambda e, h=h, c4=c4, w=w, b0=b0: e.matmul(bank(b0 + c4, w), iqT[:, h, :], ikT[:, c4 * 512:c4 * 512 + w], start=True, stop=True),
                         reads=[BiqT] + BkT[c4 * 4:c4 * 4 + 4], writes=[B_ps[b0 + c4]])
                P.op("act", lambda e, b0=b0, W=W: e.activation(out=work[:, 0:W], in_=psum[:, b0 * 512:b0 * 512 + W], func=AF.Relu),
                     reads=[B_ps[b0 + c] for c in range(4)] + [Bwork], writes=[Bwork])
                if h == 0:
                    P.op("dve", lambda e, W=W: e.tensor_scalar(out=score[:, 0:W], in0=work[:, 0:W], scalar1=iq32[:, 512:513], scalar2=None, op0=ALU.mult),
                         reads=[Bwork, Biq32, Bscore], writes=[Bscore])
                else:
                    P.op("dve", lambda e, W=W, h=h: e.scalar_tensor_tensor(out=score[:, 0:W], in0=work[:, 0:W], scalar=iq32[:, 512 + h:513 + h], in1=score[:, 0:W],
                                                                           op0=ALU.mult, op1=ALU.add), reads=[Bwork, Biq32, Bscore], writes=[Bscore])
            P.op("dve", lambda e, tk=tk: e.tensor_tensor(out=score[:, tk:tk + 128], in0=score[:, tk:tk + 128], in1=negm[:, :], op=ALU.add), reads=[Bscore, Bw2], writes=[Bscore])
            if W <= TOPK:
                P.op("dve", lambda e: e.memset(thr[:], -1e29), reads=[Bthr], writes=[Bthr])
            else:
                P.op("act", lambda e, W=W: e.activation(out=work[:, 0:W], in_=score[:, 0:W], func=AF.Copy), reads=[Bscore, Bwork], writes=[Bwork])
                for r in range(TOPK // 8):
                    P.op("dve", lambda e, W=W: e.max(out=m8[:, :], in_=work[:, 0:W]), reads=[Bwork, Bm8], writes=[Bm8])
                    if r < TOPK // 8 - 1:
                        P.op("dve", lambda e, W=W: e.match_replace(out=work[:, 0:W], in_to_replace=m8[:, :], in_values=work[:, 0:W], imm_value=-1e30),
                             reads=[Bwork, Bm8], writes=[Bwork])
                P.op("dve", lambda e: e.tensor_scalar(out=thr[:], in0=m8[:, 7:8], scalar1=-1e29, scalar2=None, op0=ALU.max), reads=[Bm8, Bthr], writes=[Bthr])
            P.op("dve", lambda e, W=W: e.tensor_scalar(out=work[:, 0:W], in0=score[:, 0:W], scalar1=thr[:, 0:1], scalar2=None, op0=ALU.is_ge),
                 reads=[Bscore, Bthr, Bwork], writes=[Bwork])
            for kc in range(nk):
                bt = 4 + kc % 4
                P.op("pe", lambda e, kc=kc, bt=bt: e.transpose(bank(bt, 128), work[:, kc * 128:(kc + 1) * 128], ident[:, :]), reads=[Bwork, B_const], writes=[B_ps[bt]])
                P.op("act", lambda e, kc=kc, bt=bt: e.activation(out=maskT[:, kc, :], in_=bank(bt, 128), func=AF.Copy), reads=[B_ps[bt], BmaskT], writes=[BmaskT])
            for kvh in range(4):
                for kc in range(nk):
                    sb_ = nb2(0, 2)
                    pi = kc % 2
                    P.op("pe", lambda e, kvh=kvh, kc=kc, sb_=sb_: e.matmul(bank(sb_, 512), kT[:, kvh, kc * 128:(kc + 1) * 128],
                                                                          qT[:, kvh * 4:(kvh + 1) * 4, :].rearrange("p h q -> p (h q)"), start=True, stop=True),
                         reads=[BkT[kc], BqT], writes=[B_ps[sb_]])
                    P.op("act", lambda e, pi=pi, sb_=sb_: e.activation(out=Pt[:, pi, :], in_=bank(sb_, 512), func=AF.Exp, scale=0.125), reads=[B_ps[sb_], BPt[pi]], writes=[BPt[pi]])
                    P.op("dve", lambda e, pi=pi, kc=kc: e.tensor_tensor(out=Pt[:, pi, :].rearrange("p (h q) -> p h q", h=4), in0=Pt[:, pi, :].rearrange("p (h q) -> p h q", h=4),
                                                                        in1=maskT[:, kc:kc + 1, :].to_broadcast([128, 4, 128]), op=ALU.mult), reads=[BPt[pi], BmaskT], writes=[BPt[pi]])
                    for hq in range(4):
                        P.op("pe", lambda e, pi=pi, hq=hq, kc=kc, kvh=kvh, nk=nk: e.matmul(bank(4 + hq, 65), Pt[:, pi, hq * 128:(hq + 1) * 128], vaug[:, kc, kvh, :],
                                                                                     start=(kc == 0), stop=(kc == nk - 1)), reads=[BPt[pi], Bva[kc]], writes=[B_ps[4 + hq]])
                for hq in range(4):
                    h = kvh * 4 + hq
                    P.op("dve", lambda e, h=h, hq=hq: e.reciprocal(out=rden[:, h:h + 1], in_=bank(4 + hq, 65)[:, 64:65]), reads=[B_ps[4 + hq], Brden], writes=[Brden])
                    P.op("dve", lambda e, h=h, hq=hq: e.tensor_scalar(out=o32[:, h * 64:(h + 1) * 64], in0=bank(4 + hq, 65)[:, 0:64], scalar1=rden[:, h:h + 1], scalar2=None, op0=ALU.mult),
                         reads=[B_ps[4 + hq], Brden, Bo32], writes=[Bo32])
            for c in range(KC):
                bt = nb2(0, 4)
                P.op("pe", lambda e, c=c, bt=bt: e.transpose(bank(bt, 128), o32[:, c * 128:(c + 1) * 128], ident[:, :]), reads=[Bo32, B_const], writes=[B_ps[bt]])
                P.op("act", lambda e, c=c, bt=bt: e.activation(out=oT[:, c, :], in_=bank(bt, 128), func=AF.Copy), reads=[B_ps[bt]], writes=[BoT[c]])
            for m in range(KC):
                so = 0
                P.dma("pool", lambda e, so=so, m=m: e.dma_start(out=wo_s[:, so, :, :], in_=att_w_out_d[:, m * 128:(m + 1) * 128].rearrange("(kc p) n -> p kc n", p=128)), writes=[Bwo[so]])
                b = nb2(0, 4)
                for c in range(KC):
                    P.op("pe", lambda e, so=so, c=c, b=b: e.matmul(bank(b, 128), wo_s[:, so, c, :], oT[:, c, :], start=(c == 0), stop=(c == KC - 1)),
                         reads=[Bwo[so], BoT[c]], writes=[B_ps[b]])
                P.op("dve", lambda e, m=m, b=b, tk=tk: e.scalar_tensor_tensor(out=x32[:, m, tk:tk + 128], in0=bank(b, 128), scalar=1.0 / ALPHA, in1=x32[:, m, tk:tk + 128],
                                                                             op0=ALU.mult, op1=ALU.add), reads=[B_ps[b], B_x32[m][ti]], writes=[B_x32[m][ti]])

        P.barrier()
        sb.release(mB)
        NPG = 65
        GP = 16
        selc = sb.alloc([128, NS, 128], F32)
        onesc = sb.alloc([128, 128], F32)
        negp = sb.alloc([128, 1], F32)
        pio = sb.alloc([128, 1], F32)
        ptb = sb.alloc([128, NS * 64], I32)
        idx = sb.alloc([128, NS * 64], I32)
        qs32 = sb.alloc([128, 1024], F32)
        iqs32 = sb.alloc([128, 520], F32)
        q_bc = sb.alloc([128, 1024], F32)
        iq_bc = sb.alloc([128, 520], F32)
        sc_km = sb.alloc([128, NS, NPG], F32)
        mk_km = sb.alloc([128, NS, NPG], F32)
        thr_bc = sb.alloc([128, NS], F32)
        oTs = sb.alloc([64, 16, NS], BF16)
        Bcw, Bcc, Bidx, Bqs, Biqs, Bqbc, Biqbc, Bikg, Bkg, Bvg, Bvag, Btmpc, Bshh, Bsc, Bmk, Bflat, Bm8c, Bthrc, BS, BPk, Bop, Brdc, BoTs = (
            Buf(n) for n in ("wq_c", "cconst", "idx", "qs32", "iqs32", "q_bc", "iq_bc", "ikg", "kg", "vg", "vag", "tmpc", "shh", "sc_km", "mk_km",
                             "flat", "m8c", "thr_bc", "S_km", "P_km", "o_pad", "rdc", "oTs"))
        mC = sb.mark()
        wq_c = sb.alloc([128, KC, 1024], BF16)
        wiq_c = sb.alloc([128, KC, 520], BF16)
        P.dma("pool", lambda e: e.dma_start(out=wq_c[:], in_=att_w_in_d[:, 0:1024].rearrange("(kc p) n -> p kc n", p=128)), writes=[Bcw])
        P.dma("pool", lambda e: e.dma_start(out=wiq_c[:, :, 0:512], in_=att_w_in_d[:, 1536:2048].rearrange("(kc p) n -> p kc n", p=128)), reads=[Bcw], writes=[Bcw])
        P.dma("pool", lambda e: e.dma_start(out=wiq_c[:, :, 512:520], in_=att_w_in_d[:, 2112:2120].rearrange("(kc p) n -> p kc n", p=128)), reads=[Bcw], writes=[Bcw])
        for dst, src in ((selc, ssm_sel_d), (onesc, onesf_d), (negp, negp_d), (pio, piota_d), (ptb, pt_bc_d)):
            P.dma("sp", lambda e, dst=dst, src=src: e.dma_start(out=dst[:], in_=src), reads=[Bcc], writes=[Bcc])
        P.op("dve", lambda e: e.tensor_scalar(out=idx[:], in0=ptb[:], scalar1=128.0, scalar2=pio[:, 0:1], op0=ALU.mult, op1=ALU.add), reads=[Bcc], writes=[Bidx])
        TOKS = NCOL - 128
        for half in range(2):
            b = nb2(0, 4)
            for kc in range(KC):
                P.op("pe", lambda e, kc=kc, half=half, b=b: e.matmul(bank(b, 512), xbf[:, kc, TOKS:TOKS + 128], wq_c[:, kc, half * 512:(half + 1) * 512],
                                                                  start=(kc == 0), stop=(kc == KC - 1)), reads=[Bcw] + allx(kc), writes=[B_ps[b]])
            P.op("act", lambda e, half=half, b=b: e.activation(out=qs32[:, half * 512:(half + 1) * 512], in_=bank(b, 512), func=AF.Copy), reads=[B_ps[b], Bqs], writes=[Bqs])
        b = nb2(0, 4)
        for kc in range(KC):
            P.op("pe", lambda e, kc=kc, b=b: e.matmul(bank(b, 512), xbf[:, kc, TOKS:TOKS + 128], wiq_c[:, kc, 0:512], start=(kc == 0), stop=(kc == KC - 1)),
                 reads=[Bcw] + allx(kc), writes=[B_ps[b]])
        P.op("act", lambda e, b=b: e.activation(out=iqs32[:, 0:512], in_=bank(b, 512), func=AF.Copy), reads=[B_ps[b], Biqs], writes=[Biqs])
        b = nb2(0, 4)
        for kc in range(KC):
            P.op("pe", lambda e, kc=kc, b=b: e.matmul(bank(b, 8), xbf[:, kc, TOKS:TOKS + 128], wiq_c[:, kc, 512:520], start=(kc == 0), stop=(kc == KC - 1)),
                 reads=[Bcw] + allx(kc), writes=[B_ps[b]])
        P.op("act", lambda e, b=b: e.activation(out=iqs32[:, 512:520], in_=bank(b, 8), func=AF.Copy, scale=float(8 ** -0.5 * 64 ** -0.5)), reads=[B_ps[b], Biqs], writes=[Biqs])
        rope_apply(qs32[:, :].rearrange("p (h d) -> p h d", h=16), 16, 16, 0, Bqs)
        rope_apply(iqs32[:, 0:512].rearrange("p (h d) -> p h d", h=8), 8, 16, 1, Biqs)

        def gather(dst, cache_d, b_, j0, nj, Bd, src_off, width):
            for jj in range(nj):
                j = j0 + jj
                if j < 64:
                    c = b_ * 64 + j
                    P.dma("pool", lambda e, jj=jj, c=c: e.indirect_dma_start(
                        out=dst[:, jj, :], out_offset=None, in_=cache_d[:, :], in_offset=bass.IndirectOffsetOnAxis(ap=idx[:, c:c + 1], axis=0)),
                        reads=[Bidx, Bd], writes=[Bd])
                else:
                    P.op("act", lambda e, jj=jj: e.activation(out=dst[:, jj, :], in_=snew[:, b_, src_off:src_off + width], func=AF.Copy), reads=[Bsnew, Bd], writes=[Bd])

        groups = [(0, 16), (16, 16), (32, 16), (48, 16), (64, 1)]
        P.barrier()
        sb.release(mC)
        ikg = sb.alloc([128, GP, 64], F32)
        tmpc = sb.alloc([128, GP * 64], F32)
        shh = sb.alloc([128, GP], F32)
        for b_ in range(NS):
            for bank_i, w in ((0, 512), (1, 8)):
                P.op("pe", lambda e, b_=b_, bank_i=bank_i, w=w: e.matmul(bank(bank_i, w), selc[:, b_, :], iqs32[:, bank_i * 512:bank_i * 512 + w], start=True, stop=True),
                     reads=[Bcc, Biqs], writes=[B_ps[bank_i]])
            P.op("act", lambda e: e.activation(out=iq_bc[:, 0:512], in_=bank(0, 512), func=AF.Copy), reads=[B_ps[0], Biqbc], writes=[Biqbc])
            P.op("act", lambda e: e.activation(out=iq_bc[:, 512:520], in_=bank(1, 8), func=AF.Copy), reads=[B_ps[1], Biqbc], writes=[Biqbc])
            for (j0, nj) in groups:
                gather(ikg, cache_ik_d, b_, j0, nj, Bikg, 512, 64)
                for h in range(8):
                    P.op("dve", lambda e, h=h, nj=nj: e.tensor_tensor(out=tmpc[:, 0:nj * 64].rearrange("p (j d) -> p j d", j=nj), in0=ikg[:, 0:nj, :],
                                                                  in1=iq_bc[:, h * 64:(h + 1) * 64].unsqueeze(1).to_broadcast([128, nj, 64]), op=ALU.mult),
                         reads=[Bikg, Biqbc, Btmpc], writes=[Btmpc])
                    P.op("dve", lambda e, nj=nj: e.tensor_reduce(out=shh[:, 0:nj], in_=tmpc[:, 0:nj * 64].rearrange("p (j d) -> p j d", j=nj), axis=AX.X, op=ALU.add),
                         reads=[Btmpc, Bshh], writes=[Bshh])
                    P.op("act", lambda e, nj=nj: e.activation(out=shh[:, 0:nj], in_=shh[:, 0:nj], func=AF.Relu), reads=[Bshh], writes=[Bshh])
                    if h == 0:
                        P.op("dve", lambda e, nj=nj, j0=j0, b_=b_: e.tensor_scalar(out=sc_km[:, b_, j0:j0 + nj], in0=shh[:, 0:nj], scalar1=iq_bc[:, 512:513], scalar2=None, op0=ALU.mult),
                             reads=[Bshh, Biqbc, Bsc], writes=[Bsc])
                    else:
                        P.op("dve", lambda e, nj=nj, j0=j0, b_=b_, h=h: e.scalar_tensor_tensor(out=sc_km[:, b_, j0:j0 + nj], in0=shh[:, 0:nj], scalar=iq_bc[:, 512 + h:513 + h],
                                                                                            in1=sc_km[:, b_, j0:j0 + nj], op0=ALU.mult, op1=ALU.add), reads=[Bshh, Biqbc, Bsc], writes=[Bsc])
            P.op("dve", lambda e, b_=b_: e.tensor_tensor(out=sc_km[:, b_, 64:65], in0=sc_km[:, b_, 64:65], in1=negp[:, 0:1], op=ALU.add), reads=[Bsc, Bcc], writes=[Bsc])
            P.dma("sp", lambda e, b_=b_: e.dma_start(out=scr_d[b_, :].rearrange("(p j) -> p j", j=NPG), in_=sc_km[:, b_, :]), reads=[Bsc], writes=[Bflat])
        P.barrier()
        sb.release(mC)
        flat = sb.alloc([NS, NPG * 128], F32)
        m8c = sb.alloc([NS, 8], F32)
        r4 = sb.alloc([NS, NS], F32)
        P.dma("sp", lambda e: e.dma_start(out=flat[:, :], in_=scr_d[:, :]), reads=[Bflat], writes=[Bflat])
        for r in range(TOPK // 8):
            P.op("dve", lambda e: e.max(out=m8c[:, :], in_=flat[:, :]), reads=[Bflat, Bm8c], writes=[Bm8c])
            if r < TOPK // 8 - 1:
                P.op("dve", lambda e: e.match_replace(out=flat[:, :], in_to_replace=m8c[:, :], in_values=flat[:, :], imm_value=-1e30), reads=[Bflat, Bm8c], writes=[Bflat])
        P.op("dve", lambda e: e.tensor_scalar(out=r4[:, :], in0=ident[0:NS, 0:NS], scalar1=m8c[:, 7:8], scalar2=None, op0=ALU.mult), reads=[Bm8c, B_const], writes=[Bthrc])
        P.op("pe", lambda e: e.matmul(bank(2, NS), onesc[0:NS, :], r4[:, :], start=True, stop=True), reads=[Bcc, Bthrc], writes=[B_ps[2]])
        P.op("act", lambda e: e.activation(out=thr_bc[:, :], in_=bank(2, NS), func=AF.Copy), reads=[B_ps[2], Bthrc], writes=[Bthrc])
        for b_ in range(NS):
            P.op("dve", lambda e, b_=b_: e.tensor_scalar(out=mk_km[:, b_, :], in0=sc_km[:, b_, :], scalar1=thr_bc[:, b_:b_ + 1], scalar2=None, op0=ALU.is_ge),
                 reads=[Bsc, Bthrc, Bmk], writes=[Bmk])
        P.barrier()
        sb.release(mC)
        kg = sb.alloc([128, GP, 256], F32)
        vg = sb.alloc([128, GP, 256], F32)
        vag = sb.alloc([128, GP, 4, 65], BF16)
        tmpc2 = sb.alloc([128, GP * 64], F32)
        S_km = sb.alloc([128, GP, 16], F32)
        P_km = sb.alloc([128, GP, 16], BF16)
        o_pad = sb.alloc([128, 4, 64], F32)
        rdc = sb.alloc([NS, 4], F32)
        Btmpc2 = Buf("tmpc2")
        P.op("dve", lambda e: e.memset(vag[:], 1.0), writes=[Bvag])
        P.op("dve", lambda e: e.memset(o_pad[:], 0.0), writes=[Bop])
        for b_ in range(NS):
            for half in range(2):
                P.op("pe", lambda e, b_=b_, half=half: e.matmul(bank(half, 512), selc[:, b_, :], qs32[:, half * 512:(half + 1) * 512], start=True, stop=True),
                     reads=[Bcc, Bqs], writes=[B_ps[half]])
                P.op("act", lambda e, half=half: e.activation(out=q_bc[:, half * 512:(half + 1) * 512], in_=bank(half, 512), func=AF.Copy), reads=[B_ps[half], Bqbc], writes=[Bqbc])
            for gi, (j0, nj) in enumerate(groups):
                gather(kg, cache_k_d, b_, j0, nj, Bkg, 0, 256)
                gather(vg, cache_v_d, b_, j0, nj, Bvg, 256, 256)
                P.op("act", lambda e, nj=nj: e.activation(out=vag[:, 0:nj, :, 0:64], in_=vg[:, 0:nj, :].rearrange("p j (h d) -> p j h d", h=4), func=AF.Copy),
                     reads=[Bvg, Bvag], writes=[Bvag])
                for h in range(16):
                    kvh = h // 4
                    P.op("dve", lambda e, h=h, kvh=kvh, nj=nj: e.tensor_tensor(out=tmpc2[:, 0:nj * 64].rearrange("p (j d) -> p j d", j=nj), in0=kg[:, 0:nj, kvh * 64:(kvh + 1) * 64],
                                                                           in1=q_bc[:, h * 64:(h + 1) * 64].unsqueeze(1).to_broadcast([128, nj, 64]), op=ALU.mult),
                         reads=[Bkg, Bqbc, Btmpc2], writes=[Btmpc2])
                    P.op("dve", lambda e, h=h, nj=nj: e.tensor_reduce(out=S_km[:, 0:nj, h], in_=tmpc2[:, 0:nj * 64].rearrange("p (j d) -> p j d", j=nj), axis=AX.X, op=ALU.add),
                         reads=[Btmpc2, BS], writes=[BS])
                P.op("act", lambda e, nj=nj: e.activation(out=S_km[:, 0:nj, :], in_=S_km[:, 0:nj, :], func=AF.Exp, scale=0.125), reads=[BS], writes=[BS])
                P.op("dve", lambda e, nj=nj, j0=j0, b_=b_: e.tensor_tensor(out=P_km[:, 0:nj, :], in0=S_km[:, 0:nj, :],
                                                                       in1=mk_km[:, b_, j0:j0 + nj].unsqueeze(2).to_broadcast([128, nj, 16]), op=ALU.mult),
                     reads=[BS, Bmk, BPk], writes=[BPk])
                for jj in range(nj):
                    for kvh in range(4):
                        first = (gi == 0 and jj == 0)
                        last = (gi == len(groups) - 1 and jj == nj - 1)
                        P.op("pe", lambda e, jj=jj, kvh=kvh, first=first, last=last: e.matmul(bank(4 + kvh, 65)[0:4, :], P_km[:, jj, kvh * 4:(kvh + 1) * 4], vag[:, jj, kvh, :],
                                                                                        start=first, stop=last), reads=[BPk, Bvag], writes=[B_ps[4 + kvh]])
            for kvh in range(4):
                P.op("dve", lambda e, kvh=kvh: e.reciprocal(out=rdc[:, kvh:kvh + 1], in_=bank(4 + kvh, 65)[0:4, 64:65]), reads=[B_ps[4 + kvh], Brdc], writes=[Brdc])
                P.op("dve", lambda e, kvh=kvh: e.tensor_scalar(out=o_pad[0:4, kvh, :], in0=bank(4 + kvh, 65)[0:4, 0:64], scalar1=rdc[:, kvh:kvh + 1], scalar2=None, op0=ALU.mult),
                     reads=[B_ps[4 + kvh], Brdc, Bop], writes=[Bop])
            for kvh in range(4):
                bt = nb2(0, 4)
                P.op("pe", lambda e, kvh=kvh, bt=bt: e.transpose(bank(bt, 128)[0:64, :], o_pad[:, kvh, :], ident[:, :]), reads=[Bop, B_const], writes=[B_ps[bt]])
                P.op("act", lambda e, kvh=kvh, bt=bt, b_=b_: e.activation(out=oTs[:, kvh * 4:(kvh + 1) * 4, b_], in_=bank(bt, 128)[0:64, 0:4], func=AF.Copy),
                     reads=[B_ps[bt], BoTs], writes=[BoTs])
        P.barrier()
        sb.release(mC)
        woh = sb.alloc([64, 16, D], BF16)
        Bwoh = Buf("woh")
        P.dma("pool", lambda e: e.dma_start(out=woh[:], in_=att_w_out_d[:, :].rearrange("(h p) n -> p h n", p=64)), writes=[Bwoh])
        for m in range(KC):
            b = nb2(0, 4)
            for h in range(16):
                P.op("pe", lambda e, m=m, h=h, b=b: e.matmul(bank(b, NS), woh[:, h, m * 128:(m + 1) * 128], oTs[:, h, :], start=(h == 0), stop=(h == 15)),
                     reads=[Bwoh, BoTs], writes=[B_ps[b]])
            P.op("dve", lambda e, m=m, b=b: e.scalar_tensor_tensor(out=x32[:, m, SEQ:SEQ + NS], in0=bank(b, NS), scalar=1.0 / ALPHA, in1=x32[:, m, SEQ:SEQ + NS],
                                                                 op0=ALU.mult, op1=ALU.add), reads=[B_ps[b], B_x32[m][4]], writes=[B_x32[m][4]])
        sb.release(mk)
        P.barrier()


    def rwkv_sample():
        mk = sb.mark()
        R0, R1 = 96, 128
        TOK = NCOL - 128
        muc = sb.alloc([128, 48], F32)
        gnc = sb.alloc([128, 16], F32)
        vec = sb.alloc([128, D], F32)
        Bvec = Buf("rwvec")
        selr = sb.alloc([128, NS, 128], F32)
        blk = sb.alloc([128, 128], F32)
        epsg = sb.alloc([128, 1], F32)
        xsh = sb.alloc([128, KC, NS], F32)
        xmp = sb.alloc([128, 6, KC, 128], BF16)
        tm = sb.alloc([128, 6, D], F32)
        wsl = sb.alloc([128, 1, KC, 512], BF16)
        w1s = sb.alloc([128, KC, 256], BF16)
        w2s = sb.alloc([128, 3, D], BF16)
        t1T = sb.alloc([128, 3, 128], BF16)
        fmv = sb.alloc([128, 6, KC, NS], F32)
        wk = sb.alloc([128, 3, D], F32)
        hs = sb.alloc([128, 2, 16], F32)
        bc = sb.alloc([128, 5, D], F32)
        S = sb.alloc([128, KC, 64], F32)
        tS = sb.alloc([128, KC, 64], F32)
        sa = sb.alloc([128, 2, KC], F32)
        zb = sb.alloc([128, KC, NS], BF16)
        Bk, Bxsh, Bxmp, Btm, Bw1, Bt1, Bfm, Bwk, Bhs, Bbc, BS, BtS, Bsa, Bzb = (Buf(n) for n in (
            "rwc", "xsh", "xmp", "tm", "w1s", "t1T", "fmv", "wk", "hs", "bc", "S", "tS", "sa", "zb"))
        Bwsl = [Buf("wsl") for _ in range(2)]
        stg = sb.alloc([128, 128], F32)
        Bstg = Buf("stg")
        for dst, src, rows in ((muc, rw_mu_c_d, 48), (gnc, rw_gn_c_d, 16)):
            P.op("dve", lambda e: e.memset(stg[:, :], 0.0), writes=[Bstg])
            P.dma("sp", lambda e, src=src, rows=rows: e.dma_start(out=stg[:rows, :], in_=src[:, :]), writes=[Bstg])
            P.op("pe", lambda e: e.transpose(bank(7, 128), stg[:, :], ident[:, :]), reads=[Bstg, B_const], writes=[B_ps[7]])
            P.op("dve", lambda e, dst=dst, rows=rows: e.tensor_copy(out=dst[:, 0:rows], in_=bank(7, rows)), reads=[B_ps[7], Bk], writes=[Bk])
        P.dma("sp", lambda e: e.dma_start(out=selr[:], in_=ssm_sel_d[:, :, :]), reads=[Bk], writes=[Bk])
        P.dma("sp", lambda e: e.dma_start(out=blk[:], in_=rw_blk_d[:, :]), reads=[Bk], writes=[Bk])
        P.op("dve", lambda e: e.memset(epsg[:], 64e-5), reads=[Bk], writes=[Bk])
        P.dma("pool", lambda e: e.dma_start(out=w1s[:, :, 0:64], in_=rw_w1_d[:, :].rearrange("(kc p) n -> p kc n", p=128)), writes=[Bw1])
        P.dma("pool", lambda e: e.dma_start(out=w1s[:, :, 64:128], in_=rw_a1_d[:, :].rearrange("(kc p) n -> p kc n", p=128)), reads=[Bw1], writes=[Bw1])
        P.dma("pool", lambda e: e.dma_start(out=w1s[:, :, 128:256], in_=rw_g1_d[:, :].rearrange("(kc p) n -> p kc n", p=128)), reads=[Bw1], writes=[Bw1])
        P.dma("pool", lambda e: e.dma_start(out=w2s[0:64, 0, :], in_=rw_w2_d[:, :]), reads=[Bw1], writes=[Bw1])
        P.dma("pool", lambda e: e.dma_start(out=w2s[0:64, 1, :], in_=rw_a2_d[:, :]), reads=[Bw1], writes=[Bw1])
        P.dma("pool", lambda e: e.dma_start(out=w2s[:, 2, :], in_=rw_g2_d[:, :]), reads=[Bw1], writes=[Bw1])
        P.op("dve", lambda e: e.memset(stg[:, :], 0.0), writes=[Bstg])
        for c in range(KC):
            bt = 4 + c % 4
            P.dma("sp", lambda e, c=c: e.dma_start(out=stg[0:NS, :], in_=st_shift_d[:, c * 128:(c + 1) * 128]), writes=[Bstg])
            P.op("pe", lambda e, bt=bt: e.transpose(bank(bt, 128), stg[:, :], ident[:, :]), reads=[Bstg, B_const], writes=[B_ps[bt]])
            P.op("dve", lambda e, c=c, bt=bt: e.tensor_tensor(out=xsh[:, c, :], in0=bank(bt, 128)[:, 0:NS], in1=x32[:, c, SEQ:SEQ + NS], op=ALU.subtract),
                 reads=[B_ps[bt], B_x32[c][4], Bxsh], writes=[Bxsh])
        P.op("dve", lambda e: e.memset(xmp[:], 0.0), writes=[Bxmp])
        P.op("dve", lambda e: e.memset(tm[:], 0.0), writes=[Btm])
        P.op("dve", lambda e: e.memset(wk[:], 0.0), writes=[Bwk])
        xs4 = x32[:, :, SEQ:SEQ + NS]
        for j in range(6):
            P.op("dve", lambda e, j=j: e.tensor_tensor(out=fmv[:, 4], in0=xsh[:, :, :], in1=muc[:, j * 8:(j + 1) * 8].unsqueeze(2).to_broadcast([128, KC, NS]), op=ALU.mult),
                 reads=[Bxsh, Bk, Bfm], writes=[Bfm])
            P.op("dve", lambda e, j=j: e.tensor_tensor(out=xmp[:, j, :, 124:128], in0=fmv[:, 4], in1=xs4, op=ALU.add),
                 reads=[Bfm, Bxmp] + [B_x32[c][4] for c in range(KC)], writes=[Bxmp])
        st3 = {"cnt": 0, "nw": 0}

        def nb3(lo, n):
            b = lo + st3["cnt"] % n
            st3["cnt"] += 1
            return b
        for ti_, (nm, jx) in enumerate((("r", 0), ("k", 2), ("v", 3))):
            for half in range(2):
                sl = 0
                P.dma("pool", lambda e, sl=sl, nm=nm, half=half: e.dma_start(out=wsl[:, sl, :, :], in_=rw_w_d[nm][:, half * 512:(half + 1) * 512].rearrange("(kc p) n -> p kc n", p=128)),
                      writes=[Bwsl[sl]])
                b = nb3(0, 4)
                for kc in range(KC):
                    P.op("pe", lambda e, kc=kc, jx=jx, sl=sl, b=b: e.matmul(bank(b, 512), xmp[:, jx, kc, :], wsl[:, sl, kc, :], start=(kc == 0), stop=(kc == KC - 1)),
                         reads=[Bxmp, Bwsl[sl]], writes=[B_ps[b]])
                P.op("act", lambda e, ti_=ti_, half=half, b=b: e.activation(out=tm[R0:R1, ti_, half * 512:(half + 1) * 512], in_=bank(b, 512)[R0:R1, :], func=AF.Copy),
                     reads=[B_ps[b], Btm], writes=[Btm])
        for li, (jx, c0, M, fn) in enumerate(((1, 0, 64, AF.Tanh), (4, 64, 64, AF.Copy), (5, 128, 128, AF.Sigmoid))):
            b = nb3(4, 4)
            for kc in range(KC):
                P.op("pe", lambda e, kc=kc, jx=jx, c0=c0, M=M, b=b: e.matmul(bank(b, 128)[0:M, :], w1s[:, kc, c0:c0 + M], xmp[:, jx, kc, :], start=(kc == 0), stop=(kc == KC - 1)),
                     reads=[Bw1, Bxmp], writes=[B_ps[b]])
            P.op("act", lambda e, li=li, M=M, fn=fn, b=b: e.activation(out=t1T[0:M, li, :], in_=bank(b, 128)[0:M, :], func=fn), reads=[B_ps[b], Bt1], writes=[Bt1])
            for half in range(2):
                b2 = nb3(0, 4)
                P.op("pe", lambda e, li=li, M=M, half=half, b2=b2: e.matmul(bank(b2, 512), t1T[0:M, li, :], w2s[0:M, li, half * 512:(half + 1) * 512], start=True, stop=True),
                     reads=[Bt1, Bw1], writes=[B_ps[b2]])
                P.op("act", lambda e, li=li, half=half, b2=b2: e.activation(out=tm[R0:R1, 3 + li, half * 512:(half + 1) * 512], in_=bank(b2, 512)[R0:R1, :], func=AF.Copy),
                     reads=[B_ps[b2], Btm], writes=[Btm])
        T = lambda k: tm[R0:R1, k, :]
        def V(k):
            P.dma("sp", lambda e, k=k: e.dma_start(out=vec[R0:R1, :], in_=rw_vec_bc_d[:, k, :]), reads=[Bvec], writes=[Bvec])
            return vec[R0:R1, :]
        Wk = lambda k: wk[R0:R1, k, :]
        h3 = lambda ap: ap.rearrange("p (h d) -> p h d", h=16)
        v_ = V(0)
        P.op("dve", lambda e, v_=v_: e.tensor_tensor(out=T(3), in0=T(3), in1=v_, op=ALU.add), reads=[Btm, Bvec], writes=[Btm])
        P.op("act", lambda e: e.activation(out=T(3), in_=T(3), func=AF.Exp, scale=-1.0), reads=[Btm], writes=[Btm])
        P.op("dve", lambda e: e.tensor_scalar(out=T(3), in0=T(3), scalar1=1.0, scalar2=None, op0=ALU.add), reads=[Btm], writes=[Btm])
        P.op("dve", lambda e: e.reciprocal(out=T(3), in_=T(3)), reads=[Btm], writes=[Btm])
        P.op("act", lambda e: e.activation(out=T(3), in_=T(3), func=AF.Exp, scale=-float(np.exp(-0.5))), reads=[Btm], writes=[Btm])
        v_ = V(1)
        P.op("dve", lambda e, v_=v_: e.tensor_tensor(out=T(4), in0=T(4), in1=v_, op=ALU.add), reads=[Btm, Bvec], writes=[Btm])
        P.op("act", lambda e: e.activation(out=T(4), in_=T(4), func=AF.Sigmoid), reads=[Btm], writes=[Btm])
        v_ = V(2)
        P.op("dve", lambda e, v_=v_: e.tensor_tensor(out=Wk(0), in0=T(1), in1=v_, op=ALU.mult), reads=[Btm, Bvec, Bwk], writes=[Bwk])
        P.op("dve", lambda e: e.tensor_tensor(out=Wk(1), in0=Wk(0), in1=Wk(0), op=ALU.mult), reads=[Bwk], writes=[Bwk])
        P.op("dve", lambda e: e.tensor_reduce(out=hs[R0:R1, 0, :], in_=h3(Wk(1)), axis=AX.X, op=ALU.add), reads=[Bwk, Bhs], writes=[Bhs])
        P.op("act", lambda e: e.activation(out=hs[R0:R1, 0, :], in_=hs[R0:R1, 0, :], func=AF.Sqrt), reads=[Bhs], writes=[Bhs])
        P.op("dve", lambda e: e.tensor_scalar(out=hs[R0:R1, 0, :], in0=hs[R0:R1, 0, :], scalar1=1e-12, scalar2=None, op0=ALU.max), reads=[Bhs], writes=[Bhs])
        P.op("dve", lambda e: e.reciprocal(out=hs[R0:R1, 0, :], in_=hs[R0:R1, 0, :]), reads=[Bhs], writes=[Bhs])
        P.op("dve", lambda e: e.tensor_tensor(out=h3(Wk(0)), in0=h3(Wk(0)), in1=hs[R0:R1, 0, :].unsqueeze(2).to_broadcast([32, 16, 64]), op=ALU.mult), reads=[Bwk, Bhs], writes=[Bwk])
        P.op("dve", lambda e: e.tensor_tensor(out=Wk(1), in0=Wk(0), in1=T(4), op=ALU.mult), reads=[Bwk, Btm], writes=[Bwk])
        v_ = V(3)
        P.op("dve", lambda e, v_=v_: e.scalar_tensor_tensor(out=Wk(2), in0=T(4), scalar=-1.0, in1=v_, op0=ALU.add, op1=ALU.mult), reads=[Btm, Bvec, Bwk], writes=[Bwk])
        P.op("dve", lambda e: e.scalar_tensor_tensor(out=Wk(2), in0=Wk(2), scalar=1.0, in1=T(1), op0=ALU.add, op1=ALU.mult), reads=[Bwk, Btm], writes=[Bwk])
        P.op("dve", lambda e: e.tensor_tensor(out=T(4), in0=T(0), in1=Wk(2), op=ALU.mult), reads=[Btm, Bwk], writes=[Btm])
        v_ = V(4)
        P.op("dve", lambda e, v_=v_: e.tensor_tensor(out=T(4), in0=T(4), in1=v_, op=ALU.mult), reads=[Btm, Bvec], writes=[Btm])
        P.op("dve", lambda e: e.tensor_reduce(out=hs[R0:R1, 1, :], in_=h3(T(4)), axis=AX.X, op=ALU.add), reads=[Btm, Bhs], writes=[Bhs])
        P.op("dve", lambda e: e.tensor_tensor(out=h3(T(4)), in0=h3(T(2)), in1=hs[R0:R1, 1, :].unsqueeze(2).to_broadcast([32, 16, 64]), op=ALU.mult), reads=[Btm, Bhs], writes=[Btm])
        def tm_to_fm(k_src, k_dst, src_tile, Bsrc):
            for c in range(KC):
                bt = nb3(4, 4)
                P.op("pe", lambda e, c=c, bt=bt: e.transpose(bank(bt, 128), src_tile[:, k_src, c * 128:(c + 1) * 128], ident[:, :]), reads=[Bsrc, B_const], writes=[B_ps[bt]])
                P.op("act", lambda e, c=c, bt=bt: e.activation(out=fmv[:, k_dst, c, :], in_=bank(bt, 128)[:, 124:128], func=AF.Copy), reads=[B_ps[bt], Bfm], writes=[Bfm])
        tm_to_fm(2, 0, tm, Btm)
        tm_to_fm(4, 2, tm, Btm)
        tm_to_fm(5, 3, tm, Btm)
        srcs = ((wk, 0, Bwk), (tm, 3, Btm), (wk, 1, Bwk), (wk, 2, Bwk), (tm, 0, Btm))
        for b_ in range(NS):
            for q, (tile_, k_, Bsrc) in enumerate(srcs):
                for half in range(2):
                    bb_ = nb3(0, 4)
                    P.op("pe", lambda e, b_=b_, tile_=tile_, k_=k_, half=half, bb_=bb_: e.matmul(bank(bb_, 512), selr[:, b_, :], tile_[:, k_, half * 512:(half + 1) * 512], start=True, stop=True),
                         reads=[Bk, Bsrc], writes=[B_ps[bb_]])
                    P.op("act", lambda e, q=q, half=half, bb_=bb_: e.activation(out=bc[:, q, half * 512:(half + 1) * 512], in_=bank(bb_, 512), func=AF.Copy), reads=[B_ps[bb_], Bbc], writes=[Bbc])
            P.dma("sp", lambda e, b_=b_: e.dma_start(out=S[:], in_=st_wkv_d[b_].rearrange("(q hp) i j -> (hp i) q j", hp=2)), reads=[BS], writes=[BS])

            def rowv(q, hp):
                return bc[hp * 64:(hp + 1) * 64, q, :].rearrange("p (pr two j) -> p pr two j", two=2, j=64)[:, :, hp, :]
            Sh = lambda t_, hp: t_[hp * 64:(hp + 1) * 64, :, :]
            for hp in range(2):
                P.op("dve", lambda e, hp=hp: e.tensor_tensor(out=Sh(tS, hp), in0=Sh(S, hp), in1=rowv(0, hp), op=ALU.mult), reads=[BS, Bbc, BtS], writes=[BtS])
            P.op("dve", lambda e: e.tensor_reduce(out=sa[:, 0, :], in_=tS[:, :, :], axis=AX.X, op=ALU.add), reads=[BtS, Bsa], writes=[Bsa])
            for hp in range(2):
                P.op("dve", lambda e, hp=hp: e.tensor_tensor(out=Sh(S, hp), in0=Sh(S, hp), in1=rowv(1, hp), op=ALU.mult), reads=[BS, Bbc], writes=[BS])
                P.op("dve", lambda e, hp=hp: e.tensor_tensor(out=Sh(tS, hp), in0=rowv(2, hp), in1=sa[hp * 64:(hp + 1) * 64, 0, :].unsqueeze(2).to_broadcast([64, KC, 64]), op=ALU.mult),
                     reads=[Bbc, Bsa, BtS], writes=[BtS])
            P.op("dve", lambda e: e.tensor_tensor(out=S[:], in0=S[:], in1=tS[:], op=ALU.subtract), reads=[BS, BtS], writes=[BS])
            for hp in range(2):
                P.op("dve", lambda e, hp=hp, b_=b_: e.tensor_tensor(out=Sh(tS, hp), in0=rowv(3, hp), in1=fmv[hp * 64:(hp + 1) * 64, 0, :, b_:b_ + 1].to_broadcast([64, KC, 64]), op=ALU.mult),
                     reads=[Bbc, Bfm, BtS], writes=[BtS])
            P.op("dve", lambda e: e.tensor_tensor(out=S[:], in0=S[:], in1=tS[:], op=ALU.add), reads=[BS, BtS], writes=[BS])
            P.dma("sp", lambda e, b_=b_: e.dma_start(out=wkv_s_d[b_].rearrange("(q hp) i j -> (hp i) q j", hp=2), in_=S[:]), reads=[BS])
            for hp in range(2):
                P.op("dve", lambda e, hp=hp: e.tensor_tensor(out=Sh(tS, hp), in0=Sh(S, hp), in1=rowv(4, hp), op=ALU.mult), reads=[BS, Bbc, BtS], writes=[BtS])
            P.op("dve", lambda e, b_=b_: e.tensor_reduce(out=fmv[:, 1, :, b_], in_=tS[:, :, :], axis=AX.X, op=ALU.add), reads=[BtS, Bfm], writes=[Bfm])
        yv = fmv[:, 1].rearrange("p c b -> p (c b)")
        t4 = fmv[:, 4].rearrange("p c b -> p (c b)")
        t5 = fmv[:, 5].rearrange("p c b -> p (c b)")
        P.op("dve", lambda e: e.tensor_tensor(out=t4, in0=yv, in1=yv, op=ALU.mult), reads=[Bfm], writes=[Bfm])
        P.op("pe", lambda e: e.matmul(bank(0, 32), blk[:, :], yv, start=True, stop=True), reads=[Bk, Bfm], writes=[B_ps[0]])
        P.op("pe", lambda e: e.matmul(bank(1, 32), blk[:, :], t4, start=True, stop=True), reads=[Bk, Bfm], writes=[B_ps[1]])
        P.op("act", lambda e: e.activation(out=t5, in_=bank(0, 32), func=AF.Copy), reads=[B_ps[0], Bfm], writes=[Bfm])
        P.op("dve", lambda e: e.tensor_tensor(out=t4, in0=t5, in1=t5, op=ALU.mult), reads=[Bfm], writes=[Bfm])
        P.op("dve", lambda e: e.tensor_tensor(out=t4, in0=bank(1, 32), in1=t4, op=ALU.subtract), reads=[B_ps[1], Bfm], writes=[Bfm])
        P.op("act", lambda e: e.activation(out=t4, in_=t4, func=AF.Sqrt, bias=epsg[:, 0:1], scale=1.0), reads=[Bfm, Bk], writes=[Bfm])
        P.op("dve", lambda e: e.reciprocal(out=t4, in_=t4), reads=[Bfm], writes=[Bfm])
        P.op("dve", lambda e: e.tensor_tensor(out=yv, in0=yv, in1=t5, op=ALU.subtract), reads=[Bfm], writes=[Bfm])
        P.op("dve", lambda e: e.tensor_tensor(out=yv, in0=yv, in1=t4, op=ALU.mult), reads=[Bfm], writes=[Bfm])
        P.op("dve", lambda e: e.tensor_tensor(out=fmv[:, 1], in0=fmv[:, 1], in1=gnc[:, 0:8].unsqueeze(2).to_broadcast([128, KC, NS]), op=ALU.mult), reads=[Bfm, Bk], writes=[Bfm])
        P.op("dve", lambda e: e.tensor_tensor(out=fmv[:, 1], in0=fmv[:, 1], in1=gnc[:, 8:16].unsqueeze(2).to_broadcast([128, KC, NS]), op=ALU.add), reads=[Bfm, Bk], writes=[Bfm])
        P.op("dve", lambda e: e.tensor_tensor(out=fmv[:, 1], in0=fmv[:, 1], in1=fmv[:, 2], op=ALU.add), reads=[Bfm], writes=[Bfm])
        P.op("dve", lambda e: e.tensor_tensor(out=zb[:], in0=fmv[:, 1], in1=fmv[:, 3], op=ALU.mult), reads=[Bfm, Bzb], writes=[Bzb])
        for half in range(2):
            sl = 0
            P.dma("pool", lambda e, sl=sl, half=half: e.dma_start(out=wsl[:, sl, :, :], in_=rw_w_d["o"][:, half * 512:(half + 1) * 512].rearrange("(kc p) n -> p kc n", p=128)),
                  writes=[Bwsl[sl]])
            for mm in range(4):
                m = half * 4 + mm
                b = nb3(0, 4)
                for kc in range(KC):
                    P.op("pe", lambda e, kc=kc, sl=sl, mm=mm, b=b: e.matmul(bank(b, NS), wsl[:, sl, kc, mm * 128:(mm + 1) * 128], zb[:, kc, :], start=(kc == 0), stop=(kc == KC - 1)),
                         reads=[Bwsl[sl], Bzb], writes=[B_ps[b]])
                P.op("dve", lambda e, m=m, b=b: e.scalar_tensor_tensor(out=x32[:, m, SEQ:SEQ + NS], in0=bank(b, NS), scalar=1.0 / ALPHA, in1=x32[:, m, SEQ:SEQ + NS],
                                                                     op0=ALU.mult, op1=ALU.add), reads=[B_ps[b], B_x32[m][4]], writes=[B_x32[m][4]])
        sb.release(mk)
        P.barrier()


    def rwkv_prompt():
        mk = sb.mark()
        TS = 16
        muc = sb.alloc([128, 48], F32)
        vec = sb.alloc([128, D], F32)
        mbd = sb.alloc([16, D], F32)
        epsg = sb.alloc([128, 1], F32)
        xlast = sb.alloc([128, KC, 1], F32)
        ST = sb.alloc([64, D], F32)
        STb = sb.alloc([64, D], BF16)
        tm = sb.alloc([128, 6, D], F32)
        w1s = sb.alloc([128, KC, 256], BF16)
        w2s = sb.alloc([128, 3, D], BF16)
        Bk, Bvec, Bxl, BST, BSTb, Btm, Bw1 = (Buf(n) for n in ("rwpc", "rwpvec", "xlast", "ST", "STb", "tmp_", "w1s"))
        Bscr = [Buf("scr_h") for _ in range(3)]
        Bscy = Buf("scr_y")
        stg = sb.alloc([128, 128], F32)
        Bstg = Buf("stg")
        P.op("dve", lambda e: e.memset(stg[:, :], 0.0), writes=[Bstg])
        P.dma("sp", lambda e: e.dma_start(out=stg[:48, :], in_=rw_mu_c_d[:, :]), writes=[Bstg])
        P.op("pe", lambda e: e.transpose(bank(7, 128), stg[:, :], ident[:, :]), reads=[Bstg, B_const], writes=[B_ps[7]])
        P.op("dve", lambda e: e.tensor_copy(out=muc[:, 0:48], in_=bank(7, 48)), reads=[B_ps[7], Bk], writes=[Bk])
        P.dma("sp", lambda e: e.dma_start(out=mbd[:], in_=rw_mbd_d[:, :]), reads=[Bk], writes=[Bk])
        P.op("dve", lambda e: e.memset(epsg[:], 64e-5), reads=[Bk], writes=[Bk])
        P.op("dve", lambda e: e.memset(xlast[:], 0.0), writes=[Bxl])
        P.op("dve", lambda e: e.memset(ST[:], 0.0), writes=[BST])
        P.op("dve", lambda e: e.memset(STb[:], 0.0), writes=[BSTb])
        P.dma("pool", lambda e: e.dma_start(out=w1s[:, :, 0:64], in_=rw_w1_d[:, :].rearrange("(kc p) n -> p kc n", p=128)), writes=[Bw1])
        P.dma("pool", lambda e: e.dma_start(out=w1s[:, :, 64:128], in_=rw_a1_d[:, :].rearrange("(kc p) n -> p kc n", p=128)), reads=[Bw1], writes=[Bw1])
        P.dma("pool", lambda e: e.dma_start(out=w1s[:, :, 128:256], in_=rw_g1_d[:, :].rearrange("(kc p) n -> p kc n", p=128)), reads=[Bw1], writes=[Bw1])
        P.dma("pool", lambda e: e.dma_start(out=w2s[0:64, 0, :], in_=rw_w2_d[:, :]), reads=[Bw1], writes=[Bw1])
        P.dma("pool", lambda e: e.dma_start(out=w2s[0:64, 1, :], in_=rw_a2_d[:, :]), reads=[Bw1], writes=[Bw1])
        P.dma("pool", lambda e: e.dma_start(out=w2s[:, 2, :], in_=rw_g2_d[:, :]), reads=[Bw1], writes=[Bw1])
        st3 = {"cnt": 0}

        def nb3(lo, n):
            b = lo + st3["cnt"] % n
            st3["cnt"] += 1
            return b

        def V(k):
            P.dma("sp", lambda e, k=k: e.dma_start(out=vec[:, :], in_=rw_vec128_d[:, k, :]), reads=[Bvec], writes=[Bvec])
            return vec[:, :]
        T = lambda k: tm[:, k, :]
        h3 = lambda ap: ap.rearrange("p (h d) -> p h d", h=16)
        mwork = sb.mark()

        import os
        NCHK = int(os.environ.get("KRWP_CH", SEQ // 128))
        NSTEP = int(os.environ.get("KRWP_STEPS", TS))
        for ch in range(NCHK):
            tk = ch * 128
            ti = tk // 512
            mA_ = sb.mark()
            dx = sb.alloc([128, KC, 128], F32)
            xmp = sb.alloc([128, 6, KC, 128], BF16)
            wsl = sb.alloc([128, KC, 512], BF16)
            t1T = sb.alloc([128, 3, 128], BF16)
            wk = sb.alloc([128, 3, D], F32)
            tmpx = wk[:, 0, :].rearrange("p (c t) -> p c t", c=KC)
            hs = sb.alloc([128, 2, 16], F32)
            wkb = sb.alloc([128, 3, D], BF16)
            Bdx, Bxmp, Bwsl, Bt1, Bwk, Bhs, Bwkb = (Buf(n) for n in ("dx", "xmp", "wsl", "t1T", "wk", "hs", "wkb"))
            Btx = Bwk
            Wk = lambda k, wk=wk: wk[:, k, :]
            xin = [B_x32[c][ti] for c in range(KC)]
            P.op("dve", lambda e, tk=tk, dx=dx: e.tensor_tensor(out=dx[:, :, 0:1], in0=xlast[:, :, :], in1=x32[:, :, tk:tk + 1], op=ALU.subtract), reads=[Bxl] + xin, writes=[Bdx])
            P.op("dve", lambda e, tk=tk, dx=dx: e.tensor_tensor(out=dx[:, :, 1:128], in0=x32[:, :, tk:tk + 127], in1=x32[:, :, tk + 1:tk + 128], op=ALU.subtract), reads=xin + [Bdx], writes=[Bdx])
            P.op("dve", lambda e, tk=tk: e.tensor_copy(out=xlast[:, :, :], in_=x32[:, :, tk + 127:tk + 128]), reads=xin + [Bdx], writes=[Bxl])
            for j in range(6):
                P.op("dve", lambda e, j=j, dx=dx, tmpx=tmpx: e.tensor_tensor(out=tmpx[:], in0=dx[:], in1=muc[:, j * 8:(j + 1) * 8].unsqueeze(2).to_broadcast([128, KC, 128]), op=ALU.mult),
                     reads=[Bdx, Bk, Btx], writes=[Btx])
                P.op("dve", lambda e, j=j, tk=tk, tmpx=tmpx, xmp=xmp: e.tensor_tensor(out=xmp[:, j], in0=tmpx[:], in1=x32[:, :, tk:tk + 128], op=ALU.add), reads=[Btx, Bxmp] + xin, writes=[Bxmp])
            for ti_, (nm, jx) in enumerate((("r", 0), ("k", 2), ("v", 3))):
                for half in range(2):
                    P.dma("pool", lambda e, nm=nm, half=half, wsl=wsl: e.dma_start(out=wsl[:], in_=rw_w_d[nm][:, half * 512:(half + 1) * 512].rearrange("(kc p) n -> p kc n", p=128)),
                          writes=[Bwsl])
                    b = nb3(0, 4)
                    for kc in range(KC):
                        P.op("pe", lambda e, kc=kc, jx=jx, b=b, xmp=xmp, wsl=wsl: e.matmul(bank(b, 512), xmp[:, jx, kc, :], wsl[:, kc, :], start=(kc == 0), stop=(kc == KC - 1)),
                             reads=[Bxmp, Bwsl], writes=[B_ps[b]])
                    P.op("act", lambda e, ti_=ti_, half=half, b=b: e.activation(out=tm[:, ti_, half * 512:(half + 1) * 512], in_=bank(b, 512), func=AF.Copy), reads=[B_ps[b], Btm], writes=[Btm])
            for li, (jx, c0, M, fn) in enumerate(((1, 0, 64, AF.Tanh), (4, 64, 64, AF.Copy), (5, 128, 128, AF.Sigmoid))):
                b = nb3(4, 4)
                for kc in range(KC):
                    P.op("pe", lambda e, kc=kc, jx=jx, c0=c0, M=M, b=b, xmp=xmp: e.matmul(bank(b, 128)[0:M, :], w1s[:, kc, c0:c0 + M], xmp[:, jx, kc, :], start=(kc == 0), stop=(kc == KC - 1)),
                         reads=[Bw1, Bxmp], writes=[B_ps[b]])
                P.op("act", lambda e, li=li, M=M, fn=fn, b=b, t1T=t1T: e.activation(out=t1T[0:M, li, :], in_=bank(b, 128)[0:M, :], func=fn), reads=[B_ps[b], Bt1], writes=[Bt1])
                for half in range(2):
                    b2 = nb3(0, 4)
                    P.op("pe", lambda e, li=li, M=M, half=half, b2=b2, t1T=t1T: e.matmul(bank(b2, 512), t1T[0:M, li, :], w2s[0:M, li, half * 512:(half + 1) * 512], start=True, stop=True),
                         reads=[Bt1, Bw1], writes=[B_ps[b2]])
                    P.op("act", lambda e, li=li, half=half, b2=b2: e.activation(out=tm[:, 3 + li, half * 512:(half + 1) * 512], in_=bank(b2, 512), func=AF.Copy), reads=[B_ps[b2], Btm], writes=[Btm])
            v_ = V(0)
            P.op("dve", lambda e, v_=v_: e.tensor_tensor(out=T(3), in0=T(3), in1=v_, op=ALU.add), reads=[Btm, Bvec], writes=[Btm])
            P.op("act", lambda e: e.activation(out=T(3), in_=T(3), func=AF.Exp, scale=-1.0), reads=[Btm], writes=[Btm])
            P.op("dve", lambda e: e.tensor_scalar(out=T(3), in0=T(3), scalar1=1.0, scalar2=None, op0=ALU.add), reads=[Btm], writes=[Btm])
            P.op("dve", lambda e: e.reciprocal(out=T(3), in_=T(3)), reads=[Btm], writes=[Btm])
            P.op("act", lambda e: e.activation(out=T(3), in_=T(3), func=AF.Exp, scale=-float(np.exp(-0.5))), reads=[Btm], writes=[Btm])
            v_ = V(1)
            P.op("dve", lambda e, v_=v_: e.tensor_tensor(out=T(4), in0=T(4), in1=v_, op=ALU.add), reads=[Btm, Bvec], writes=[Btm])
            P.op("act", lambda e: e.activation(out=T(4), in_=T(4), func=AF.Sigmoid), reads=[Btm], writes=[Btm])
            v_ = V(2)
            P.op("dve", lambda e, v_=v_, Wk=Wk: e.tensor_tensor(out=Wk(0), in0=T(1), in1=v_, op=ALU.mult), reads=[Btm, Bvec, Bwk], writes=[Bwk])
            P.op("dve", lambda e, Wk=Wk: e.tensor_tensor(out=Wk(1), in0=Wk(0), in1=Wk(0), op=ALU.mult), reads=[Bwk], writes=[Bwk])
            P.op("dve", lambda e, Wk=Wk, hs=hs: e.tensor_reduce(out=hs[:, 0, :], in_=h3(Wk(1)), axis=AX.X, op=ALU.add), reads=[Bwk, Bhs], writes=[Bhs])
            P.op("act", lambda e, hs=hs: e.activation(out=hs[:, 0, :], in_=hs[:, 0, :], func=AF.Sqrt), reads=[Bhs], writes=[Bhs])
            P.op("dve", lambda e, hs=hs: e.tensor_scalar(out=hs[:, 0, :], in0=hs[:, 0, :], scalar1=1e-12, scalar2=None, op0=ALU.max), reads=[Bhs], writes=[Bhs])
            P.op("dve", lambda e, hs=hs: e.reciprocal(out=hs[:, 0, :], in_=hs[:, 0, :]), reads=[Bhs], writes=[Bhs])
            P.op("dve", lambda e, Wk=Wk, hs=hs: e.tensor_tensor(out=h3(Wk(0)), in0=h3(Wk(0)), in1=hs[:, 0, :].unsqueeze(2).to_broadcast([128, 16, 64]), op=ALU.mult), reads=[Bwk, Bhs], writes=[Bwk])
            P.op("dve", lambda e, Wk=Wk: e.tensor_tensor(out=Wk(1), in0=Wk(0), in1=T(4), op=ALU.mult), reads=[Bwk, Btm], writes=[Bwk])
            v_ = V(3)
            P.op("dve", lambda e, v_=v_, Wk=Wk: e.scalar_tensor_tensor(out=Wk(2), in0=T(4), scalar=-1.0, in1=v_, op0=ALU.add, op1=ALU.mult), reads=[Btm, Bvec, Bwk], writes=[Bwk])
            P.op("dve", lambda e, Wk=Wk: e.scalar_tensor_tensor(out=Wk(2), in0=Wk(2), scalar=1.0, in1=T(1), op0=ALU.add, op1=ALU.mult), reads=[Bwk, Btm], writes=[Bwk])
            P.op("act", lambda e, Wk=Wk, wkb=wkb: e.activation(out=wkb[:, 0, :], in_=Wk(2), func=AF.Copy), reads=[Bwk, Bwkb], writes=[Bwkb])
            P.op("act", lambda e, Wk=Wk, wkb=wkb: e.activation(out=wkb[:, 1, :], in_=Wk(1), func=AF.Copy, scale=-1.0), reads=[Bwk, Bwkb], writes=[Bwkb])
            P.op("act", lambda e, wkb=wkb: e.activation(out=wkb[:, 2, :], in_=T(2), func=AF.Copy), reads=[Btm, Bwkb], writes=[Bwkb])
            for q in range(3):
                P.dma("sp", lambda e, q=q, tk=tk, wkb=wkb: e.dma_start(out=scr_h_d[q, tk:tk + 128, :], in_=wkb[:, q, :]), reads=[Bwkb, Bscr[q]], writes=[Bscr[q]])
            P.op("dve", lambda e, Wk=Wk: e.tensor_tensor(out=T(4), in0=T(0), in1=Wk(2), op=ALU.mult), reads=[Btm, Bwk], writes=[Btm])
            v_ = V(4)
            P.op("dve", lambda e, v_=v_: e.tensor_tensor(out=T(4), in0=T(4), in1=v_, op=ALU.mult), reads=[Btm, Bvec], writes=[Btm])
            P.op("dve", lambda e, hs=hs: e.tensor_reduce(out=hs[:, 1, :], in_=h3(T(4)), axis=AX.X, op=ALU.add), reads=[Btm, Bhs], writes=[Bhs])
            P.op("dve", lambda e, hs=hs: e.tensor_tensor(out=h3(T(4)), in0=h3(T(2)), in1=hs[:, 1, :].unsqueeze(2).to_broadcast([128, 16, 64]), op=ALU.mult), reads=[Btm, Bhs], writes=[Btm])
            m_hm = sb.mark()
            kkH = sb.alloc([64, 128, 16], BF16)
            rH = sb.alloc([64, 128, 16], BF16)
            dH = sb.alloc([64, 128, 16], F32)
            BkkH, BrH, BdH, ByFM = Buf("kkH"), Buf("rH"), Buf("dH"), Buf("yFM")
            for (src, dst, Bs, Bd) in ((wk[:, 0, :], kkH, Bwk, BkkH), (tm[:, 0, :], rH, Btm, BrH), (tm[:, 3, :], dH, Btm, BdH)):
                for h in range(16):
                    bt = nb3(4, 4)
                    P.op("pe", lambda e, src=src, h=h, bt=bt: e.transpose(bank(bt, 128)[0:64, :], src[:, h * 64:(h + 1) * 64], ident[:, :]), reads=[Bs, B_const], writes=[B_ps[bt]])
                    P.op("act", lambda e, dst=dst, h=h, bt=bt: e.activation(out=dst[:, :, h], in_=bank(bt, 128)[0:64, :], func=AF.Copy), reads=[B_ps[bt], Bd], writes=[Bd])
            P.barrier()
            mtop = sb.mark()
            sb.release(mA_)
            yFM = sb.alloc([128, KC, 128], F32)
            Hop = sb.alloc([16, 2, 3, TS, 64], BF16)
            SAb = sb.alloc([16, D], BF16)
            VBb = sb.alloc([16, D], BF16)
            STd = sb.alloc([64, D], F32)
            assert sb.mark() <= m_hm, "scan scratch overlaps head-major operands"
            sb.release(mtop)
            BHop = [Buf("Hop") for _ in range(2)]
            BSAb, BVBb, BSTd = Buf("SAb"), Buf("VBb"), Buf("STd")
            pend = []
            for sbk in range(128 // TS):
                sl = sbk % 2
                t0 = sbk * TS
                for q in range(3):
                    P.dma("sp", lambda e, q=q, sl=sl, tk=tk, t0=t0, Hop=Hop: e.dma_start(out=Hop[:, sl, q, :, :], in_=scr_h_d[q, tk + t0:tk + t0 + TS, :].rearrange("t (h j) -> h t j", h=16)),
                          reads=[Bscr[q], BHop[sl]], writes=[BHop[sl]])
                for tt in range(NSTEP):
                    t = t0 + tt
                    for half in range(2):
                        P.op("pe", lambda e, t=t, half=half, kkH=kkH: e.matmul(bank(half, 512)[0:16, :], kkH[:, t, :], STb[:, half * 512:(half + 1) * 512], start=True, stop=True),
                             reads=[BkkH, BSTb], writes=[B_ps[half]])
                    prev = pend.pop(0) if pend else None
                    if prev:
                        prev[0]()
                    P.op("dve", lambda e, SAb=SAb: e.tensor_tensor(out=SAb[:, :], in0=psum[0:16, 0:D], in1=mbd[:, :], op=ALU.mult), reads=[B_ps[0], B_ps[1], Bk, BSAb], writes=[BSAb])
                    if prev:
                        prev[1]()
                    P.op("pool", lambda e, sl=sl, tt=tt, VBb=VBb, Hop=Hop: e.tensor_tensor(out=VBb[:, :].rearrange("p (h i) -> p h i", h=16), in0=mbd[:, :].rearrange("p (h i) -> p h i", h=16),
                                                                                in1=Hop[:, sl, 2, tt:tt + 1, :].to_broadcast([16, 16, 64]), op=ALU.mult), reads=[BHop[sl], Bk, BVBb], writes=[BVBb])
                    for half in range(2):
                        P.op("pe", lambda e, sl=sl, tt=tt, half=half, Hop=Hop, SAb=SAb: e.matmul(bank(2 + half, 512)[0:64, :], Hop[:, sl, 1, tt, :], SAb[:, half * 512:(half + 1) * 512], start=True, stop=False),
                             reads=[BHop[sl], BSAb], writes=[B_ps[2 + half]])
                        P.op("pe", lambda e, sl=sl, tt=tt, half=half, Hop=Hop, VBb=VBb: e.matmul(bank(2 + half, 512)[0:64, :], Hop[:, sl, 0, tt, :], VBb[:, half * 512:(half + 1) * 512], start=False, stop=True),
                             reads=[BHop[sl], BVBb], writes=[B_ps[2 + half]])
                    P.op("pool", lambda e, t=t, dH=dH, STd=STd: e.tensor_tensor(out=STd[:, :].rearrange("p (h i) -> p h i", h=16), in0=ST[:, :].rearrange("p (h i) -> p h i", h=16),
                                                                             in1=dH[:, t, :].unsqueeze(2).to_broadcast([64, 16, 64]), op=ALU.mult), reads=[BST, BdH, BSTd], writes=[BSTd])
                    P.op("dve", lambda e, STd=STd: e.tensor_tensor(out=STb[:, :], in0=STd[:, :], in1=psum[0:64, 2 * 512:2 * 512 + D], op=ALU.add), reads=[BSTd, B_ps[2], B_ps[3], BSTb], writes=[BSTb])
                    P.op("dve", lambda e, STd=STd: e.tensor_tensor(out=ST[:, :], in0=STd[:, :], in1=psum[0:64, 2 * 512:2 * 512 + D], op=ALU.add), reads=[BSTd, B_ps[2], B_ps[3], BST], writes=[BST])
                    def y_pe(t=t, rH=rH):
                        yb = 4 + t % 2
                        for c in range(KC):
                            P.op("pe", lambda e, t=t, c=c, yb=yb, rH=rH: e.matmul(bank(yb, 16)[:, 2 * c:2 * c + 2], STb[:, c * 128:(c + 1) * 128], rH[:, t, 2 * c:2 * c + 2], start=True, stop=True),
                                 reads=[BrH, BSTb], writes=[B_ps[yb]])

                    def y_dve(t=t, yFM=yFM):
                        yb = 4 + t % 2
                        for hp in range(2):
                            P.op("dve", lambda e, t=t, hp=hp, yb=yb, yFM=yFM: e.tensor_copy(out=yFM[hp * 64:(hp + 1) * 64, :, t],
                                                                                        in_=bank(yb, 16)[hp * 64:(hp + 1) * 64, :].rearrange("p (c two) -> p c two", two=2)[:, :, hp]),
                                 reads=[B_ps[yb], ByFM], writes=[ByFM])
                    pend.append((y_pe, y_dve))
            while pend:
                pe_, dv_ = pend.pop(0)
                pe_()
                dv_()
            P.barrier()
            sb.release(mA_)
            _keep_yFM = sb.alloc([128, KC, 128], F32)
            ytm = sb.alloc([128, D], F32)
            sq = sb.alloc([128, D], F32)
            hs2 = sb.alloc([128, 2, 16], F32)
            zbf = sb.alloc([128, KC, 128], BF16)
            wsl2 = sb.alloc([128, KC, 512], BF16)
            Bytm, Bsq, Bhs2, Bwsl2 = Buf("ytm"), Buf("sq"), Buf("hs2"), Buf("wsl2")
            Bzbf = [Buf("zbf") for _ in range(KC)]
            for c in range(KC):
                bt = nb3(4, 4)
                P.op("pe", lambda e, c=c, bt=bt, yFM=yFM: e.transpose(bank(bt, 128), yFM[:, c, :], ident[:, :]), reads=[ByFM, B_const], writes=[B_ps[bt]])
                P.op("act", lambda e, c=c, bt=bt, ytm=ytm: e.activation(out=ytm[:, c * 128:(c + 1) * 128], in_=bank(bt, 128), func=AF.Copy), reads=[B_ps[bt], Bytm], writes=[Bytm])
            P.op("dve", lambda e, ytm=ytm, hs2=hs2: e.tensor_reduce(out=hs2[:, 0, :], in_=h3(ytm[:, :]), axis=AX.X, op=ALU.add), reads=[Bytm, Bhs2], writes=[Bhs2])
            P.op("dve", lambda e, hs2=hs2: e.tensor_scalar(out=hs2[:, 0, :], in0=hs2[:, 0, :], scalar1=1.0 / 64, scalar2=None, op0=ALU.mult), reads=[Bhs2], writes=[Bhs2])
            P.op("dve", lambda e, ytm=ytm, hs2=hs2: e.tensor_tensor(out=h3(ytm[:, :]), in0=h3(ytm[:, :]), in1=hs2[:, 0, :].unsqueeze(2).to_broadcast([128, 16, 64]), op=ALU.subtract), reads=[Bytm, Bhs2], writes=[Bytm])
            P.op("dve", lambda e, ytm=ytm, sq=sq: e.tensor_tensor(out=sq[:, :], in0=ytm[:, :], in1=ytm[:, :], op=ALU.mult), reads=[Bytm, Bsq], writes=[Bsq])
            P.op("dve", lambda e, sq=sq, hs2=hs2: e.tensor_reduce(out=hs2[:, 1, :], in_=h3(sq[:, :]), axis=AX.X, op=ALU.add), reads=[Bsq, Bhs2], writes=[Bhs2])
            P.op("act", lambda e, hs2=hs2: e.activation(out=hs2[:, 1, :], in_=hs2[:, 1, :], func=AF.Sqrt, bias=epsg[:, 0:1], scale=1.0 / 64), reads=[Bhs2, Bk], writes=[Bhs2])
            P.op("dve", lambda e, hs2=hs2: e.reciprocal(out=hs2[:, 1, :], in_=hs2[:, 1, :]), reads=[Bhs2], writes=[Bhs2])
            P.op("dve", lambda e, ytm=ytm, hs2=hs2: e.tensor_tensor(out=h3(ytm[:, :]), in0=h3(ytm[:, :]), in1=hs2[:, 1, :].unsqueeze(2).to_broadcast([128, 16, 64]), op=ALU.mult), reads=[Bytm, Bhs2], writes=[Bytm])
            v_ = V(5)
            P.op("dve", lambda e, v_=v_, ytm=ytm: e.tensor_tensor(out=ytm[:, :], in0=ytm[:, :], in1=v_, op=ALU.mult), reads=[Bytm, Bvec], writes=[Bytm])
            v_ = V(6)
            P.op("dve", lambda e, v_=v_, ytm=ytm: e.tensor_tensor(out=ytm[:, :], in0=ytm[:, :], in1=v_, op=ALU.add), reads=[Bytm, Bvec], writes=[Bytm])
            P.op("dve", lambda e, ytm=ytm: e.tensor_tensor(out=ytm[:, :], in0=ytm[:, :], in1=T(4), op=ALU.add), reads=[Bytm, Btm], writes=[Bytm])
            P.op("dve", lambda e, ytm=ytm: e.tensor_tensor(out=ytm[:, :], in0=ytm[:, :], in1=T(5), op=ALU.mult), reads=[Bytm, Btm], writes=[Bytm])
            for c in range(KC):
                bt = nb3(4, 4)
                P.op("pe", lambda e, c=c, bt=bt, ytm=ytm: e.transpose(bank(bt, 128), ytm[:, c * 128:(c + 1) * 128], ident[:, :]), reads=[Bytm, B_const], writes=[B_ps[bt]])
                P.op("act", lambda e, c=c, bt=bt, zbf=zbf: e.activation(out=zbf[:, c, :], in_=bank(bt, 128), func=AF.Copy), reads=[B_ps[bt], Bzbf[c]], writes=[Bzbf[c]])
            for half in range(2):
                P.dma("pool", lambda e, half=half, wsl2=wsl2: e.dma_start(out=wsl2[:], in_=rw_w_d["o"][:, half * 512:(half + 1) * 512].rearrange("(kc p) n -> p kc n", p=128)), writes=[Bwsl2])
                for mm in range(4):
                    m = half * 4 + mm
                    b = nb3(0, 4)
                    for kc in range(KC):
                        P.op("pe", lambda e, kc=kc, mm=mm, b=b, wsl2=wsl2, zbf=zbf: e.matmul(bank(b, 128), wsl2[:, kc, mm * 128:(mm + 1) * 128], zbf[:, kc, :], start=(kc == 0), stop=(kc == KC - 1)),
                             reads=[Bwsl2, Bzbf[kc]], writes=[B_ps[b]])
                    P.op("dve", lambda e, m=m, b=b, tk=tk: e.scalar_tensor_tensor(out=x32[:, m, tk:tk + 128], in0=bank(b, 128), scalar=1.0 / ALPHA, in1=x32[:, m, tk:tk + 128],
                                                                                 op0=ALU.mult, op1=ALU.add), reads=[B_ps[b], B_x32[m][ti]], writes=[B_x32[m][ti]])
            P.barrier()
            sb.release(mwork)
        so = sb.alloc([64, 2, 64], F32)
        Bso = [Buf("so") for _ in range(2)]
        for h in range(16):
            bt = nb3(4, 4)
            sl = h % 2
            P.op("pe", lambda e, h=h, bt=bt: e.transpose(bank(bt, 64)[0:64, :], ST[:, h * 64:(h + 1) * 64], ident[0:64, 0:64]), reads=[BST, B_const], writes=[B_ps[bt]])
            P.op("act", lambda e, sl=sl, bt=bt: e.activation(out=so[:, sl, :], in_=bank(bt, 64)[0:64, :], func=AF.Copy), reads=[B_ps[bt], Bso[sl]], writes=[Bso[sl]])
            P.dma("sp", lambda e, h=h, sl=sl: e.dma_start(out=wkv_p_d[h], in_=so[:, sl, :]), reads=[Bso[sl]])
        sb.release(mk)
        P.barrier()

    def mixer_ln(i):
        mk = sb.mark()
        ln = alloc_ln()
        layer_norm(i * 3 + 1, ln)
        sb.release(mk)
        P.barrier()

    def store_fm_to_tm(dst_d, ntok, col0):
        mk = sb.mark()
        tout = sb.alloc([128, 2, D], F32)
        Bt = [Buf("tout0"), Buf("tout1")]
        k = 0
        for t0 in range(0, ntok, 128):
            n = min(128, ntok - t0)
            s = k % 2
            col = col0 + t0 + n - 128
            tis = sorted(set([min(col // 512, 4), min((col + 127) // 512, 4)]))
            for c in range(KC):
                b = (k * KC + c) % 8
                P.op("pe", lambda e, c=c, col=col, b=b: e.transpose(bank(b, 128), x32[:, c, col:col + 128], ident[:, :]),
                     reads=[B_x32[c][ti] for ti in tis] + [B_const], writes=[B_ps[b]])
                if c % 2 == 0:
                    P.op("dve", lambda e, c=c, s=s, b=b: e.tensor_copy(out=tout[:, s, c * 128:(c + 1) * 128], in_=bank(b, 128)),
                         reads=[B_ps[b]], writes=[Bt[s]])
                else:
                    P.op("act", lambda e, c=c, s=s, b=b: e.activation(out=tout[:, s, c * 128:(c + 1) * 128], in_=bank(b, 128), func=AF.Copy),
                         reads=[B_ps[b]], writes=[Bt[s]])
            P.dma("sp", lambda e, t0=t0, n=n, s=s: e.dma_start(out=dst_d[t0:t0 + n, :], in_=tout[128 - n:128, s, :]), reads=[Bt[s]])
            k += 1
        sb.release(mk)


    def store_last_cols(dst_d, col_end, nrows):
        mk = sb.mark()
        tout = sb.alloc([128, D], F32)
        Bt = Buf("tout_s")
        col = col_end - 128
        tis = sorted(set([min(col // 512, 4), min((col_end - 1) // 512, 4)]))
        for c in range(KC):
            b = c % 8
            P.op("pe", lambda e, c=c, col=col, b=b: e.transpose(bank(b, 128), x32[:, c, col:col + 128], ident[:, :]),
                 reads=[B_x32[c][ti] for ti in tis] + [B_const], writes=[B_ps[b]])
            P.op("act", lambda e, c=c, b=b: e.activation(out=tout[:, c * 128:(c + 1) * 128], in_=bank(b, 128), func=AF.Copy), reads=[B_ps[b], Bt], writes=[Bt])
        P.dma("sp", lambda e: e.dma_start(out=dst_d[:, :], in_=tout[128 - nrows:128, :]), reads=[Bt])
        sb.release(mk)
        P.barrier()

    if only_mixer == "rw":
        import os
        if os.environ.get("KRW", "sp") in ("s", "sp"):
            rwkv_sample()
        if os.environ.get("KRW", "sp") in ("p", "sp"):
            rwkv_prompt()
        mixer_ln(3)
    if only_mixer == "att":
        dsa()
        mixer_ln(2)
    if only_mixer == "ssm":
        if mamba_prompt() == "stop":
            P.emit()
            nc._k_in_names = in_names
            nc._k_out_names = out_names
            nc._k_dbg = dbg_names
            return nc
        mixer_ln(1)
    for i in range(DEPTH):
        if i >= stop_after:
            break
        ffn(i, 0)
        if i == 3:
            store_last_cols(sh_p_d, SEQ, 1)
            store_last_cols(sh_s_d, NCOL, NS)
        if i == 0 and "gm" in mixers:
            gmlp()
        if i == 1 and "ssm" in mixers:
            mamba_prompt()
        if i == 2 and "att" in mixers:
            dsa()
        if i == 3 and "rw" in mixers:
            rwkv_sample()
            if "rwp" in mixers:
                rwkv_prompt()
        mixer_ln(i)
        ffn(i, 1)
        ple(i)

    if dbg >= 1:
        store_fm_to_tm(yp_d, SEQ, 0)
    P.barrier()
    if dbg >= 5:
        store_fm_to_tm(ys_d, NS, SEQ)
    P.emit()
    nc._k_in_names = in_names
    nc._k_dbg = dbg_names
    nc._k_out_names = out_names
    return nc


_W_NAMES = ["ffn_w_up", "ffn_w_down", "ple_w_p", "ple_w_g", "gm_w_in", "gm_w_out"]


def make_in_maps(inp):
    f = lambda a: np.ascontiguousarray(a, dtype=np.float32)
    shared = {k: f(inp[k]) for k in _W_NAMES}
    shared["ln_g"] = f(inp["ln_g"]).reshape(DEPTH * 3 * KC, 128)
    shared["ln_b"] = f(inp["ln_b"]).reshape(DEPTH * 3 * KC, 128)
    shared["ple_b_g"] = f(inp["ple_b_g"]).reshape(DEPTH * KC, 128)
    shared["gm_lng_c"] = f(inp["gm_ln_g"]).reshape(16, 128)
    shared["gm_lnb_c"] = f(inp["gm_ln_b"]).reshape(16, 128)
    shared["gm_wT"] = f(np.transpose(inp["gm_ws"], (2, 0, 1)))
    shared["gm_bs_bc"] = f(np.broadcast_to(inp["gm_bs"][None], (128, 8, 128)))
    shared["gm_w00"] = f(np.broadcast_to(inp["gm_ws"][None, :, 0, 0], (128, 8)))
    shared["gm_bs0"] = f(np.broadcast_to(inp["gm_bs"][None, :, 0], (128, 8)))
    shared["gm_lng_bc"] = f(np.broadcast_to(inp["gm_ln_g"][None], (32, 2048)))
    shared["gm_lnb_bc"] = f(np.broadcast_to(inp["gm_ln_b"][None], (32, 2048)))
    shared["ssm_w_in"] = f(inp["ssm_w_in"])
    shared["ssm_w_out"] = f(inp["ssm_w_out"])
    shared["ssm_cw_c"] = f(inp["ssm_conv_w"]).reshape(96, 128)
    shared["ssm_cb_c"] = f(inp["ssm_conv_b"]).reshape(24, 128)
    shared["ssm_ng_c"] = f(inp["ssm_norm_g"]).reshape(16, 128)
    shared["ssm_hv_bc"] = f(np.broadcast_to(np.stack([inp["ssm_dt_bias"], inp["ssm_a_log"], inp["ssm_d"]])[None], (128, 3, 32)))
    shared["ssm_cw_bc"] = f(np.broadcast_to(inp["ssm_conv_w"][None], (32, 4, 3072)))
    shared["ssm_cb_bc"] = f(np.broadcast_to(inp["ssm_conv_b"][None], (32, 3072)))
    shared["ssm_d_c"] = f(np.repeat(inp["ssm_d"], 64)).reshape(16, 128)
    sel = np.zeros((128, NS, 128), np.float32)
    for b in range(NS):
        sel[124 + b, b, :] = 1.0
    shared["ssm_sel"] = sel
    shared["att_w_in"] = f(inp["att_w_in"])
    shared["att_w_out"] = f(inp["att_w_out"])
    shared["att_kn_bc"] = f(np.broadcast_to(np.stack([inp["att_kn_g"], inp["att_kn_b"]])[None], (128, 2, 64)))
    inv = (np.float32(500000.0) ** (-np.arange(8, dtype=np.float32) / np.float32(8))).astype(np.float32)
    pos = np.concatenate([np.arange(SEQ, dtype=np.float32), np.full((128,), 8192.0, np.float32)])
    ang = (pos[:, None] * inv[None, :]).astype(np.float32)
    cs = np.concatenate([np.cos(ang), np.sin(ang)], 1).astype(np.float32)
    shared["rope"] = f(cs.reshape(17, 128, 16).transpose(1, 0, 2))
    shared["cache_k"] = f(inp["cache_k"]).reshape(2560 * 128, 256)
    shared["cache_v"] = f(inp["cache_v"]).reshape(2560 * 128, 256)
    shared["cache_ik"] = f(inp["cache_idx_k"]).reshape(2560 * 128, 64)
    shared["piota"] = np.arange(128, dtype=np.float32).reshape(128, 1)
    shared["negp"] = np.where(np.arange(128) == 0, 0.0, -1e30).astype(np.float32).reshape(128, 1)
    shared["rw_mu_c"] = f(inp["rw_mu"]).reshape(48, 128)
    shared["rw_gn_c"] = f(np.concatenate([inp["rw_gn_g"].reshape(8, 128), inp["rw_gn_b"].reshape(8, 128)], 0))
    shared["rw_vec_bc"] = f(np.broadcast_to(np.stack([inp["rw_w0"], inp["rw_a0"], inp["rw_k_k"], inp["rw_k_a"], inp["rw_r_k"].reshape(-1)])[None], (32, 5, D)))
    shared["rw_vec128"] = f(np.broadcast_to(np.stack([inp["rw_w0"], inp["rw_a0"], inp["rw_k_k"], inp["rw_k_a"], inp["rw_r_k"].reshape(-1),
                                                      inp["rw_gn_g"], inp["rw_gn_b"]])[None], (128, 7, D)))
    shared["rw_mbd"] = np.repeat(np.eye(16, dtype=np.float32), 64, axis=1)
    blk = np.zeros((128, 128), np.float32); blk[:64, :64] = 1.0 / 64; blk[64:, 64:] = 1.0 / 64
    shared["rw_blk"] = blk
    for n in ("r", "k", "v", "o"):
        shared["rw_w_" + n] = f(inp["rw_w_" + n])
    for n in ("w1", "w2", "a1", "a2", "g1", "g2"):
        shared["rw_" + n] = f(inp["rw_" + n])
    shared["negm"] = np.where(np.tril(np.ones((128, 128), dtype=bool)), 0.0, -1e30).astype(np.float32)
    shared["m1"] = np.tril(np.ones((128, 128), dtype=np.float32), -1)
    shared["onesf"] = np.ones((128, 128), dtype=np.float32)
    shared["ident"] = np.eye(128, dtype=np.float32)
    shared["cmask"] = np.triu(np.ones((128, 128), dtype=np.float32))
    maps = []
    for c in range(NCORES):
        m = dict(shared)
        m["xp"] = f(inp["x_prompt"][c])
        m["xs"] = f(inp["x_sample"][NS * c:NS * (c + 1), 0])
        m["pp"] = f(inp["p_prompt"][:, c])
        m["psm"] = f(inp["p_sample"][:, NS * c:NS * (c + 1), 0])
        m["pt_bc"] = np.ascontiguousarray(np.broadcast_to(inp["page_table"][NS * c:NS * (c + 1)].reshape(1, NS * 64), (128, NS * 64)).astype(np.int32))
        m["st_shift"] = f(inp["state_rwkv_shift"][NS * c:NS * (c + 1)])
        m["st_wkv"] = f(inp["state_rwkv_wkv"][NS * c:NS * (c + 1)])
        m["st_conv"] = f(inp["state_ssm_conv"][NS * c:NS * (c + 1)])
        m["st_ssm"] = f(inp["state_ssm"][NS * c:NS * (c + 1)]).reshape(NS, 2048, 128)
        maps.append(m)
    return maps


def kernel(**inp):
    nc = build()
    maps = make_in_maps(inp)
    res = run_bass_kernel_spmd(nc, maps, core_ids=list(range(NCORES)))
    R = res.results
    cat = lambda k: np.concatenate([R[c][k] for c in range(NCORES)], 0)
    stk = lambda k: np.stack([R[c][k] for c in range(NCORES)], 0)
    return (stk("yp"), cat("ys")[:, None, :], cat("gmv")[:, None, :],
            stk("conv_p"), stk("ssm_p"), cat("conv_s"), cat("ssm_s"),
            stk("k_p").reshape(8, SEQ, 4, 64), stk("v_p").reshape(8, SEQ, 4, 64), stk("ik_p"),
            cat("k_s").reshape(32, 1, 4, 64), cat("v_s").reshape(32, 1, 4, 64), cat("ik_s")[:, None, :],
            cat("sh_p"), stk("wkv_p"), cat("sh_s"), cat("wkv_s"))
```
